# Optimizing a Trainium2 kernel written in Bass

```python
import jax
import jax.numpy as jnp
from jax import lax

D_MODEL = 1024
BATCH = 8
SEQ = 2048
DEPTH = 2
DEC_BATCH = 128
DEC_SEQ = 8
PAST_LEN = 16384
PAGE_SIZE = 128

D_MIX = 2 * D_MODEL
D_FF = ((8 * D_MODEL // 3 + 127) // 128) * 128
CONV_W = 4
CHUNK = 64
DN_HEADS = 4
DN_DK = D_MIX // 4 // DN_HEADS
DN_DV = D_MIX // 4 // DN_HEADS
GLA_HEADS = 4
GLA_DV = D_MIX // 4 // GLA_HEADS
GLA_DK = GLA_DV // 2
GLA_RANK = 16
GLA_NORMALIZER = 16.0
M2_DIM = D_MIX // 2
M2_HEADDIM = 64
M2_HEADS = M2_DIM // M2_HEADDIM
M2_STATE = 128
M2_GROUPS = 2
EPS = 1e-6

DN_QK = DN_HEADS * DN_DK
DN_V = DN_HEADS * DN_DV
GLA_QK = GLA_HEADS * GLA_DK
GLA_V = GLA_HEADS * GLA_DV
M2_BC = M2_GROUPS * M2_STATE
CONV_SIZES = (DN_QK, DN_QK, DN_V, M2_DIM, M2_BC, M2_BC)
CONV_CH = sum(CONV_SIZES)
OTHER_SIZES = (DN_V, DN_HEADS, DN_HEADS, GLA_QK, GLA_QK, GLA_V, GLA_RANK, GLA_V, M2_DIM, M2_HEADS)
IN_COLS = CONV_CH + sum(OTHER_SIZES)

kernel_name = 'hybrid_deltanet_gla_ssd_macaron_step'


def rmsnorm(x, g):
    xf = x.astype(jnp.float32)
    y = xf * lax.rsqrt(jnp.mean(xf * xf, axis=-1, keepdims=True) + EPS)
    return (y * g.astype(jnp.float32)).astype(x.dtype)


def l2norm(x):
    xf = x.astype(jnp.float32)
    return xf * lax.rsqrt(jnp.sum(xf * xf, axis=-1, keepdims=True) + EPS)


def split_cols(x, sizes):
    offs, acc = [], 0
    for s in sizes[:-1]:
        acc += s
        offs.append(acc)
    return jnp.split(x, offs, axis=-1)


def swiglu_ffn(x, norm_g, w_gu, w_down):
    gate, up = jnp.split(rmsnorm(x, norm_g) @ w_gu, 2, axis=-1)
    return (jax.nn.silu(gate) * up) @ w_down


def causal_dwconv(x, prev, w, b):
    seq = x.shape[1]
    xp = jnp.concatenate([prev.astype(x.dtype), x], axis=1)
    y = lax.conv_general_dilated(xp, w[:, None, :].astype(x.dtype), (1,), 'VALID',
                                 dimension_numbers=('NWC', 'WIO', 'NWC'),
                                 feature_group_count=x.shape[-1])
    return jax.nn.silu(y + b.astype(x.dtype)), xp[:, seq:]


def to_chunks(x, c):
    b, l = x.shape[:2]
    nc = -(-l // c)
    x = jnp.pad(x, [(0, 0), (0, nc * c - l)] + [(0, 0)] * (x.ndim - 2))
    x = x.reshape(b, nc, c, *x.shape[2:])
    return jnp.moveaxis(jnp.moveaxis(x, 1, 0), 2, 3)


def from_chunks(x, l):
    x = jnp.moveaxis(jnp.moveaxis(x, 3, 2), 0, 1)
    b, nc, c = x.shape[:3]
    return x.reshape(b, nc * c, *x.shape[3:])[:, :l]


def gated_delta_chunked(q, k, v, g, beta, s0):
    seq = q.shape[1]
    c = min(CHUNK, seq)
    xs = (to_chunks(q * DN_DK ** -0.5, c), to_chunks(k, c), to_chunks(v, c),
          to_chunks(g, c), to_chunks(beta, c))
    incl = jnp.tril(jnp.ones((c, c), bool))
    strict = jnp.tril(jnp.ones((c, c), bool), -1)
    eye = jnp.eye(c, dtype=jnp.float32)

    def step(s, inp):
        q_, k_, v_, g_, b_ = inp
        gc = jnp.cumsum(g_, axis=-1)
        decay = jnp.exp(jnp.where(incl, gc[..., :, None] - gc[..., None, :], -jnp.inf))
        kk = jnp.einsum('bhid,bhjd->bhij', k_, k_)
        a_mat = jnp.where(strict, kk * decay, 0.0) * b_[..., None]
        rhs = jnp.concatenate([(b_ * jnp.exp(gc))[..., None] * k_, b_[..., None] * v_], axis=-1)
        sol = lax.linalg.triangular_solve(eye + a_mat, rhs, left_side=True, lower=True)
        w_, u_ = sol[..., :DN_DK], sol[..., DN_DK:]
        v_new = u_ - jnp.einsum('bhcd,bhde->bhce', w_, s)
        qk = jnp.einsum('bhid,bhjd->bhij', q_, k_) * decay
        o = (jnp.einsum('bhcd,bhde->bhce', q_ * jnp.exp(gc)[..., None], s)
             + jnp.einsum('bhij,bhje->bhie', qk, v_new))
        glast = gc[..., -1]
        s = (jnp.exp(glast)[..., None, None] * s
             + jnp.einsum('bhcd,bhce->bhde', k_ * jnp.exp(glast[..., None] - gc)[..., None], v_new))
        return s, o

    s, oc = lax.scan(step, s0.astype(jnp.float32), xs)
    return from_chunks(oc, seq), s


def gla_chunked(q, k, v, gk, s0):
    seq = q.shape[1]
    c = min(CHUNK, seq)
    xs = (to_chunks(q * GLA_DK ** -0.5, c), to_chunks(k, c), to_chunks(v, c), to_chunks(gk, c))
    incl = jnp.tril(jnp.ones((c, c), bool))

    def step(s, inp):
        q_, k_, v_, g_ = inp
        gc = jnp.cumsum(g_, axis=-2)
        diff = jnp.where(incl[:, :, None], gc[:, :, :, None, :] - gc[:, :, None, :, :], -jnp.inf)
        scores = jnp.einsum('bhid,bhijd,bhjd->bhij', q_, jnp.exp(diff), k_)
        o = (jnp.einsum('bhid,bhde->bhie', q_ * jnp.exp(gc), s)
             + jnp.einsum('bhij,bhje->bhie', scores, v_))
        glast = gc[:, :, -1]
        s = (jnp.exp(glast)[..., None] * s
             + jnp.einsum('bhcd,bhce->bhde', k_ * jnp.exp(glast[:, :, None] - gc), v_))
        return s, o

    s, oc = lax.scan(step, s0.astype(jnp.float32), xs)
    return from_chunks(oc, seq), s


def ssd_chunked(x, dt, a, bm, cm, s0):
    bsz, seq, nh, hd = x.shape
    hpg = nh // M2_GROUPS
    c = min(CHUNK, seq)
    xc = to_chunks(x * dt[..., None], c)
    nc = xc.shape[0]
    xc = xc.reshape(nc, bsz, M2_GROUPS, hpg, c, hd)
    lc = to_chunks(dt * a, c).reshape(nc, bsz, M2_GROUPS, hpg, c)
    xs = (xc, lc, to_chunks(bm, c), to_chunks(cm, c))
    incl = jnp.tril(jnp.ones((c, c), bool))

    def step(s, inp):
        x_, l_, b_, c_ = inp
        lcum = jnp.cumsum(l_, axis=-1)
        decay = jnp.exp(jnp.where(incl, lcum[..., :, None] - lcum[..., None, :], -jnp.inf))
        cb = jnp.einsum('bgin,bgjn->bgij', c_, b_)
        y = (jnp.einsum('bghij,bghjp->bghip', cb[:, :, None] * decay, x_)
             + jnp.einsum('bgin,bghnp->bghip', c_, s) * jnp.exp(lcum)[..., None])
        llast = lcum[..., -1]
        s = (jnp.exp(llast)[..., None, None] * s
             + jnp.einsum('bgjn,bghjp->bghnp', b_, x_ * jnp.exp(llast[..., None] - lcum)[..., None]))
        return s, y

    s0g = s0.astype(jnp.float32).reshape(bsz, M2_GROUPS, hpg, M2_STATE, hd)
    s, yc = lax.scan(step, s0g, xs)
    yc = yc.reshape(nc, bsz, nh, c, hd)
    return from_chunks(yc, seq), s.reshape(bsz, nh, M2_STATE, hd)


def hybrid_layer(x, conv_prev, s_dn, s_gla, s_ssm, lp):
    f32 = jnp.float32
    bsz, seq = x.shape[:2]
    x = x + 0.5 * swiglu_ffn(x, lp['ffn1_norm'], lp['ffn1_w_gu'], lp['ffn1_w_down'])
    proj = rmsnorm(x, lp['mix_norm']) @ lp['w_in']
    conv_in, other = proj[..., :CONV_CH], proj[..., CONV_CH:]
    conv_out, conv_new = causal_dwconv(conv_in, conv_prev, lp['conv_w'], lp['conv_b'])
    dq, dk, dv, mx, mb, mc = split_cols(conv_out, CONV_SIZES)
    dz, da, db, gq, gkk, gv, glow, gg, mz, mdt = split_cols(other, OTHER_SIZES)

    q = l2norm(dq.reshape(bsz, seq, DN_HEADS, DN_DK))
    k = l2norm(dk.reshape(bsz, seq, DN_HEADS, DN_DK))
    v = dv.reshape(bsz, seq, DN_HEADS, DN_DV).astype(f32)
    g = -jnp.exp(lp['dn_a_log'].astype(f32)) * jax.nn.softplus(da.astype(f32) + lp['dn_dt_bias'].astype(f32))
    beta = jax.nn.sigmoid(db.astype(f32))
    o_dn, s_dn_new = gated_delta_chunked(q, k, v, g, beta, s_dn)
    o_dn = rmsnorm(o_dn, lp['dn_norm']) * jax.nn.silu(dz.reshape(bsz, seq, DN_HEADS, DN_DV).astype(f32))

    gk = jax.nn.log_sigmoid((glow @ lp['gla_w_up'] + lp['gla_b_up']).astype(f32)) / GLA_NORMALIZER
    o_gla, s_gla_new = gla_chunked(gq.reshape(bsz, seq, GLA_HEADS, GLA_DK).astype(f32),
                                   gkk.reshape(bsz, seq, GLA_HEADS, GLA_DK).astype(f32),
                                   gv.reshape(bsz, seq, GLA_HEADS, GLA_DV).astype(f32),
                                   gk.reshape(bsz, seq, GLA_HEADS, GLA_DK), s_gla)
    o_gla = rmsnorm(o_gla, lp['gla_norm']) * jax.nn.silu(gg.reshape(bsz, seq, GLA_HEADS, GLA_DV).astype(f32))

    xh = mx.reshape(bsz, seq, M2_HEADS, M2_HEADDIM).astype(f32)
    dt = jax.nn.softplus(mdt.astype(f32) + lp['m2_dt_bias'].astype(f32))
    a = -jnp.exp(lp['m2_a_log'].astype(f32))
    y, s_ssm_new = ssd_chunked(xh, dt, a,
                               mb.reshape(bsz, seq, M2_GROUPS, M2_STATE).astype(f32),
                               mc.reshape(bsz, seq, M2_GROUPS, M2_STATE).astype(f32), s_ssm)
    y = y + lp['m2_d'].astype(f32)[:, None] * xh
    o_m2 = rmsnorm(y.reshape(bsz, seq, M2_DIM) * jax.nn.silu(mz.astype(f32)), lp['m2_norm'])

    mix = jnp.concatenate([o_dn.reshape(bsz, seq, DN_V), o_gla.reshape(bsz, seq, GLA_V), o_m2], axis=-1)
    x = x + mix.astype(x.dtype) @ lp['w_out']
    x = x + 0.5 * swiglu_ffn(x, lp['ffn2_norm'], lp['ffn2_w_gu'], lp['ffn2_w_down'])
    return x, conv_new, s_dn_new, s_gla_new, s_ssm_new


def _dt_bias(key, shape):
    dt = jnp.exp(jax.random.uniform(key, shape, dtype=jnp.float32,
                                    minval=jnp.log(1e-3), maxval=jnp.log(1e-1)))
    return dt + jnp.log(-jnp.expm1(-dt))


def setup_inputs(seed: int = 0) -> dict:
    key = jax.random.key(seed)
    ks = jax.random.split(key, 32)
    f32 = jnp.float32

    def nrm(k, shape, scale):
        return scale * jax.random.normal(k, shape, f32)

    def gain(k, shape):
        return 1.0 + 0.02 * jax.random.normal(k, shape, f32)

    L = DEPTH
    return {
        'x_prompt': nrm(ks[0], (BATCH, SEQ, D_MODEL), 1.0),
        'x_sample': nrm(ks[1], (DEC_BATCH, DEC_SEQ, D_MODEL), 1.0),
        'state_conv': nrm(ks[2], (L, DEC_BATCH, CONV_W - 1, CONV_CH), 1.0),
        'state_delta': nrm(ks[3], (L, DEC_BATCH, DN_HEADS, DN_DK, DN_DV), 0.1),
        'state_gla': nrm(ks[4], (L, DEC_BATCH, GLA_HEADS, GLA_DK, GLA_DV), 0.3),
        'state_ssm': nrm(ks[5], (L, DEC_BATCH, M2_HEADS, M2_STATE, M2_HEADDIM), 0.1),
        'ffn1_norm': gain(ks[6], (L, D_MODEL)),
        'ffn1_w_gu': nrm(ks[7], (L, D_MODEL, 2 * D_FF), D_MODEL ** -0.5),
        'ffn1_w_down': nrm(ks[8], (L, D_FF, D_MODEL), D_FF ** -0.5),
        'mix_norm': gain(ks[9], (L, D_MODEL)),
        'w_in': nrm(ks[10], (L, D_MODEL, IN_COLS), D_MODEL ** -0.5),
        'conv_w': nrm(ks[11], (L, CONV_W, CONV_CH), CONV_W ** -0.5),
        'conv_b': nrm(ks[12], (L, CONV_CH), 0.02),
        'dn_a_log': jnp.log(jax.random.uniform(ks[13], (L, DN_HEADS), dtype=f32, minval=1.0, maxval=16.0)),
        'dn_dt_bias': _dt_bias(ks[14], (L, DN_HEADS)),
        'dn_norm': gain(ks[15], (L, DN_DV)),
        'gla_w_up': nrm(ks[16], (L, GLA_RANK, GLA_QK), GLA_RANK ** -0.5),
        'gla_b_up': nrm(ks[17], (L, GLA_QK), 0.1),
        'gla_norm': gain(ks[18], (L, GLA_DV)),
        'm2_a_log': jnp.log(jax.random.uniform(ks[19], (L, M2_HEADS), dtype=f32, minval=1.0, maxval=16.0)),
        'm2_dt_bias': _dt_bias(ks[20], (L, M2_HEADS)),
        'm2_d': gain(ks[21], (L, M2_HEADS)),
        'm2_norm': gain(ks[22], (L, M2_DIM)),
        'w_out': nrm(ks[23], (L, D_MIX, D_MODEL), D_MIX ** -0.5),
        'ffn2_norm': gain(ks[24], (L, D_MODEL)),
        'ffn2_w_gu': nrm(ks[25], (L, D_MODEL, 2 * D_FF), D_MODEL ** -0.5),
        'ffn2_w_down': nrm(ks[26], (L, D_FF, D_MODEL), D_FF ** -0.5),
        'final_norm': gain(ks[27], (D_MODEL,)),
    }


def reference(x_prompt, x_sample, state_conv, state_delta, state_gla, state_ssm,
              ffn1_norm, ffn1_w_gu, ffn1_w_down, mix_norm, w_in, conv_w, conv_b,
              dn_a_log, dn_dt_bias, dn_norm, gla_w_up, gla_b_up, gla_norm,
              m2_a_log, m2_dt_bias, m2_d, m2_norm, w_out,
              ffn2_norm, ffn2_w_gu, ffn2_w_down, final_norm):
    f32 = jnp.float32
    bp = x_prompt.shape[0]
    zero_conv = jnp.zeros((bp, CONV_W - 1, CONV_CH), x_prompt.dtype)
    zero_dn = jnp.zeros((bp, DN_HEADS, DN_DK, DN_DV), f32)
    zero_gla = jnp.zeros((bp, GLA_HEADS, GLA_DK, GLA_DV), f32)
    zero_ssm = jnp.zeros((bp, M2_HEADS, M2_STATE, M2_HEADDIM), f32)

    hp, hs = x_prompt, x_sample
    pc, pd, pg, ps = [], [], [], []
    sc, sd, sg, ss = [], [], [], []
    for l in range(DEPTH):
        lp = {
            'ffn1_norm': ffn1_norm[l], 'ffn1_w_gu': ffn1_w_gu[l], 'ffn1_w_down': ffn1_w_down[l],
            'mix_norm': mix_norm[l], 'w_in': w_in[l], 'conv_w': conv_w[l], 'conv_b': conv_b[l],
            'dn_a_log': dn_a_log[l], 'dn_dt_bias': dn_dt_bias[l], 'dn_norm': dn_norm[l],
            'gla_w_up': gla_w_up[l], 'gla_b_up': gla_b_up[l], 'gla_norm': gla_norm[l],
            'm2_a_log': m2_a_log[l], 'm2_dt_bias': m2_dt_bias[l], 'm2_d': m2_d[l], 'm2_norm': m2_norm[l],
            'w_out': w_out[l],
            'ffn2_norm': ffn2_norm[l], 'ffn2_w_gu': ffn2_w_gu[l], 'ffn2_w_down': ffn2_w_down[l],
        }
        hp, c_, d_, g_, s_ = hybrid_layer(hp, zero_conv, zero_dn, zero_gla, zero_ssm, lp)
        pc.append(c_); pd.append(d_); pg.append(g_); ps.append(s_)
        hs, c_, d_, g_, s_ = hybrid_layer(hs, state_conv[l], state_delta[l], state_gla[l], state_ssm[l], lp)
        sc.append(c_); sd.append(d_); sg.append(g_); ss.append(s_)

    y_prompt = rmsnorm(hp, final_norm)
    y_sample = rmsnorm(hs, final_norm)
    new_conv_prompt = jnp.stack(pc, axis=0)
    new_delta_prompt = jnp.stack(pd, axis=0)
    new_gla_prompt = jnp.stack(pg, axis=0)
    new_ssm_prompt = jnp.stack(ps, axis=0)
    new_conv_sample = jnp.stack(sc, axis=0)
    new_delta_sample = jnp.stack(sd, axis=0)
    new_gla_sample = jnp.stack(sg, axis=0)
    new_ssm_sample = jnp.stack(ss, axis=0)
    return (y_prompt, y_sample, new_conv_prompt, new_delta_prompt, new_gla_prompt, new_ssm_prompt,
            new_conv_sample, new_delta_sample, new_gla_sample, new_ssm_sample)
```

```python
import numpy as np
import concourse.bass as bass
import concourse.mybir as mybir
from concourse.bass_utils import run_bass_kernel_spmd

F32 = mybir.dt.float32
BF16 = mybir.dt.bfloat16
AF = mybir.ActivationFunctionType
ALU = mybir.AluOpType
AX = mybir.AxisListType

D = 1024
DFF = 2816
EPS = 1e-6
NEG = -30000.0
CONV_CH = 3072
IN_COLS = 6184
O_DQ, O_DK, O_DV, O_MX, O_MB, O_MC = 0, 512, 1024, 1536, 2560, 2816
O_DZ, O_DA, O_DB, O_GQ, O_GK, O_GV, O_GLOW, O_GG, O_MZ, O_MDT = 3072, 3584, 3588, 3592, 3848, 4104, 4616, 4632, 5144, 6168

C_IDENT = 0
C_ONES = 128
C_U01 = 256
C_NUI = 384
C_NUS = 512
C_NLS = 640
C_SEG = 768
C_SU01 = 896
C_SNUI = 1024
C_SNUS = 1152
C_SNLS = 1280
C_ESEL = 1408
C_ROWM = 1920
C_ROWM8 = 1936
C_INV128 = 1952
NCONST = 1984

P_DN_ALOG, P_DN_DTB, P_M2_ALOG, P_M2_DTB, P_M2_D = 0, 4, 8, 24, 40
P_DN_NORM, P_GLA_NORM, P_M2_NORM = 64, 192, 320
NPAR = 320


def make_consts():
    c = np.zeros((128, NCONST), np.float32)
    p = np.arange(128)[:, None]
    f = np.arange(128)[None, :]
    seg = (p // 8) == (f // 8)
    c[:, C_IDENT:C_IDENT + 128] = (p == f)
    c[:, C_ONES:C_ONES + 128] = 1.0
    c[:, C_U01:C_U01 + 128] = (p <= f)
    c[:, C_NUI:C_NUI + 128] = np.where(p <= f, 0.0, NEG)
    c[:, C_NUS:C_NUS + 128] = np.where(p < f, 0.0, NEG)
    c[:, C_NLS:C_NLS + 128] = np.where(p > f, 0.0, NEG)
    c[:, C_SEG:C_SEG + 128] = seg
    c[:, C_SU01:C_SU01 + 128] = seg & (p <= f)
    c[:, C_SNUI:C_SNUI + 128] = np.where(seg & (p <= f), 0.0, NEG)
    c[:, C_SNUS:C_SNUS + 128] = np.where(seg & (p < f), 0.0, NEG)
    c[:, C_SNLS:C_SNLS + 128] = np.where(seg & (p > f), 0.0, NEG)
    for h in range(4):
        c[h, C_ESEL + h * 128:C_ESEL + (h + 1) * 128] = 1.0
    s = np.arange(16)[None, :]
    c[:, C_ROWM:C_ROWM + 16] = ((p // 8) == s)
    c[:, C_ROWM8:C_ROWM8 + 16] = ((p // 8) == s) / 8.0
    c[:, C_INV128] = 1.0 / 128
    return c


class TT:
    def __init__(self, t, key):
        self.t = t
        self.key = key

    def __getitem__(self, idx):
        return self.t[idx]


class Prog:
    def __init__(self, nc):
        self.nc = nc
        self.engs = {'pe': nc.tensor, 'act': nc.scalar, 'dve': nc.vector, 'pool': nc.gpsimd, 'sp': nc.sync}
        self.esem = {e: nc.alloc_semaphore("sem_" + e) for e in self.engs}
        self.cnt = {e: 0 for e in self.engs}
        self.seen = {e: {} for e in self.engs}
        self.lastw = {}
        self.readers = {}
        self.dsem = {}
        self.dcnt = {}
        self.dtot = {}
        self.nwaits = 0
        self.nops = 0
        self.nuid = 0

    def sb(self, name, shape, dtype=F32):
        self.nuid += 1
        nm = "%s_%d" % (name, self.nuid)
        return TT(self.nc.alloc_sbuf_tensor(nm, list(shape), dtype), nm)

    @staticmethod
    def _keys(lst):
        out = []
        for x in lst:
            k = x.key if isinstance(x, TT) else x
            if isinstance(k, (tuple, list)):
                out.extend(k)
            else:
                out.append(k)
        return out

    def _wait(self, eng, ev):
        sem, val = ev
        k = id(sem)
        if k in self.dtot:
            val = max(val, self.dtot[k])
        if self.seen[eng].get(k, 0) >= val:
            return
        if sem is self.esem[eng] and val > self.cnt[eng]:
            return
        self.seen[eng][k] = val
        self.engs[eng].wait_ge(sem, val)
        self.nwaits += 1

    def _deps(self, eng, reads, writes, skipsem=None):
        evs = []
        for k in reads:
            if k in self.lastw:
                evs.append(self.lastw[k])
        for k in writes:
            if k in self.lastw and self.lastw[k][0] is not skipsem:
                evs.append(self.lastw[k])
            for ev in self.readers.get(k, {}).values():
                evs.append(ev)
        for ev in evs:
            self._wait(eng, ev)

    def _record(self, ev, reads, writes):
        for k in reads:
            self.readers.setdefault(k, {})[id(ev[0])] = ev
        for k in writes:
            self.lastw[k] = ev
            self.readers[k] = {}

    def op(self, eng, fn, reads=(), writes=(), inc=True):
        reads = self._keys(reads)
        writes = self._keys(writes)
        self._deps(eng, reads, writes)
        ins = fn(self.engs[eng])
        self.nops += 1
        if inc:
            self.cnt[eng] += 1
            ins.then_inc(self.esem[eng], 1)
            ev = (self.esem[eng], self.cnt[eng])
        else:
            ev = (self.esem[eng], self.cnt[eng] + 1)
        self._record(ev, reads, writes)
        return ins

    def dma(self, eng, out, in_, group, reads=(), writes=()):
        reads = self._keys(reads)
        writes = self._keys(writes)
        if group not in self.dsem:
            self.dsem[group] = self.nc.alloc_semaphore("dsem_%d" % len(self.dsem))
            self.dcnt[group] = 0
        sem = self.dsem[group]
        self._deps(eng, reads, writes, skipsem=sem)
        ins = self.engs[eng].dma_start(out=out, in_=in_)
        self.nops += 1
        self.dcnt[group] += 16
        self.dtot[id(sem)] = self.dcnt[group]
        ins.then_inc(sem, 16)
        ev = (sem, self.dcnt[group])
        self._record(ev, reads, writes)
        return ins

    def finish(self, eng='sp'):
        for g, sem in self.dsem.items():
            self._wait(eng, (sem, self.dcnt[g]))
        for e in self.engs:
            if e != eng and self.cnt[e] > 0:
                self._wait(eng, (self.esem[e], self.cnt[e]))


class Ctx:
    pass


def build(NPT=16, DEPTH=2, stages=('ffn', 'mix')):
    NT = NPT + 1
    NTOK = NT * 128
    nc = bass.Bass("TRN2", target_bir_lowering=False)
    P = Prog(nc)
    g = Ctx()
    g.nc, g.P, g.NPT, g.NT, g.DEPTH = nc, P, NPT, NT, DEPTH
    g.mixers = [m for m in ('delta', 'gla', 'ssd') if m in stages or 'mix' in stages]

    def din(name, shape):
        return nc.dram_tensor(name, list(shape), F32, kind="ExternalInput").ap()

    def dout(name, shape):
        return nc.dram_tensor(name, list(shape), F32, kind="ExternalOutput").ap()

    g.xin = din("xin", [NTOK, D])
    g.consts_d = din("consts", [128, NCONST])
    g.params_d = din("params", [DEPTH, NPAR])
    g.st_conv = din("st_conv", [DEPTH, 48, CONV_CH])
    g.st_delta = din("st_delta", [DEPTH, 16, 4, 128, 128])
    g.st_gla = din("st_gla", [DEPTH, 16, 4, 64, 128])
    g.st_ssm = din("st_ssm", [DEPTH, 16, 16, 128, 64])
    g.norms_d = din("norms", [4 * DEPTH + 1, D])
    g.w_gu = [din("ffn1_w_gu", [DEPTH, D, 2 * DFF]), din("ffn2_w_gu", [DEPTH, D, 2 * DFF])]
    g.w_down = [din("ffn1_w_down", [DEPTH, DFF, D]), din("ffn2_w_down", [DEPTH, DFF, D])]
    g.w_in = din("w_in", [DEPTH, D, IN_COLS])
    g.w_out = din("w_out", [DEPTH, 2 * D, D])
    g.convw_d = din("convw", [DEPTH, 128, 24 * 4])
    g.convb_d = din("convb", [DEPTH, 128, 24])
    g.gla_wup = din("gla_w_up", [DEPTH, 16, 256])
    g.gla_bup = din("gla_b_up", [DEPTH, 1, 256])

    g.y = dout("y", [NTOK, D])
    g.o_conv_p = dout("o_conv_p", [DEPTH, 3, CONV_CH])
    g.o_delta_p = dout("o_delta_p", [DEPTH, 4, 128, 128])
    g.o_gla_p = dout("o_gla_p", [DEPTH, 4, 64, 128])
    g.o_ssm_p = dout("o_ssm_p", [DEPTH, 16, 128, 64])
    g.o_conv_s = dout("o_conv_s", [DEPTH, 48, CONV_CH])
    g.o_delta_s = dout("o_delta_s", [DEPTH, 16, 4, 128, 128])
    g.o_gla_s = dout("o_gla_s", [DEPTH, 16, 4, 64, 128])
    g.o_ssm_s = dout("o_ssm_s", [DEPTH, 16, 16, 128, 64])

    g.x = P.sb("x", [128, NT, D])
    g.xk = ["x_t%d" % t for t in range(NT)]
    g.consts = P.sb("consts", [128, NCONST])
    g.identb = P.sb("identb", [128, 128], BF16)
    g.stat = P.sb("stat", [128, 64])
    g.ps = [TT(nc.alloc_psum_tensor("psb%d" % i, [128, 512], F32), "psb%d" % i) for i in range(8)]
    g.psi = 0
    g.held = set()

    P.dma('sp', g.consts[:, :], g.consts_d, 'consts', writes=[g.consts])
    for t in range(NT):
        P.dma('sp', g.x[:, t, :], g.xin[t * 128:(t + 1) * 128, :], 'xload', writes=[g.xk[t]])
    P.op('dve', lambda e: e.tensor_copy(g.identb[:, :], g.consts[:, C_IDENT:C_IDENT + 128]), reads=[g.consts], writes=[g.identb])

    for l in range(DEPTH):
        if 'ffn' in stages:
            ffn_phase(g, l, 0)
        if g.mixers:
            mix_phase(g, l)
        if 'ffn' in stages:
            ffn_phase(g, l, 1)
    final_phase(g)
    P.finish('sp')
    g.stats = (P.nops, P.nwaits)
    return nc, g


def nextbank(g, hold=False):
    while True:
        b = g.ps[g.psi % 8]
        g.psi += 1
        if b.key not in g.held:
            break
    if hold:
        g.held.add(b.key)
    return b


def release(g, *banks):
    for b in banks:
        g.held.discard(b.key)


def barrier(g):
    P = g.P
    for e in P.engs:
        for gname, sem in P.dsem.items():
            P._wait(e, (sem, P.dcnt[gname]))
        for e2 in P.engs:
            if e2 != e and P.cnt[e2] > 0:
                P._wait(e, (P.esem[e2], P.cnt[e2]))


def norm_tiles(g, tiles, gain_row, hT, hn, gB, tok0=0):
    P, nc = g.P, g.nc
    n = len(tiles)
    P.dma('sp', gB[:, :], g.norms_d[gain_row:gain_row + 1, :].partition_broadcast(128), 'gB', writes=[gB])
    junk = hn[0]
    for i, t in enumerate(tiles):
        P.op('act', lambda e: e.activation(junk[:, :], g.x[:, t, :], AF.Square, accum_out=g.stat[:, i:i + 1]),
             reads=[g.xk[t]], writes=[junk, g.stat])
    P.op('act', lambda e: e.activation(g.stat[:, 16:16 + n], g.stat[:, 0:n], AF.Ln, bias=EPS, scale=1.0 / D),
         reads=[g.stat], writes=[g.stat])
    P.op('act', lambda e: e.activation(g.stat[:, 32:32 + n], g.stat[:, 16:16 + n], AF.Exp, scale=-0.5),
         reads=[g.stat], writes=[g.stat])
    for i, t in enumerate(tiles):
        h = hn[i % 2]
        P.op('dve', lambda e: e.scalar_tensor_tensor(h[:, :], g.x[:, t, :], g.stat[:, 32 + i:33 + i], gB[:, :], ALU.mult, ALU.mult),
             reads=[g.xk[t], g.stat, gB], writes=[h])
        bank = nextbank(g)
        pb = bank.t[:, :].bitcast(BF16)
        for kc in range(8):
            P.op('pe', lambda e: e.transpose(pb[:, kc * 128:(kc + 1) * 128], h[:, kc * 128:(kc + 1) * 128], g.identb[:, :]),
                 reads=[h, g.identb], writes=[bank], inc=(kc == 7))
        c0 = tok0 + i * 128
        P.op('act', lambda e: e.activation(hT[:, :, c0:c0 + 128], pb.rearrange("p (k n) -> p k n", k=8), AF.Copy),
             reads=[bank], writes=[hT])


def ffn_phase(g, l, which):
    P, nc = g.P, g.nc
    NT = g.NT
    half = (NT + 1) // 2
    sbs = [list(range(0, NT - half)), list(range(NT - half, NT))]
    SBW = half * 128
    barrier(g)
    w_gu = g.w_gu[which][l]
    w_down = g.w_down[which][l]
    tag = "f%d_%d_" % (l, which)
    with nc.sbuf_tensor(tag + "hT", [128, 8, SBW], BF16) as hT_, \
            nc.sbuf_tensor(tag + "actT", [128, 22, SBW], BF16) as actT_, \
            nc.sbuf_tensor(tag + "wgu0", [128, 8, 512], BF16) as wgu0, nc.sbuf_tensor(tag + "wgu1", [128, 8, 512], BF16) as wgu1, \
            nc.sbuf_tensor(tag + "wd0", [128, 22, 256], BF16) as wd0, nc.sbuf_tensor(tag + "wd1", [128, 22, 256], BF16) as wd1, \
            nc.sbuf_tensor(tag + "gB", [128, D], F32) as gB_, \
            nc.sbuf_tensor(tag + "hn0", [128, D], BF16) as hn0, nc.sbuf_tensor(tag + "hn1", [128, D], BF16) as hn1, \
            nc.sbuf_tensor(tag + "sg0", [128, 512], F32) as sg0, nc.sbuf_tensor(tag + "sg1", [128, 512], F32) as sg1:
        hT = TT(hT_, tag + "hT")
        actT = TT(actT_, tag + "actT")
        wgu = [TT(wgu0, tag + "wgu0"), TT(wgu1, tag + "wgu1")]
        wd = [TT(wd0, tag + "wd0"), TT(wd1, tag + "wd1")]
        gB = TT(gB_, tag + "gB")
        hn = [TT(hn0, tag + "hn0"), TT(hn1, tag + "hn1")]
        sg = [TT(sg0, tag + "sg0"), TT(sg1, tag + "sg1")]
        nsg = 0
        for sbi, tiles in enumerate(sbs):
            ntok = len(tiles) * 128
            norm_tiles(g, tiles, 3 * l + (0 if which == 0 else 2), hT, hn, gB)
            tgs = [(c0, min(512, ntok - c0)) for c0 in range(0, ntok, 512)]
            for j in range(11):
                wb = wgu[j % 2]
                P.dma('pool', wb[:, :, 0:256], w_gu[:, 256 * j:256 * j + 256].rearrange("(k p) n -> p k n", p=128),
                      wb.key, writes=[wb])
                P.dma('pool', wb[:, :, 256:512], w_gu[:, DFF + 256 * j:DFF + 256 * j + 256].rearrange("(k p) n -> p k n", p=128),
                      wb.key, writes=[wb])
                for c in range(2):
                    for (c0, w) in tgs:
                        bA = nextbank(g)
                        bB = nextbank(g)
                        for kc in range(8):
                            P.op('pe', lambda e: e.matmul(bA.t[:, 0:w], wb[:, kc, c * 128:(c + 1) * 128], hT[:, kc, c0:c0 + w],
                                                          start=(kc == 0), stop=(kc == 7)),
                                 reads=[wb, hT], writes=[bA], inc=(kc == 7))
                        for kc in range(8):
                            P.op('pe', lambda e: e.matmul(bB.t[:, 0:w], wb[:, kc, 256 + c * 128:256 + (c + 1) * 128], hT[:, kc, c0:c0 + w],
                                                          start=(kc == 0), stop=(kc == 7)),
                                 reads=[wb, hT], writes=[bB], inc=(kc == 7))
                        s_ = sg[nsg % 2]
                        nsg += 1
                        P.op('act', lambda e: e.activation(s_[:, 0:w], bA.t[:, 0:w], AF.Silu), reads=[bA], writes=[s_])
                        P.op('dve', lambda e: e.tensor_tensor(actT[:, 2 * j + c, c0:c0 + w], s_[:, 0:w], bB.t[:, 0:w], ALU.mult),
                             reads=[s_, bB], writes=[actT])
            for q in range(4):
                wq = wd[q % 2]
                P.dma('pool', wq[:, :, :], w_down[:, 256 * q:256 * q + 256].rearrange("(k p) n -> p k n", p=128),
                      wq.key, writes=[wq])
                for i, t in enumerate(tiles):
                    bD = nextbank(g)
                    for kc in range(22):
                        P.op('pe', lambda e: e.matmul(bD.t[:, 0:256], actT[:, kc, i * 128:(i + 1) * 128], wq[:, kc, :],
                                                      start=(kc == 0), stop=(kc == 21)),
                             reads=[actT, wq], writes=[bD], inc=(kc == 21))
                    P.op('dve', lambda e: e.scalar_tensor_tensor(g.x[:, t, 256 * q:256 * q + 256], bD.t[:, 0:256], 0.5,
                                                                 g.x[:, t, 256 * q:256 * q + 256], ALU.mult, ALU.add),
                         reads=[bD, g.xk[t]], writes=[g.xk[t]])
        barrier(g)


def final_phase(g):
    P, nc = g.P, g.nc
    NT = g.NT
    barrier(g)
    with nc.sbuf_tensor("fin_gB", [128, D], F32) as gB_, nc.sbuf_tensor("fin_o0", [128, D], F32) as o0, \
            nc.sbuf_tensor("fin_o1", [128, D], F32) as o1:
        gB = TT(gB_, "fin_gB")
        ob = [TT(o0, "fin_o0"), TT(o1, "fin_o1")]
        P.dma('sp', gB[:, :], g.norms_d[3 * g.DEPTH:3 * g.DEPTH + 1, :].partition_broadcast(128), 'fin_gB', writes=[gB])
        for t in range(NT):
            o = ob[t % 2]
            P.op('act', lambda e: e.activation(o[:, :], g.x[:, t, :], AF.Square, accum_out=g.stat[:, 0:1]),
                 reads=[g.xk[t]], writes=[o, g.stat])
            P.op('act', lambda e: e.activation(g.stat[:, 1:2], g.stat[:, 0:1], AF.Ln, bias=EPS, scale=1.0 / D),
                 reads=[g.stat], writes=[g.stat])
            P.op('act', lambda e: e.activation(g.stat[:, 2:3], g.stat[:, 1:2], AF.Exp, scale=-0.5),
                 reads=[g.stat], writes=[g.stat])
            P.op('dve', lambda e: e.scalar_tensor_tensor(o[:, :], g.x[:, t, :], g.stat[:, 2:3], gB[:, :], ALU.mult, ALU.mult),
                 reads=[g.xk[t], g.stat, gB], writes=[o])
            P.dma('sp', g.y[t * 128:(t + 1) * 128, :], o[:, :], o.key + "_out", reads=[o])
        barrier(g)


class Scope:
    def __init__(self, g, tag):
        import contextlib
        self.g, self.tag = g, tag
        self.es = contextlib.ExitStack()

    def __enter__(self):
        self.es.__enter__()
        return self

    def __exit__(self, *a):
        return self.es.__exit__(*a)

    def sb(self, name, shape, dtype=F32):
        g = self.g
        g.P.nuid += 1
        nm = "%s_%s_%d" % (self.tag, name, g.P.nuid)
        t = self.es.enter_context(g.nc.sbuf_tensor(nm, list(shape), dtype))
        return TT(t, nm)


def cst(g, off, n=128, rows=128):
    return g.consts[0:rows, off:off + n]


def mix_phase(g, l):
    P, nc = g.P, g.nc
    NPT = g.NPT
    barrier(g)
    sbs = [list(range(a, min(a + 6, NPT))) for a in range(0, NPT, 6)] + [[NPT]]
    with Scope(g, "mx%d" % l) as LS:
        L = Ctx()
        L.l = l
        L.parB = LS.sb("parB", [128, NPAR])
        L.convw = LS.sb("convw", [128, 24, 4])
        L.convb = LS.sb("convb", [128, 24])
        L.wup = LS.sb("wup", [16, 256])
        L.bup = LS.sb("bup", [1, 256])
        L.pd = LS.sb("pd", [128, 64])
        P.dma('sp', L.parB[:, :], g.params_d[l:l + 1, :].partition_broadcast(128), 'lay_small', writes=[L.parB])
        P.dma('sp', L.convw[:, :, :], g.convw_d[l].rearrange("p (c j) -> p c j", j=4), 'lay_small', writes=[L.convw])
        P.dma('sp', L.convb[:, :], g.convb_d[l], 'lay_small', writes=[L.convb])
        P.dma('sp', L.wup[:, :], g.gla_wup[l], 'lay_small', writes=[L.wup])
        P.dma('sp', L.bup[:, :], g.gla_bup[l], 'lay_small', writes=[L.bup])
        P.op('act', lambda e: e.activation(L.pd[:, 0:4], L.parB[:, P_DN_ALOG:P_DN_ALOG + 4], AF.Exp), reads=[L.parB], writes=[L.pd])
        P.op('act', lambda e: e.activation(L.pd[:, 4:20], L.parB[:, P_M2_ALOG:P_M2_ALOG + 16], AF.Exp), reads=[L.parB], writes=[L.pd])
        P.op('dve', lambda e: e.tensor_scalar(L.pd[:, 0:20], L.pd[:, 0:20], -1.0, None, ALU.mult), reads=[L.pd], writes=[L.pd])
        L.Sd = LS.sb("Sd", [128, 4, 128])
        L.Sg = LS.sb("Sg", [128, 2, 128])
        L.Ss = LS.sb("Ss", [128, 16, 64])
        L.tailD = LS.sb("tailD", [128, 12, 3])
        L.tailS = LS.sb("tailS", [128, 12, 3])
        for t_ in (L.Sd, L.Sg, L.Ss, L.tailD, L.tailS):
            P.op('pool', lambda e: e.memset(t_.t[:], 0.0), writes=[t_])
        import os
        for sbi, tiles in enumerate(sbs):
            is_s = tiles[0] == NPT
            if os.environ.get('DBG_SKIP') == ('s' if is_s else 'p'):
                continue
            ntok = len(tiles) * 128
            with Scope(g, "mx%d_%d" % (l, sbi)) as SS:
                hT = SS.sb("hT", [128, 8, ntok], BF16)
                with Scope(g, "mx%d_%d_n" % (l, sbi)) as NS:
                    hn = [NS.sb("hn0", [128, D], BF16), NS.sb("hn1", [128, D], BF16)]
                    gB = NS.sb("gB", [128, D])
                    norm_tiles(g, tiles, 3 * l + 1, hT, hn, gB)
                    barrier(g)
                for mixer in g.mixers:
                    with Scope(g, "mx%d_%d_%s" % (l, sbi, mixer)) as MS:
                        if mixer == 'delta':
                            delta_mixer(g, L, MS, tiles, hT, is_s)
                        elif mixer == 'gla':
                            gla_mixer(g, L, MS, tiles, hT, is_s)
                        else:
                            ssd_mixer(g, L, MS, tiles, hT, is_s)
                    barrier(g)
        barrier(g)


def load_wpart(g, l, MS, col_ranges, out_rows, tagname):
    P = g.P
    W = sum(n for _, n in col_ranges)
    wpart = MS.sb("wpart", [128, 8, W], BF16)
    c = 0
    for (c0, n) in col_ranges:
        for a in range(0, n, 512):
            w = min(512, n - a)
            P.dma('pool', wpart[:, :, c + a:c + a + w], g.w_in[l][:, c0 + a:c0 + a + w].rearrange("(k p) n -> p k n", p=128),
                  wpart.key, writes=[wpart])
        c += n
    r0, nr = out_rows
    wo = MS.sb("wo", [128, nr // 128, D], BF16)
    for a in range(0, nr, 512):
        P.dma('pool', wo[:, a // 128:a // 128 + 4, :], g.w_out[l][r0 + a:r0 + a + 512, :].rearrange("(k p) n -> p k n", p=128),
              wo.key, writes=[wo])
    return wpart, wo


def proj_fm(g, wpart, col0, ncols, hT, tok0, bank, slot, rows0=0):
    P = g.P
    for kc in range(8):
        P.op('pe', lambda e: e.matmul(bank.t[rows0:rows0 + ncols, slot * 128:(slot + 1) * 128], wpart[:, kc, col0:col0 + ncols],
                                      hT[:, kc, tok0:tok0 + 128], start=(kc == 0), stop=(kc == 7)),
             reads=[wpart, hT], writes=[bank], inc=(kc == 7))


def proj_tm(g, wpart, col0, ncols, hT, tok0, bank, o0=0):
    P = g.P
    for kc in range(8):
        P.op('pe', lambda e: e.matmul(bank.t[:, o0:o0 + ncols], hT[:, kc, tok0:tok0 + 128], wpart[:, kc, col0:col0 + ncols],
                                      start=(kc == 0), stop=(kc == 7)),
             reads=[wpart, hT], writes=[bank], inc=(kc == 7))


def conv_block(g, L, MS, B, wpart, hT, tok0, is_s, ch0, tail, stc, last_prompt, conv_out_col0):
    P, l = g.P, L.l
    cin, cout, acc, tmp = B.cin, B.cout, B.acc, B.tmp
    if is_s:
        cin4 = cin.t[:, :, :].rearrange("p c (r s) -> p c r s", s=16)
    for grp in range(3):
        bank = nextbank(g)
        for c4 in range(4):
            proj_fm(g, wpart, (grp * 4 + c4) * 128, 128, hT, tok0, bank, c4)
        if not is_s:
            P.op('act', lambda e: e.activation(cin[:, grp * 4:grp * 4 + 4, 3:131], bank.t[:, :].rearrange("p (c n) -> p c n", c=4), AF.Copy),
                 reads=[bank], writes=[cin])
        else:
            P.op('act', lambda e: e.activation(cin4[:, grp * 4:grp * 4 + 4, 3:11, :],
                                               bank.t[:, :].rearrange("p (c s t) -> p c t s", c=4, s=16), AF.Copy),
                 reads=[bank], writes=[cin])
    if not is_s:
        P.op('pool', lambda e: e.tensor_copy(cin[:, :, 0:3], tail[:, :, :]), reads=[tail], writes=[cin])
        P.op('pool', lambda e: e.tensor_copy(tail[:, :, :], cin[:, :, 128:131]), reads=[cin], writes=[tail])
    else:
        for grp in range(3):
            bank = nextbank(g)
            for c4 in range(4):
                ch = grp * 4 + c4
                P.op('pe', lambda e: e.transpose(bank.t[:, c4 * 48:(c4 + 1) * 48], stc[0:48, ch * 128:(ch + 1) * 128], cst(g, C_IDENT, 48, 48)),
                     reads=[stc, g.consts], writes=[bank], inc=(c4 == 3))
            P.op('act', lambda e: e.activation(cin4[:, grp * 4:grp * 4 + 4, 0:3, :],
                                               bank.t[:, 0:192].rearrange("p (c s r) -> p c r s", c=4, r=3), AF.Copy),
                 reads=[bank], writes=[cin])
    if is_s or last_prompt:
        for grp in range(3):
            bank = nextbank(g)
            for c4 in range(4):
                ch = grp * 4 + c4
                if is_s:
                    src = cin.t[:, ch, 128:176]
                    n = 48
                else:
                    src = cin[:, ch, 128:131]
                    n = 3
                P.op('pe', lambda e: e.transpose(bank.t[0:n, c4 * 128:(c4 + 1) * 128], src, cst(g, C_IDENT)),
                     reads=[cin, g.consts], writes=[bank], inc=(c4 == 3))
            n = 48 if is_s else 3
            ob = B.cvo
            P.op('dve', lambda e: e.tensor_copy(ob[0:n, grp * 512:(grp + 1) * 512], bank.t[0:n, :]), reads=[bank], writes=[ob])
        if is_s:
            dv = g.o_conv_s[l].rearrange("(s r) c -> r s c", r=3)
            for r_ in range(3):
                P.dma('sp', dv[r_, :, conv_out_col0:conv_out_col0 + 1536], B.cvo[r_ * 16:(r_ + 1) * 16, :], "cvo_o", reads=[B.cvo])
        else:
            P.dma('sp', g.o_conv_p[l][:, conv_out_col0:conv_out_col0 + 1536], B.cvo[0:3, :], "cvo_o", reads=[B.cvo])
    def view(j):
        if is_s:
            return cin4[:, :, j:j + 8, :]
        return cin[:, :, j:j + 128]

    def wv(j, shape):
        return L.convw[:, ch0:ch0 + 12, j:j + 1].to_broadcast(shape) if not is_s else \
            L.convw[:, ch0:ch0 + 12, j:j + 1].unsqueeze(3).to_broadcast(shape)
    if is_s:
        shape = [128, 12, 8, 16]
        accv = acc.t[:, :, :].rearrange("p c (t s) -> p c t s", s=16)
        tmpv = tmp.t[:, :, :].rearrange("p c (t s) -> p c t s", s=16)
        bv_ = L.convb[:, ch0:ch0 + 12].unsqueeze(2).unsqueeze(3).to_broadcast(shape)
    else:
        shape = [128, 12, 128]
        accv = acc.t[:, :, :]
        tmpv = tmp.t[:, :, :]
        bv_ = L.convb[:, ch0:ch0 + 12].unsqueeze(2).to_broadcast(shape)
    P.op('dve', lambda e: e.tensor_tensor(accv, view(0), wv(0, shape), ALU.mult), reads=[cin, L.convw], writes=[acc])
    for j in range(1, 4):
        P.op('pool', lambda e: e.tensor_tensor(tmpv, view(j), wv(j, shape), ALU.mult), reads=[cin, L.convw], writes=[tmp])
        P.op('dve', lambda e: e.tensor_tensor(accv, accv, tmpv, ALU.add), reads=[acc, tmp], writes=[acc])
    P.op('dve', lambda e: e.tensor_tensor(accv, accv, bv_, ALU.add), reads=[acc, L.convb], writes=[acc])
    if is_s:
        P.op('act', lambda e: e.activation(cout.t[:, :, :].rearrange("p c (s t) -> p c t s", s=16), accv, AF.Silu),
             reads=[acc], writes=[cout])
    else:
        P.op('act', lambda e: e.activation(cout[:, :, :], accv, AF.Silu), reads=[acc], writes=[cout])


def out_proj(g, MS, B, on_bf, nk, wo, t):
    P = g.P
    bank = nextbank(g)
    pb = bank.t[:, :].bitcast(BF16)
    for kc in range(nk):
        P.op('pe', lambda e: e.transpose(pb[:, kc * 128:(kc + 1) * 128], on_bf[:, kc * 128:(kc + 1) * 128], g.identb[:, :]),
             reads=[on_bf, g.identb], writes=[bank], inc=(kc == nk - 1))
    P.op('act', lambda e: e.activation(B.onT[:, 0:nk, :], pb[:, 0:nk * 128].rearrange("p (k n) -> p k n", k=nk), AF.Copy),
         reads=[bank], writes=[B.onT])
    for hh in range(2):
        bo = nextbank(g)
        for kc in range(nk):
            P.op('pe', lambda e: e.matmul(bo.t[:, :], B.onT[:, kc, :], wo[:, kc, hh * 512:(hh + 1) * 512], start=(kc == 0), stop=(kc == nk - 1)),
                 reads=[B.onT, wo], writes=[bo], inc=(kc == nk - 1))
        P.op('dve', lambda e: e.tensor_tensor(g.x[:, t, hh * 512:(hh + 1) * 512], g.x[:, t, hh * 512:(hh + 1) * 512], bo.t[:, :], ALU.add),
             reads=[bo, g.xk[t]], writes=[g.xk[t]])


def gated_norm(g, L, B, o_sb, nh, hd, gate_banks, norm_off, per_head, out_bf, norm_tt=None):
    P = g.P
    W = nh * hd
    sm = B.sm2
    gt = B.gate
    for bi, bank in enumerate(gate_banks):
        w = min(512, W - bi * 512)
        P.op('act', lambda e: e.activation(gt[:, bi * 512:bi * 512 + w], bank.t[:, 0:w], AF.Exp, scale=-1.0), reads=[bank], writes=[gt])
        P.op('dve', lambda e: e.tensor_scalar(gt[:, bi * 512:bi * 512 + w], gt[:, bi * 512:bi * 512 + w], 1.0, None, ALU.add), reads=[gt], writes=[gt])
        P.op('dve', lambda e: e.reciprocal(gt[:, bi * 512:bi * 512 + w], gt[:, bi * 512:bi * 512 + w]), reads=[gt], writes=[gt])
        P.op('dve', lambda e: e.tensor_tensor(gt[:, bi * 512:bi * 512 + w], gt[:, bi * 512:bi * 512 + w], bank.t[:, 0:w], ALU.mult), reads=[gt, bank], writes=[gt])
    sq = getattr(B, 'sq', None)
    if per_head:
        P.op('pool', lambda e: e.tensor_tensor(sq[:, 0:W], o_sb[:, 0:W], o_sb[:, 0:W], ALU.mult), reads=[o_sb], writes=[sq])
        P.op('dve', lambda e: e.tensor_reduce(sm[:, 0:nh], sq[:, 0:W].rearrange("p (h e) -> p h e", h=nh), AX.X, ALU.add), reads=[sq], writes=[sm])
        P.op('act', lambda e: e.activation(sm[:, 16:16 + nh], sm[:, 0:nh], AF.Ln, bias=EPS, scale=1.0 / hd), reads=[sm], writes=[sm])
        P.op('act', lambda e: e.activation(sm[:, 32:32 + nh], sm[:, 16:16 + nh], AF.Exp, scale=-0.5), reads=[sm], writes=[sm])
        P.op('dve', lambda e: e.tensor_tensor(sq[:, 0:W].rearrange("p (h e) -> p h e", h=nh), o_sb[:, 0:W].rearrange("p (h e) -> p h e", h=nh),
                                              sm[:, 32:32 + nh].unsqueeze(2).to_broadcast([128, nh, hd]), ALU.mult), reads=[o_sb, sm], writes=[sq])
        P.op('pool', lambda e: e.tensor_tensor(sq[:, 0:W].rearrange("p (h e) -> p h e", h=nh), sq[:, 0:W].rearrange("p (h e) -> p h e", h=nh),
                                               L.parB[:, norm_off:norm_off + hd].unsqueeze(1).to_broadcast([128, nh, hd]), ALU.mult),
             reads=[sq, L.parB], writes=[sq])
        P.op('dve', lambda e: e.tensor_tensor(out_bf[:, 0:W], sq[:, 0:W], gt[:, 0:W], ALU.mult), reads=[sq, gt], writes=[out_bf])
    else:
        P.op('dve', lambda e: e.tensor_tensor(gt[:, 0:W], gt[:, 0:W], o_sb[:, 0:W], ALU.mult), reads=[gt, o_sb], writes=[gt])
        P.op('act', lambda e: e.activation(out_bf[:, 0:W], gt[:, 0:W], AF.Square, accum_out=sm[:, 0:1]), reads=[gt], writes=[out_bf, sm])
        P.op('act', lambda e: e.activation(sm[:, 16:17], sm[:, 0:1], AF.Ln, bias=EPS, scale=1.0 / W), reads=[sm], writes=[sm])
        P.op('act', lambda e: e.activation(sm[:, 32:33], sm[:, 16:17], AF.Exp, scale=-0.5), reads=[sm], writes=[sm])
        P.op('dve', lambda e: e.scalar_tensor_tensor(out_bf[:, 0:W], gt[:, 0:W], sm[:, 32:33], norm_tt[:, 0:W], ALU.mult, ALU.mult),
             reads=[gt, sm, norm_tt], writes=[out_bf])


def expo_bank(g, B, mask_off, rowT, colT, nheads=4):
    P = g.P
    bank = nextbank(g)
    for h in range(nheads):
        o = bank.t[:, h * 128:(h + 1) * 128]
        esel = g.consts[0:4, C_ESEL + h * 128:C_ESEL + (h + 1) * 128]
        P.op('pe', lambda e: e.matmul(o, cst(g, C_IDENT), cst(g, mask_off), start=True, stop=False), reads=[g.consts], writes=[bank], inc=False)
        P.op('pe', lambda e: e.matmul(o, esel, rowT, start=False, stop=False), reads=[g.consts, B.rT], writes=[bank], inc=False)
        P.op('pe', lambda e: e.matmul(o, colT, esel, start=False, stop=True), reads=[g.consts, B.rT], writes=[bank], inc=(h == nheads - 1))
    return bank


class Arena:
    def __init__(self, MS, name, n):
        self.tt = MS.sb(name, [128, n, 512])
        self.name = self.tt.key
        self.n = n

    def keys(self, i0, n):
        return tuple("%s_s%d" % (self.name, i) for i in range(i0, i0 + n))

    def slot(self, i, shape3=None):
        ap = self.tt.t[:, i, :]
        if shape3 is not None:
            ap = ap.rearrange("p (a b) -> p a b", a=shape3[0])
        return TT(ap, self.keys(i, 1))

    def span(self, i0, n, inner):
        ap = self.tt.t[:, i0:i0 + n, :].rearrange("p s (a b) -> p (s a) b", b=inner)
        return TT(ap, self.keys(i0, n))


def delta_mixer(g, L, MS, tiles, hT, is_s):
    P, l = g.P, L.l
    wpart, wo = load_wpart(g, l, MS, [(O_DQ, 1536), (O_DZ, 520)], (0, 512), "d")
    B = Ctx()
    A = Arena(MS, "arA", 5)
    Bc = Arena(MS, "arB", 3)
    G = Arena(MS, "arG", 6)
    M = Arena(MS, "arM", 3)
    Lt = Arena(MS, "arL", 3)
    B.cin = TT(A.tt.t[:, :, :].rearrange("p s f -> p (s f)")[:, 0:12 * 176].rearrange("p (c n) -> p c n", c=12), A.keys(0, 5))
    B.cout = Bc.span(0, 3, 128)
    B.acc = G.span(0, 3, 128)
    B.tmp = G.span(3, 3, 128)
    B.cvo = TT(M.tt.t[0:48, :, :].rearrange("p s f -> p (s f)"), M.keys(0, 3))
    B.rn = M.span(0, 2, 128)
    B.E = [M.slot(i, (4, 128)) for i in range(3)]
    B.X = [G.slot(0, (4, 128)), G.slot(1, (4, 128))]
    B.Y = [G.slot(2, (4, 128)), G.slot(3, (4, 128))]
    B.PT = [G.slot(4, (4, 128)), G.slot(5, (4, 128))]
    B.kbg, B.kd, B.QKm, B.bv = [A.slot(i, (4, 128)) for i in range(4)]
    B.nwT, B.vnew, B.qgT = [Bc.slot(i, (4, 128)) for i in range(3)]
    B.gate = Lt.slot(0)
    B.sq = Lt.slot(1)
    B.o = Lt.slot(2)
    B.onT = MS.sb("onT", [128, 8, 128], BF16)
    B.sm = MS.sb("sm", [128, 64])
    B.sm2 = MS.sb("sm2", [128, 64])
    B.rT = MS.sb("rT", [4, 512])
    B.qkn = MS.sb("qkn", [128, 8, 128])
    B.on = MS.sb("on", [128, 512], BF16)
    stc = None
    if is_s:
        stc = TT(Lt.tt.t[0:48, :, :].rearrange("p s f -> p (s f)"), Lt.keys(0, 3))
        P.dma('sp', stc[:, :], g.st_conv[l][:, 0:1536], "stc_d", writes=[stc])
        B.Sh = MS.sb("Sh", [128, 16, 128])
        B.padf = MS.sb("pad", [128, 17 * 128])
        B.egts = MS.sb("egts", [128, 16, 4])
    sm = B.sm
    for ti, t in enumerate(tiles):
        tok0 = ti * 128
        last_prompt = (not is_s) and (t == g.NPT - 1)
        conv_block(g, L, MS, B, wpart, hT, tok0, is_s, 0, L.tailD, stc, last_prompt, 0)
        cout = B.cout
        M_U01 = C_SU01 if is_s else C_U01
        M_SEG = C_SEG if is_s else C_ONES
        P.op('pool', lambda e: e.tensor_tensor(B.rn[:, :, :], cout[:, 0:8, :], cout[:, 0:8, :], ALU.mult), reads=[cout], writes=[B.rn])
        for hh in range(2):
            bank = nextbank(g)
            P.op('pe', lambda e: e.matmul(bank.t[:, :], cst(g, C_ONES), B.rn[:, hh * 4:hh * 4 + 4, :], start=True, stop=True),
                 reads=[g.consts, B.rn], writes=[bank])
            P.op('act', lambda e: e.activation(B.qkn[:, hh * 4:hh * 4 + 4, :], bank.t[:, :].rearrange("p (c n) -> p c n", c=4), AF.Ln, bias=EPS),
                 reads=[bank], writes=[B.qkn])
        P.op('act', lambda e: e.activation(B.rn[:, :, :], B.qkn[:, :, :], AF.Exp, scale=-0.5), reads=[B.qkn, B.rn], writes=[B.rn])
        P.op('dve', lambda e: e.tensor_tensor(B.qkn[:, :, :], cout[:, 0:8, :], B.rn[:, :, :], ALU.mult), reads=[cout, B.rn], writes=[B.qkn])
        bS = nextbank(g)
        proj_tm(g, wpart, 2048, 8, hT, tok0, bS)
        P.op('dve', lambda e: e.tensor_tensor(sm[:, 40:44], bS.t[:, 0:4], L.parB[:, P_DN_DTB:P_DN_DTB + 4], ALU.add), reads=[bS, L.parB], writes=[sm])
        P.op('act', lambda e: e.activation(sm[:, 40:44], sm[:, 40:44], AF.Exp), reads=[sm], writes=[sm])
        P.op('act', lambda e: e.activation(sm[:, 44:48], bS.t[:, 4:8], AF.Exp, scale=-1.0), reads=[bS], writes=[sm])
        P.op('act', lambda e: e.activation(sm[:, 0:8], sm[:, 40:48], AF.Ln, bias=1.0), reads=[sm], writes=[sm])
        P.op('dve', lambda e: e.tensor_tensor(sm[:, 0:4], sm[:, 0:4], L.pd[:, 0:4], ALU.mult), reads=[sm, L.pd], writes=[sm])
        P.op('act', lambda e: e.activation(sm[:, 8:12], sm[:, 4:8], AF.Exp, scale=-1.0), reads=[sm], writes=[sm])
        bC = nextbank(g)
        P.op('pe', lambda e: e.matmul(bC.t[:, 0:4], cst(g, M_U01), sm[:, 0:4], start=True, stop=True), reads=[g.consts, sm], writes=[bC], inc=False)
        P.op('pe', lambda e: e.matmul(bC.t[:, 4:8], cst(g, M_SEG), sm[:, 0:4], start=True, stop=True), reads=[g.consts, sm], writes=[bC])
        P.op('dve', lambda e: e.tensor_copy(sm[:, 12:20], bC.t[:, 0:8]), reads=[bC], writes=[sm])
        P.op('dve', lambda e: e.tensor_tensor(sm[:, 20:24], sm[:, 12:16], sm[:, 4:8], ALU.subtract), reads=[sm], writes=[sm])
        P.op('dve', lambda e: e.tensor_scalar(sm[:, 24:28], sm[:, 12:16], -1.0, None, ALU.mult), reads=[sm], writes=[sm])
        P.op('dve', lambda e: e.tensor_tensor(sm[:, 32:36], sm[:, 16:20], sm[:, 12:16], ALU.subtract), reads=[sm], writes=[sm])
        P.op('act', lambda e: e.activation(sm[:, 28:32], sm[:, 12:16], AF.Exp), reads=[sm], writes=[sm])
        P.op('act', lambda e: e.activation(sm[:, 32:36], sm[:, 32:36], AF.Exp), reads=[sm], writes=[sm])
        P.op('dve', lambda e: e.tensor_tensor(sm[:, 36:40], sm[:, 8:12], sm[:, 28:32], ALU.mult), reads=[sm], writes=[sm])

        def bc(c0):
            return sm[:, c0:c0 + 4].unsqueeze(2).to_broadcast([128, 4, 128])
        bKT = nextbank(g)
        bVT = nextbank(g)
        for h in range(4):
            P.op('pe', lambda e: e.transpose(bKT.t[:, h * 128:(h + 1) * 128], B.qkn[:, 4 + h, :], cst(g, C_IDENT)), reads=[B.qkn, g.consts], writes=[bKT], inc=(h == 3))
        for h in range(4):
            P.op('pe', lambda e: e.transpose(bVT.t[:, h * 128:(h + 1) * 128], cout[:, 8 + h, :], cst(g, C_IDENT)), reads=[cout, g.consts], writes=[bVT], inc=(h == 3))
        kt3 = bKT.t[:, :].rearrange("p (h n) -> p h n", h=4)
        P.op('dve', lambda e: e.tensor_tensor(B.kbg[:, :, :], kt3, bc(36), ALU.mult), reads=[bKT, sm, B.cin], writes=[B.kbg])
        P.op('dve', lambda e: e.tensor_tensor(B.kd[:, :, :], kt3, bc(32), ALU.mult), reads=[bKT, sm], writes=[B.kd])
        P.op('dve', lambda e: e.tensor_tensor(B.bv[:, :, :], bVT.t[:, :].rearrange("p (h n) -> p h n", h=4), bc(8), ALU.mult), reads=[bVT, sm], writes=[B.bv])
        bT = nextbank(g)
        for qi, c0 in enumerate((12, 20, 24)):
            P.op('pe', lambda e: e.transpose(bT.t[0:4, qi * 128:(qi + 1) * 128], sm[:, c0:c0 + 4], cst(g, C_IDENT)),
                 reads=[sm, g.consts], writes=[bT], inc=(qi == 2))
        P.op('dve', lambda e: e.tensor_copy(B.rT[:, 0:384], bT.t[0:4, 0:384]), reads=[bT], writes=[B.rT])
        gcT, r2T, ngcT = B.rT[:, 0:128], B.rT[:, 128:256], B.rT[:, 256:384]
        mNLS, mNUS, mNUI = (C_SNLS, C_SNUS, C_SNUI) if is_s else (C_NLS, C_NUS, C_NUI)
        for ei, (moff, rowT, colT) in enumerate(((mNLS, ngcT, r2T), (mNUS, r2T, ngcT), (mNUI, gcT, ngcT))):
            bank = expo_bank(g, B, moff, rowT, colT)
            P.op('act', lambda e: e.activation(B.E[ei][:, :, :], bank.t[:, :].rearrange("p (h n) -> p h n", h=4), AF.Exp), reads=[bank], writes=[B.E[ei]])
        bKK = nextbank(g)
        bQK = nextbank(g)
        for h in range(4):
            P.op('pe', lambda e: e.matmul(bKK.t[:, h * 128:(h + 1) * 128], B.qkn[:, 4 + h, :], B.qkn[:, 4 + h, :], start=True, stop=True),
                 reads=[B.qkn], writes=[bKK], inc=(h == 3))
        for h in range(4):
            P.op('pe', lambda e: e.matmul(bQK.t[:, h * 128:(h + 1) * 128], B.qkn[:, 4 + h, :], B.qkn[:, h, :], start=True, stop=True),
                 reads=[B.qkn], writes=[bQK], inc=(h == 3))
        kk3 = bKK.t[:, :].rearrange("p (h n) -> p h n", h=4)
        P.op('dve', lambda e: e.scalar_tensor_tensor(B.X[0][:, :, :], B.E[0][:, :, :], -1.0, kk3, ALU.mult, ALU.mult), reads=[B.E[0], bKK], writes=[B.X[0]])
        P.op('dve', lambda e: e.scalar_tensor_tensor(B.Y[0][:, :, :], B.E[1][:, :, :], -1.0, kk3, ALU.mult, ALU.mult), reads=[B.E[1], bKK], writes=[B.Y[0]])
        P.op('dve', lambda e: e.scalar_tensor_tensor(B.QKm[:, :, :], B.E[2][:, :, :], 128.0 ** -0.5, bQK.t[:, :].rearrange("p (h n) -> p h n", h=4),
                                                     ALU.mult, ALU.mult), reads=[B.E[2], bQK], writes=[B.QKm])
        P.op('pool', lambda e: e.tensor_tensor(B.PT[0][:, :, :], B.Y[0][:, :, :], cst(g, C_IDENT).unsqueeze(1).to_broadcast([128, 4, 128]), ALU.add),
             reads=[B.Y[0], g.consts], writes=[B.PT[0]])
        nlev = 2 if is_s else 6
        cur = 0
        for k in range(1, nlev + 1):
            nxt = 1 - cur
            bX = nextbank(g)
            for h in range(4):
                P.op('pe', lambda e: e.matmul(bX.t[:, h * 128:(h + 1) * 128], B.Y[cur][:, h, :], B.X[cur][:, h, :], start=True, stop=True),
                     reads=[B.X[cur], B.Y[cur]], writes=[bX], inc=(h == 3))
            if k < nlev:
                bY = nextbank(g)
                for h in range(4):
                    P.op('pe', lambda e: e.matmul(bY.t[:, h * 128:(h + 1) * 128], B.X[cur][:, h, :], B.Y[cur][:, h, :], start=True, stop=True),
                         reads=[B.X[cur], B.Y[cur]], writes=[bY], inc=(h == 3))
            P.op('act', lambda e: e.activation(B.X[nxt][:, :, :], bX.t[:, :].rearrange("p (h n) -> p h n", h=4), AF.Copy), reads=[bX], writes=[B.X[nxt]])
            if k < nlev:
                P.op('dve', lambda e: e.tensor_copy(B.Y[nxt][:, :, :], bY.t[:, :].rearrange("p (h n) -> p h n", h=4)), reads=[bY], writes=[B.Y[nxt]])
            bP = nextbank(g)
            for h in range(4):
                P.op('pe', lambda e: e.matmul(bP.t[:, h * 128:(h + 1) * 128], cst(g, C_IDENT), B.PT[cur][:, h, :], start=True, stop=False),
                     reads=[g.consts, B.PT[cur]], writes=[bP], inc=False)
                P.op('pe', lambda e: e.matmul(bP.t[:, h * 128:(h + 1) * 128], B.X[nxt][:, h, :], B.PT[cur][:, h, :], start=False, stop=True),
                     reads=[B.X[nxt], B.PT[cur]], writes=[bP], inc=(h == 3))
            P.op('dve', lambda e: e.tensor_copy(B.PT[nxt][:, :, :], bP.t[:, :].rearrange("p (h n) -> p h n", h=4)), reads=[bP], writes=[B.PT[nxt]])
            cur = nxt
        TTm = B.PT[cur]
        bW = nextbank(g)
        for h in range(4):
            P.op('pe', lambda e: e.matmul(bW.t[:, h * 128:(h + 1) * 128], B.kbg[:, h, :], TTm[:, h, :], start=True, stop=True),
                 reads=[B.kbg, TTm], writes=[bW], inc=(h == 3))
        P.op('act', lambda e: e.activation(B.nwT[:, :, :], bW.t[:, :].rearrange("p (h n) -> p h n", h=4), AF.Copy, scale=-1.0), reads=[bW, cout], writes=[B.nwT])
        bG = nextbank(g)
        for h in range(4):
            esel = g.consts[0:4, C_ESEL + h * 128:C_ESEL + (h + 1) * 128]
            P.op('pe', lambda e: e.matmul(bG.t[:, h * 128:(h + 1) * 128], esel, gcT, start=True, stop=True), reads=[g.consts, B.rT], writes=[bG], inc=(h == 3))
        P.op('act', lambda e: e.activation(B.qgT[:, :, :], bG.t[:, :].rearrange("p (h n) -> p h n", h=4), AF.Exp), reads=[bG], writes=[B.qgT])
        P.op('dve', lambda e: e.scalar_tensor_tensor(B.qgT[:, :, :], B.qkn[:, 0:4, :], 128.0 ** -0.5, B.qgT[:, :, :], ALU.mult, ALU.mult),
             reads=[B.qkn, B.qgT], writes=[B.qgT])
        if not is_s:
            P.op('act', lambda e: e.activation(sm[:, 48:52], sm[:, 16:20], AF.Exp), reads=[sm], writes=[sm])
        else:
            gtv = B.sm2[:, 0:64].rearrange("p (s h) -> p s h", h=4)
            P.op('dve', lambda e: e.tensor_tensor(gtv, sm[:, 16:20].unsqueeze(1).to_broadcast([128, 16, 4]),
                                                  g.consts[:, C_ROWM8:C_ROWM8 + 16].unsqueeze(2).to_broadcast([128, 16, 4]), ALU.mult),
                 reads=[sm, g.consts], writes=[B.sm2])
            bE = nextbank(g)
            P.op('pe', lambda e: e.matmul(bE.t[:, 0:64], cst(g, C_ONES), B.sm2[:, 0:64], start=True, stop=True), reads=[g.consts, B.sm2], writes=[bE])
            P.op('act', lambda e: e.activation(B.egts[:, :, :], bE.t[:, 0:64].rearrange("p (s h) -> p s h", h=4), AF.Exp), reads=[bE], writes=[B.egts])
        bVN = nextbank(g, hold=True)
        bO = nextbank(g, hold=True)
        hgroups = [[0, 1, 2, 3]] if not is_s else [[0], [1], [2], [3]]
        padv = None
        if is_s:
            padd = B.padf.t[:, 0:16 * 136].rearrange("p (s q) -> p s q", q=136)[:, :, 0:8]
            padr = B.padf.t[:, 0:2048].rearrange("p (s n) -> p s n", s=16)
        for hs in hgroups:
            h0, nh = hs[0], len(hs)
            if is_s:
                h = h0
                P.dma('sp', B.Sh[:, :, :], g.st_delta[l][:, h, :, :].rearrange("s d e -> d s e"), "Sh_d", writes=[B.Sh])

                def padcol(src):
                    P.op('pool', lambda e: e.memset(B.padf[:, :], 0.0), writes=[B.padf])
                    P.op('pool', lambda e: e.tensor_copy(padd, src.rearrange("p (s r) -> p s r", r=8)), reads=[B.nwT, B.qgT, B.padf], writes=[B.padf])
            for h in hs:
                o = bVN.t[:, h * 128:(h + 1) * 128]
                P.op('pe', lambda e: e.matmul(o, TTm[:, h, :], B.bv[:, h, :], start=True, stop=False), reads=[TTm, B.bv], writes=[bVN], inc=False)
                if not is_s:
                    P.op('pe', lambda e: e.matmul(o, B.nwT[:, h, :], L.Sd[:, h, :], start=False, stop=True), reads=[B.nwT, L.Sd], writes=[bVN], inc=(h == hs[-1]))
                else:
                    padcol(B.nwT[:, h, :])
                    for s_ in range(16):
                        P.op('pe', lambda e: e.matmul(o, B.padf[:, s_ * 128:(s_ + 1) * 128], B.Sh[:, s_, :], start=False, stop=(s_ == 15)),
                             reads=[B.padf, B.Sh], writes=[bVN], inc=(s_ == 15))
            P.op('act', lambda e: e.activation(B.vnew[:, h0:h0 + nh, :], bVN.t[:, h0 * 128:(h0 + nh) * 128].rearrange("p (h n) -> p h n", h=nh), AF.Copy),
                 reads=[bVN], writes=[B.vnew])
            for h in hs:
                o = bO.t[:, h * 128:(h + 1) * 128]
                P.op('pe', lambda e: e.matmul(o, B.QKm[:, h, :], B.vnew[:, h, :], start=True, stop=False), reads=[B.QKm, B.vnew], writes=[bO], inc=False)
                if not is_s:
                    P.op('pe', lambda e: e.matmul(o, B.qgT[:, h, :], L.Sd[:, h, :], start=False, stop=True), reads=[B.qgT, L.Sd], writes=[bO], inc=(h == hs[-1]))
                else:
                    padcol(B.qgT[:, h, :])
                    for s_ in range(16):
                        P.op('pe', lambda e: e.matmul(o, B.padf[:, s_ * 128:(s_ + 1) * 128], B.Sh[:, s_, :], start=False, stop=(s_ == 15)),
                             reads=[B.padf, B.Sh], writes=[bO], inc=(s_ == 15))
            if not is_s:
                bSu = nextbank(g)
                for h in hs:
                    P.op('pe', lambda e: e.matmul(bSu.t[:, h * 128:(h + 1) * 128], B.kd[:, h, :], B.vnew[:, h, :], start=True, stop=True),
                         reads=[B.kd, B.vnew], writes=[bSu], inc=(h == 3))
                P.op('dve', lambda e: e.tensor_tensor(L.Sd[:, :, :], L.Sd[:, :, :], sm[:, 48:52].unsqueeze(2).to_broadcast([128, 4, 128]), ALU.mult),
                     reads=[L.Sd, sm], writes=[L.Sd])
                P.op('dve', lambda e: e.tensor_tensor(L.Sd[:, :, :], L.Sd[:, :, :], bSu.t[:, :].rearrange("p (h n) -> p h n", h=4), ALU.add),
                     reads=[L.Sd, bSu], writes=[L.Sd])
                if last_prompt:
                    P.dma('sp', g.o_delta_p[l].rearrange("h d e -> d h e"), L.Sd[:, :, :], L.Sd.key + "_o", reads=[L.Sd])
            else:
                h = h0
                P.op('pool', lambda e: e.tensor_tensor(padr, B.kd[:, h, :].unsqueeze(1).to_broadcast([128, 16, 128]),
                                                       g.consts[:, C_ROWM:C_ROWM + 16].unsqueeze(2).to_broadcast([128, 16, 128]), ALU.mult),
                     reads=[B.kd, g.consts], writes=[B.padf])
                for s4 in range(4):
                    bSu = nextbank(g)
                    for si in range(4):
                        s_ = s4 * 4 + si
                        P.op('pe', lambda e: e.matmul(bSu.t[:, si * 128:(si + 1) * 128], B.padf[:, s_ * 128:(s_ + 1) * 128], B.vnew[:, h, :], start=True, stop=True),
                             reads=[B.padf, B.vnew], writes=[bSu], inc=(si == 3))
                    sl = B.Sh[:, s4 * 4:s4 * 4 + 4, :]
                    P.op('dve', lambda e: e.tensor_tensor(sl, sl, B.egts[:, s4 * 4:s4 * 4 + 4, h:h + 1].to_broadcast([128, 4, 128]), ALU.mult),
                         reads=[B.Sh, B.egts], writes=[B.Sh])
                    P.op('dve', lambda e: e.tensor_tensor(sl, sl, bSu.t[:, :].rearrange("p (s n) -> p s n", s=4), ALU.add),
                         reads=[B.Sh, bSu], writes=[B.Sh])
                P.dma('sp', g.o_delta_s[l][:, h, :, :].rearrange("s d e -> d s e"), B.Sh[:, :, :], "Sh_o", reads=[B.Sh])
        P.op('act', lambda e: e.activation(B.o[:, :], bO.t[:, :], AF.Copy), reads=[bO], writes=[B.o])
        release(g, bVN, bO)
        bZ = nextbank(g)
        proj_tm(g, wpart, 1536, 512, hT, tok0, bZ)
        gated_norm(g, L, B, B.o, 4, 128, [bZ], P_DN_NORM, True, B.on)
        out_proj(g, MS, B, B.on, 4, wo, t)


def gla_mixer(g, L, MS, tiles, hT, is_s):
    import os
    DBG = int(os.environ.get('DBG_GLA', '99'))
    P, l = g.P, L.l
    wpart, wo = load_wpart(g, l, MS, [(O_GQ, 1552)], (512, 512), "g")
    B = Ctx()
    B.v = MS.sb("v", [128, 512])
    B.gk = MS.sb("gk", [128, 256])
    B.gc = MS.sb("gc", [128, 256])
    B.Eq = MS.sb("Eq", [128, 2, 128])
    B.Ek = MS.sb("Ek", [128, 2, 128])
    B.qt = MS.sb("qt", [128, 2, 128])
    B.kt = MS.sb("kt", [128, 2, 128])
    B.qz = MS.sb("qz", [128, 4, 128])
    P.op('pool', lambda e: e.memset(B.qz[:, :, :], 0.0), writes=[B.qz])
    B.PTm = MS.sb("PTm", [128, 4, 128])
    B.ktm = MS.sb("ktm", [128, 256])
    B.glT = MS.sb("glT", [16, 128])
    B.o = MS.sb("o", [128, 512])
    B.gate = MS.sb("gate", [128, 512])
    B.sq = MS.sb("sq", [128, 512])
    B.on = MS.sb("on", [128, 512], BF16)
    B.onT = MS.sb("onT", [128, 8, 128], BF16)
    B.sm2 = MS.sb("sm2", [128, 64])
    if is_s:
        B.Sh = MS.sb("Sh", [128, 16, 128])
        B.padf = MS.sb("pad", [128, 17 * 128])
        padd = B.padf.t[:, 0:16 * 136].rearrange("p (s q) -> p s q", q=136)[:, :, 0:8]
        padr = B.padf.t[:, 0:2048].rearrange("p (s n) -> p s n", s=16)
    M_U01 = C_SU01 if is_s else C_U01
    for ti, t in enumerate(tiles):
        tok0 = ti * 128
        last_prompt = (not is_s) and (t == g.NPT - 1)
        bFM = nextbank(g)
        for c in range(4):
            proj_fm(g, wpart, c * 128, 128, hT, tok0, bFM, c)
        bGL = nextbank(g)
        proj_fm(g, wpart, 1024, 16, hT, tok0, bGL, 0)
        P.op('dve', lambda e: e.tensor_copy(B.glT[:, :], bGL.t[0:16, 0:128]), reads=[bGL], writes=[B.glT])
        bV = nextbank(g)
        proj_tm(g, wpart, 512, 512, hT, tok0, bV)
        P.op('act', lambda e: e.activation(B.v[:, :], bV.t[:, :], AF.Copy), reads=[bV], writes=[B.v])
        if DBG <= 1:
            g.held.clear()
            continue
        bK = nextbank(g)
        P.op('pe', lambda e: e.matmul(bK.t[:, 0:256], B.glT[:, :], L.wup[:, :], start=True, stop=False), reads=[B.glT, L.wup], writes=[bK], inc=False)
        P.op('pe', lambda e: e.matmul(bK.t[:, 0:256], g.consts[0:1, C_ONES:C_ONES + 128], L.bup[:, :], start=False, stop=True),
             reads=[g.consts, L.bup], writes=[bK])
        P.op('act', lambda e: e.activation(B.gk[:, :], bK.t[:, 0:256], AF.Exp, scale=-1.0), reads=[bK], writes=[B.gk])
        P.op('act', lambda e: e.activation(B.gk[:, :], B.gk[:, :], AF.Ln, bias=1.0), reads=[B.gk], writes=[B.gk])
        P.op('dve', lambda e: e.tensor_scalar(B.gk[:, :], B.gk[:, :], -1.0 / 16.0, None, ALU.mult), reads=[B.gk], writes=[B.gk])
        if DBG <= 2:
            g.held.clear()
            continue
        bC = nextbank(g)
        P.op('pe', lambda e: e.matmul(bC.t[:, 0:256], cst(g, M_U01), B.gk[:, :], start=True, stop=True), reads=[g.consts, B.gk], writes=[bC])
        P.op('dve', lambda e: e.tensor_copy(B.gc[:, :], bC.t[:, 0:256]), reads=[bC], writes=[B.gc])
        bT = nextbank(g)
        for c in range(2):
            P.op('pe', lambda e: e.transpose(bT.t[:, c * 128:(c + 1) * 128], B.gc[:, c * 128:(c + 1) * 128], cst(g, C_IDENT)),
                 reads=[B.gc, g.consts], writes=[bT], inc=(c == 1))
        gT3 = bT.t[:, 0:256].rearrange("p (c n) -> p c n", c=2)
        P.op('act', lambda e: e.activation(B.Eq[:, :, :], gT3, AF.Exp), reads=[bT], writes=[B.Eq])
        P.op('act', lambda e: e.activation(B.Ek[:, :, :], gT3, AF.Exp, scale=-1.0), reads=[bT], writes=[B.Ek])
        fm3 = bFM.t[:, :].rearrange("p (c n) -> p c n", c=4)
        P.op('dve', lambda e: e.scalar_tensor_tensor(B.qt[:, :, :], fm3[:, 0:2, :], 64.0 ** -0.5, B.Eq[:, :, :], ALU.mult, ALU.mult),
             reads=[bFM, B.Eq], writes=[B.qt])
        P.op('dve', lambda e: e.tensor_tensor(B.kt[:, :, :], fm3[:, 2:4, :], B.Ek[:, :, :], ALU.mult), reads=[bFM, B.Ek], writes=[B.kt])
        if DBG <= 3:
            g.held.clear()
            continue
        qz4 = B.qz.t[:, :, :].rearrange("p (hh hl) n -> p hl hh n", hl=2)
        P.op('dve', lambda e: e.tensor_copy(qz4[0:64, 0, :, :], B.qt[0:64, :, :]), reads=[B.qt], writes=[B.qz])
        P.op('dve', lambda e: e.tensor_copy(qz4[64:128, 1, :, :], B.qt[64:128, :, :]), reads=[B.qt], writes=[B.qz])
        bSc = nextbank(g)
        for h in range(4):
            hl, hh = h % 2, h // 2
            P.op('pe', lambda e: e.matmul(bSc.t[:, h * 128:(h + 1) * 128], B.kt[:, hh, :], B.qz[:, h, :], start=True, stop=True),
                 reads=[B.kt, B.qz], writes=[bSc], inc=(h == 3))
        P.op('dve', lambda e: e.tensor_tensor(B.PTm[:, :, :], bSc.t[:, :].rearrange("p (h n) -> p h n", h=4),
                                              cst(g, M_U01).unsqueeze(1).to_broadcast([128, 4, 128]), ALU.mult), reads=[bSc, g.consts], writes=[B.PTm])
        bKT = nextbank(g)
        for c in range(2):
            P.op('pe', lambda e: e.transpose(bKT.t[:, c * 128:(c + 1) * 128], B.kt[:, c, :], cst(g, C_IDENT)), reads=[B.kt, g.consts], writes=[bKT], inc=(c == 1))
        P.op('act', lambda e: e.activation(B.ktm[:, :], bKT.t[:, 0:256], AF.Copy), reads=[bKT], writes=[B.ktm])
        if DBG <= 4:
            g.held.clear()
            continue
        bO = nextbank(g, hold=True)
        for hh in range(2):
            if is_s:
                P.dma('sp', B.Sh[:, :, :], g.st_gla[l][:, 2 * hh:2 * hh + 2, :, :].rearrange("s hl d e -> (hl d) s e"), "Shg_d", writes=[B.Sh])
            for hl in range(2):
                h = 2 * hh + hl
                r = slice(hl * 64, hl * 64 + 64)
                o = bO.t[:, h * 128:(h + 1) * 128]
                P.op('pe', lambda e: e.matmul(o, B.PTm[:, h, :], B.v[:, h * 128:(h + 1) * 128], start=True, stop=False), reads=[B.PTm, B.v], writes=[bO], inc=False)
                if not is_s:
                    P.op('pe', lambda e: e.matmul(o, B.qz[:, h, :], L.Sg[:, hh, :], start=False, stop=True), reads=[B.qz, L.Sg], writes=[bO], inc=True)
                else:
                    P.op('pool', lambda e: e.memset(B.padf[:, :], 0.0), writes=[B.padf])
                    P.op('pool', lambda e: e.tensor_copy(padd, B.qz[:, h, :].rearrange("p (s r) -> p s r", r=8)), reads=[B.qz, B.padf], writes=[B.padf])
                    for s_ in range(16):
                        P.op('pe', lambda e: e.matmul(o, B.padf[:, s_ * 128:(s_ + 1) * 128], B.Sh[:, s_, :], start=False, stop=(s_ == 15)),
                             reads=[B.padf, B.Sh], writes=[bO], inc=(s_ == 15))
            if not is_s:
                bSu = nextbank(g)
                P.op('pe', lambda e: e.matmul(bSu.t[:, 0:256], B.ktm[:, hh * 128:(hh + 1) * 128], B.v[:, hh * 256:(hh + 1) * 256], start=True, stop=True),
                     reads=[B.ktm, B.v], writes=[bSu])
                for hl in range(2):
                    r = slice(hl * 64, hl * 64 + 64)
                    P.op('dve', lambda e: e.tensor_tensor(L.Sg[r, hh, :], L.Sg[r, hh, :], bSu.t[r, hl * 128:(hl + 1) * 128], ALU.add), reads=[L.Sg, bSu], writes=[L.Sg])
                    P.op('dve', lambda e: e.tensor_scalar(L.Sg[r, hh, :], L.Sg[r, hh, :], B.Eq[r, hh, 127:128], None, ALU.mult), reads=[L.Sg, B.Eq], writes=[L.Sg])
            else:
                P.op('pool', lambda e: e.tensor_tensor(padr, B.ktm[:, hh * 128:(hh + 1) * 128].unsqueeze(1).to_broadcast([128, 16, 128]),
                                                       g.consts[:, C_ROWM:C_ROWM + 16].unsqueeze(2).to_broadcast([128, 16, 128]), ALU.mult),
                     reads=[B.ktm, g.consts], writes=[B.padf])
                for s2 in range(8):
                    bSu = nextbank(g)
                    for si in range(2):
                        s_ = s2 * 2 + si
                        P.op('pe', lambda e: e.matmul(bSu.t[:, si * 256:(si + 1) * 256], B.padf[:, s_ * 128:(s_ + 1) * 128], B.v[:, hh * 256:(hh + 1) * 256],
                                                      start=True, stop=True), reads=[B.padf, B.v], writes=[bSu], inc=(si == 1))
                    for hl in range(2):
                        r = slice(hl * 64, hl * 64 + 64)
                        sl = B.Sh[r, 2 * s2:2 * s2 + 2, :]
                        ps_ = bSu.t[r, :].rearrange("p (s c) -> p s c", s=2)[:, :, hl * 128:(hl + 1) * 128]
                        P.op('dve', lambda e: e.tensor_tensor(sl, sl, ps_, ALU.add), reads=[B.Sh, bSu], writes=[B.Sh])
                        egl = B.Eq[r, hh, :].rearrange("p (s r) -> p s r", r=8)[:, 2 * s2:2 * s2 + 2, 7:8].to_broadcast([64, 2, 128])
                        P.op('dve', lambda e: e.tensor_tensor(sl, sl, egl, ALU.mult), reads=[B.Sh, B.Eq], writes=[B.Sh])
                P.dma('sp', g.o_gla_s[l][:, 2 * hh:2 * hh + 2, :, :].rearrange("s hl d e -> (hl d) s e"), B.Sh[:, :, :], "Shg_o", reads=[B.Sh])
        if last_prompt:
            P.dma('sp', g.o_gla_p[l].rearrange("(hh hl) d e -> (hl d) hh e", hl=2), L.Sg[:, :, :], "Sg_o", reads=[L.Sg])
        if DBG <= 5:
            g.held.clear()
            continue
        P.op('act', lambda e: e.activation(B.o[:, :], bO.t[:, :], AF.Copy), reads=[bO], writes=[B.o])
        release(g, bO)
        bZ = nextbank(g)
        proj_tm(g, wpart, 1040, 512, hT, tok0, bZ)
        gated_norm(g, L, B, B.o, 4, 128, [bZ], P_GLA_NORM, True, B.on)
        out_proj(g, MS, B, B.on, 4, wo, t)


def ssd_mixer(g, L, MS, tiles, hT, is_s):
    P, l = g.P, L.l
    wpart, wo = load_wpart(g, l, MS, [(O_MX, 1536), (O_MZ, 1040)], (1024, 1024), "s")
    B = Ctx()
    A = Arena(MS, "sA", 5)
    Bc = Arena(MS, "sB", 3)
    G = Arena(MS, "sG", 6)
    Lt = Arena(MS, "sL", 4)
    B.cin = TT(A.tt.t[:, :, :].rearrange("p s f -> p (s f)")[:, 0:12 * 176].rearrange("p (c n) -> p c n", c=12), A.keys(0, 5))
    B.cout = Bc.span(0, 3, 128)
    B.acc = G.span(0, 3, 128)
    B.tmp = G.span(3, 3, 128)
    B.cvo = TT(Lt.tt.t[0:48, 0:3, :].rearrange("p s f -> p (s f)"), Lt.keys(0, 3))
    x_tm = TT(A.tt.t[:, 0:2, :].rearrange("p s f -> p (s f)"), A.keys(0, 2))
    xdt = TT(A.tt.t[:, 2:4, :].rearrange("p s f -> p (s f)"), A.keys(2, 2))
    xw = TT(Bc.tt.t[:, 0:2, :].rearrange("p s f -> p (s f)"), Bc.keys(0, 2))
    ysb = TT(G.tt.t[:, 0:2, :].rearrange("p s f -> p (s f)"), G.keys(0, 2))
    MT = G.slot(2, (4, 128))
    ET = G.slot(3, (4, 128))
    cbT = TT(G.tt.t[:, 4, 0:256].rearrange("p (c n) -> p c n", c=2), G.keys(4, 1))
    Btm = TT(G.tt.t[:, 4, 256:512], G.keys(4, 1))
    B.gate = TT(Lt.tt.t[:, 0:2, :].rearrange("p s f -> p (s f)"), Lt.keys(0, 2))
    gBm = TT(Lt.tt.t[:, 2:4, :].rearrange("p s f -> p (s f)"), Lt.keys(2, 2))
    B.on = MS.sb("on", [128, 1024], BF16)
    B.onT = MS.sb("onT", [128, 8, 128], BF16)
    B.sm2 = MS.sb("sm2", [128, 64])
    B.rT = MS.sb("rT", [4, 512])
    sm = MS.sb("sm", [128, 160])
    stc = None
    if is_s:
        stc = TT(G.tt.t[0:48, 0:3, :].rearrange("p s f -> p (s f)"), G.keys(0, 3))
        P.dma('sp', stc[:, :], g.st_conv[l][:, 1536:3072], "stcs_d", writes=[stc])
        B.Sh = [MS.sb("Sh0", [128, 8, 64]), MS.sb("Sh1", [128, 8, 64])]
        B.padf = MS.sb("pad", [128, 17 * 128])
        padd = B.padf.t[:, 0:16 * 136].rearrange("p (s q) -> p s q", q=136)[:, :, 0:8]
        padr = B.padf.t[:, 0:2048].rearrange("p (s n) -> p s n", s=16)
        B.egts = MS.sb("egts", [128, 16, 16])
    M_U01 = C_SU01 if is_s else C_U01
    M_SEG = C_SEG if is_s else C_ONES
    mNUI = C_SNUI if is_s else C_NUI
    nsh = 0
    for ti, t in enumerate(tiles):
        tok0 = ti * 128
        last_prompt = (not is_s) and (t == g.NPT - 1)
        conv_block(g, L, MS, B, wpart, hT, tok0, is_s, 12, L.tailS, stc, last_prompt, 1536)
        cout = B.cout
        bD = nextbank(g)
        proj_tm(g, wpart, 2560, 16, hT, tok0, bD)
        P.op('dve', lambda e: e.tensor_tensor(sm[:, 128:144], bD.t[:, 0:16], L.parB[:, P_M2_DTB:P_M2_DTB + 16], ALU.add), reads=[bD, L.parB], writes=[sm])
        P.op('act', lambda e: e.activation(sm[:, 128:144], sm[:, 128:144], AF.Exp), reads=[sm], writes=[sm])
        P.op('act', lambda e: e.activation(sm[:, 0:16], sm[:, 128:144], AF.Ln, bias=1.0), reads=[sm], writes=[sm])
        P.op('dve', lambda e: e.tensor_tensor(sm[:, 16:32], sm[:, 0:16], L.pd[:, 4:20], ALU.mult), reads=[sm, L.pd], writes=[sm])
        bC = nextbank(g)
        P.op('pe', lambda e: e.matmul(bC.t[:, 0:16], cst(g, M_U01), sm[:, 16:32], start=True, stop=True), reads=[g.consts, sm], writes=[bC], inc=False)
        P.op('pe', lambda e: e.matmul(bC.t[:, 16:32], cst(g, M_SEG), sm[:, 16:32], start=True, stop=True), reads=[g.consts, sm], writes=[bC])
        P.op('dve', lambda e: e.tensor_copy(sm[:, 32:64], bC.t[:, 0:32]), reads=[bC], writes=[sm])
        P.op('dve', lambda e: e.tensor_scalar(sm[:, 64:80], sm[:, 32:48], -1.0, None, ALU.mult), reads=[sm], writes=[sm])
        P.op('act', lambda e: e.activation(sm[:, 80:96], sm[:, 32:48], AF.Exp), reads=[sm], writes=[sm])
        P.op('dve', lambda e: e.tensor_tensor(sm[:, 96:112], sm[:, 48:64], sm[:, 32:48], ALU.subtract), reads=[sm], writes=[sm])
        P.op('act', lambda e: e.activation(sm[:, 96:112], sm[:, 96:112], AF.Exp), reads=[sm], writes=[sm])
        P.op('act', lambda e: e.activation(sm[:, 112:128], sm[:, 48:64], AF.Exp), reads=[sm], writes=[sm])

        def bc16(c0, g_=None):
            if g_ is None:
                return sm[:, c0:c0 + 16].unsqueeze(2).to_broadcast([128, 16, 64])
            return sm[:, c0 + 8 * g_:c0 + 8 * g_ + 8].unsqueeze(2).to_broadcast([128, 8, 64])
        for hb in range(2):
            bX = nextbank(g)
            for c in range(4):
                P.op('pe', lambda e: e.transpose(bX.t[:, c * 128:(c + 1) * 128], cout[:, hb * 4 + c, :], cst(g, C_IDENT)), reads=[cout, g.consts], writes=[bX], inc=(c == 3))
            P.op('act', lambda e: e.activation(x_tm[:, hb * 512:(hb + 1) * 512], bX.t[:, :], AF.Copy), reads=[bX, B.cin], writes=[x_tm])
        x3 = x_tm.t.rearrange("p (h e) -> p h e", h=16)
        xdt3 = xdt.t.rearrange("p (h e) -> p h e", h=16)
        xw3 = xw.t.rearrange("p (h e) -> p h e", h=16)
        P.op('dve', lambda e: e.tensor_tensor(xdt3, x3, bc16(0), ALU.mult), reads=[x_tm, sm], writes=[xdt])
        P.op('pool', lambda e: e.tensor_tensor(xw3, xdt3, bc16(96), ALU.mult), reads=[xdt, sm, cout], writes=[xw])
        bB = nextbank(g)
        for g_ in range(2):
            P.op('pe', lambda e: e.transpose(bB.t[:, g_ * 128:(g_ + 1) * 128], cout[:, 8 + g_, :], cst(g, C_IDENT)), reads=[cout, g.consts], writes=[bB], inc=False)
        for g_ in range(2):
            P.op('pe', lambda e: e.matmul(bB.t[:, 256 + g_ * 128:256 + (g_ + 1) * 128], cout[:, 8 + g_, :], cout[:, 10 + g_, :], start=True, stop=True),
                 reads=[cout], writes=[bB], inc=(g_ == 1))
        P.op('act', lambda e: e.activation(Btm[:, :], bB.t[:, 0:256], AF.Copy), reads=[bB, B.acc, B.tmp], writes=[Btm])
        P.op('act', lambda e: e.activation(cbT[:, :, :], bB.t[:, 256:512].rearrange("p (c n) -> p c n", c=2), AF.Copy), reads=[bB], writes=[cbT])
        bY = [nextbank(g, hold=True), nextbank(g, hold=True)]
        for q in range(4):
            bT = nextbank(g)
            P.op('pe', lambda e: e.transpose(bT.t[0:4, 0:128], sm[:, 32 + 4 * q:36 + 4 * q], cst(g, C_IDENT)), reads=[sm, g.consts], writes=[bT], inc=False)
            P.op('pe', lambda e: e.transpose(bT.t[0:4, 128:256], sm[:, 64 + 4 * q:68 + 4 * q], cst(g, C_IDENT)), reads=[sm, g.consts], writes=[bT])
            P.op('dve', lambda e: e.tensor_copy(B.rT[:, 0:256], bT.t[0:4, 0:256]), reads=[bT], writes=[B.rT])
            bank = expo_bank(g, B, mNUI, B.rT[:, 0:128], B.rT[:, 128:256])
            P.op('act', lambda e: e.activation(ET[:, :, :], bank.t[:, :].rearrange("p (h n) -> p h n", h=4), AF.Exp), reads=[bank], writes=[ET])
            g_ = q // 2
            P.op('dve', lambda e: e.tensor_tensor(MT[:, :, :], ET[:, :, :], cbT[:, g_, :].unsqueeze(1).to_broadcast([128, 4, 128]), ALU.mult),
                 reads=[ET, cbT], writes=[MT])
            for hq in range(4):
                h = 4 * q + hq
                P.op('pe', lambda e: e.matmul(bY[g_].t[:, (h % 8) * 64:(h % 8) * 64 + 64], MT[:, hq, :], xdt[:, h * 64:(h + 1) * 64], start=True, stop=True),
                     reads=[MT, xdt], writes=[bY[g_]], inc=(hq == 3))
        bYI = [nextbank(g, hold=True), nextbank(g, hold=True)]
        if not is_s:
            for g_ in range(2):
                P.op('pe', lambda e: e.matmul(bYI[g_].t[:, :], cout[:, 10 + g_, :], L.Ss.t[:, :, :].rearrange("p h e -> p (h e)")[:, g_ * 512:(g_ + 1) * 512], start=True, stop=True),
                     reads=[cout, L.Ss], writes=[bYI[g_]])
        else:
            lt3 = B.padf.t[:, 0:256].rearrange("p (s h) -> p s h", h=16)
            P.op('dve', lambda e: e.tensor_tensor(lt3, sm[:, 48:64].unsqueeze(1).to_broadcast([128, 16, 16]),
                                                  g.consts[:, C_ROWM8:C_ROWM8 + 16].unsqueeze(2).to_broadcast([128, 16, 16]), ALU.mult),
                 reads=[sm, g.consts], writes=[B.padf])
            bE = nextbank(g)
            P.op('pe', lambda e: e.matmul(bE.t[:, 0:256], cst(g, C_ONES), B.padf[:, 0:256], start=True, stop=True), reads=[g.consts, B.padf], writes=[bE])
            P.op('act', lambda e: e.activation(B.egts[:, :, :], bE.t[:, 0:256].rearrange("p (s h) -> p s h", h=16), AF.Exp), reads=[bE], writes=[B.egts])
            for g_ in range(2):
                P.op('pool', lambda e: e.memset(B.padf[:, :], 0.0), writes=[B.padf])
                P.op('pool', lambda e: e.tensor_copy(padd, cout[:, 10 + g_, :].rearrange("p (s r) -> p s r", r=8)), reads=[cout, B.padf], writes=[B.padf])
                for s_ in range(16):
                    Sh = B.Sh[nsh % 2]
                    nsh += 1
                    P.dma('sp', Sh[:, :, :], g.st_ssm[l][s_, 8 * g_:8 * g_ + 8, :, :].rearrange("h n p -> n h p"), Sh.key, writes=[Sh])
                    P.op('pe', lambda e: e.matmul(bYI[g_].t[:, :], B.padf[:, s_ * 128:(s_ + 1) * 128], Sh.t[:, :, :].rearrange("p h e -> p (h e)"), start=(s_ == 0), stop=(s_ == 15)),
                         reads=[B.padf, Sh], writes=[bYI[g_]], inc=True)
        y3 = ysb.t.rearrange("p (h e) -> p h e", h=16)
        for g_ in range(2):
            P.op('dve', lambda e: e.tensor_tensor(y3[:, 8 * g_:8 * g_ + 8, :], bYI[g_].t[:, :].rearrange("p (h e) -> p h e", h=8), bc16(80, g_), ALU.mult),
                 reads=[bYI[g_], sm], writes=[ysb])
            P.op('dve', lambda e: e.tensor_tensor(ysb[:, g_ * 512:(g_ + 1) * 512], ysb[:, g_ * 512:(g_ + 1) * 512], bY[g_].t[:, :], ALU.add),
                 reads=[ysb, bY[g_]], writes=[ysb])
        release(g, bY[0], bY[1], bYI[0], bYI[1])
        gt3 = B.gate.t.rearrange("p (h e) -> p h e", h=16)
        P.op('pool', lambda e: e.tensor_tensor(gt3, x3, L.parB[:, P_M2_D:P_M2_D + 16].unsqueeze(2).to_broadcast([128, 16, 64]), ALU.mult),
             reads=[x_tm, L.parB, B.cvo], writes=[B.gate])
        P.op('dve', lambda e: e.tensor_tensor(ysb[:, :], ysb[:, :], B.gate[:, :], ALU.add), reads=[ysb, B.gate], writes=[ysb])
        P.dma('sp', gBm[:, :], g.norms_d[3 * g.DEPTH + 1 + l:3 * g.DEPTH + 2 + l, :].partition_broadcast(128), "gBm_d", reads=[B.cvo], writes=[gBm])
        bZ = [nextbank(g, hold=True), nextbank(g, hold=True)]
        proj_tm(g, wpart, 1536, 512, hT, tok0, bZ[0])
        proj_tm(g, wpart, 2048, 512, hT, tok0, bZ[1])
        gated_norm(g, L, B, ysb, 16, 64, bZ, 0, False, B.on, norm_tt=gBm)
        release(g, bZ[0], bZ[1])
        out_proj(g, MS, B, B.on, 8, wo, t)
        if not is_s:
            for g_ in range(2):
                bSu = nextbank(g)
                P.op('pe', lambda e: e.matmul(bSu.t[:, :], Btm[:, g_ * 128:(g_ + 1) * 128], xw[:, g_ * 512:(g_ + 1) * 512], start=True, stop=True),
                     reads=[Btm, xw], writes=[bSu])
                sl = L.Ss[:, 8 * g_:8 * g_ + 8, :]
                P.op('dve', lambda e: e.tensor_tensor(sl, sl, bc16(112, g_), ALU.mult), reads=[L.Ss, sm], writes=[L.Ss])
                P.op('dve', lambda e: e.tensor_tensor(sl, sl, bSu.t[:, :].rearrange("p (h e) -> p h e", h=8), ALU.add), reads=[L.Ss, bSu], writes=[L.Ss])
            if last_prompt:
                P.dma('sp', g.o_ssm_p[l].rearrange("h n p -> n h p"), L.Ss[:, :, :], "Ss_o", reads=[L.Ss])
        else:
            for g_ in range(2):
                P.op('pool', lambda e: e.tensor_tensor(padr, Btm[:, g_ * 128:(g_ + 1) * 128].unsqueeze(1).to_broadcast([128, 16, 128]),
                                                       g.consts[:, C_ROWM:C_ROWM + 16].unsqueeze(2).to_broadcast([128, 16, 128]), ALU.mult),
                     reads=[Btm, g.consts], writes=[B.padf])
                for s_ in range(16):
                    Sh = B.Sh[nsh % 2]
                    nsh += 1
                    P.dma('sp', Sh[:, :, :], g.st_ssm[l][s_, 8 * g_:8 * g_ + 8, :, :].rearrange("h n p -> n h p"), Sh.key, writes=[Sh])
                    bSu = nextbank(g)
                    P.op('pe', lambda e: e.matmul(bSu.t[:, :], B.padf[:, s_ * 128:(s_ + 1) * 128], xw[:, g_ * 512:(g_ + 1) * 512], start=True, stop=True),
                         reads=[B.padf, xw], writes=[bSu])
                    P.op('dve', lambda e: e.tensor_tensor(Sh[:, :, :], Sh[:, :, :], B.egts[:, s_, 8 * g_:8 * g_ + 8].unsqueeze(2).to_broadcast([128, 8, 64]), ALU.mult),
                         reads=[Sh, B.egts], writes=[Sh])
                    P.op('dve', lambda e: e.tensor_tensor(Sh[:, :, :], Sh[:, :, :], bSu.t[:, :].rearrange("p (h e) -> p h e", h=8), ALU.add),
                         reads=[Sh, bSu], writes=[Sh])
                    P.dma('sp', g.o_ssm_s[l][s_, 8 * g_:8 * g_ + 8, :, :].rearrange("h n p -> n h p"), Sh[:, :, :], Sh.key + "_o", reads=[Sh])


def make_in_maps(inp, ncores=8, NPT=16, DEPTH=2):
    f = lambda a: np.ascontiguousarray(np.asarray(a, dtype=np.float32))
    consts = make_consts()
    params = np.zeros((DEPTH, NPAR), np.float32)
    for l in range(DEPTH):
        params[l, P_DN_ALOG:P_DN_ALOG + 4] = inp['dn_a_log'][l]
        params[l, P_DN_DTB:P_DN_DTB + 4] = inp['dn_dt_bias'][l]
        params[l, P_M2_ALOG:P_M2_ALOG + 16] = inp['m2_a_log'][l]
        params[l, P_M2_DTB:P_M2_DTB + 16] = inp['m2_dt_bias'][l]
        params[l, P_M2_D:P_M2_D + 16] = inp['m2_d'][l]
        params[l, P_DN_NORM:P_DN_NORM + 128] = inp['dn_norm'][l]
        params[l, P_GLA_NORM:P_GLA_NORM + 128] = inp['gla_norm'][l]
    norms = np.zeros((4 * DEPTH + 1, D), np.float32)
    for l in range(DEPTH):
        norms[3 * l] = inp['ffn1_norm'][l]
        norms[3 * l + 1] = inp['mix_norm'][l]
        norms[3 * l + 2] = inp['ffn2_norm'][l]
    norms[3 * DEPTH] = inp['final_norm']
    for l in range(DEPTH):
        norms[3 * DEPTH + 1 + l] = inp['m2_norm'][l]
    convw = f(np.asarray(inp['conv_w'])[:DEPTH].reshape(DEPTH, 4, 24, 128).transpose(0, 3, 2, 1).reshape(DEPTH, 128, 96))
    convb = f(np.asarray(inp['conv_b'])[:DEPTH].reshape(DEPTH, 24, 128).transpose(0, 2, 1))
    shared = {
        'consts': consts, 'params': params, 'norms': norms, 'convw': convw, 'convb': convb,
        'ffn1_w_gu': f(inp['ffn1_w_gu'][:DEPTH]), 'ffn2_w_gu': f(inp['ffn2_w_gu'][:DEPTH]),
        'ffn1_w_down': f(inp['ffn1_w_down'][:DEPTH]), 'ffn2_w_down': f(inp['ffn2_w_down'][:DEPTH]),
        'w_in': f(inp['w_in'][:DEPTH]), 'w_out': f(inp['w_out'][:DEPTH]),
        'gla_w_up': f(inp['gla_w_up'][:DEPTH]), 'gla_b_up': f(np.asarray(inp['gla_b_up'])[:DEPTH].reshape(DEPTH, 1, 256)),
    }
    maps = []
    for c in range(ncores):
        m = dict(shared)
        xp = np.asarray(inp['x_prompt'])[c, :NPT * 128]
        xs = np.asarray(inp['x_sample'])[16 * c:16 * c + 16].reshape(128, D)
        m['xin'] = f(np.concatenate([xp, xs], 0))
        m['st_conv'] = f(np.asarray(inp['state_conv'])[:DEPTH, 16 * c:16 * c + 16].reshape(DEPTH, 48, CONV_CH))
        m['st_delta'] = f(np.asarray(inp['state_delta'])[:DEPTH, 16 * c:16 * c + 16])
        m['st_gla'] = f(np.asarray(inp['state_gla'])[:DEPTH, 16 * c:16 * c + 16])
        m['st_ssm'] = f(np.asarray(inp['state_ssm'])[:DEPTH, 16 * c:16 * c + 16])
        maps.append(m)
    return maps


def gather_outputs(res, ncores=8, NPT=16, DEPTH=2):
    yp = np.stack([r['y'][:NPT * 128] for r in res], 0)
    ys = np.concatenate([r['y'][NPT * 128:].reshape(16, 8, D) for r in res], 0)
    cp = np.stack([r['o_conv_p'] for r in res], 1)
    dp = np.stack([r['o_delta_p'] for r in res], 1)
    gp = np.stack([r['o_gla_p'] for r in res], 1)
    sp = np.stack([r['o_ssm_p'] for r in res], 1)
    cs = np.concatenate([r['o_conv_s'].reshape(DEPTH, 16, 3, CONV_CH) for r in res], 1)
    ds = np.concatenate([r['o_delta_s'] for r in res], 1)
    gs = np.concatenate([r['o_gla_s'] for r in res], 1)
    ss = np.concatenate([r['o_ssm_s'] for r in res], 1)
    return tuple(np.ascontiguousarray(a, dtype=np.float32) for a in (yp, ys, cp, dp, gp, sp, cs, ds, gs, ss))


_NC_CACHE = {}


def kernel(**inputs):
    if 'nc' not in _NC_CACHE:
        _NC_CACHE['nc'] = build()[0]
    nc = _NC_CACHE['nc']
    in_maps = make_in_maps(inputs)
    res = run_bass_kernel_spmd(nc, in_maps, core_ids=list(range(8)))
    return gather_outputs(res.results)
```

```python
import numpy as np
import concourse.bass as bass
import concourse.mybir as mybir
from concourse.bass_utils import run_bass_kernel_spmd

F32 = mybir.dt.float32
BF16 = mybir.dt.bfloat16
AF = mybir.ActivationFunctionType
ALU = mybir.AluOpType
AX = mybir.AxisListType

D = 1024
DFF = 2816
EPS = 1e-6
NEG = -30000.0
CONV_CH = 3072
IN_COLS = 6184
O_DQ, O_DK, O_DV, O_MX, O_MB, O_MC = 0, 512, 1024, 1536, 2560, 2816
O_DZ, O_DA, O_DB, O_GQ, O_GK, O_GV, O_GLOW, O_GG, O_MZ, O_MDT = 3072, 3584, 3588, 3592, 3848, 4104, 4616, 4632, 5144, 6168

C_IDENT = 0
C_ONES = 128
C_U01 = 256
C_NUI = 384
C_NUS = 512
C_NLS = 640
C_SEG = 768
C_SU01 = 896
C_SNUI = 1024
C_SNUS = 1152
C_SNLS = 1280
C_ESEL = 1408
C_ROWM = 1920
C_ROWM8 = 1936
C_INV128 = 1952
NCONST = 1984

P_DN_ALOG, P_DN_DTB, P_M2_ALOG, P_M2_DTB, P_M2_D = 0, 4, 8, 24, 40
P_DN_NORM, P_GLA_NORM, P_M2_NORM = 64, 192, 320
NPAR = 320


def make_consts():
    c = np.zeros((128, NCONST), np.float32)
    p = np.arange(128)[:, None]
    f = np.arange(128)[None, :]
    seg = (p // 8) == (f // 8)
    c[:, C_IDENT:C_IDENT + 128] = (p == f)
    c[:, C_ONES:C_ONES + 128] = 1.0
    c[:, C_U01:C_U01 + 128] = (p <= f)
    c[:, C_NUI:C_NUI + 128] = np.where(p <= f, 0.0, NEG)
    c[:, C_NUS:C_NUS + 128] = np.where(p < f, 0.0, NEG)
    c[:, C_NLS:C_NLS + 128] = np.where(p > f, 0.0, NEG)
    c[:, C_SEG:C_SEG + 128] = seg
    c[:, C_SU01:C_SU01 + 128] = seg & (p <= f)
    c[:, C_SNUI:C_SNUI + 128] = np.where(seg & (p <= f), 0.0, NEG)
    c[:, C_SNUS:C_SNUS + 128] = np.where(seg & (p < f), 0.0, NEG)
    c[:, C_SNLS:C_SNLS + 128] = np.where(seg & (p > f), 0.0, NEG)
    for h in range(4):
        c[h, C_ESEL + h * 128:C_ESEL + (h + 1) * 128] = 1.0
    s = np.arange(16)[None, :]
    c[:, C_ROWM:C_ROWM + 16] = ((p // 8) == s)
    c[:, C_ROWM8:C_ROWM8 + 16] = ((p // 8) == s) / 8.0
    c[:, C_INV128] = 1.0 / 128
    return c


class TT:
    def __init__(self, t, key):
        self.t = t
        self.key = key

    def __getitem__(self, idx):
        return self.t[idx]


class Prog:
    def __init__(self, nc):
        self.nc = nc
        self.engs = {'pe': nc.tensor, 'act': nc.scalar, 'dve': nc.vector, 'pool': nc.gpsimd, 'sp': nc.sync}
        self.esem = {e: nc.alloc_semaphore("sem_" + e) for e in self.engs}
        self.cnt = {e: 0 for e in self.engs}
        self.seen = {e: {} for e in self.engs}
        self.lastw = {}
        self.readers = {}
        self.dsem = {}
        self.dcnt = {}
        self.dtot = {}
        self.nwaits = 0
        self.nops = 0
        self.nuid = 0

    def sb(self, name, shape, dtype=F32):
        self.nuid += 1
        nm = "%s_%d" % (name, self.nuid)
        return TT(self.nc.alloc_sbuf_tensor(nm, list(shape), dtype), nm)

    @staticmethod
    def _keys(lst):
        out = []
        for x in lst:
            k = x.key if isinstance(x, TT) else x
            if isinstance(k, (tuple, list)):
                out.extend(k)
            else:
                out.append(k)
        return out

    def _wait(self, eng, ev):
        sem, val = ev
        k = id(sem)
        if k in self.dtot:
            val = max(val, self.dtot[k])
        if self.seen[eng].get(k, 0) >= val:
            return
        if sem is self.esem[eng] and val > self.cnt[eng]:
            return
        self.seen[eng][k] = val
        self.engs[eng].wait_ge(sem, val)
        self.nwaits += 1

    def _deps(self, eng, reads, writes, skipsem=None):
        evs = []
        for k in reads:
            if k in self.lastw:
                evs.append(self.lastw[k])
        for k in writes:
            if k in self.lastw and self.lastw[k][0] is not skipsem:
                evs.append(self.lastw[k])
            for ev in self.readers.get(k, {}).values():
                evs.append(ev)
        for ev in evs:
            self._wait(eng, ev)

    def _record(self, ev, reads, writes):
        for k in reads:
            self.readers.setdefault(k, {})[id(ev[0])] = ev
        for k in writes:
            self.lastw[k] = ev
            self.readers[k] = {}

    def op(self, eng, fn, reads=(), writes=(), inc=True):
        reads = self._keys(reads)
        writes = self._keys(writes)
        self._deps(eng, reads, writes)
        ins = fn(self.engs[eng])
        self.nops += 1
        if inc:
            self.cnt[eng] += 1
            ins.then_inc(self.esem[eng], 1)
            ev = (self.esem[eng], self.cnt[eng])
        else:
            ev = (self.esem[eng], self.cnt[eng] + 1)
        self._record(ev, reads, writes)
        return ins

    def dma(self, eng, out, in_, group, reads=(), writes=()):
        reads = self._keys(reads)
        writes = self._keys(writes)
        if group not in self.dsem:
            self.dsem[group] = self.nc.alloc_semaphore("dsem_%d" % len(self.dsem))
            self.dcnt[group] = 0
        sem = self.dsem[group]
        self._deps(eng, reads, writes, skipsem=sem)
        ins = self.engs[eng].dma_start(out=out, in_=in_)
        self.nops += 1
        self.dcnt[group] += 16
        self.dtot[id(sem)] = self.dcnt[group]
        ins.then_inc(sem, 16)
        ev = (sem, self.dcnt[group])
        self._record(ev, reads, writes)
        return ins

    def finish(self, eng='sp'):
        for g, sem in self.dsem.items():
            self._wait(eng, (sem, self.dcnt[g]))
        for e in self.engs:
            if e != eng and self.cnt[e] > 0:
                self._wait(eng, (self.esem[e], self.cnt[e]))


class Ctx:
    pass


def build(NPT=16, DEPTH=2, stages=('ffn', 'mix')):
    NT = NPT + 1
    NTOK = NT * 128
    nc = bass.Bass("TRN2", target_bir_lowering=False)
    P = Prog(nc)
    g = Ctx()
    g.nc, g.P, g.NPT, g.NT, g.DEPTH = nc, P, NPT, NT, DEPTH
    g.mixers = [m for m in ('delta', 'gla', 'ssd') if m in stages or 'mix' in stages]

    def din(name, shape):
        return nc.dram_tensor(name, list(shape), F32, kind="ExternalInput").ap()

    def dout(name, shape):
        return nc.dram_tensor(name, list(shape), F32, kind="ExternalOutput").ap()

    g.xin = din("xin", [NTOK, D])
    g.consts_d = din("consts", [128, NCONST])
    g.params_d = din("params", [DEPTH, NPAR])
    g.st_conv = din("st_conv", [DEPTH, 48, CONV_CH])
    g.st_delta = din("st_delta", [DEPTH, 16, 4, 128, 128])
    g.st_gla = din("st_gla", [DEPTH, 16, 4, 64, 128])
    g.st_ssm = din("st_ssm", [DEPTH, 16, 16, 128, 64])
    g.norms_d = din("norms", [4 * DEPTH + 1, D])
    g.w_gu = [din("ffn1_w_gu", [DEPTH, D, 2 * DFF]), din("ffn2_w_gu", [DEPTH, D, 2 * DFF])]
    g.w_down = [din("ffn1_w_down", [DEPTH, DFF, D]), din("ffn2_w_down", [DEPTH, DFF, D])]
    g.w_in = din("w_in", [DEPTH, D, IN_COLS])
    g.w_out = din("w_out", [DEPTH, 2 * D, D])
    g.convw_d = din("convw", [DEPTH, 128, 24 * 4])
    g.convb_d = din("convb", [DEPTH, 128, 24])
    g.gla_wup = din("gla_w_up", [DEPTH, 16, 256])
    g.gla_bup = din("gla_b_up", [DEPTH, 1, 256])

    g.y = dout("y", [NTOK, D])
    g.o_conv_p = dout("o_conv_p", [DEPTH, 3, CONV_CH])
    g.o_delta_p = dout("o_delta_p", [DEPTH, 4, 128, 128])
    g.o_gla_p = dout("o_gla_p", [DEPTH, 4, 64, 128])
    g.o_ssm_p = dout("o_ssm_p", [DEPTH, 16, 128, 64])
    g.o_conv_s = dout("o_conv_s", [DEPTH, 48, CONV_CH])
    g.o_delta_s = dout("o_delta_s", [DEPTH, 16, 4, 128, 128])
    g.o_gla_s = dout("o_gla_s", [DEPTH, 16, 4, 64, 128])
    g.o_ssm_s = dout("o_ssm_s", [DEPTH, 16, 16, 128, 64])

    g.x = P.sb("x", [128, NT, D])
    g.xk = ["x_t%d" % t for t in range(NT)]
    g.consts = P.sb("consts", [128, NCONST])
    g.identb = P.sb("identb", [128, 128], BF16)
    g.stat = P.sb("stat", [128, 64])
    g.ps = [TT(nc.alloc_psum_tensor("psb%d" % i, [128, 512], F32), "psb%d" % i) for i in range(8)]
    g.psi = 0
    g.held = set()

    P.dma('sp', g.consts[:, :], g.consts_d, 'consts', writes=[g.consts])
    for t in range(NT):
        P.dma('sp', g.x[:, t, :], g.xin[t * 128:(t + 1) * 128, :], 'xload', writes=[g.xk[t]])
    P.op('dve', lambda e: e.tensor_copy(g.identb[:, :], g.consts[:, C_IDENT:C_IDENT + 128]), reads=[g.consts], writes=[g.identb])
    g.maskb = P.sb("maskb", [128, 6, 128], BF16)
    g.maskb_idx = {}
    for i_, off_ in enumerate((C_NUI, C_NUS, C_NLS, C_SNUI, C_SNUS, C_SNLS)):
        g.maskb_idx[off_] = i_
        P.op('dve', lambda e: e.tensor_copy(g.maskb[:, i_, :], g.consts[:, off_:off_ + 128]), reads=[g.consts], writes=[g.maskb])

    for l in range(DEPTH):
        if 'ffn' in stages:
            ffn_phase(g, l, 0)
        if g.mixers:
            mix_phase(g, l)
        if 'ffn' in stages:
            ffn_phase(g, l, 1)
    final_phase(g)
    P.finish('sp')
    g.stats = (P.nops, P.nwaits, len(P.dsem))
    return nc, g


def nextbank(g, hold=False):
    while True:
        b = g.ps[g.psi % 8]
        g.psi += 1
        if b.key not in g.held:
            break
    if hold:
        g.held.add(b.key)
    return b


def release(g, *banks):
    for b in banks:
        g.held.discard(b.key)


def barrier(g, skip=()):
    P = g.P
    for e in P.engs:
        for gname, sem in P.dsem.items():
            if gname in skip:
                continue
            P._wait(e, (sem, P.dcnt[gname]))
        for e2 in P.engs:
            if e2 != e and P.cnt[e2] > 0:
                P._wait(e, (P.esem[e2], P.cnt[e2]))


def norm_tiles(g, tiles, gain_row, hT, hn, gB, tok0=0):
    P, nc = g.P, g.nc
    n = len(tiles)
    P.dma('sp', gB[:, :], g.norms_d[gain_row:gain_row + 1, :].partition_broadcast(128), 'gB_d', writes=[gB])
    junk = hn[0]
    for i, t in enumerate(tiles):
        P.op('act', lambda e: e.activation(junk[:, :], g.x[:, t, :], AF.Square, accum_out=g.stat[:, i:i + 1]),
             reads=[g.xk[t]], writes=[junk, g.stat])
    P.op('act', lambda e: e.activation(g.stat[:, 16:16 + n], g.stat[:, 0:n], AF.Ln, bias=EPS, scale=1.0 / D),
         reads=[g.stat], writes=[g.stat])
    P.op('act', lambda e: e.activation(g.stat[:, 32:32 + n], g.stat[:, 16:16 + n], AF.Exp, scale=-0.5),
         reads=[g.stat], writes=[g.stat])
    for i, t in enumerate(tiles):
        h = hn[i % 2]
        P.op('dve', lambda e: e.scalar_tensor_tensor(h[:, :], g.x[:, t, :], g.stat[:, 32 + i:33 + i], gB[:, :], ALU.mult, ALU.mult),
             reads=[g.xk[t], g.stat, gB], writes=[h])
        bank = nextbank(g)
        pb = bank.t[:, :].bitcast(BF16)
        for kc in range(8):
            P.op('pe', lambda e: e.transpose(pb[:, kc * 128:(kc + 1) * 128], h[:, kc * 128:(kc + 1) * 128], g.identb[:, :]),
                 reads=[h, g.identb], writes=[bank], inc=(kc == 7))
        c0 = tok0 + i * 128
        P.op('act', lambda e: e.activation(hT[:, :, c0:c0 + 128], pb.rearrange("p (k n) -> p k n", k=8), AF.Copy),
             reads=[bank], writes=[hT])


def ffn_phase(g, l, which):
    P, nc = g.P, g.nc
    NT = g.NT
    half = (NT + 1) // 2
    sbs = [list(range(0, NT - half)), list(range(NT - half, NT))]
    SBW = half * 128
    barrier(g)
    w_gu = g.w_gu[which][l]
    w_down = g.w_down[which][l]
    tag = "f%d_%d_" % (l, which)
    with nc.sbuf_tensor(tag + "hT", [128, 8, SBW], BF16) as hT_, \
            nc.sbuf_tensor(tag + "actT", [128, 22, SBW], BF16) as actT_, \
            nc.sbuf_tensor(tag + "wgu0", [128, 8, 512], BF16) as wgu0, nc.sbuf_tensor(tag + "wgu1", [128, 8, 512], BF16) as wgu1, \
            nc.sbuf_tensor(tag + "wd0", [128, 22, 256], BF16) as wd0, nc.sbuf_tensor(tag + "wd1", [128, 22, 256], BF16) as wd1, \
            nc.sbuf_tensor(tag + "gB", [128, D], F32) as gB_, \
            nc.sbuf_tensor(tag + "hn0", [128, D], BF16) as hn0, nc.sbuf_tensor(tag + "hn1", [128, D], BF16) as hn1, \
            nc.sbuf_tensor(tag + "sg0", [128, 512], F32) as sg0, nc.sbuf_tensor(tag + "sg1", [128, 512], F32) as sg1:
        hT = TT(hT_, tag + "hT")
        actT = TT(actT_, tag + "actT")
        wgu = [TT(wgu0, tag + "wgu0"), TT(wgu1, tag + "wgu1")]
        wd = [TT(wd0, tag + "wd0"), TT(wd1, tag + "wd1")]
        gB = TT(gB_, tag + "gB")
        hn = [TT(hn0, tag + "hn0"), TT(hn1, tag + "hn1")]
        sg = [TT(sg0, tag + "sg0"), TT(sg1, tag + "sg1")]
        nsg = 0
        for sbi, tiles in enumerate(sbs):
            ntok = len(tiles) * 128
            norm_tiles(g, tiles, 3 * l + (0 if which == 0 else 2), hT, hn, gB)
            tgs = [(c0, min(512, ntok - c0)) for c0 in range(0, ntok, 512)]
            for j in range(11):
                wb = wgu[j % 2]
                P.dma('pool', wb[:, :, 0:256], w_gu[:, 256 * j:256 * j + 256].rearrange("(k p) n -> p k n", p=128),
                      "wgu%d_d" % (j % 2), writes=[wb])
                P.dma('pool', wb[:, :, 256:512], w_gu[:, DFF + 256 * j:DFF + 256 * j + 256].rearrange("(k p) n -> p k n", p=128),
                      "wgu%d_d" % (j % 2), writes=[wb])
                for c in range(2):
                    for (c0, w) in tgs:
                        bA = nextbank(g)
                        bB = nextbank(g)
                        for kc in range(8):
                            P.op('pe', lambda e: e.matmul(bA.t[:, 0:w], wb[:, kc, c * 128:(c + 1) * 128], hT[:, kc, c0:c0 + w],
                                                          start=(kc == 0), stop=(kc == 7)),
                                 reads=[wb, hT], writes=[bA], inc=(kc == 7))
                        for kc in range(8):
                            P.op('pe', lambda e: e.matmul(bB.t[:, 0:w], wb[:, kc, 256 + c * 128:256 + (c + 1) * 128], hT[:, kc, c0:c0 + w],
                                                          start=(kc == 0), stop=(kc == 7)),
                                 reads=[wb, hT], writes=[bB], inc=(kc == 7))
                        s_ = sg[nsg % 2]
                        nsg += 1
                        P.op('act', lambda e: e.activation(s_[:, 0:w], bA.t[:, 0:w], AF.Silu), reads=[bA], writes=[s_])
                        P.op('dve', lambda e: e.tensor_tensor(actT[:, 2 * j + c, c0:c0 + w], s_[:, 0:w], bB.t[:, 0:w], ALU.mult),
                             reads=[s_, bB], writes=[actT])
            for q in range(4):
                wq = wd[q % 2]
                P.dma('pool', wq[:, :, :], w_down[:, 256 * q:256 * q + 256].rearrange("(k p) n -> p k n", p=128),
                      "wd%d_d" % (q % 2), writes=[wq])
                for i, t in enumerate(tiles):
                    bD = nextbank(g)
                    for kc in range(22):
                        P.op('pe', lambda e: e.matmul(bD.t[:, 0:256], actT[:, kc, i * 128:(i + 1) * 128], wq[:, kc, :],
                                                      start=(kc == 0), stop=(kc == 21)),
                             reads=[actT, wq], writes=[bD], inc=(kc == 21))
                    P.op('dve', lambda e: e.scalar_tensor_tensor(g.x[:, t, 256 * q:256 * q + 256], bD.t[:, 0:256], 0.5,
                                                                 g.x[:, t, 256 * q:256 * q + 256], ALU.mult, ALU.add),
                         reads=[bD, g.xk[t]], writes=[g.xk[t]])
        barrier(g)


def final_phase(g):
    P, nc = g.P, g.nc
    NT = g.NT
    barrier(g)
    with nc.sbuf_tensor("fin_gB", [128, D], F32) as gB_, nc.sbuf_tensor("fin_o0", [128, D], F32) as o0, \
            nc.sbuf_tensor("fin_o1", [128, D], F32) as o1:
        gB = TT(gB_, "fin_gB")
        ob = [TT(o0, "fin_o0"), TT(o1, "fin_o1")]
        P.dma('sp', gB[:, :], g.norms_d[3 * g.DEPTH:3 * g.DEPTH + 1, :].partition_broadcast(128), 'fin_gB', writes=[gB])
        for t in range(NT):
            o = ob[t % 2]
            P.op('act', lambda e: e.activation(o[:, :], g.x[:, t, :], AF.Square, accum_out=g.stat[:, 0:1]),
                 reads=[g.xk[t]], writes=[o, g.stat])
            P.op('act', lambda e: e.activation(g.stat[:, 1:2], g.stat[:, 0:1], AF.Ln, bias=EPS, scale=1.0 / D),
                 reads=[g.stat], writes=[g.stat])
            P.op('act', lambda e: e.activation(g.stat[:, 2:3], g.stat[:, 1:2], AF.Exp, scale=-0.5),
                 reads=[g.stat], writes=[g.stat])
            P.op('dve', lambda e: e.scalar_tensor_tensor(o[:, :], g.x[:, t, :], g.stat[:, 2:3], gB[:, :], ALU.mult, ALU.mult),
                 reads=[g.xk[t], g.stat, gB], writes=[o])
            P.dma('sp', g.y[t * 128:(t + 1) * 128, :], o[:, :], o.key + "_out", reads=[o])
        barrier(g)


class Scope:
    def __init__(self, g, tag):
        import contextlib
        self.g, self.tag = g, tag
        self.es = contextlib.ExitStack()

    def __enter__(self):
        self.es.__enter__()
        return self

    def __exit__(self, *a):
        return self.es.__exit__(*a)

    def sb(self, name, shape, dtype=F32):
        g = self.g
        g.P.nuid += 1
        nm = "%s_%s_%d" % (self.tag, name, g.P.nuid)
        t = self.es.enter_context(g.nc.sbuf_tensor(nm, list(shape), dtype))
        return TT(t, nm)


def cst(g, off, n=128, rows=128):
    return g.consts[0:rows, off:off + n]


def mix_phase(g, l):
    P, nc = g.P, g.nc
    NPT = g.NPT
    barrier(g)
    sbs = [list(range(a, min(a + 6, NPT))) for a in range(0, NPT, 6)] + [[NPT]]
    with Scope(g, "mx%d" % l) as LS:
        L = Ctx()
        L.l = l
        L.parB = LS.sb("parB", [128, NPAR])
        L.convw = LS.sb("convw", [128, 24, 4])
        L.convb = LS.sb("convb", [128, 24])
        L.wup = LS.sb("wup", [16, 256])
        L.bup = LS.sb("bup", [1, 256])
        L.pd = LS.sb("pd", [128, 64])
        P.dma('sp', L.parB[:, :], g.params_d[l:l + 1, :].partition_broadcast(128), 'lay_small', writes=[L.parB])
        P.dma('sp', L.convw[:, :, :], g.convw_d[l].rearrange("p (c j) -> p c j", j=4), 'lay_small', writes=[L.convw])
        P.dma('sp', L.convb[:, :], g.convb_d[l], 'lay_small', writes=[L.convb])
        P.dma('sp', L.wup[:, :], g.gla_wup[l], 'lay_small', writes=[L.wup])
        P.dma('sp', L.bup[:, :], g.gla_bup[l], 'lay_small', writes=[L.bup])
        P.op('act', lambda e: e.activation(L.pd[:, 0:4], L.parB[:, P_DN_ALOG:P_DN_ALOG + 4], AF.Exp), reads=[L.parB], writes=[L.pd])
        P.op('act', lambda e: e.activation(L.pd[:, 4:20], L.parB[:, P_M2_ALOG:P_M2_ALOG + 16], AF.Exp), reads=[L.parB], writes=[L.pd])
        P.op('dve', lambda e: e.tensor_scalar(L.pd[:, 0:20], L.pd[:, 0:20], -1.0, None, ALU.mult), reads=[L.pd], writes=[L.pd])
        L.Sd = LS.sb("Sd", [128, 4, 128])
        L.Sg = LS.sb("Sg", [128, 2, 128])
        L.Ss = LS.sb("Ss", [128, 16, 64])
        L.tailD = LS.sb("tailD", [128, 12, 3])
        L.tailS = LS.sb("tailS", [128, 12, 3])
        for t_ in (L.Sd, L.Sg, L.Ss, L.tailD, L.tailS):
            P.op('pool', lambda e: e.memset(t_.t[:], 0.0), writes=[t_])
        import os
        L.WB = LS.sb("WB", [128, 8, 2576], BF16)
        L.WO = LS.sb("WO", [128, 8, 1024], BF16)
        phases = [(sbi_, m_) for sbi_ in range(len(sbs)) for m_ in g.mixers]
        L.phases = phases
        L.phase_i = 0
        load_wpart(g, L, phases[0][1])
        load_wo(g, L, phases[0][1])
        for sbi, tiles in enumerate(sbs):
            is_s = tiles[0] == NPT
            ntok = len(tiles) * 128
            with Scope(g, "mx%d_%d" % (l, sbi)) as SS:
                hT = SS.sb("hT", [128, 8, ntok], BF16)
                with Scope(g, "mx%d_%d_n" % (l, sbi)) as NS:
                    hn = [NS.sb("hn0", [128, D], BF16), NS.sb("hn1", [128, D], BF16)]
                    gB = NS.sb("gB", [128, D])
                    norm_tiles(g, tiles, 3 * l + 1, hT, hn, gB)
                    barrier(g)
                for mixer in g.mixers:
                    with Scope(g, "mx%d_%d_%s" % (l, sbi, mixer)) as MS:
                        if mixer == 'delta':
                            delta_mixer(g, L, MS, tiles, hT, is_s)
                        elif mixer == 'gla':
                            gla_mixer(g, L, MS, tiles, hT, is_s)
                        else:
                            ssd_mixer(g, L, MS, tiles, hT, is_s)
                    L.phase_i += 1
                    if L.phase_i < len(L.phases):
                        load_wo(g, L, L.phases[L.phase_i][1])
                    barrier(g, skip=("wpart_d", "wo_d"))
        barrier(g)


MIX_COLS = {'delta': [(O_DQ, 1536), (O_DZ, 520)], 'gla': [(O_GQ, 1552)], 'ssd': [(O_MX, 1536), (O_MZ, 1040)]}
MIX_ROWS = {'delta': (0, 512), 'gla': (512, 512), 'ssd': (1024, 1024)}


def load_wpart(g, L, mixer):
    P, l = g.P, L.l
    c = 0
    for (c0, n) in MIX_COLS[mixer]:
        for a in range(0, n, 512):
            w = min(512, n - a)
            P.dma('pool', L.WB[:, :, c + a:c + a + w], g.w_in[l][:, c0 + a:c0 + a + w].rearrange("(k p) n -> p k n", p=128),
                  "wpart_d", writes=[L.WB])
        c += n


def load_wo(g, L, mixer):
    P, l = g.P, L.l
    r0, nr = MIX_ROWS[mixer]
    for a in range(0, nr, 512):
        P.dma('pool', L.WO[:, a // 128:a // 128 + 4, :], g.w_out[l][r0 + a:r0 + a + 512, :].rearrange("(k p) n -> p k n", p=128),
              "wo_d", writes=[L.WO])


def prefetch_next(g, L):
    if L.phase_i + 1 < len(L.phases):
        load_wpart(g, L, L.phases[L.phase_i + 1][1])


def proj_fm(g, wpart, col0, ncols, hT, tok0, bank, slot, rows0=0):
    P = g.P
    for kc in range(8):
        P.op('pe', lambda e: e.matmul(bank.t[rows0:rows0 + ncols, slot * 128:(slot + 1) * 128], wpart[:, kc, col0:col0 + ncols],
                                      hT[:, kc, tok0:tok0 + 128], start=(kc == 0), stop=(kc == 7)),
             reads=[wpart, hT], writes=[bank], inc=(kc == 7))


def proj_tm(g, wpart, col0, ncols, hT, tok0, bank, o0=0):
    P = g.P
    for kc in range(8):
        P.op('pe', lambda e: e.matmul(bank.t[:, o0:o0 + ncols], hT[:, kc, tok0:tok0 + 128], wpart[:, kc, col0:col0 + ncols],
                                      start=(kc == 0), stop=(kc == 7)),
             reads=[wpart, hT], writes=[bank], inc=(kc == 7))


def conv_block(g, L, MS, B, wpart, hT, tok0, is_s, ch0, tail, stc, last_prompt, conv_out_col0):
    P, l = g.P, L.l
    cin, cout, acc, tmp = B.cin, B.cout, B.acc, B.tmp
    if is_s:
        cin4 = cin.t[:, :, :].rearrange("p c (r s) -> p c r s", s=16)
    for grp in range(3):
        bank = nextbank(g)
        for c4 in range(4):
            proj_fm(g, wpart, (grp * 4 + c4) * 128, 128, hT, tok0, bank, c4)
        if not is_s:
            P.op('act', lambda e: e.activation(cin[:, grp * 4:grp * 4 + 4, 3:131], bank.t[:, :].rearrange("p (c n) -> p c n", c=4), AF.Copy),
                 reads=[bank], writes=[cin])
        else:
            P.op('act', lambda e: e.activation(cin4[:, grp * 4:grp * 4 + 4, 3:11, :],
                                               bank.t[:, :].rearrange("p (c s t) -> p c t s", c=4, s=16), AF.Copy),
                 reads=[bank], writes=[cin])
    if not is_s:
        P.op('pool', lambda e: e.tensor_copy(cin[:, :, 0:3], tail[:, :, :]), reads=[tail], writes=[cin])
        P.op('pool', lambda e: e.tensor_copy(tail[:, :, :], cin[:, :, 128:131]), reads=[cin], writes=[tail])
    else:
        for grp in range(3):
            bank = nextbank(g)
            for c4 in range(4):
                ch = grp * 4 + c4
                P.op('pe', lambda e: e.transpose(bank.t[:, c4 * 48:(c4 + 1) * 48], stc[0:48, ch * 128:(ch + 1) * 128], cst(g, C_IDENT, 48, 48)),
                     reads=[stc, g.consts], writes=[bank], inc=(c4 == 3))
            P.op('act', lambda e: e.activation(cin4[:, grp * 4:grp * 4 + 4, 0:3, :],
                                               bank.t[:, 0:192].rearrange("p (c s r) -> p c r s", c=4, r=3), AF.Copy),
                 reads=[bank], writes=[cin])
    if is_s or last_prompt:
        for grp in range(3):
            bank = nextbank(g)
            for c4 in range(4):
                ch = grp * 4 + c4
                if is_s:
                    src = cin.t[:, ch, 128:176]
                    n = 48
                else:
                    src = cin[:, ch, 128:131]
                    n = 3
                P.op('pe', lambda e: e.transpose(bank.t[0:n, c4 * 128:(c4 + 1) * 128], src, cst(g, C_IDENT)),
                     reads=[cin, g.consts], writes=[bank], inc=(c4 == 3))
            n = 48 if is_s else 3
            ob = B.cvo
            P.op('dve', lambda e: e.tensor_copy(ob[0:n, grp * 512:(grp + 1) * 512], bank.t[0:n, :]), reads=[bank], writes=[ob])
        if is_s:
            dv = g.o_conv_s[l].rearrange("(s r) c -> r s c", r=3)
            for r_ in range(3):
                P.dma('sp', dv[r_, :, conv_out_col0:conv_out_col0 + 1536], B.cvo[r_ * 16:(r_ + 1) * 16, :], "cvo_o", reads=[B.cvo])
        else:
            P.dma('sp', g.o_conv_p[l][:, conv_out_col0:conv_out_col0 + 1536], B.cvo[0:3, :], "cvo_o", reads=[B.cvo])
    def view(j):
        if is_s:
            return cin4[:, :, j:j + 8, :]
        return cin[:, :, j:j + 128]

    def wv(j, shape):
        return L.convw[:, ch0:ch0 + 12, j:j + 1].to_broadcast(shape) if not is_s else \
            L.convw[:, ch0:ch0 + 12, j:j + 1].unsqueeze(3).to_broadcast(shape)
    if is_s:
        shape = [128, 12, 8, 16]
        accv = acc.t[:, :, :].rearrange("p c (t s) -> p c t s", s=16)
        tmpv = tmp.t[:, :, :].rearrange("p c (t s) -> p c t s", s=16)
        bv_ = L.convb[:, ch0:ch0 + 12].unsqueeze(2).unsqueeze(3).to_broadcast(shape)
    else:
        shape = [128, 12, 128]
        accv = acc.t[:, :, :]
        tmpv = tmp.t[:, :, :]
        bv_ = L.convb[:, ch0:ch0 + 12].unsqueeze(2).to_broadcast(shape)
    P.op('dve', lambda e: e.tensor_tensor(accv, view(0), wv(0, shape), ALU.mult), reads=[cin, L.convw], writes=[acc])
    for j in range(1, 4):
        P.op('pool', lambda e: e.tensor_tensor(tmpv, view(j), wv(j, shape), ALU.mult), reads=[cin, L.convw], writes=[tmp])
        P.op('dve', lambda e: e.tensor_tensor(accv, accv, tmpv, ALU.add), reads=[acc, tmp], writes=[acc])
    P.op('dve', lambda e: e.tensor_tensor(accv, accv, bv_, ALU.add), reads=[acc, L.convb], writes=[acc])
    if is_s:
        P.op('act', lambda e: e.activation(cout.t[:, :, :].rearrange("p c (s t) -> p c t s", s=16), accv, AF.Silu),
             reads=[acc], writes=[cout])
    else:
        P.op('act', lambda e: e.activation(cout[:, :, :], accv, AF.Silu), reads=[acc], writes=[cout])


def out_proj(g, MS, B, on_bf, nk, wo, t):
    P = g.P
    bank = nextbank(g)
    pb = bank.t[:, :].bitcast(BF16)
    for kc in range(nk):
        P.op('pe', lambda e: e.transpose(pb[:, kc * 128:(kc + 1) * 128], on_bf[:, kc * 128:(kc + 1) * 128], g.identb[:, :]),
             reads=[on_bf, g.identb], writes=[bank], inc=(kc == nk - 1))
    P.op('act', lambda e: e.activation(B.onT[:, 0:nk, :], pb[:, 0:nk * 128].rearrange("p (k n) -> p k n", k=nk), AF.Copy),
         reads=[bank], writes=[B.onT])
    for hh in range(2):
        bo = nextbank(g)
        for kc in range(nk):
            P.op('pe', lambda e: e.matmul(bo.t[:, :], B.onT[:, kc, :], wo[:, kc, hh * 512:(hh + 1) * 512], start=(kc == 0), stop=(kc == nk - 1)),
                 reads=[B.onT, wo], writes=[bo], inc=(kc == nk - 1))
        P.op('dve', lambda e: e.tensor_tensor(g.x[:, t, hh * 512:(hh + 1) * 512], g.x[:, t, hh * 512:(hh + 1) * 512], bo.t[:, :], ALU.add),
             reads=[bo, g.xk[t]], writes=[g.xk[t]])


def gate_compute(g, B, gate_banks, W):
    P = g.P
    gt = B.gate
    for bi, bank in enumerate(gate_banks):
        w = min(512, W - bi * 512)
        P.op('act', lambda e: e.activation(gt[:, bi * 512:bi * 512 + w], bank.t[:, 0:w], AF.Exp, scale=-1.0), reads=[bank], writes=[gt])
        P.op('dve', lambda e: e.tensor_scalar(gt[:, bi * 512:bi * 512 + w], gt[:, bi * 512:bi * 512 + w], 1.0, None, ALU.add), reads=[gt], writes=[gt])
        P.op('dve', lambda e: e.reciprocal(gt[:, bi * 512:bi * 512 + w], gt[:, bi * 512:bi * 512 + w]), reads=[gt], writes=[gt])
        P.op('dve', lambda e: e.tensor_tensor(gt[:, bi * 512:bi * 512 + w], gt[:, bi * 512:bi * 512 + w], bank.t[:, 0:w], ALU.mult), reads=[gt, bank], writes=[gt])


def gated_norm(g, L, B, o_sb, nh, hd, norm_off, per_head, out_bf, norm_tt=None):
    P = g.P
    W = nh * hd
    sm = B.sm2
    gt = B.gate
    sq = getattr(B, 'sq', None)
    if per_head:
        P.op('pool', lambda e: e.tensor_tensor(sq[:, 0:W], o_sb[:, 0:W], o_sb[:, 0:W], ALU.mult), reads=[o_sb], writes=[sq])
        P.op('dve', lambda e: e.tensor_reduce(sm[:, 0:nh], sq[:, 0:W].rearrange("p (h e) -> p h e", h=nh), AX.X, ALU.add), reads=[sq], writes=[sm])
        P.op('act', lambda e: e.activation(sm[:, 16:16 + nh], sm[:, 0:nh], AF.Ln, bias=EPS, scale=1.0 / hd), reads=[sm], writes=[sm])
        P.op('act', lambda e: e.activation(sm[:, 32:32 + nh], sm[:, 16:16 + nh], AF.Exp, scale=-0.5), reads=[sm], writes=[sm])
        P.op('dve', lambda e: e.tensor_tensor(sq[:, 0:W].rearrange("p (h e) -> p h e", h=nh), o_sb[:, 0:W].rearrange("p (h e) -> p h e", h=nh),
                                              sm[:, 32:32 + nh].unsqueeze(2).to_broadcast([128, nh, hd]), ALU.mult), reads=[o_sb, sm], writes=[sq])
        P.op('pool', lambda e: e.tensor_tensor(sq[:, 0:W].rearrange("p (h e) -> p h e", h=nh), sq[:, 0:W].rearrange("p (h e) -> p h e", h=nh),
                                               L.parB[:, norm_off:norm_off + hd].unsqueeze(1).to_broadcast([128, nh, hd]), ALU.mult),
             reads=[sq, L.parB], writes=[sq])
        P.op('dve', lambda e: e.tensor_tensor(out_bf[:, 0:W], sq[:, 0:W], gt[:, 0:W], ALU.mult), reads=[sq, gt], writes=[out_bf])
    else:
        P.op('dve', lambda e: e.tensor_tensor(gt[:, 0:W], gt[:, 0:W], o_sb[:, 0:W], ALU.mult), reads=[gt, o_sb], writes=[gt])
        P.op('act', lambda e: e.activation(out_bf[:, 0:W], gt[:, 0:W], AF.Square, accum_out=sm[:, 0:1]), reads=[gt], writes=[out_bf, sm])
        P.op('act', lambda e: e.activation(sm[:, 16:17], sm[:, 0:1], AF.Ln, bias=EPS, scale=1.0 / W), reads=[sm], writes=[sm])
        P.op('act', lambda e: e.activation(sm[:, 32:33], sm[:, 16:17], AF.Exp, scale=-0.5), reads=[sm], writes=[sm])
        P.op('dve', lambda e: e.scalar_tensor_tensor(out_bf[:, 0:W], gt[:, 0:W], sm[:, 32:33], norm_tt[:, 0:W], ALU.mult, ALU.mult),
             reads=[gt, sm, norm_tt], writes=[out_bf])


def expo_bank(g, B, mask_off, rowT, colT, nheads=4):
    P = g.P
    bank = nextbank(g)
    for h in range(nheads):
        o = bank.t[:, h * 128:(h + 1) * 128]
        esel = g.consts[0:4, C_ESEL + h * 128:C_ESEL + (h + 1) * 128]
        P.op('pe', lambda e: e.matmul(o, g.identb[:, :], g.maskb[:, g.maskb_idx[mask_off], :], start=True, stop=False), reads=[g.identb, g.maskb], writes=[bank], inc=False)
        P.op('pe', lambda e: e.matmul(o, esel, rowT, start=False, stop=False), reads=[g.consts, B.rT], writes=[bank], inc=False)
        P.op('pe', lambda e: e.matmul(o, colT, esel, start=False, stop=True), reads=[g.consts, B.rT], writes=[bank], inc=(h == nheads - 1))
    return bank


class Arena:
    def __init__(self, MS, name, n):
        self.tt = MS.sb(name, [128, n, 512])
        self.name = self.tt.key
        self.n = n

    def keys(self, i0, n):
        return tuple("%s_s%d" % (self.name, i) for i in range(i0, i0 + n))

    def slot(self, i, shape3=None):
        ap = self.tt.t[:, i, :]
        if shape3 is not None:
            ap = ap.rearrange("p (a b) -> p a b", a=shape3[0])
        return TT(ap, self.keys(i, 1))

    def span(self, i0, n, inner):
        ap = self.tt.t[:, i0:i0 + n, :].rearrange("p s (a b) -> p (s a) b", b=inner)
        return TT(ap, self.keys(i0, n))


def delta_mixer(g, L, MS, tiles, hT, is_s):
    P, l = g.P, L.l
    wpart, wo = L.WB, L.WO
    B = Ctx()
    A = Arena(MS, "arA", 5)
    Bc = Arena(MS, "arB", 3)
    G = Arena(MS, "arG", 7 if is_s else 6)
    M = Arena(MS, "arM", 3)
    Lt = Arena(MS, "arL", 3)
    B.cin = TT(A.tt.t[:, :, :].rearrange("p s f -> p (s f)")[:, 0:12 * 176].rearrange("p (c n) -> p c n", c=12), A.keys(0, 5))
    B.cout = Bc.span(0, 3, 128)
    B.acc = G.span(0, 3, 128)
    B.tmp = G.span(3, 3, 128)
    B.cvo = TT(M.tt.t[0:48, :, :].rearrange("p s f -> p (s f)"), M.keys(0, 3))
    B.rn = M.span(0, 2, 128)
    B.E = [M.slot(i, (4, 128)) for i in range(3)]
    B.PT = [G.slot(0, (4, 128)), G.slot(1, (4, 128))]
    B.X = [G.slot(2, (4, 128)), G.slot(3, (4, 128))]
    B.Y = [G.slot(4, (4, 128)), G.slot(5, (4, 128))]
    B.kbg, B.kd, B.QKm, B.bv = [A.slot(i, (4, 128)) for i in range(4)]
    B.nwT, B.vnew, B.qgT = [Bc.slot(i, (4, 128)) for i in range(3)]
    B.gate = Lt.slot(0)
    B.sq = Lt.slot(1)
    B.o = Lt.slot(2)
    B.onT = MS.sb("onT", [128, 8, 128], BF16)
    B.sm = MS.sb("sm", [128, 64])
    B.sm2 = MS.sb("sm2", [128, 64])
    B.rT = MS.sb("rT", [4, 512])
    B.qkn = MS.sb("qkn", [128, 8, 128])
    B.on = MS.sb("on", [128, 512], BF16)
    stc = None
    if is_s:
        stc = TT(Lt.tt.t[0:48, :, :].rearrange("p s f -> p (s f)"), Lt.keys(0, 3))
        P.dma('sp', stc[:, :], g.st_conv[l][:, 0:1536], "stc_d", writes=[stc])
        B.Sh = MS.sb("Sh", [128, 16, 128])
        B.padf = TT(G.tt.t[:, 2:7, :].rearrange("p s f -> p (s f)")[:, 0:17 * 128], G.keys(2, 5))
        B.egts = MS.sb("egts", [128, 16, 4])
    sm = B.sm
    for ti, t in enumerate(tiles):
        tok0 = ti * 128
        if ti == 0:
            g.minfree = min(getattr(g, 'minfree', 1 << 30), g.nc.sbuf_bytes_remaining)
        last_prompt = (not is_s) and (t == g.NPT - 1)
        conv_block(g, L, MS, B, wpart, hT, tok0, is_s, 0, L.tailD, stc, last_prompt, 0)
        cout = B.cout
        M_U01 = C_SU01 if is_s else C_U01
        M_SEG = C_SEG if is_s else C_ONES
        P.op('pool', lambda e: e.tensor_tensor(B.rn[:, :, :], cout[:, 0:8, :], cout[:, 0:8, :], ALU.mult), reads=[cout], writes=[B.rn])
        for hh in range(2):
            bank = nextbank(g)
            P.op('pe', lambda e: e.matmul(bank.t[:, :], cst(g, C_ONES), B.rn[:, hh * 4:hh * 4 + 4, :], start=True, stop=True),
                 reads=[g.consts, B.rn], writes=[bank])
            P.op('act', lambda e: e.activation(B.qkn[:, hh * 4:hh * 4 + 4, :], bank.t[:, :].rearrange("p (c n) -> p c n", c=4), AF.Ln, bias=EPS),
                 reads=[bank], writes=[B.qkn])
        P.op('act', lambda e: e.activation(B.rn[:, :, :], B.qkn[:, :, :], AF.Exp, scale=-0.5), reads=[B.qkn, B.rn], writes=[B.rn])
        P.op('dve', lambda e: e.tensor_tensor(B.qkn[:, :, :], cout[:, 0:8, :], B.rn[:, :, :], ALU.mult), reads=[cout, B.rn], writes=[B.qkn])
        bS = nextbank(g)
        proj_tm(g, wpart, 2048, 8, hT, tok0, bS)
        bZ = nextbank(g)
        proj_tm(g, wpart, 1536, 512, hT, tok0, bZ)
        if ti == len(tiles) - 1:
            prefetch_next(g, L)
        gate_compute(g, B, [bZ], 512)
        P.op('dve', lambda e: e.tensor_tensor(sm[:, 40:44], bS.t[:, 0:4], L.parB[:, P_DN_DTB:P_DN_DTB + 4], ALU.add), reads=[bS, L.parB], writes=[sm])
        P.op('act', lambda e: e.activation(sm[:, 40:44], sm[:, 40:44], AF.Exp), reads=[sm], writes=[sm])
        P.op('act', lambda e: e.activation(sm[:, 44:48], bS.t[:, 4:8], AF.Exp, scale=-1.0), reads=[bS], writes=[sm])
        P.op('act', lambda e: e.activation(sm[:, 0:8], sm[:, 40:48], AF.Ln, bias=1.0), reads=[sm], writes=[sm])
        P.op('dve', lambda e: e.tensor_tensor(sm[:, 0:4], sm[:, 0:4], L.pd[:, 0:4], ALU.mult), reads=[sm, L.pd], writes=[sm])
        P.op('act', lambda e: e.activation(sm[:, 8:12], sm[:, 4:8], AF.Exp, scale=-1.0), reads=[sm], writes=[sm])
        bC = nextbank(g)
        P.op('pe', lambda e: e.matmul(bC.t[:, 0:4], cst(g, M_U01), sm[:, 0:4], start=True, stop=True), reads=[g.consts, sm], writes=[bC], inc=False)
        P.op('pe', lambda e: e.matmul(bC.t[:, 4:8], cst(g, M_SEG), sm[:, 0:4], start=True, stop=True), reads=[g.consts, sm], writes=[bC])
        P.op('dve', lambda e: e.tensor_copy(sm[:, 12:20], bC.t[:, 0:8]), reads=[bC], writes=[sm])
        P.op('dve', lambda e: e.tensor_tensor(sm[:, 20:24], sm[:, 12:16], sm[:, 4:8], ALU.subtract), reads=[sm], writes=[sm])
        P.op('dve', lambda e: e.tensor_scalar(sm[:, 24:28], sm[:, 12:16], -1.0, None, ALU.mult), reads=[sm], writes=[sm])
        P.op('dve', lambda e: e.tensor_tensor(sm[:, 32:36], sm[:, 16:20], sm[:, 12:16], ALU.subtract), reads=[sm], writes=[sm])
        P.op('act', lambda e: e.activation(sm[:, 28:32], sm[:, 12:16], AF.Exp), reads=[sm], writes=[sm])
        P.op('act', lambda e: e.activation(sm[:, 32:36], sm[:, 32:36], AF.Exp), reads=[sm], writes=[sm])
        P.op('dve', lambda e: e.tensor_tensor(sm[:, 36:40], sm[:, 8:12], sm[:, 28:32], ALU.mult), reads=[sm], writes=[sm])

        def bc(c0):
            return sm[:, c0:c0 + 4].unsqueeze(2).to_broadcast([128, 4, 128])
        bKT = nextbank(g)
        bVT = nextbank(g)
        for h in range(4):
            P.op('pe', lambda e: e.transpose(bKT.t[:, h * 128:(h + 1) * 128], B.qkn[:, 4 + h, :], cst(g, C_IDENT)), reads=[B.qkn, g.consts], writes=[bKT], inc=(h == 3))
        for h in range(4):
            P.op('pe', lambda e: e.transpose(bVT.t[:, h * 128:(h + 1) * 128], cout[:, 8 + h, :], cst(g, C_IDENT)), reads=[cout, g.consts], writes=[bVT], inc=(h == 3))
        kt3 = bKT.t[:, :].rearrange("p (h n) -> p h n", h=4)
        P.op('dve', lambda e: e.tensor_tensor(B.kbg[:, :, :], kt3, bc(36), ALU.mult), reads=[bKT, sm, B.cin], writes=[B.kbg])
        P.op('dve', lambda e: e.tensor_tensor(B.kd[:, :, :], kt3, bc(32), ALU.mult), reads=[bKT, sm], writes=[B.kd])
        P.op('dve', lambda e: e.tensor_tensor(B.bv[:, :, :], bVT.t[:, :].rearrange("p (h n) -> p h n", h=4), bc(8), ALU.mult), reads=[bVT, sm], writes=[B.bv])
        bT = nextbank(g)
        for qi, c0 in enumerate((12, 20, 24)):
            P.op('pe', lambda e: e.transpose(bT.t[0:4, qi * 128:(qi + 1) * 128], sm[:, c0:c0 + 4], cst(g, C_IDENT)),
                 reads=[sm, g.consts], writes=[bT], inc=(qi == 2))
        P.op('dve', lambda e: e.tensor_copy(B.rT[:, 0:384], bT.t[0:4, 0:384]), reads=[bT], writes=[B.rT])
        gcT, r2T, ngcT = B.rT[:, 0:128], B.rT[:, 128:256], B.rT[:, 256:384]
        mNLS, mNUS, mNUI = (C_SNLS, C_SNUS, C_SNUI) if is_s else (C_NLS, C_NUS, C_NUI)
        for ei, (moff, rowT, colT) in enumerate(((mNLS, ngcT, r2T), (mNUS, r2T, ngcT), (mNUI, gcT, ngcT))):
            bank = expo_bank(g, B, moff, rowT, colT)
            P.op('act', lambda e: e.activation(B.E[ei][:, :, :], bank.t[:, :].rearrange("p (h n) -> p h n", h=4), AF.Exp), reads=[bank], writes=[B.E[ei]])
        bKK = nextbank(g)
        bQK = nextbank(g)
        for h in range(4):
            P.op('pe', lambda e: e.matmul(bKK.t[:, h * 128:(h + 1) * 128], B.qkn[:, 4 + h, :], B.qkn[:, 4 + h, :], start=True, stop=True),
                 reads=[B.qkn], writes=[bKK], inc=(h == 3))
        for h in range(4):
            P.op('pe', lambda e: e.matmul(bQK.t[:, h * 128:(h + 1) * 128], B.qkn[:, 4 + h, :], B.qkn[:, h, :], start=True, stop=True),
                 reads=[B.qkn], writes=[bQK], inc=(h == 3))
        kk3 = bKK.t[:, :].rearrange("p (h n) -> p h n", h=4)
        P.op('dve', lambda e: e.scalar_tensor_tensor(B.X[0][:, :, :], B.E[0][:, :, :], -1.0, kk3, ALU.mult, ALU.mult), reads=[B.E[0], bKK], writes=[B.X[0]])
        P.op('dve', lambda e: e.scalar_tensor_tensor(B.Y[0][:, :, :], B.E[1][:, :, :], -1.0, kk3, ALU.mult, ALU.mult), reads=[B.E[1], bKK], writes=[B.Y[0]])
        P.op('dve', lambda e: e.scalar_tensor_tensor(B.QKm[:, :, :], B.E[2][:, :, :], 128.0 ** -0.5, bQK.t[:, :].rearrange("p (h n) -> p h n", h=4),
                                                     ALU.mult, ALU.mult), reads=[B.E[2], bQK], writes=[B.QKm])
        P.op('pool', lambda e: e.tensor_tensor(B.PT[0][:, :, :], B.Y[0][:, :, :], cst(g, C_IDENT).unsqueeze(1).to_broadcast([128, 4, 128]), ALU.add),
             reads=[B.Y[0], g.consts], writes=[B.PT[0]])
        nlev = 2 if is_s else 6
        cur = 0
        for k in range(1, nlev + 1):
            nxt = 1 - cur
            bX = nextbank(g)
            for h in range(4):
                P.op('pe', lambda e: e.matmul(bX.t[:, h * 128:(h + 1) * 128], B.Y[cur][:, h, :], B.X[cur][:, h, :], start=True, stop=True),
                     reads=[B.X[cur], B.Y[cur]], writes=[bX], inc=(h == 3))
            if k < nlev:
                bY = nextbank(g)
                for h in range(4):
                    P.op('pe', lambda e: e.matmul(bY.t[:, h * 128:(h + 1) * 128], B.X[cur][:, h, :], B.Y[cur][:, h, :], start=True, stop=True),
                         reads=[B.X[cur], B.Y[cur]], writes=[bY], inc=(h == 3))
            P.op('act', lambda e: e.activation(B.X[nxt][:, :, :], bX.t[:, :].rearrange("p (h n) -> p h n", h=4), AF.Copy), reads=[bX], writes=[B.X[nxt]])
            if k < nlev:
                P.op('dve', lambda e: e.tensor_copy(B.Y[nxt][:, :, :], bY.t[:, :].rearrange("p (h n) -> p h n", h=4)), reads=[bY], writes=[B.Y[nxt]])
            bP = nextbank(g)
            for h in range(4):
                P.op('pe', lambda e: e.matmul(bP.t[:, h * 128:(h + 1) * 128], B.X[nxt][:, h, :], B.PT[cur][:, h, :], start=True, stop=True),
                     reads=[B.X[nxt], B.PT[cur]], writes=[bP], inc=(h == 3))
            P.op('dve', lambda e: e.tensor_tensor(B.PT[nxt][:, :, :], bP.t[:, :].rearrange("p (h n) -> p h n", h=4), B.PT[cur][:, :, :], ALU.add),
                 reads=[bP, B.PT[cur]], writes=[B.PT[nxt]])
            cur = nxt
        TTm = B.PT[cur]
        bW = nextbank(g)
        for h in range(4):
            P.op('pe', lambda e: e.matmul(bW.t[:, h * 128:(h + 1) * 128], B.kbg[:, h, :], TTm[:, h, :], start=True, stop=True),
                 reads=[B.kbg, TTm], writes=[bW], inc=(h == 3))
        P.op('act', lambda e: e.activation(B.nwT[:, :, :], bW.t[:, :].rearrange("p (h n) -> p h n", h=4), AF.Copy, scale=-1.0), reads=[bW, cout], writes=[B.nwT])
        bG = nextbank(g)
        for h in range(4):
            esel = g.consts[0:4, C_ESEL + h * 128:C_ESEL + (h + 1) * 128]
            P.op('pe', lambda e: e.matmul(bG.t[:, h * 128:(h + 1) * 128], esel, gcT, start=True, stop=True), reads=[g.consts, B.rT], writes=[bG], inc=(h == 3))
        P.op('act', lambda e: e.activation(B.qgT[:, :, :], bG.t[:, :].rearrange("p (h n) -> p h n", h=4), AF.Exp), reads=[bG], writes=[B.qgT])
        P.op('dve', lambda e: e.scalar_tensor_tensor(B.qgT[:, :, :], B.qkn[:, 0:4, :], 128.0 ** -0.5, B.qgT[:, :, :], ALU.mult, ALU.mult),
             reads=[B.qkn, B.qgT], writes=[B.qgT])
        if not is_s:
            P.op('act', lambda e: e.activation(sm[:, 48:52], sm[:, 16:20], AF.Exp), reads=[sm], writes=[sm])
        else:
            gtv = B.sm2[:, 0:64].rearrange("p (s h) -> p s h", h=4)
            P.op('dve', lambda e: e.tensor_tensor(gtv, sm[:, 16:20].unsqueeze(1).to_broadcast([128, 16, 4]),
                                                  g.consts[:, C_ROWM8:C_ROWM8 + 16].unsqueeze(2).to_broadcast([128, 16, 4]), ALU.mult),
                 reads=[sm, g.consts], writes=[B.sm2])
            bE = nextbank(g)
            P.op('pe', lambda e: e.matmul(bE.t[:, 0:64], cst(g, C_ONES), B.sm2[:, 0:64], start=True, stop=True), reads=[g.consts, B.sm2], writes=[bE])
            P.op('act', lambda e: e.activation(B.egts[:, :, :], bE.t[:, 0:64].rearrange("p (s h) -> p s h", h=4), AF.Exp), reads=[bE], writes=[B.egts])
        bVN = nextbank(g, hold=True)
        bO = nextbank(g, hold=True)
        hgroups = [[0, 1, 2, 3]] if not is_s else [[0], [1], [2], [3]]
        padv = None
        if is_s:
            padd = B.padf.t[:, 0:16 * 136].rearrange("p (s q) -> p s q", q=136)[:, :, 0:8]
            padr = B.padf.t[:, 0:2048].rearrange("p (s n) -> p s n", s=16)
        for hs in hgroups:
            h0, nh = hs[0], len(hs)
            if is_s:
                h = h0
                P.dma('sp', B.Sh[:, :, :], g.st_delta[l][:, h, :, :].rearrange("s d e -> d s e"), "Sh_d", writes=[B.Sh])

                def padcol(src):
                    P.op('pool', lambda e: e.memset(B.padf[:, :], 0.0), writes=[B.padf])
                    P.op('pool', lambda e: e.tensor_copy(padd, src.rearrange("p (s r) -> p s r", r=8)), reads=[B.nwT, B.qgT, B.padf], writes=[B.padf])
            for h in hs:
                o = bVN.t[:, h * 128:(h + 1) * 128]
                P.op('pe', lambda e: e.matmul(o, TTm[:, h, :], B.bv[:, h, :], start=True, stop=False), reads=[TTm, B.bv], writes=[bVN], inc=False)
                if not is_s:
                    P.op('pe', lambda e: e.matmul(o, B.nwT[:, h, :], L.Sd[:, h, :], start=False, stop=True), reads=[B.nwT, L.Sd], writes=[bVN], inc=(h == hs[-1]))
                else:
                    padcol(B.nwT[:, h, :])
                    for s_ in range(16):
                        P.op('pe', lambda e: e.matmul(o, B.padf[:, s_ * 128:(s_ + 1) * 128], B.Sh[:, s_, :], start=False, stop=(s_ == 15)),
                             reads=[B.padf, B.Sh], writes=[bVN], inc=(s_ == 15))
            P.op('act', lambda e: e.activation(B.vnew[:, h0:h0 + nh, :], bVN.t[:, h0 * 128:(h0 + nh) * 128].rearrange("p (h n) -> p h n", h=nh), AF.Copy),
                 reads=[bVN], writes=[B.vnew])
            for h in hs:
                o = bO.t[:, h * 128:(h + 1) * 128]
                P.op('pe', lambda e: e.matmul(o, B.QKm[:, h, :], B.vnew[:, h, :], start=True, stop=False), reads=[B.QKm, B.vnew], writes=[bO], inc=False)
                if not is_s:
                    P.op('pe', lambda e: e.matmul(o, B.qgT[:, h, :], L.Sd[:, h, :], start=False, stop=True), reads=[B.qgT, L.Sd], writes=[bO], inc=(h == hs[-1]))
                else:
                    padcol(B.qgT[:, h, :])
                    for s_ in range(16):
                        P.op('pe', lambda e: e.matmul(o, B.padf[:, s_ * 128:(s_ + 1) * 128], B.Sh[:, s_, :], start=False, stop=(s_ == 15)),
                             reads=[B.padf, B.Sh], writes=[bO], inc=(s_ == 15))
            if not is_s:
                bSu = nextbank(g)
                for h in hs:
                    P.op('pe', lambda e: e.matmul(bSu.t[:, h * 128:(h + 1) * 128], B.kd[:, h, :], B.vnew[:, h, :], start=True, stop=True),
                         reads=[B.kd, B.vnew], writes=[bSu], inc=(h == 3))
                P.op('dve', lambda e: e.tensor_tensor(L.Sd[:, :, :], L.Sd[:, :, :], sm[:, 48:52].unsqueeze(2).to_broadcast([128, 4, 128]), ALU.mult),
                     reads=[L.Sd, sm], writes=[L.Sd])
                P.op('dve', lambda e: e.tensor_tensor(L.Sd[:, :, :], L.Sd[:, :, :], bSu.t[:, :].rearrange("p (h n) -> p h n", h=4), ALU.add),
                     reads=[L.Sd, bSu], writes=[L.Sd])
                if last_prompt:
                    P.dma('sp', g.o_delta_p[l].rearrange("h d e -> d h e"), L.Sd[:, :, :], L.Sd.key + "_o", reads=[L.Sd])
            else:
                h = h0
                P.op('pool', lambda e: e.tensor_tensor(padr, B.kd[:, h, :].unsqueeze(1).to_broadcast([128, 16, 128]),
                                                       g.consts[:, C_ROWM:C_ROWM + 16].unsqueeze(2).to_broadcast([128, 16, 128]), ALU.mult),
                     reads=[B.kd, g.consts], writes=[B.padf])
                for s4 in range(4):
                    bSu = nextbank(g)
                    for si in range(4):
                        s_ = s4 * 4 + si
                        P.op('pe', lambda e: e.matmul(bSu.t[:, si * 128:(si + 1) * 128], B.padf[:, s_ * 128:(s_ + 1) * 128], B.vnew[:, h, :], start=True, stop=True),
                             reads=[B.padf, B.vnew], writes=[bSu], inc=(si == 3))
                    sl = B.Sh[:, s4 * 4:s4 * 4 + 4, :]
                    P.op('dve', lambda e: e.tensor_tensor(sl, sl, B.egts[:, s4 * 4:s4 * 4 + 4, h:h + 1].to_broadcast([128, 4, 128]), ALU.mult),
                         reads=[B.Sh, B.egts], writes=[B.Sh])
                    P.op('dve', lambda e: e.tensor_tensor(sl, sl, bSu.t[:, :].rearrange("p (s n) -> p s n", s=4), ALU.add),
                         reads=[B.Sh, bSu], writes=[B.Sh])
                P.dma('sp', g.o_delta_s[l][:, h, :, :].rearrange("s d e -> d s e"), B.Sh[:, :, :], "Sh_o", reads=[B.Sh])
        P.op('act', lambda e: e.activation(B.o[:, :], bO.t[:, :], AF.Copy), reads=[bO], writes=[B.o])
        release(g, bVN, bO)
        gated_norm(g, L, B, B.o, 4, 128, P_DN_NORM, True, B.on)
        out_proj(g, MS, B, B.on, 4, wo, t)


def gla_mixer(g, L, MS, tiles, hT, is_s):
    import os
    DBG = int(os.environ.get('DBG_GLA', '99'))
    P, l = g.P, L.l
    wpart, wo = L.WB, L.WO
    B = Ctx()
    B.v = MS.sb("v", [128, 512])
    B.gk = MS.sb("gk", [128, 256])
    B.gc = MS.sb("gc", [128, 256])
    B.Eq = MS.sb("Eq", [128, 2, 128])
    B.Ek = MS.sb("Ek", [128, 2, 128])
    B.qt = MS.sb("qt", [128, 2, 128])
    B.kt = MS.sb("kt", [128, 2, 128])
    B.qz = MS.sb("qz", [128, 4, 128])
    P.op('pool', lambda e: e.memset(B.qz[:, :, :], 0.0), writes=[B.qz])
    B.PTm = MS.sb("PTm", [128, 4, 128])
    B.ktm = MS.sb("ktm", [128, 256])
    B.glT = MS.sb("glT", [16, 128])
    B.o = MS.sb("o", [128, 512])
    B.gate = MS.sb("gate", [128, 512])
    B.sq = MS.sb("sq", [128, 512])
    B.on = MS.sb("on", [128, 512], BF16)
    B.onT = MS.sb("onT", [128, 8, 128], BF16)
    B.sm2 = MS.sb("sm2", [128, 64])
    if is_s:
        B.Sh = MS.sb("Sh", [128, 16, 128])
        B.padf = MS.sb("pad", [128, 17 * 128])
        padd = B.padf.t[:, 0:16 * 136].rearrange("p (s q) -> p s q", q=136)[:, :, 0:8]
        padr = B.padf.t[:, 0:2048].rearrange("p (s n) -> p s n", s=16)
    M_U01 = C_SU01 if is_s else C_U01
    for ti, t in enumerate(tiles):
        tok0 = ti * 128
        last_prompt = (not is_s) and (t == g.NPT - 1)
        bFM = nextbank(g)
        for c in range(4):
            proj_fm(g, wpart, c * 128, 128, hT, tok0, bFM, c)
        bGL = nextbank(g)
        proj_fm(g, wpart, 1024, 16, hT, tok0, bGL, 0)
        P.op('dve', lambda e: e.tensor_copy(B.glT[:, :], bGL.t[0:16, 0:128]), reads=[bGL], writes=[B.glT])
        bV = nextbank(g)
        proj_tm(g, wpart, 512, 512, hT, tok0, bV)
        P.op('act', lambda e: e.activation(B.v[:, :], bV.t[:, :], AF.Copy), reads=[bV], writes=[B.v])
        bZ = nextbank(g)
        proj_tm(g, wpart, 1040, 512, hT, tok0, bZ)
        if ti == len(tiles) - 1:
            prefetch_next(g, L)
        gate_compute(g, B, [bZ], 512)
        if DBG <= 1:
            g.held.clear()
            continue
        bK = nextbank(g)
        P.op('pe', lambda e: e.matmul(bK.t[:, 0:256], B.glT[:, :], L.wup[:, :], start=True, stop=False), reads=[B.glT, L.wup], writes=[bK], inc=False)
        P.op('pe', lambda e: e.matmul(bK.t[:, 0:256], g.consts[0:1, C_ONES:C_ONES + 128], L.bup[:, :], start=False, stop=True),
             reads=[g.consts, L.bup], writes=[bK])
        P.op('act', lambda e: e.activation(B.gk[:, :], bK.t[:, 0:256], AF.Exp, scale=-1.0), reads=[bK], writes=[B.gk])
        P.op('act', lambda e: e.activation(B.gk[:, :], B.gk[:, :], AF.Ln, bias=1.0), reads=[B.gk], writes=[B.gk])
        P.op('dve', lambda e: e.tensor_scalar(B.gk[:, :], B.gk[:, :], -1.0 / 16.0, None, ALU.mult), reads=[B.gk], writes=[B.gk])
        if DBG <= 2:
            g.held.clear()
            continue
        bC = nextbank(g)
        P.op('pe', lambda e: e.matmul(bC.t[:, 0:256], cst(g, M_U01), B.gk[:, :], start=True, stop=True), reads=[g.consts, B.gk], writes=[bC])
        P.op('dve', lambda e: e.tensor_copy(B.gc[:, :], bC.t[:, 0:256]), reads=[bC], writes=[B.gc])
        bT = nextbank(g)
        for c in range(2):
            P.op('pe', lambda e: e.transpose(bT.t[:, c * 128:(c + 1) * 128], B.gc[:, c * 128:(c + 1) * 128], cst(g, C_IDENT)),
                 reads=[B.gc, g.consts], writes=[bT], inc=(c == 1))
        gT3 = bT.t[:, 0:256].rearrange("p (c n) -> p c n", c=2)
        P.op('act', lambda e: e.activation(B.Eq[:, :, :], gT3, AF.Exp), reads=[bT], writes=[B.Eq])
        P.op('act', lambda e: e.activation(B.Ek[:, :, :], gT3, AF.Exp, scale=-1.0), reads=[bT], writes=[B.Ek])
        fm3 = bFM.t[:, :].rearrange("p (c n) -> p c n", c=4)
        P.op('dve', lambda e: e.scalar_tensor_tensor(B.qt[:, :, :], fm3[:, 0:2, :], 64.0 ** -0.5, B.Eq[:, :, :], ALU.mult, ALU.mult),
             reads=[bFM, B.Eq], writes=[B.qt])
        P.op('dve', lambda e: e.tensor_tensor(B.kt[:, :, :], fm3[:, 2:4, :], B.Ek[:, :, :], ALU.mult), reads=[bFM, B.Ek], writes=[B.kt])
        if DBG <= 3:
            g.held.clear()
            continue
        qz4 = B.qz.t[:, :, :].rearrange("p (hh hl) n -> p hl hh n", hl=2)
        P.op('dve', lambda e: e.tensor_copy(qz4[0:64, 0, :, :], B.qt[0:64, :, :]), reads=[B.qt], writes=[B.qz])
        P.op('dve', lambda e: e.tensor_copy(qz4[64:128, 1, :, :], B.qt[64:128, :, :]), reads=[B.qt], writes=[B.qz])
        bSc = nextbank(g)
        for h in range(4):
            hl, hh = h % 2, h // 2
            P.op('pe', lambda e: e.matmul(bSc.t[:, h * 128:(h + 1) * 128], B.kt[:, hh, :], B.qz[:, h, :], start=True, stop=True),
                 reads=[B.kt, B.qz], writes=[bSc], inc=(h == 3))
        P.op('dve', lambda e: e.tensor_tensor(B.PTm[:, :, :], bSc.t[:, :].rearrange("p (h n) -> p h n", h=4),
                                              cst(g, M_U01).unsqueeze(1).to_broadcast([128, 4, 128]), ALU.mult), reads=[bSc, g.consts], writes=[B.PTm])
        bKT = nextbank(g)
        for c in range(2):
            P.op('pe', lambda e: e.transpose(bKT.t[:, c * 128:(c + 1) * 128], B.kt[:, c, :], cst(g, C_IDENT)), reads=[B.kt, g.consts], writes=[bKT], inc=(c == 1))
        P.op('act', lambda e: e.activation(B.ktm[:, :], bKT.t[:, 0:256], AF.Copy), reads=[bKT], writes=[B.ktm])
        if DBG <= 4:
            g.held.clear()
            continue
        bO = nextbank(g, hold=True)
        for hh in range(2):
            if is_s:
                P.dma('sp', B.Sh[:, :, :], g.st_gla[l][:, 2 * hh:2 * hh + 2, :, :].rearrange("s hl d e -> (hl d) s e"), "Shg_d", writes=[B.Sh])
            for hl in range(2):
                h = 2 * hh + hl
                r = slice(hl * 64, hl * 64 + 64)
                o = bO.t[:, h * 128:(h + 1) * 128]
                P.op('pe', lambda e: e.matmul(o, B.PTm[:, h, :], B.v[:, h * 128:(h + 1) * 128], start=True, stop=False), reads=[B.PTm, B.v], writes=[bO], inc=False)
                if not is_s:
                    P.op('pe', lambda e: e.matmul(o, B.qz[:, h, :], L.Sg[:, hh, :], start=False, stop=True), reads=[B.qz, L.Sg], writes=[bO], inc=True)
                else:
                    P.op('pool', lambda e: e.memset(B.padf[:, :], 0.0), writes=[B.padf])
                    P.op('pool', lambda e: e.tensor_copy(padd, B.qz[:, h, :].rearrange("p (s r) -> p s r", r=8)), reads=[B.qz, B.padf], writes=[B.padf])
                    for s_ in range(16):
                        P.op('pe', lambda e: e.matmul(o, B.padf[:, s_ * 128:(s_ + 1) * 128], B.Sh[:, s_, :], start=False, stop=(s_ == 15)),
                             reads=[B.padf, B.Sh], writes=[bO], inc=(s_ == 15))
            if not is_s:
                bSu = nextbank(g)
                P.op('pe', lambda e: e.matmul(bSu.t[:, 0:256], B.ktm[:, hh * 128:(hh + 1) * 128], B.v[:, hh * 256:(hh + 1) * 256], start=True, stop=True),
                     reads=[B.ktm, B.v], writes=[bSu])
                for hl in range(2):
                    r = slice(hl * 64, hl * 64 + 64)
                    P.op('dve', lambda e: e.tensor_tensor(L.Sg[r, hh, :], L.Sg[r, hh, :], bSu.t[r, hl * 128:(hl + 1) * 128], ALU.add), reads=[L.Sg, bSu], writes=[L.Sg])
                    P.op('dve', lambda e: e.tensor_scalar(L.Sg[r, hh, :], L.Sg[r, hh, :], B.Eq[r, hh, 127:128], None, ALU.mult), reads=[L.Sg, B.Eq], writes=[L.Sg])
            else:
                P.op('pool', lambda e: e.tensor_tensor(padr, B.ktm[:, hh * 128:(hh + 1) * 128].unsqueeze(1).to_broadcast([128, 16, 128]),
                                                       g.consts[:, C_ROWM:C_ROWM + 16].unsqueeze(2).to_broadcast([128, 16, 128]), ALU.mult),
                     reads=[B.ktm, g.consts], writes=[B.padf])
                for s2 in range(8):
                    bSu = nextbank(g)
                    for si in range(2):
                        s_ = s2 * 2 + si
                        P.op('pe', lambda e: e.matmul(bSu.t[:, si * 256:(si + 1) * 256], B.padf[:, s_ * 128:(s_ + 1) * 128], B.v[:, hh * 256:(hh + 1) * 256],
                                                      start=True, stop=True), reads=[B.padf, B.v], writes=[bSu], inc=(si == 1))
                    for hl in range(2):
                        r = slice(hl * 64, hl * 64 + 64)
                        sl = B.Sh[r, 2 * s2:2 * s2 + 2, :]
                        ps_ = bSu.t[r, :].rearrange("p (s c) -> p s c", s=2)[:, :, hl * 128:(hl + 1) * 128]
                        P.op('dve', lambda e: e.tensor_tensor(sl, sl, ps_, ALU.add), reads=[B.Sh, bSu], writes=[B.Sh])
                        egl = B.Eq[r, hh, :].rearrange("p (s r) -> p s r", r=8)[:, 2 * s2:2 * s2 + 2, 7:8].to_broadcast([64, 2, 128])
                        P.op('dve', lambda e: e.tensor_tensor(sl, sl, egl, ALU.mult), reads=[B.Sh, B.Eq], writes=[B.Sh])
                P.dma('sp', g.o_gla_s[l][:, 2 * hh:2 * hh + 2, :, :].rearrange("s hl d e -> (hl d) s e"), B.Sh[:, :, :], "Shg_o", reads=[B.Sh])
        if last_prompt:
            P.dma('sp', g.o_gla_p[l].rearrange("(hh hl) d e -> (hl d) hh e", hl=2), L.Sg[:, :, :], "Sg_o", reads=[L.Sg])
        if DBG <= 5:
            g.held.clear()
            continue
        P.op('act', lambda e: e.activation(B.o[:, :], bO.t[:, :], AF.Copy), reads=[bO], writes=[B.o])
        release(g, bO)
        gated_norm(g, L, B, B.o, 4, 128, P_GLA_NORM, True, B.on)
        out_proj(g, MS, B, B.on, 4, wo, t)


def ssd_mixer(g, L, MS, tiles, hT, is_s):
    P, l = g.P, L.l
    wpart, wo = L.WB, L.WO
    B = Ctx()
    A = Arena(MS, "sA", 5)
    Bc = Arena(MS, "sB", 3)
    G = Arena(MS, "sG", 6)
    Lt = Arena(MS, "sL", 4)
    B.cin = TT(A.tt.t[:, :, :].rearrange("p s f -> p (s f)")[:, 0:12 * 176].rearrange("p (c n) -> p c n", c=12), A.keys(0, 5))
    B.cout = Bc.span(0, 3, 128)
    B.acc = G.span(0, 3, 128)
    B.tmp = G.span(3, 3, 128)
    B.cvo = TT(Lt.tt.t[0:48, 0:3, :].rearrange("p s f -> p (s f)"), Lt.keys(0, 3))
    x_tm = TT(A.tt.t[:, 0:2, :].rearrange("p s f -> p (s f)"), A.keys(0, 2))
    xdt = TT(A.tt.t[:, 2:4, :].rearrange("p s f -> p (s f)"), A.keys(2, 2))
    xw = TT(Bc.tt.t[:, 0:2, :].rearrange("p s f -> p (s f)"), Bc.keys(0, 2))
    ysb = TT(G.tt.t[:, 0:2, :].rearrange("p s f -> p (s f)"), G.keys(0, 2))
    MT = G.slot(2, (4, 128))
    ET = G.slot(3, (4, 128))
    cbT = TT(G.tt.t[:, 4, 0:256].rearrange("p (c n) -> p c n", c=2), G.keys(4, 1))
    Btm = TT(G.tt.t[:, 4, 256:512], G.keys(4, 1))
    B.gate = TT(Lt.tt.t[:, 0:2, :].rearrange("p s f -> p (s f)"), Lt.keys(0, 2))
    gBm = TT(Lt.tt.t[:, 2:4, :].rearrange("p s f -> p (s f)"), Lt.keys(2, 2))
    B.on = MS.sb("on", [128, 1024], BF16)
    B.onT = MS.sb("onT", [128, 8, 128], BF16)
    B.sm2 = MS.sb("sm2", [128, 64])
    B.rT = MS.sb("rT", [4, 512])
    sm = MS.sb("sm", [128, 160])
    stc = None
    if is_s:
        stc = TT(G.tt.t[0:48, 0:3, :].rearrange("p s f -> p (s f)"), G.keys(0, 3))
        P.dma('sp', stc[:, :], g.st_conv[l][:, 1536:3072], "stcs_d", writes=[stc])
        B.Sh = [MS.sb("Sh0", [128, 8, 64]), MS.sb("Sh1", [128, 8, 64])]
        B.padf = MS.sb("pad", [128, 17 * 128])
        padd = B.padf.t[:, 0:16 * 136].rearrange("p (s q) -> p s q", q=136)[:, :, 0:8]
        padr = B.padf.t[:, 0:2048].rearrange("p (s n) -> p s n", s=16)
        B.egts = MS.sb("egts", [128, 16, 16])
    M_U01 = C_SU01 if is_s else C_U01
    M_SEG = C_SEG if is_s else C_ONES
    mNUI = C_SNUI if is_s else C_NUI
    nsh = 0
    for ti, t in enumerate(tiles):
        tok0 = ti * 128
        last_prompt = (not is_s) and (t == g.NPT - 1)
        if ti == 0:
            g.minfree = min(getattr(g, 'minfree', 1 << 30), g.nc.sbuf_bytes_remaining)
        conv_block(g, L, MS, B, wpart, hT, tok0, is_s, 12, L.tailS, stc, last_prompt, 1536)
        cout = B.cout
        bD = nextbank(g)
        proj_tm(g, wpart, 2560, 16, hT, tok0, bD)
        bZ = [nextbank(g), nextbank(g)]
        proj_tm(g, wpart, 1536, 512, hT, tok0, bZ[0])
        proj_tm(g, wpart, 2048, 512, hT, tok0, bZ[1])
        if ti == len(tiles) - 1:
            prefetch_next(g, L)
        gate_compute(g, B, bZ, 1024)
        P.op('dve', lambda e: e.tensor_tensor(sm[:, 128:144], bD.t[:, 0:16], L.parB[:, P_M2_DTB:P_M2_DTB + 16], ALU.add), reads=[bD, L.parB], writes=[sm])
        P.op('act', lambda e: e.activation(sm[:, 128:144], sm[:, 128:144], AF.Exp), reads=[sm], writes=[sm])
        P.op('act', lambda e: e.activation(sm[:, 0:16], sm[:, 128:144], AF.Ln, bias=1.0), reads=[sm], writes=[sm])
        P.op('dve', lambda e: e.tensor_tensor(sm[:, 16:32], sm[:, 0:16], L.pd[:, 4:20], ALU.mult), reads=[sm, L.pd], writes=[sm])
        bC = nextbank(g)
        P.op('pe', lambda e: e.matmul(bC.t[:, 0:16], cst(g, M_U01), sm[:, 16:32], start=True, stop=True), reads=[g.consts, sm], writes=[bC], inc=False)
        P.op('pe', lambda e: e.matmul(bC.t[:, 16:32], cst(g, M_SEG), sm[:, 16:32], start=True, stop=True), reads=[g.consts, sm], writes=[bC])
        P.op('dve', lambda e: e.tensor_copy(sm[:, 32:64], bC.t[:, 0:32]), reads=[bC], writes=[sm])
        P.op('dve', lambda e: e.tensor_scalar(sm[:, 64:80], sm[:, 32:48], -1.0, None, ALU.mult), reads=[sm], writes=[sm])
        P.op('act', lambda e: e.activation(sm[:, 80:96], sm[:, 32:48], AF.Exp), reads=[sm], writes=[sm])
        P.op('dve', lambda e: e.tensor_tensor(sm[:, 96:112], sm[:, 48:64], sm[:, 32:48], ALU.subtract), reads=[sm], writes=[sm])
        P.op('act', lambda e: e.activation(sm[:, 96:112], sm[:, 96:112], AF.Exp), reads=[sm], writes=[sm])
        P.op('act', lambda e: e.activation(sm[:, 112:128], sm[:, 48:64], AF.Exp), reads=[sm], writes=[sm])

        def bc16(c0, g_=None):
            if g_ is None:
                return sm[:, c0:c0 + 16].unsqueeze(2).to_broadcast([128, 16, 64])
            return sm[:, c0 + 8 * g_:c0 + 8 * g_ + 8].unsqueeze(2).to_broadcast([128, 8, 64])
        for hb in range(2):
            bX = nextbank(g)
            for c in range(4):
                P.op('pe', lambda e: e.transpose(bX.t[:, c * 128:(c + 1) * 128], cout[:, hb * 4 + c, :], cst(g, C_IDENT)), reads=[cout, g.consts], writes=[bX], inc=(c == 3))
            P.op('act', lambda e: e.activation(x_tm[:, hb * 512:(hb + 1) * 512], bX.t[:, :], AF.Copy), reads=[bX, B.cin], writes=[x_tm])
        x3 = x_tm.t.rearrange("p (h e) -> p h e", h=16)
        xdt3 = xdt.t.rearrange("p (h e) -> p h e", h=16)
        xw3 = xw.t.rearrange("p (h e) -> p h e", h=16)
        P.op('dve', lambda e: e.tensor_tensor(xdt3, x3, bc16(0), ALU.mult), reads=[x_tm, sm], writes=[xdt])
        P.op('pool', lambda e: e.tensor_tensor(xw3, xdt3, bc16(96), ALU.mult), reads=[xdt, sm, cout], writes=[xw])
        bB = nextbank(g)
        for g_ in range(2):
            P.op('pe', lambda e: e.transpose(bB.t[:, g_ * 128:(g_ + 1) * 128], cout[:, 8 + g_, :], cst(g, C_IDENT)), reads=[cout, g.consts], writes=[bB], inc=False)
        for g_ in range(2):
            P.op('pe', lambda e: e.matmul(bB.t[:, 256 + g_ * 128:256 + (g_ + 1) * 128], cout[:, 8 + g_, :], cout[:, 10 + g_, :], start=True, stop=True),
                 reads=[cout], writes=[bB], inc=(g_ == 1))
        P.op('act', lambda e: e.activation(Btm[:, :], bB.t[:, 0:256], AF.Copy), reads=[bB, B.acc, B.tmp], writes=[Btm])
        P.op('act', lambda e: e.activation(cbT[:, :, :], bB.t[:, 256:512].rearrange("p (c n) -> p c n", c=2), AF.Copy), reads=[bB], writes=[cbT])
        bY = [nextbank(g, hold=True), nextbank(g, hold=True)]
        for q in range(4):
            bT = nextbank(g)
            P.op('pe', lambda e: e.transpose(bT.t[0:4, 0:128], sm[:, 32 + 4 * q:36 + 4 * q], cst(g, C_IDENT)), reads=[sm, g.consts], writes=[bT], inc=False)
            P.op('pe', lambda e: e.transpose(bT.t[0:4, 128:256], sm[:, 64 + 4 * q:68 + 4 * q], cst(g, C_IDENT)), reads=[sm, g.consts], writes=[bT])
            P.op('dve', lambda e: e.tensor_copy(B.rT[:, 0:256], bT.t[0:4, 0:256]), reads=[bT], writes=[B.rT])
            bank = expo_bank(g, B, mNUI, B.rT[:, 0:128], B.rT[:, 128:256])
            P.op('act', lambda e: e.activation(ET[:, :, :], bank.t[:, :].rearrange("p (h n) -> p h n", h=4), AF.Exp), reads=[bank], writes=[ET])
            g_ = q // 2
            P.op('dve', lambda e: e.tensor_tensor(MT[:, :, :], ET[:, :, :], cbT[:, g_, :].unsqueeze(1).to_broadcast([128, 4, 128]), ALU.mult),
                 reads=[ET, cbT], writes=[MT])
            for hq in range(4):
                h = 4 * q + hq
                P.op('pe', lambda e: e.matmul(bY[g_].t[:, (h % 8) * 64:(h % 8) * 64 + 64], MT[:, hq, :], xdt[:, h * 64:(h + 1) * 64], start=True, stop=True),
                     reads=[MT, xdt], writes=[bY[g_]], inc=(hq == 3))
        bYI = [nextbank(g, hold=True), nextbank(g, hold=True)]
        if not is_s:
            for g_ in range(2):
                P.op('pe', lambda e: e.matmul(bYI[g_].t[:, :], cout[:, 10 + g_, :], L.Ss.t[:, :, :].rearrange("p h e -> p (h e)")[:, g_ * 512:(g_ + 1) * 512], start=True, stop=True),
                     reads=[cout, L.Ss], writes=[bYI[g_]])
        else:
            lt3 = B.padf.t[:, 0:256].rearrange("p (s h) -> p s h", h=16)
            P.op('dve', lambda e: e.tensor_tensor(lt3, sm[:, 48:64].unsqueeze(1).to_broadcast([128, 16, 16]),
                                                  g.consts[:, C_ROWM8:C_ROWM8 + 16].unsqueeze(2).to_broadcast([128, 16, 16]), ALU.mult),
                 reads=[sm, g.consts], writes=[B.padf])
            bE = nextbank(g)
            P.op('pe', lambda e: e.matmul(bE.t[:, 0:256], cst(g, C_ONES), B.padf[:, 0:256], start=True, stop=True), reads=[g.consts, B.padf], writes=[bE])
            P.op('act', lambda e: e.activation(B.egts[:, :, :], bE.t[:, 0:256].rearrange("p (s h) -> p s h", h=16), AF.Exp), reads=[bE], writes=[B.egts])
            for g_ in range(2):
                P.op('pool', lambda e: e.memset(B.padf[:, :], 0.0), writes=[B.padf])
                P.op('pool', lambda e: e.tensor_copy(padd, cout[:, 10 + g_, :].rearrange("p (s r) -> p s r", r=8)), reads=[cout, B.padf], writes=[B.padf])
                for s_ in range(16):
                    Sh = B.Sh[nsh % 2]
                    nsh += 1
                    P.dma('sp', Sh[:, :, :], g.st_ssm[l][s_, 8 * g_:8 * g_ + 8, :, :].rearrange("h n p -> n h p"), "Shs%d_d" % ((nsh - 1) % 2), writes=[Sh])
                    P.op('pe', lambda e: e.matmul(bYI[g_].t[:, :], B.padf[:, s_ * 128:(s_ + 1) * 128], Sh.t[:, :, :].rearrange("p h e -> p (h e)"), start=(s_ == 0), stop=(s_ == 15)),
                         reads=[B.padf, Sh], writes=[bYI[g_]], inc=True)
        y3 = ysb.t.rearrange("p (h e) -> p h e", h=16)
        for g_ in range(2):
            P.op('dve', lambda e: e.tensor_tensor(y3[:, 8 * g_:8 * g_ + 8, :], bYI[g_].t[:, :].rearrange("p (h e) -> p h e", h=8), bc16(80, g_), ALU.mult),
                 reads=[bYI[g_], sm], writes=[ysb])
            P.op('dve', lambda e: e.tensor_tensor(ysb[:, g_ * 512:(g_ + 1) * 512], ysb[:, g_ * 512:(g_ + 1) * 512], bY[g_].t[:, :], ALU.add),
                 reads=[ysb, bY[g_]], writes=[ysb])
        release(g, bY[0], bY[1], bYI[0], bYI[1])
        P.op('pool', lambda e: e.tensor_tensor(xdt3, x3, L.parB[:, P_M2_D:P_M2_D + 16].unsqueeze(2).to_broadcast([128, 16, 64]), ALU.mult),
             reads=[x_tm, L.parB, xdt], writes=[xdt])
        P.op('dve', lambda e: e.tensor_tensor(ysb[:, :], ysb[:, :], xdt[:, :], ALU.add), reads=[ysb, xdt], writes=[ysb])
        P.dma('sp', gBm[:, :], g.norms_d[3 * g.DEPTH + 1 + l:3 * g.DEPTH + 2 + l, :].partition_broadcast(128), "gBm_d", reads=[B.cvo], writes=[gBm])
        gated_norm(g, L, B, ysb, 16, 64, 0, False, B.on, norm_tt=gBm)
        out_proj(g, MS, B, B.on, 8, wo, t)
        if not is_s:
            for g_ in range(2):
                bSu = nextbank(g)
                P.op('pe', lambda e: e.matmul(bSu.t[:, :], Btm[:, g_ * 128:(g_ + 1) * 128], xw[:, g_ * 512:(g_ + 1) * 512], start=True, stop=True),
                     reads=[Btm, xw], writes=[bSu])
                sl = L.Ss[:, 8 * g_:8 * g_ + 8, :]
                P.op('dve', lambda e: e.tensor_tensor(sl, sl, bc16(112, g_), ALU.mult), reads=[L.Ss, sm], writes=[L.Ss])
                P.op('dve', lambda e: e.tensor_tensor(sl, sl, bSu.t[:, :].rearrange("p (h e) -> p h e", h=8), ALU.add), reads=[L.Ss, bSu], writes=[L.Ss])
            if last_prompt:
                P.dma('sp', g.o_ssm_p[l].rearrange("h n p -> n h p"), L.Ss[:, :, :], "Ss_o", reads=[L.Ss])
        else:
            for g_ in range(2):
                P.op('pool', lambda e: e.tensor_tensor(padr, Btm[:, g_ * 128:(g_ + 1) * 128].unsqueeze(1).to_broadcast([128, 16, 128]),
                                                       g.consts[:, C_ROWM:C_ROWM + 16].unsqueeze(2).to_broadcast([128, 16, 128]), ALU.mult),
                     reads=[Btm, g.consts], writes=[B.padf])
                for s_ in range(16):
                    Sh = B.Sh[nsh % 2]
                    nsh += 1
                    P.dma('sp', Sh[:, :, :], g.st_ssm[l][s_, 8 * g_:8 * g_ + 8, :, :].rearrange("h n p -> n h p"), "Shs%d_d" % ((nsh - 1) % 2), writes=[Sh])
                    bSu = nextbank(g)
                    P.op('pe', lambda e: e.matmul(bSu.t[:, :], B.padf[:, s_ * 128:(s_ + 1) * 128], xw[:, g_ * 512:(g_ + 1) * 512], start=True, stop=True),
                         reads=[B.padf, xw], writes=[bSu])
                    P.op('dve', lambda e: e.tensor_tensor(Sh[:, :, :], Sh[:, :, :], B.egts[:, s_, 8 * g_:8 * g_ + 8].unsqueeze(2).to_broadcast([128, 8, 64]), ALU.mult),
                         reads=[Sh, B.egts], writes=[Sh])
                    P.op('dve', lambda e: e.tensor_tensor(Sh[:, :, :], Sh[:, :, :], bSu.t[:, :].rearrange("p (h e) -> p h e", h=8), ALU.add),
                         reads=[Sh, bSu], writes=[Sh])
                    P.dma('sp', g.o_ssm_s[l][s_, 8 * g_:8 * g_ + 8, :, :].rearrange("h n p -> n h p"), Sh[:, :, :], "Shs%d_o" % ((nsh - 1) % 2), reads=[Sh])


def make_in_maps(inp, ncores=8, NPT=16, DEPTH=2):
    f = lambda a: np.ascontiguousarray(np.asarray(a, dtype=np.float32))
    consts = make_consts()
    params = np.zeros((DEPTH, NPAR), np.float32)
    for l in range(DEPTH):
        params[l, P_DN_ALOG:P_DN_ALOG + 4] = inp['dn_a_log'][l]
        params[l, P_DN_DTB:P_DN_DTB + 4] = inp['dn_dt_bias'][l]
        params[l, P_M2_ALOG:P_M2_ALOG + 16] = inp['m2_a_log'][l]
        params[l, P_M2_DTB:P_M2_DTB + 16] = inp['m2_dt_bias'][l]
        params[l, P_M2_D:P_M2_D + 16] = inp['m2_d'][l]
        params[l, P_DN_NORM:P_DN_NORM + 128] = inp['dn_norm'][l]
        params[l, P_GLA_NORM:P_GLA_NORM + 128] = inp['gla_norm'][l]
    norms = np.zeros((4 * DEPTH + 1, D), np.float32)
    for l in range(DEPTH):
        norms[3 * l] = inp['ffn1_norm'][l]
        norms[3 * l + 1] = inp['mix_norm'][l]
        norms[3 * l + 2] = inp['ffn2_norm'][l]
    norms[3 * DEPTH] = inp['final_norm']
    for l in range(DEPTH):
        norms[3 * DEPTH + 1 + l] = inp['m2_norm'][l]
    convw = f(np.asarray(inp['conv_w'])[:DEPTH].reshape(DEPTH, 4, 24, 128).transpose(0, 3, 2, 1).reshape(DEPTH, 128, 96))
    convb = f(np.asarray(inp['conv_b'])[:DEPTH].reshape(DEPTH, 24, 128).transpose(0, 2, 1))
    shared = {
        'consts': consts, 'params': params, 'norms': norms, 'convw': convw, 'convb': convb,
        'ffn1_w_gu': f(inp['ffn1_w_gu'][:DEPTH]), 'ffn2_w_gu': f(inp['ffn2_w_gu'][:DEPTH]),
        'ffn1_w_down': f(inp['ffn1_w_down'][:DEPTH]), 'ffn2_w_down': f(inp['ffn2_w_down'][:DEPTH]),
        'w_in': f(inp['w_in'][:DEPTH]), 'w_out': f(inp['w_out'][:DEPTH]),
        'gla_w_up': f(inp['gla_w_up'][:DEPTH]), 'gla_b_up': f(np.asarray(inp['gla_b_up'])[:DEPTH].reshape(DEPTH, 1, 256)),
    }
    maps = []
    for c in range(ncores):
        m = dict(shared)
        xp = np.asarray(inp['x_prompt'])[c, :NPT * 128]
        xs = np.asarray(inp['x_sample'])[16 * c:16 * c + 16].reshape(128, D)
        m['xin'] = f(np.concatenate([xp, xs], 0))
        m['st_conv'] = f(np.asarray(inp['state_conv'])[:DEPTH, 16 * c:16 * c + 16].reshape(DEPTH, 48, CONV_CH))
        m['st_delta'] = f(np.asarray(inp['state_delta'])[:DEPTH, 16 * c:16 * c + 16])
        m['st_gla'] = f(np.asarray(inp['state_gla'])[:DEPTH, 16 * c:16 * c + 16])
        m['st_ssm'] = f(np.asarray(inp['state_ssm'])[:DEPTH, 16 * c:16 * c + 16])
        maps.append(m)
    return maps


def gather_outputs(res, ncores=8, NPT=16, DEPTH=2):
    yp = np.stack([r['y'][:NPT * 128] for r in res], 0)
    ys = np.concatenate([r['y'][NPT * 128:].reshape(16, 8, D) for r in res], 0)
    cp = np.stack([r['o_conv_p'] for r in res], 1)
    dp = np.stack([r['o_delta_p'] for r in res], 1)
    gp = np.stack([r['o_gla_p'] for r in res], 1)
    sp = np.stack([r['o_ssm_p'] for r in res], 1)
    cs = np.concatenate([r['o_conv_s'].reshape(DEPTH, 16, 3, CONV_CH) for r in res], 1)
    ds = np.concatenate([r['o_delta_s'] for r in res], 1)
    gs = np.concatenate([r['o_gla_s'] for r in res], 1)
    ss = np.concatenate([r['o_ssm_s'] for r in res], 1)
    return tuple(np.ascontiguousarray(a, dtype=np.float32) for a in (yp, ys, cp, dp, gp, sp, cs, ds, gs, ss))


_NC_CACHE = {}


def kernel(**inputs):
    if 'nc' not in _NC_CACHE:
        _NC_CACHE['nc'] = build()[0]
    nc = _NC_CACHE['nc']
    in_maps = make_in_maps(inputs)
    res = run_bass_kernel_spmd(nc, in_maps, core_ids=list(range(8)))
    return gather_outputs(res.results)
```

```python
import numpy as np
import concourse.bass as bass
import concourse.mybir as mybir
from concourse.bass_utils import run_bass_kernel_spmd

F32 = mybir.dt.float32
BF16 = mybir.dt.bfloat16
AF = mybir.ActivationFunctionType
ALU = mybir.AluOpType
AX = mybir.AxisListType

D = 1024
DFF = 2816
EPS = 1e-6
NEG = -30000.0
CONV_CH = 3072
IN_COLS = 6184
O_DQ, O_DK, O_DV, O_MX, O_MB, O_MC = 0, 512, 1024, 1536, 2560, 2816
O_DZ, O_DA, O_DB, O_GQ, O_GK, O_GV, O_GLOW, O_GG, O_MZ, O_MDT = 3072, 3584, 3588, 3592, 3848, 4104, 4616, 4632, 5144, 6168

C_IDENT = 0
C_ONES = 128
C_U01 = 256
C_NUI = 384
C_NUS = 512
C_NLS = 640
C_SEG = 768
C_SU01 = 896
C_SNUI = 1024
C_SNUS = 1152
C_SNLS = 1280
C_ESEL = 1408
C_ROWM = 1920
C_ROWM8 = 1936
C_INV128 = 1952
NCONST = 1984

P_DN_ALOG, P_DN_DTB, P_M2_ALOG, P_M2_DTB, P_M2_D = 0, 4, 8, 24, 40
P_DN_NORM, P_GLA_NORM, P_M2_NORM = 64, 192, 320
NPAR = 320


def make_consts():
    c = np.zeros((128, NCONST), np.float32)
    p = np.arange(128)[:, None]
    f = np.arange(128)[None, :]
    seg = (p // 8) == (f // 8)
    c[:, C_IDENT:C_IDENT + 128] = (p == f)
    c[:, C_ONES:C_ONES + 128] = 1.0
    c[:, C_U01:C_U01 + 128] = (p <= f)
    c[:, C_NUI:C_NUI + 128] = np.where(p <= f, 0.0, NEG)
    c[:, C_NUS:C_NUS + 128] = np.where(p < f, 0.0, NEG)
    c[:, C_NLS:C_NLS + 128] = np.where(p > f, 0.0, NEG)
    c[:, C_SEG:C_SEG + 128] = seg
    c[:, C_SU01:C_SU01 + 128] = seg & (p <= f)
    c[:, C_SNUI:C_SNUI + 128] = np.where(seg & (p <= f), 0.0, NEG)
    c[:, C_SNUS:C_SNUS + 128] = np.where(seg & (p < f), 0.0, NEG)
    c[:, C_SNLS:C_SNLS + 128] = np.where(seg & (p > f), 0.0, NEG)
    for h in range(4):
        c[h, C_ESEL + h * 128:C_ESEL + (h + 1) * 128] = 1.0
    s = np.arange(16)[None, :]
    c[:, C_ROWM:C_ROWM + 16] = ((p // 8) == s)
    c[:, C_ROWM8:C_ROWM8 + 16] = ((p // 8) == s) / 8.0
    c[:, C_INV128] = 1.0 / 128
    return c


class TT:
    def __init__(self, t, key):
        self.t = t
        self.key = key

    def __getitem__(self, idx):
        return self.t[idx]


class Prog:
    def __init__(self, nc):
        self.nc = nc
        self.engs = {'pe': nc.tensor, 'act': nc.scalar, 'dve': nc.vector, 'pool': nc.gpsimd, 'sp': nc.sync}
        self.esem = {e: nc.alloc_semaphore("sem_" + e) for e in self.engs}
        self.cnt = {e: 0 for e in self.engs}
        self.seen = {e: {} for e in self.engs}
        self.lastw = {}
        self.readers = {}
        self.dsem = {}
        self.dcnt = {}
        self.dtot = {}
        self.nwaits = 0
        self.nops = 0
        self.nuid = 0

    def sb(self, name, shape, dtype=F32):
        self.nuid += 1
        nm = "%s_%d" % (name, self.nuid)
        return TT(self.nc.alloc_sbuf_tensor(nm, list(shape), dtype), nm)

    @staticmethod
    def _keys(lst):
        out = []
        for x in lst:
            k = x.key if isinstance(x, TT) else x
            if isinstance(k, (tuple, list)):
                out.extend(k)
            else:
                out.append(k)
        return out

    def _wait(self, eng, ev):
        sem, val = ev
        k = id(sem)
        if k in self.dtot:
            val = max(val, self.dtot[k])
        if self.seen[eng].get(k, 0) >= val:
            return
        if sem is self.esem[eng] and val > self.cnt[eng]:
            return
        self.seen[eng][k] = val
        self.engs[eng].wait_ge(sem, val)
        self.nwaits += 1

    def _deps(self, eng, reads, writes, skipsem=None):
        evs = []
        for k in reads:
            if k in self.lastw:
                evs.append(self.lastw[k])
        for k in writes:
            if k in self.lastw and self.lastw[k][0] is not skipsem:
                evs.append(self.lastw[k])
            for ev in self.readers.get(k, {}).values():
                evs.append(ev)
        for ev in evs:
            self._wait(eng, ev)

    def _record(self, ev, reads, writes):
        for k in reads:
            self.readers.setdefault(k, {})[id(ev[0])] = ev
        for k in writes:
            self.lastw[k] = ev
            self.readers[k] = {}

    def op(self, eng, fn, reads=(), writes=(), inc=True):
        reads = self._keys(reads)
        writes = self._keys(writes)
        self._deps(eng, reads, writes)
        ins = fn(self.engs[eng])
        self.nops += 1
        if inc:
            self.cnt[eng] += 1
            ins.then_inc(self.esem[eng], 1)
            ev = (self.esem[eng], self.cnt[eng])
        else:
            ev = (self.esem[eng], self.cnt[eng] + 1)
        self._record(ev, reads, writes)
        return ins

    def dma(self, eng, out, in_, group, reads=(), writes=()):
        reads = self._keys(reads)
        writes = self._keys(writes)
        if group not in self.dsem:
            self.dsem[group] = self.nc.alloc_semaphore("dsem_%d" % len(self.dsem))
            self.dcnt[group] = 0
        sem = self.dsem[group]
        self._deps(eng, reads, writes, skipsem=sem)
        ins = self.engs[eng].dma_start(out=out, in_=in_)
        self.nops += 1
        self.dcnt[group] += 16
        self.dtot[id(sem)] = self.dcnt[group]
        ins.then_inc(sem, 16)
        ev = (sem, self.dcnt[group])
        self._record(ev, reads, writes)
        return ins

    def finish(self, eng='sp'):
        for g, sem in self.dsem.items():
            self._wait(eng, (sem, self.dcnt[g]))
        for e in self.engs:
            if e != eng and self.cnt[e] > 0:
                self._wait(eng, (self.esem[e], self.cnt[e]))


class Ctx:
    pass


def build(NPT=16, DEPTH=2, stages=('ffn', 'mix')):
    NT = NPT + 1
    NTOK = NT * 128
    nc = bass.Bass("TRN2", target_bir_lowering=False)
    P = Prog(nc)
    g = Ctx()
    g.nc, g.P, g.NPT, g.NT, g.DEPTH = nc, P, NPT, NT, DEPTH
    g.mixers = [m for m in ('delta', 'gla', 'ssd') if m in stages or 'mix' in stages]

    def din(name, shape):
        return nc.dram_tensor(name, list(shape), F32, kind="ExternalInput").ap()

    def dout(name, shape):
        return nc.dram_tensor(name, list(shape), F32, kind="ExternalOutput").ap()

    g.xin = din("xin", [NTOK, D])
    g.consts_d = din("consts", [128, NCONST])
    g.params_d = din("params", [DEPTH, NPAR])
    g.st_conv = din("st_conv", [DEPTH, 48, CONV_CH])
    g.st_delta = din("st_delta", [DEPTH, 16, 4, 128, 128])
    g.st_gla = din("st_gla", [DEPTH, 16, 4, 64, 128])
    g.st_ssm = din("st_ssm", [DEPTH, 16, 16, 128, 64])
    g.norms_d = din("norms", [4 * DEPTH + 1, D])
    g.w_gu = [din("ffn1_w_gu", [DEPTH, D, 2 * DFF]), din("ffn2_w_gu", [DEPTH, D, 2 * DFF])]
    g.w_down = [din("ffn1_w_down", [DEPTH, DFF, D]), din("ffn2_w_down", [DEPTH, DFF, D])]
    g.w_in = din("w_in", [DEPTH, D, IN_COLS])
    g.w_out = din("w_out", [DEPTH, 2 * D, D])
    g.convw_d = din("convw", [DEPTH, 128, 24 * 4])
    g.convb_d = din("convb", [DEPTH, 128, 24])
    g.gla_wup = din("gla_w_up", [DEPTH, 16, 256])
    g.gla_bup = din("gla_b_up", [DEPTH, 1, 256])

    g.y = dout("y", [NTOK, D])
    g.o_conv_p = dout("o_conv_p", [DEPTH, 3, CONV_CH])
    g.o_delta_p = dout("o_delta_p", [DEPTH, 4, 128, 128])
    g.o_gla_p = dout("o_gla_p", [DEPTH, 4, 64, 128])
    g.o_ssm_p = dout("o_ssm_p", [DEPTH, 16, 128, 64])
    g.o_conv_s = dout("o_conv_s", [DEPTH, 48, CONV_CH])
    g.o_delta_s = dout("o_delta_s", [DEPTH, 16, 4, 128, 128])
    g.o_gla_s = dout("o_gla_s", [DEPTH, 16, 4, 64, 128])
    g.o_ssm_s = dout("o_ssm_s", [DEPTH, 16, 16, 128, 64])

    g.x = P.sb("x", [128, NT, D])
    g.xk = ["x_t%d" % t for t in range(NT)]
    g.consts = P.sb("consts", [128, NCONST])
    g.identb = P.sb("identb", [128, 128], BF16)
    g.stat = P.sb("stat", [128, 64])
    g.ps = [TT(nc.alloc_psum_tensor("psb%d" % i, [128, 512], F32), "psb%d" % i) for i in range(8)]
    g.psi = 0
    g.held = set()

    P.dma('sp', g.consts[:, :], g.consts_d, 'consts', writes=[g.consts])
    for t in range(NT):
        P.dma('sp', g.x[:, t, :], g.xin[t * 128:(t + 1) * 128, :], 'xload', writes=[g.xk[t]])
    P.op('dve', lambda e: e.tensor_copy(g.identb[:, :], g.consts[:, C_IDENT:C_IDENT + 128]), reads=[g.consts], writes=[g.identb])
    g.maskb = P.sb("maskb", [128, 6, 128], BF16)
    g.maskb_idx = {}
    for i_, off_ in enumerate((C_NUI, C_NUS, C_NLS, C_SNUI, C_SNUS, C_SNLS)):
        g.maskb_idx[off_] = i_
        P.op('dve', lambda e: e.tensor_copy(g.maskb[:, i_, :], g.consts[:, off_:off_ + 128]), reads=[g.consts], writes=[g.maskb])

    for l in range(DEPTH):
        if 'ffn' in stages:
            ffn_phase(g, l, 0)
        if g.mixers:
            mix_phase(g, l)
        if 'ffn' in stages:
            ffn_phase(g, l, 1)
    final_phase(g)
    P.finish('sp')
    g.stats = (P.nops, P.nwaits, len(P.dsem))
    return nc, g


def nextbank(g, hold=False):
    while True:
        b = g.ps[g.psi % 8]
        g.psi += 1
        if b.key not in g.held:
            break
    if hold:
        g.held.add(b.key)
    return b


def release(g, *banks):
    for b in banks:
        g.held.discard(b.key)


def barrier(g, skip=()):
    P = g.P
    for e in P.engs:
        for gname, sem in P.dsem.items():
            if gname in skip:
                continue
            P._wait(e, (sem, P.dcnt[gname]))
        for e2 in P.engs:
            if e2 != e and P.cnt[e2] > 0:
                P._wait(e, (P.esem[e2], P.cnt[e2]))


def norm_tiles(g, tiles, gain_row, hT, hn, gB, tok0=0):
    P, nc = g.P, g.nc
    n = len(tiles)
    P.dma('sp', gB[:, :], g.norms_d[gain_row:gain_row + 1, :].partition_broadcast(128), 'gB_d', writes=[gB])
    junk = hn[0]
    for i, t in enumerate(tiles):
        P.op('act', lambda e: e.activation(junk[:, :], g.x[:, t, :], AF.Square, accum_out=g.stat[:, i:i + 1]),
             reads=[g.xk[t]], writes=[junk, g.stat])
    P.op('act', lambda e: e.activation(g.stat[:, 16:16 + n], g.stat[:, 0:n], AF.Ln, bias=EPS, scale=1.0 / D),
         reads=[g.stat], writes=[g.stat])
    P.op('act', lambda e: e.activation(g.stat[:, 32:32 + n], g.stat[:, 16:16 + n], AF.Exp, scale=-0.5),
         reads=[g.stat], writes=[g.stat])
    for i, t in enumerate(tiles):
        h = hn[i % 2]
        P.op('dve', lambda e: e.scalar_tensor_tensor(h[:, :], g.x[:, t, :], g.stat[:, 32 + i:33 + i], gB[:, :], ALU.mult, ALU.mult),
             reads=[g.xk[t], g.stat, gB], writes=[h])
        bank = nextbank(g)
        pb = bank.t[:, :].bitcast(BF16)
        for kc in range(8):
            P.op('pe', lambda e: e.transpose(pb[:, kc * 128:(kc + 1) * 128], h[:, kc * 128:(kc + 1) * 128], g.identb[:, :]),
                 reads=[h, g.identb], writes=[bank], inc=(kc == 7))
        c0 = tok0 + i * 128
        P.op('act', lambda e: e.activation(hT[:, :, c0:c0 + 128], pb.rearrange("p (k n) -> p k n", k=8), AF.Copy),
             reads=[bank], writes=[hT])


def ffn_phase(g, l, which):
    P, nc = g.P, g.nc
    NT = g.NT
    half = (NT + 1) // 2
    sbs = [list(range(0, NT - half)), list(range(NT - half, NT))]
    SBW = half * 128
    barrier(g)
    w_gu = g.w_gu[which][l]
    w_down = g.w_down[which][l]
    tag = "f%d_%d_" % (l, which)
    with nc.sbuf_tensor(tag + "hT", [128, 8, SBW], BF16) as hT_, \
            nc.sbuf_tensor(tag + "actT", [128, 22, SBW], BF16) as actT_, \
            nc.sbuf_tensor(tag + "wgu0", [128, 8, 512], BF16) as wgu0, nc.sbuf_tensor(tag + "wgu1", [128, 8, 512], BF16) as wgu1, \
            nc.sbuf_tensor(tag + "wd0", [128, 22, 256], BF16) as wd0, nc.sbuf_tensor(tag + "wd1", [128, 22, 256], BF16) as wd1, \
            nc.sbuf_tensor(tag + "gB", [128, D], F32) as gB_, \
            nc.sbuf_tensor(tag + "hn0", [128, D], BF16) as hn0, nc.sbuf_tensor(tag + "hn1", [128, D], BF16) as hn1, \
            nc.sbuf_tensor(tag + "sg0", [128, 512], F32) as sg0, nc.sbuf_tensor(tag + "sg1", [128, 512], F32) as sg1:
        hT = TT(hT_, tag + "hT")
        actT = TT(actT_, tag + "actT")
        wgu = [TT(wgu0, tag + "wgu0"), TT(wgu1, tag + "wgu1")]
        wd = [TT(wd0, tag + "wd0"), TT(wd1, tag + "wd1")]
        gB = TT(gB_, tag + "gB")
        hn = [TT(hn0, tag + "hn0"), TT(hn1, tag + "hn1")]
        sg = [TT(sg0, tag + "sg0"), TT(sg1, tag + "sg1")]
        nsg = 0
        for sbi, tiles in enumerate(sbs):
            ntok = len(tiles) * 128
            norm_tiles(g, tiles, 3 * l + (0 if which == 0 else 2), hT, hn, gB)
            tgs = [(c0, min(512, ntok - c0)) for c0 in range(0, ntok, 512)]
            for j in range(11):
                wb = wgu[j % 2]
                P.dma('pool', wb[:, :, 0:256], w_gu[:, 256 * j:256 * j + 256].rearrange("(k p) n -> p k n", p=128),
                      "wgu%d_d" % (j % 2), writes=[wb])
                P.dma('pool', wb[:, :, 256:512], w_gu[:, DFF + 256 * j:DFF + 256 * j + 256].rearrange("(k p) n -> p k n", p=128),
                      "wgu%d_d" % (j % 2), writes=[wb])
                for c in range(2):
                    for (c0, w) in tgs:
                        bA = nextbank(g)
                        bB = nextbank(g)
                        for kc in range(8):
                            P.op('pe', lambda e: e.matmul(bA.t[:, 0:w], wb[:, kc, c * 128:(c + 1) * 128], hT[:, kc, c0:c0 + w],
                                                          start=(kc == 0), stop=(kc == 7)),
                                 reads=[wb, hT], writes=[bA], inc=(kc == 7))
                        for kc in range(8):
                            P.op('pe', lambda e: e.matmul(bB.t[:, 0:w], wb[:, kc, 256 + c * 128:256 + (c + 1) * 128], hT[:, kc, c0:c0 + w],
                                                          start=(kc == 0), stop=(kc == 7)),
                                 reads=[wb, hT], writes=[bB], inc=(kc == 7))
                        s_ = sg[nsg % 2]
                        nsg += 1
                        P.op('act', lambda e: e.activation(s_[:, 0:w], bA.t[:, 0:w], AF.Silu), reads=[bA], writes=[s_])
                        P.op('dve', lambda e: e.tensor_tensor(actT[:, 2 * j + c, c0:c0 + w], s_[:, 0:w], bB.t[:, 0:w], ALU.mult),
                             reads=[s_, bB], writes=[actT])
            for q in range(4):
                wq = wd[q % 2]
                P.dma('pool', wq[:, :, :], w_down[:, 256 * q:256 * q + 256].rearrange("(k p) n -> p k n", p=128),
                      "wd%d_d" % (q % 2), writes=[wq])
                for i, t in enumerate(tiles):
                    bD = nextbank(g)
                    for kc in range(22):
                        P.op('pe', lambda e: e.matmul(bD.t[:, 0:256], actT[:, kc, i * 128:(i + 1) * 128], wq[:, kc, :],
                                                      start=(kc == 0), stop=(kc == 21)),
                             reads=[actT, wq], writes=[bD], inc=(kc == 21))
                    P.op('dve', lambda e: e.scalar_tensor_tensor(g.x[:, t, 256 * q:256 * q + 256], bD.t[:, 0:256], 0.5,
                                                                 g.x[:, t, 256 * q:256 * q + 256], ALU.mult, ALU.add),
                         reads=[bD, g.xk[t]], writes=[g.xk[t]])
        barrier(g)


def final_phase(g):
    P, nc = g.P, g.nc
    NT = g.NT
    barrier(g)
    with nc.sbuf_tensor("fin_gB", [128, D], F32) as gB_, nc.sbuf_tensor("fin_o0", [128, D], F32) as o0, \
            nc.sbuf_tensor("fin_o1", [128, D], F32) as o1:
        gB = TT(gB_, "fin_gB")
        ob = [TT(o0, "fin_o0"), TT(o1, "fin_o1")]
        P.dma('sp', gB[:, :], g.norms_d[3 * g.DEPTH:3 * g.DEPTH + 1, :].partition_broadcast(128), 'fin_gB', writes=[gB])
        for t in range(NT):
            o = ob[t % 2]
            P.op('act', lambda e: e.activation(o[:, :], g.x[:, t, :], AF.Square, accum_out=g.stat[:, 0:1]),
                 reads=[g.xk[t]], writes=[o, g.stat])
            P.op('act', lambda e: e.activation(g.stat[:, 1:2], g.stat[:, 0:1], AF.Ln, bias=EPS, scale=1.0 / D),
                 reads=[g.stat], writes=[g.stat])
            P.op('act', lambda e: e.activation(g.stat[:, 2:3], g.stat[:, 1:2], AF.Exp, scale=-0.5),
                 reads=[g.stat], writes=[g.stat])
            P.op('dve', lambda e: e.scalar_tensor_tensor(o[:, :], g.x[:, t, :], g.stat[:, 2:3], gB[:, :], ALU.mult, ALU.mult),
                 reads=[g.xk[t], g.stat, gB], writes=[o])
            P.dma('sp', g.y[t * 128:(t + 1) * 128, :], o[:, :], o.key + "_out", reads=[o])
        barrier(g)


class Scope:
    def __init__(self, g, tag):
        import contextlib
        self.g, self.tag = g, tag
        self.es = contextlib.ExitStack()

    def __enter__(self):
        self.es.__enter__()
        return self

    def __exit__(self, *a):
        return self.es.__exit__(*a)

    def sb(self, name, shape, dtype=F32):
        g = self.g
        g.P.nuid += 1
        nm = "%s_%s_%d" % (self.tag, name, g.P.nuid)
        t = self.es.enter_context(g.nc.sbuf_tensor(nm, list(shape), dtype))
        return TT(t, nm)


def cst(g, off, n=128, rows=128):
    return g.consts[0:rows, off:off + n]


def mix_phase(g, l):
    P, nc = g.P, g.nc
    NPT = g.NPT
    barrier(g)
    sbs = [list(range(a, min(a + 6, NPT))) for a in range(0, NPT, 6)] + [[NPT]]
    with Scope(g, "mx%d" % l) as LS:
        L = Ctx()
        L.l = l
        L.parB = LS.sb("parB", [128, NPAR])
        L.convw = LS.sb("convw", [128, 24, 4])
        L.convb = LS.sb("convb", [128, 24])
        L.wup = LS.sb("wup", [16, 256])
        L.bup = LS.sb("bup", [1, 256])
        L.pd = LS.sb("pd", [128, 64])
        P.dma('sp', L.parB[:, :], g.params_d[l:l + 1, :].partition_broadcast(128), 'lay_small', writes=[L.parB])
        P.dma('sp', L.convw[:, :, :], g.convw_d[l].rearrange("p (c j) -> p c j", j=4), 'lay_small', writes=[L.convw])
        P.dma('sp', L.convb[:, :], g.convb_d[l], 'lay_small', writes=[L.convb])
        P.dma('sp', L.wup[:, :], g.gla_wup[l], 'lay_small', writes=[L.wup])
        P.dma('sp', L.bup[:, :], g.gla_bup[l], 'lay_small', writes=[L.bup])
        P.op('act', lambda e: e.activation(L.pd[:, 0:4], L.parB[:, P_DN_ALOG:P_DN_ALOG + 4], AF.Exp), reads=[L.parB], writes=[L.pd])
        P.op('act', lambda e: e.activation(L.pd[:, 4:20], L.parB[:, P_M2_ALOG:P_M2_ALOG + 16], AF.Exp), reads=[L.parB], writes=[L.pd])
        P.op('dve', lambda e: e.tensor_scalar(L.pd[:, 0:20], L.pd[:, 0:20], -1.0, None, ALU.mult), reads=[L.pd], writes=[L.pd])
        L.Sd = LS.sb("Sd", [128, 4, 128])
        L.Sg = LS.sb("Sg", [128, 2, 128])
        L.Ss = LS.sb("Ss", [128, 16, 64])
        L.tailD = LS.sb("tailD", [128, 12, 3])
        L.tailS = LS.sb("tailS", [128, 12, 3])
        for t_ in (L.Sd, L.Sg, L.Ss, L.tailD, L.tailS):
            P.op('pool', lambda e: e.memset(t_.t[:], 0.0), writes=[t_])
        import os
        L.WB = LS.sb("WB", [128, 8, 2576], BF16)
        L.WO = LS.sb("WO", [128, 8, 1024], BF16)
        phases = [(sbi_, m_) for sbi_ in range(len(sbs)) for m_ in g.mixers]
        L.phases = phases
        L.phase_i = 0
        load_wpart(g, L, phases[0][1])
        load_wo(g, L, phases[0][1])
        for sbi, tiles in enumerate(sbs):
            is_s = tiles[0] == NPT
            ntok = len(tiles) * 128
            with Scope(g, "mx%d_%d" % (l, sbi)) as SS:
                hT = SS.sb("hT", [128, 8, ntok], BF16)
                with Scope(g, "mx%d_%d_n" % (l, sbi)) as NS:
                    hn = [NS.sb("hn0", [128, D], BF16), NS.sb("hn1", [128, D], BF16)]
                    gB = NS.sb("gB", [128, D])
                    norm_tiles(g, tiles, 3 * l + 1, hT, hn, gB)
                    barrier(g)
                for mixer in g.mixers:
                    with Scope(g, "mx%d_%d_%s" % (l, sbi, mixer)) as MS:
                        if mixer == 'delta':
                            delta_mixer(g, L, MS, tiles, hT, is_s)
                        elif mixer == 'gla':
                            gla_mixer(g, L, MS, tiles, hT, is_s)
                        else:
                            ssd_mixer(g, L, MS, tiles, hT, is_s)
                    L.phase_i += 1
                    if L.phase_i < len(L.phases):
                        load_wo(g, L, L.phases[L.phase_i][1])
                    barrier(g, skip=("wpart_d", "wo_d"))
        barrier(g)


MIX_COLS = {'delta': [(O_DQ, 1536), (O_DZ, 520)], 'gla': [(O_GQ, 1552)], 'ssd': [(O_MX, 1536), (O_MZ, 1040)]}
MIX_ROWS = {'delta': (0, 512), 'gla': (512, 512), 'ssd': (1024, 1024)}


def load_wpart(g, L, mixer):
    P, l = g.P, L.l
    c = 0
    for (c0, n) in MIX_COLS[mixer]:
        for a in range(0, n, 512):
            w = min(512, n - a)
            P.dma('pool', L.WB[:, :, c + a:c + a + w], g.w_in[l][:, c0 + a:c0 + a + w].rearrange("(k p) n -> p k n", p=128),
                  "wpart_d", writes=[L.WB])
        c += n


def load_wo(g, L, mixer):
    P, l = g.P, L.l
    r0, nr = MIX_ROWS[mixer]
    for a in range(0, nr, 512):
        P.dma('pool', L.WO[:, a // 128:a // 128 + 4, :], g.w_out[l][r0 + a:r0 + a + 512, :].rearrange("(k p) n -> p k n", p=128),
              "wo_d", writes=[L.WO])


def prefetch_next(g, L):
    if L.phase_i + 1 < len(L.phases):
        load_wpart(g, L, L.phases[L.phase_i + 1][1])


def proj_fm(g, wpart, col0, ncols, hT, tok0, bank, slot, rows0=0):
    P = g.P
    for kc in range(8):
        P.op('pe', lambda e: e.matmul(bank.t[rows0:rows0 + ncols, slot * 128:(slot + 1) * 128], wpart[:, kc, col0:col0 + ncols],
                                      hT[:, kc, tok0:tok0 + 128], start=(kc == 0), stop=(kc == 7)),
             reads=[wpart, hT], writes=[bank], inc=(kc == 7))


def proj_tm(g, wpart, col0, ncols, hT, tok0, bank, o0=0):
    P = g.P
    for kc in range(8):
        P.op('pe', lambda e: e.matmul(bank.t[:, o0:o0 + ncols], hT[:, kc, tok0:tok0 + 128], wpart[:, kc, col0:col0 + ncols],
                                      start=(kc == 0), stop=(kc == 7)),
             reads=[wpart, hT], writes=[bank], inc=(kc == 7))


def conv_block(g, L, MS, B, wpart, hT, tok0, is_s, ch0, tail, stc, last_prompt, conv_out_col0):
    P, l = g.P, L.l
    cin, cout, acc, tmp = B.cin, B.cout, B.acc, B.tmp
    if is_s:
        cin4 = cin.t[:, :, :].rearrange("p c (r s) -> p c r s", s=16)
    for grp in range(3):
        bank = nextbank(g)
        for c4 in range(4):
            proj_fm(g, wpart, (grp * 4 + c4) * 128, 128, hT, tok0, bank, c4)
        if not is_s:
            P.op('act', lambda e: e.activation(cin[:, grp * 4:grp * 4 + 4, 3:131], bank.t[:, :].rearrange("p (c n) -> p c n", c=4), AF.Copy),
                 reads=[bank], writes=[cin])
        else:
            P.op('act', lambda e: e.activation(cin4[:, grp * 4:grp * 4 + 4, 3:11, :],
                                               bank.t[:, :].rearrange("p (c s t) -> p c t s", c=4, s=16), AF.Copy),
                 reads=[bank], writes=[cin])
    if not is_s:
        P.op('pool', lambda e: e.tensor_copy(cin[:, :, 0:3], tail[:, :, :]), reads=[tail], writes=[cin])
        P.op('pool', lambda e: e.tensor_copy(tail[:, :, :], cin[:, :, 128:131]), reads=[cin], writes=[tail])
    else:
        for grp in range(3):
            bank = nextbank(g)
            for c4 in range(4):
                ch = grp * 4 + c4
                P.op('pe', lambda e: e.transpose(bank.t[:, c4 * 48:(c4 + 1) * 48], stc[0:48, ch * 128:(ch + 1) * 128], cst(g, C_IDENT, 48, 48)),
                     reads=[stc, g.consts], writes=[bank], inc=(c4 == 3))
            P.op('act', lambda e: e.activation(cin4[:, grp * 4:grp * 4 + 4, 0:3, :],
                                               bank.t[:, 0:192].rearrange("p (c s r) -> p c r s", c=4, r=3), AF.Copy),
                 reads=[bank], writes=[cin])
    if is_s or last_prompt:
        for grp in range(3):
            bank = nextbank(g)
            for c4 in range(4):
                ch = grp * 4 + c4
                if is_s:
                    src = cin.t[:, ch, 128:176]
                    n = 48
                else:
                    src = cin[:, ch, 128:131]
                    n = 3
                P.op('pe', lambda e: e.transpose(bank.t[0:n, c4 * 128:(c4 + 1) * 128], src, cst(g, C_IDENT)),
                     reads=[cin, g.consts], writes=[bank], inc=(c4 == 3))
            n = 48 if is_s else 3
            ob = B.cvo
            P.op('dve', lambda e: e.tensor_copy(ob[0:n, grp * 512:(grp + 1) * 512], bank.t[0:n, :]), reads=[bank], writes=[ob])
        if is_s:
            dv = g.o_conv_s[l].rearrange("(s r) c -> r s c", r=3)
            for r_ in range(3):
                P.dma('sp', dv[r_, :, conv_out_col0:conv_out_col0 + 1536], B.cvo[r_ * 16:(r_ + 1) * 16, :], "cvo_o", reads=[B.cvo])
        else:
            P.dma('sp', g.o_conv_p[l][:, conv_out_col0:conv_out_col0 + 1536], B.cvo[0:3, :], "cvo_o", reads=[B.cvo])
    return


def conv_taps(g, L, B, is_s, ch0):
    P = g.P
    cin, cout, acc, tmp = B.cin, B.cout, B.acc, B.tmp
    if is_s:
        cin4 = cin.t[:, :, :].rearrange("p c (r s) -> p c r s", s=16)

    def view(j):
        if is_s:
            return cin4[:, :, j:j + 8, :]
        return cin[:, :, j:j + 128]

    def wv(j, shape):
        return L.convw[:, ch0:ch0 + 12, j:j + 1].to_broadcast(shape) if not is_s else \
            L.convw[:, ch0:ch0 + 12, j:j + 1].unsqueeze(3).to_broadcast(shape)
    if is_s:
        shape = [128, 12, 8, 16]
        accv = acc.t[:, :, :].rearrange("p c (t s) -> p c t s", s=16)
        tmpv = tmp.t[:, :, :].rearrange("p c (t s) -> p c t s", s=16)
        bv_ = L.convb[:, ch0:ch0 + 12].unsqueeze(2).unsqueeze(3).to_broadcast(shape)
    else:
        shape = [128, 12, 128]
        accv = acc.t[:, :, :]
        tmpv = tmp.t[:, :, :]
        bv_ = L.convb[:, ch0:ch0 + 12].unsqueeze(2).to_broadcast(shape)
    P.op('pool', lambda e: e.tensor_tensor(accv, view(0), wv(0, shape), ALU.mult), reads=[cin, L.convw], writes=[acc])
    for j in range(1, 4):
        P.op('pool', lambda e: e.tensor_tensor(tmpv, view(j), wv(j, shape), ALU.mult), reads=[cin, L.convw], writes=[tmp])
        P.op('pool', lambda e: e.tensor_tensor(accv, accv, tmpv, ALU.add), reads=[acc, tmp], writes=[acc])
    P.op('pool', lambda e: e.tensor_tensor(accv, accv, bv_, ALU.add), reads=[acc, L.convb], writes=[acc])
    if is_s:
        P.op('act', lambda e: e.activation(cout.t[:, :, :].rearrange("p c (s t) -> p c t s", s=16), accv, AF.Silu),
             reads=[acc], writes=[cout])
    else:
        P.op('act', lambda e: e.activation(cout[:, :, :], accv, AF.Silu), reads=[acc], writes=[cout])


def out_proj(g, MS, B, on_bf, nk, wo, t):
    P = g.P
    bank = nextbank(g)
    pb = bank.t[:, :].bitcast(BF16)
    for kc in range(nk):
        P.op('pe', lambda e: e.transpose(pb[:, kc * 128:(kc + 1) * 128], on_bf[:, kc * 128:(kc + 1) * 128], g.identb[:, :]),
             reads=[on_bf, g.identb], writes=[bank], inc=(kc == nk - 1))
    P.op('act', lambda e: e.activation(B.onT[:, 0:nk, :], pb[:, 0:nk * 128].rearrange("p (k n) -> p k n", k=nk), AF.Copy),
         reads=[bank], writes=[B.onT])
    for hh in range(2):
        bo = nextbank(g)
        for kc in range(nk):
            P.op('pe', lambda e: e.matmul(bo.t[:, :], B.onT[:, kc, :], wo[:, kc, hh * 512:(hh + 1) * 512], start=(kc == 0), stop=(kc == nk - 1)),
                 reads=[B.onT, wo], writes=[bo], inc=(kc == nk - 1))
        P.op('dve', lambda e: e.tensor_tensor(g.x[:, t, hh * 512:(hh + 1) * 512], g.x[:, t, hh * 512:(hh + 1) * 512], bo.t[:, :], ALU.add),
             reads=[bo, g.xk[t]], writes=[g.xk[t]])


def gate_compute(g, B, gate_banks, W):
    P = g.P
    gt = B.gate
    for bi, bank in enumerate(gate_banks):
        w = min(512, W - bi * 512)
        sl = gt[:, bi * 512:bi * 512 + w]
        P.op('act', lambda e: e.activation(sl, bank.t[:, 0:w], AF.Exp, scale=-1.0), reads=[bank], writes=[gt])
        P.op('act', lambda e: e.activation(sl, sl, AF.Ln, bias=1.0), reads=[gt], writes=[gt])
        P.op('act', lambda e: e.activation(sl, sl, AF.Exp, scale=-1.0), reads=[gt], writes=[gt])
        P.op('dve', lambda e: e.tensor_tensor(sl, sl, bank.t[:, 0:w], ALU.mult), reads=[gt, bank], writes=[gt])


def gated_norm(g, L, B, o_sb, nh, hd, norm_off, per_head, out_bf, norm_tt=None):
    P = g.P
    W = nh * hd
    sm = B.sm2
    gt = B.gate
    sq = getattr(B, 'sq', None)
    if per_head:
        P.op('pool', lambda e: e.tensor_tensor(sq[:, 0:W], o_sb[:, 0:W], o_sb[:, 0:W], ALU.mult), reads=[o_sb], writes=[sq])
        P.op('dve', lambda e: e.tensor_reduce(sm[:, 0:nh], sq[:, 0:W].rearrange("p (h e) -> p h e", h=nh), AX.X, ALU.add), reads=[sq], writes=[sm])
        P.op('act', lambda e: e.activation(sm[:, 16:16 + nh], sm[:, 0:nh], AF.Ln, bias=EPS, scale=1.0 / hd), reads=[sm], writes=[sm])
        P.op('act', lambda e: e.activation(sm[:, 32:32 + nh], sm[:, 16:16 + nh], AF.Exp, scale=-0.5), reads=[sm], writes=[sm])
        P.op('dve', lambda e: e.tensor_tensor(sq[:, 0:W].rearrange("p (h e) -> p h e", h=nh), o_sb[:, 0:W].rearrange("p (h e) -> p h e", h=nh),
                                              sm[:, 32:32 + nh].unsqueeze(2).to_broadcast([128, nh, hd]), ALU.mult), reads=[o_sb, sm], writes=[sq])
        P.op('pool', lambda e: e.tensor_tensor(sq[:, 0:W].rearrange("p (h e) -> p h e", h=nh), sq[:, 0:W].rearrange("p (h e) -> p h e", h=nh),
                                               L.parB[:, norm_off:norm_off + hd].unsqueeze(1).to_broadcast([128, nh, hd]), ALU.mult),
             reads=[sq, L.parB], writes=[sq])
        P.op('dve', lambda e: e.tensor_tensor(out_bf[:, 0:W], sq[:, 0:W], gt[:, 0:W], ALU.mult), reads=[sq, gt], writes=[out_bf])
    else:
        P.op('dve', lambda e: e.tensor_tensor(gt[:, 0:W], gt[:, 0:W], o_sb[:, 0:W], ALU.mult), reads=[gt, o_sb], writes=[gt])
        P.op('act', lambda e: e.activation(out_bf[:, 0:W], gt[:, 0:W], AF.Square, accum_out=sm[:, 0:1]), reads=[gt], writes=[out_bf, sm])
        P.op('act', lambda e: e.activation(sm[:, 16:17], sm[:, 0:1], AF.Ln, bias=EPS, scale=1.0 / W), reads=[sm], writes=[sm])
        P.op('act', lambda e: e.activation(sm[:, 32:33], sm[:, 16:17], AF.Exp, scale=-0.5), reads=[sm], writes=[sm])
        P.op('dve', lambda e: e.scalar_tensor_tensor(out_bf[:, 0:W], gt[:, 0:W], sm[:, 32:33], norm_tt[:, 0:W], ALU.mult, ALU.mult),
             reads=[gt, sm, norm_tt], writes=[out_bf])


def expo_bank(g, B, mask_off, rowT, colT, nheads=4):
    P = g.P
    bank = nextbank(g)
    for h in range(nheads):
        o = bank.t[:, h * 128:(h + 1) * 128]
        esel = g.consts[0:4, C_ESEL + h * 128:C_ESEL + (h + 1) * 128]
        P.op('pe', lambda e: e.matmul(o, g.identb[:, :], g.maskb[:, g.maskb_idx[mask_off], :], start=True, stop=False), reads=[g.identb, g.maskb], writes=[bank], inc=False)
        P.op('pe', lambda e: e.matmul(o, esel, rowT, start=False, stop=False), reads=[g.consts, B.rT], writes=[bank], inc=False)
        P.op('pe', lambda e: e.matmul(o, colT, esel, start=False, stop=True), reads=[g.consts, B.rT], writes=[bank], inc=(h == nheads - 1))
    return bank


class Arena:
    def __init__(self, MS, name, n):
        self.tt = MS.sb(name, [128, n, 512])
        self.name = self.tt.key
        self.n = n

    def keys(self, i0, n):
        return tuple("%s_s%d" % (self.name, i) for i in range(i0, i0 + n))

    def slot(self, i, shape3=None):
        ap = self.tt.t[:, i, :]
        if shape3 is not None:
            ap = ap.rearrange("p (a b) -> p a b", a=shape3[0])
        return TT(ap, self.keys(i, 1))

    def span(self, i0, n, inner):
        ap = self.tt.t[:, i0:i0 + n, :].rearrange("p s (a b) -> p (s a) b", b=inner)
        return TT(ap, self.keys(i0, n))


def delta_mixer(g, L, MS, tiles, hT, is_s):
    P, l = g.P, L.l
    wpart, wo = L.WB, L.WO
    B = Ctx()
    A = Arena(MS, "arA", 5)
    Bc = Arena(MS, "arB", 3)
    G = Arena(MS, "arG", 7 if is_s else 6)
    M = Arena(MS, "arM", 3)
    Lt = Arena(MS, "arL", 3)
    B.cin = TT(A.tt.t[:, :, :].rearrange("p s f -> p (s f)")[:, 0:12 * 176].rearrange("p (c n) -> p c n", c=12), A.keys(0, 5))
    B.cout = Bc.span(0, 3, 128)
    B.acc = G.span(0, 3, 128)
    B.tmp = G.span(3, 3, 128)
    B.cvo = TT(M.tt.t[0:48, :, :].rearrange("p s f -> p (s f)"), M.keys(0, 3))
    B.rn = G.span(4, 2, 128)
    B.E = [M.slot(i, (4, 128)) for i in range(3)]
    B.PT = [G.slot(0, (4, 128)), G.slot(1, (4, 128))]
    B.X = [G.slot(2, (4, 128)), G.slot(3, (4, 128))]
    B.Y = [G.slot(4, (4, 128)), G.slot(5, (4, 128))]
    B.kbg, B.kd, B.QKm, B.bv = [A.slot(i, (4, 128)) for i in range(4)]
    B.nwT, B.vnew, B.qgT = [Bc.slot(i, (4, 128)) for i in range(3)]
    B.gate = Lt.slot(0)
    B.sq = Lt.slot(1)
    B.o = Lt.slot(2)
    B.onT = MS.sb("onT", [128, 8, 128], BF16)
    B.sm = MS.sb("sm", [128, 64])
    B.sm2 = MS.sb("sm2", [128, 64])
    B.rT = MS.sb("rT", [4, 512])
    B.qkn = MS.sb("qkn", [128, 8, 128])
    B.on = MS.sb("on", [128, 512], BF16)
    stc = None
    if is_s:
        stc = TT(Lt.tt.t[0:48, :, :].rearrange("p s f -> p (s f)"), Lt.keys(0, 3))
        P.dma('sp', stc[:, :], g.st_conv[l][:, 0:1536], "stc_d", writes=[stc])
        B.Sh = MS.sb("Sh", [128, 16, 128])
        B.padf = TT(G.tt.t[:, 2:7, :].rearrange("p s f -> p (s f)")[:, 0:17 * 128], G.keys(2, 5))
        B.egts = MS.sb("egts", [128, 16, 4])
    sm = B.sm
    for ti, t in enumerate(tiles):
        tok0 = ti * 128
        if ti == 0:
            g.minfree = min(getattr(g, 'minfree', 1 << 30), g.nc.sbuf_bytes_remaining)
        last_prompt = (not is_s) and (t == g.NPT - 1)
        conv_block(g, L, MS, B, wpart, hT, tok0, is_s, 0, L.tailD, stc, last_prompt, 0)
        cout = B.cout
        M_U01 = C_SU01 if is_s else C_U01
        M_SEG = C_SEG if is_s else C_ONES
        bS = nextbank(g)
        proj_tm(g, wpart, 2048, 8, hT, tok0, bS)
        bZ = nextbank(g)
        proj_tm(g, wpart, 1536, 512, hT, tok0, bZ)
        if ti == len(tiles) - 1:
            prefetch_next(g, L)
        gate_compute(g, B, [bZ], 512)
        P.op('dve', lambda e: e.tensor_tensor(sm[:, 40:44], bS.t[:, 0:4], L.parB[:, P_DN_DTB:P_DN_DTB + 4], ALU.add), reads=[bS, L.parB], writes=[sm])
        P.op('act', lambda e: e.activation(sm[:, 40:44], sm[:, 40:44], AF.Exp), reads=[sm], writes=[sm])
        P.op('act', lambda e: e.activation(sm[:, 44:48], bS.t[:, 4:8], AF.Exp, scale=-1.0), reads=[bS], writes=[sm])
        P.op('act', lambda e: e.activation(sm[:, 0:8], sm[:, 40:48], AF.Ln, bias=1.0), reads=[sm], writes=[sm])
        P.op('dve', lambda e: e.tensor_tensor(sm[:, 0:4], sm[:, 0:4], L.pd[:, 0:4], ALU.mult), reads=[sm, L.pd], writes=[sm])
        P.op('act', lambda e: e.activation(sm[:, 8:12], sm[:, 4:8], AF.Exp, scale=-1.0), reads=[sm], writes=[sm])
        bC = nextbank(g)
        P.op('pe', lambda e: e.matmul(bC.t[:, 0:4], cst(g, M_U01), sm[:, 0:4], start=True, stop=True), reads=[g.consts, sm], writes=[bC], inc=False)
        P.op('pe', lambda e: e.matmul(bC.t[:, 4:8], cst(g, M_SEG), sm[:, 0:4], start=True, stop=True), reads=[g.consts, sm], writes=[bC])
        P.op('dve', lambda e: e.tensor_copy(sm[:, 12:20], bC.t[:, 0:8]), reads=[bC], writes=[sm])
        P.op('dve', lambda e: e.tensor_tensor(sm[:, 20:24], sm[:, 12:16], sm[:, 4:8], ALU.subtract), reads=[sm], writes=[sm])
        P.op('dve', lambda e: e.tensor_scalar(sm[:, 24:28], sm[:, 12:16], -1.0, None, ALU.mult), reads=[sm], writes=[sm])
        P.op('dve', lambda e: e.tensor_tensor(sm[:, 32:36], sm[:, 16:20], sm[:, 12:16], ALU.subtract), reads=[sm], writes=[sm])
        P.op('act', lambda e: e.activation(sm[:, 28:32], sm[:, 12:16], AF.Exp), reads=[sm], writes=[sm])
        P.op('act', lambda e: e.activation(sm[:, 32:36], sm[:, 32:36], AF.Exp), reads=[sm], writes=[sm])
        P.op('dve', lambda e: e.tensor_tensor(sm[:, 36:40], sm[:, 8:12], sm[:, 28:32], ALU.mult), reads=[sm], writes=[sm])

        def bc(c0):
            return sm[:, c0:c0 + 4].unsqueeze(2).to_broadcast([128, 4, 128])
        bT = nextbank(g)
        for qi, c0 in enumerate((12, 20, 24)):
            P.op('pe', lambda e: e.transpose(bT.t[0:4, qi * 128:(qi + 1) * 128], sm[:, c0:c0 + 4], cst(g, C_IDENT)),
                 reads=[sm, g.consts], writes=[bT], inc=(qi == 2))
        P.op('dve', lambda e: e.tensor_copy(B.rT[:, 0:384], bT.t[0:4, 0:384]), reads=[bT], writes=[B.rT])
        gcT, r2T, ngcT = B.rT[:, 0:128], B.rT[:, 128:256], B.rT[:, 256:384]
        mNLS, mNUS, mNUI = (C_SNLS, C_SNUS, C_SNUI) if is_s else (C_NLS, C_NUS, C_NUI)
        for ei, (moff, rowT, colT) in enumerate(((mNLS, ngcT, r2T), (mNUS, r2T, ngcT), (mNUI, gcT, ngcT))):
            bank = expo_bank(g, B, moff, rowT, colT)
            P.op('act', lambda e: e.activation(B.E[ei][:, :, :], bank.t[:, :].rearrange("p (h n) -> p h n", h=4), AF.Exp), reads=[bank], writes=[B.E[ei]])
        conv_taps(g, L, B, is_s, 0)
        P.op('pool', lambda e: e.tensor_tensor(B.rn[:, :, :], cout[:, 0:8, :], cout[:, 0:8, :], ALU.mult), reads=[cout], writes=[B.rn])
        for hh in range(2):
            bank = nextbank(g)
            P.op('pe', lambda e: e.matmul(bank.t[:, :], cst(g, C_ONES), B.rn[:, hh * 4:hh * 4 + 4, :], start=True, stop=True),
                 reads=[g.consts, B.rn], writes=[bank])
            P.op('act', lambda e: e.activation(B.qkn[:, hh * 4:hh * 4 + 4, :], bank.t[:, :].rearrange("p (c n) -> p c n", c=4), AF.Ln, bias=EPS),
                 reads=[bank], writes=[B.qkn])
        P.op('act', lambda e: e.activation(B.rn[:, :, :], B.qkn[:, :, :], AF.Exp, scale=-0.5), reads=[B.qkn, B.rn], writes=[B.rn])
        P.op('dve', lambda e: e.tensor_tensor(B.qkn[:, :, :], cout[:, 0:8, :], B.rn[:, :, :], ALU.mult), reads=[cout, B.rn], writes=[B.qkn])
        bKT = nextbank(g)
        bVT = nextbank(g)
        for h in range(4):
            P.op('pe', lambda e: e.transpose(bKT.t[:, h * 128:(h + 1) * 128], B.qkn[:, 4 + h, :], cst(g, C_IDENT)), reads=[B.qkn, g.consts], writes=[bKT], inc=(h == 3))
        for h in range(4):
            P.op('pe', lambda e: e.transpose(bVT.t[:, h * 128:(h + 1) * 128], cout[:, 8 + h, :], cst(g, C_IDENT)), reads=[cout, g.consts], writes=[bVT], inc=(h == 3))
        kt3 = bKT.t[:, :].rearrange("p (h n) -> p h n", h=4)
        P.op('dve', lambda e: e.tensor_tensor(B.kbg[:, :, :], kt3, bc(36), ALU.mult), reads=[bKT, sm, B.cin], writes=[B.kbg])
        P.op('dve', lambda e: e.tensor_tensor(B.kd[:, :, :], kt3, bc(32), ALU.mult), reads=[bKT, sm], writes=[B.kd])
        P.op('dve', lambda e: e.tensor_tensor(B.bv[:, :, :], bVT.t[:, :].rearrange("p (h n) -> p h n", h=4), bc(8), ALU.mult), reads=[bVT, sm], writes=[B.bv])
        bKK = nextbank(g)
        bQK = nextbank(g)
        for h in range(4):
            P.op('pe', lambda e: e.matmul(bKK.t[:, h * 128:(h + 1) * 128], B.qkn[:, 4 + h, :], B.qkn[:, 4 + h, :], start=True, stop=True),
                 reads=[B.qkn], writes=[bKK], inc=(h == 3))
        for h in range(4):
            P.op('pe', lambda e: e.matmul(bQK.t[:, h * 128:(h + 1) * 128], B.qkn[:, 4 + h, :], B.qkn[:, h, :], start=True, stop=True),
                 reads=[B.qkn], writes=[bQK], inc=(h == 3))
        kk3 = bKK.t[:, :].rearrange("p (h n) -> p h n", h=4)
        P.op('dve', lambda e: e.scalar_tensor_tensor(B.X[0][:, :, :], B.E[0][:, :, :], -1.0, kk3, ALU.mult, ALU.mult), reads=[B.E[0], bKK], writes=[B.X[0]])
        P.op('dve', lambda e: e.scalar_tensor_tensor(B.Y[0][:, :, :], B.E[1][:, :, :], -1.0, kk3, ALU.mult, ALU.mult), reads=[B.E[1], bKK], writes=[B.Y[0]])
        P.op('dve', lambda e: e.scalar_tensor_tensor(B.QKm[:, :, :], B.E[2][:, :, :], 128.0 ** -0.5, bQK.t[:, :].rearrange("p (h n) -> p h n", h=4),
                                                     ALU.mult, ALU.mult), reads=[B.E[2], bQK], writes=[B.QKm])
        P.op('pool', lambda e: e.tensor_tensor(B.PT[0][:, :, :], B.Y[0][:, :, :], cst(g, C_IDENT).unsqueeze(1).to_broadcast([128, 4, 128]), ALU.add),
             reads=[B.Y[0], g.consts], writes=[B.PT[0]])
        nlev = 2 if is_s else 6
        cur = 0
        for k in range(1, nlev + 1):
            nxt = 1 - cur
            bX = nextbank(g)
            for h in range(4):
                P.op('pe', lambda e: e.matmul(bX.t[:, h * 128:(h + 1) * 128], B.Y[cur][:, h, :], B.X[cur][:, h, :], start=True, stop=True),
                     reads=[B.X[cur], B.Y[cur]], writes=[bX], inc=(h == 3))
            if k < nlev:
                bY = nextbank(g)
                for h in range(4):
                    P.op('pe', lambda e: e.matmul(bY.t[:, h * 128:(h + 1) * 128], B.X[cur][:, h, :], B.Y[cur][:, h, :], start=True, stop=True),
                         reads=[B.X[cur], B.Y[cur]], writes=[bY], inc=(h == 3))
            P.op('act', lambda e: e.activation(B.X[nxt][:, :, :], bX.t[:, :].rearrange("p (h n) -> p h n", h=4), AF.Copy), reads=[bX], writes=[B.X[nxt]])
            if k < nlev:
                P.op('dve', lambda e: e.tensor_copy(B.Y[nxt][:, :, :], bY.t[:, :].rearrange("p (h n) -> p h n", h=4)), reads=[bY], writes=[B.Y[nxt]])
            bP = nextbank(g)
            for h in range(4):
                P.op('pe', lambda e: e.matmul(bP.t[:, h * 128:(h + 1) * 128], B.X[nxt][:, h, :], B.PT[cur][:, h, :], start=True, stop=True),
                     reads=[B.X[nxt], B.PT[cur]], writes=[bP], inc=(h == 3))
            P.op('dve', lambda e: e.tensor_tensor(B.PT[nxt][:, :, :], bP.t[:, :].rearrange("p (h n) -> p h n", h=4), B.PT[cur][:, :, :], ALU.add),
                 reads=[bP, B.PT[cur]], writes=[B.PT[nxt]])
            cur = nxt
        TTm = B.PT[cur]
        bW = nextbank(g)
        for h in range(4):
            P.op('pe', lambda e: e.matmul(bW.t[:, h * 128:(h + 1) * 128], B.kbg[:, h, :], TTm[:, h, :], start=True, stop=True),
                 reads=[B.kbg, TTm], writes=[bW], inc=(h == 3))
        P.op('act', lambda e: e.activation(B.nwT[:, :, :], bW.t[:, :].rearrange("p (h n) -> p h n", h=4), AF.Copy, scale=-1.0), reads=[bW, cout], writes=[B.nwT])
        bG = nextbank(g)
        for h in range(4):
            esel = g.consts[0:4, C_ESEL + h * 128:C_ESEL + (h + 1) * 128]
            P.op('pe', lambda e: e.matmul(bG.t[:, h * 128:(h + 1) * 128], esel, gcT, start=True, stop=True), reads=[g.consts, B.rT], writes=[bG], inc=(h == 3))
        P.op('act', lambda e: e.activation(B.qgT[:, :, :], bG.t[:, :].rearrange("p (h n) -> p h n", h=4), AF.Exp), reads=[bG], writes=[B.qgT])
        P.op('dve', lambda e: e.scalar_tensor_tensor(B.qgT[:, :, :], B.qkn[:, 0:4, :], 128.0 ** -0.5, B.qgT[:, :, :], ALU.mult, ALU.mult),
             reads=[B.qkn, B.qgT], writes=[B.qgT])
        if not is_s:
            P.op('act', lambda e: e.activation(sm[:, 48:52], sm[:, 16:20], AF.Exp), reads=[sm], writes=[sm])
        else:
            gtv = B.sm2[:, 0:64].rearrange("p (s h) -> p s h", h=4)
            P.op('dve', lambda e: e.tensor_tensor(gtv, sm[:, 16:20].unsqueeze(1).to_broadcast([128, 16, 4]),
                                                  g.consts[:, C_ROWM8:C_ROWM8 + 16].unsqueeze(2).to_broadcast([128, 16, 4]), ALU.mult),
                 reads=[sm, g.consts], writes=[B.sm2])
            bE = nextbank(g)
            P.op('pe', lambda e: e.matmul(bE.t[:, 0:64], cst(g, C_ONES), B.sm2[:, 0:64], start=True, stop=True), reads=[g.consts, B.sm2], writes=[bE])
            P.op('act', lambda e: e.activation(B.egts[:, :, :], bE.t[:, 0:64].rearrange("p (s h) -> p s h", h=4), AF.Exp), reads=[bE], writes=[B.egts])
        bVN = nextbank(g, hold=True)
        bO = nextbank(g, hold=True)
        hgroups = [[0, 1, 2, 3]] if not is_s else [[0], [1], [2], [3]]
        padv = None
        if is_s:
            padd = B.padf.t[:, 0:16 * 136].rearrange("p (s q) -> p s q", q=136)[:, :, 0:8]
            padr = B.padf.t[:, 0:2048].rearrange("p (s n) -> p s n", s=16)
        for hs in hgroups:
            h0, nh = hs[0], len(hs)
            if is_s:
                h = h0
                P.dma('sp', B.Sh[:, :, :], g.st_delta[l][:, h, :, :].rearrange("s d e -> d s e"), "Sh_d", writes=[B.Sh])

                def padcol(src):
                    P.op('pool', lambda e: e.memset(B.padf[:, :], 0.0), writes=[B.padf])
                    P.op('pool', lambda e: e.tensor_copy(padd, src.rearrange("p (s r) -> p s r", r=8)), reads=[B.nwT, B.qgT, B.padf], writes=[B.padf])
            for h in hs:
                o = bVN.t[:, h * 128:(h + 1) * 128]
                P.op('pe', lambda e: e.matmul(o, TTm[:, h, :], B.bv[:, h, :], start=True, stop=False), reads=[TTm, B.bv], writes=[bVN], inc=False)
                if not is_s:
                    P.op('pe', lambda e: e.matmul(o, B.nwT[:, h, :], L.Sd[:, h, :], start=False, stop=True), reads=[B.nwT, L.Sd], writes=[bVN], inc=(h == hs[-1]))
                else:
                    padcol(B.nwT[:, h, :])
                    for s_ in range(16):
                        P.op('pe', lambda e: e.matmul(o, B.padf[:, s_ * 128:(s_ + 1) * 128], B.Sh[:, s_, :], start=False, stop=(s_ == 15)),
                             reads=[B.padf, B.Sh], writes=[bVN], inc=(s_ == 15))
            P.op('act', lambda e: e.activation(B.vnew[:, h0:h0 + nh, :], bVN.t[:, h0 * 128:(h0 + nh) * 128].rearrange("p (h n) -> p h n", h=nh), AF.Copy),
                 reads=[bVN], writes=[B.vnew])
            for h in hs:
                o = bO.t[:, h * 128:(h + 1) * 128]
                P.op('pe', lambda e: e.matmul(o, B.QKm[:, h, :], B.vnew[:, h, :], start=True, stop=False), reads=[B.QKm, B.vnew], writes=[bO], inc=False)
                if not is_s:
                    P.op('pe', lambda e: e.matmul(o, B.qgT[:, h, :], L.Sd[:, h, :], start=False, stop=True), reads=[B.qgT, L.Sd], writes=[bO], inc=(h == hs[-1]))
                else:
                    padcol(B.qgT[:, h, :])
                    for s_ in range(16):
                        P.op('pe', lambda e: e.matmul(o, B.padf[:, s_ * 128:(s_ + 1) * 128], B.Sh[:, s_, :], start=False, stop=(s_ == 15)),
                             reads=[B.padf, B.Sh], writes=[bO], inc=(s_ == 15))
            if not is_s:
                bSu = nextbank(g)
                for h in hs:
                    P.op('pe', lambda e: e.matmul(bSu.t[:, h * 128:(h + 1) * 128], B.kd[:, h, :], B.vnew[:, h, :], start=True, stop=True),
                         reads=[B.kd, B.vnew], writes=[bSu], inc=(h == 3))
                P.op('dve', lambda e: e.tensor_tensor(L.Sd[:, :, :], L.Sd[:, :, :], sm[:, 48:52].unsqueeze(2).to_broadcast([128, 4, 128]), ALU.mult),
                     reads=[L.Sd, sm], writes=[L.Sd])
                P.op('dve', lambda e: e.tensor_tensor(L.Sd[:, :, :], L.Sd[:, :, :], bSu.t[:, :].rearrange("p (h n) -> p h n", h=4), ALU.add),
                     reads=[L.Sd, bSu], writes=[L.Sd])
                if last_prompt:
                    P.dma('sp', g.o_delta_p[l].rearrange("h d e -> d h e"), L.Sd[:, :, :], L.Sd.key + "_o", reads=[L.Sd])
            else:
                h = h0
                P.op('pool', lambda e: e.tensor_tensor(padr, B.kd[:, h, :].unsqueeze(1).to_broadcast([128, 16, 128]),
                                                       g.consts[:, C_ROWM:C_ROWM + 16].unsqueeze(2).to_broadcast([128, 16, 128]), ALU.mult),
                     reads=[B.kd, g.consts], writes=[B.padf])
                for s4 in range(4):
                    bSu = nextbank(g)
                    for si in range(4):
                        s_ = s4 * 4 + si
                        P.op('pe', lambda e: e.matmul(bSu.t[:, si * 128:(si + 1) * 128], B.padf[:, s_ * 128:(s_ + 1) * 128], B.vnew[:, h, :], start=True, stop=True),
                             reads=[B.padf, B.vnew], writes=[bSu], inc=(si == 3))
                    sl = B.Sh[:, s4 * 4:s4 * 4 + 4, :]
                    P.op('dve', lambda e: e.tensor_tensor(sl, sl, B.egts[:, s4 * 4:s4 * 4 + 4, h:h + 1].to_broadcast([128, 4, 128]), ALU.mult),
                         reads=[B.Sh, B.egts], writes=[B.Sh])
                    P.op('dve', lambda e: e.tensor_tensor(sl, sl, bSu.t[:, :].rearrange("p (s n) -> p s n", s=4), ALU.add),
                         reads=[B.Sh, bSu], writes=[B.Sh])
                P.dma('sp', g.o_delta_s[l][:, h, :, :].rearrange("s d e -> d s e"), B.Sh[:, :, :], "Sh_o", reads=[B.Sh])
        P.op('act', lambda e: e.activation(B.o[:, :], bO.t[:, :], AF.Copy), reads=[bO], writes=[B.o])
        release(g, bVN, bO)
        gated_norm(g, L, B, B.o, 4, 128, P_DN_NORM, True, B.on)
        out_proj(g, MS, B, B.on, 4, wo, t)


def gla_mixer(g, L, MS, tiles, hT, is_s):
    import os
    DBG = int(os.environ.get('DBG_GLA', '99'))
    P, l = g.P, L.l
    wpart, wo = L.WB, L.WO
    B = Ctx()
    B.v = MS.sb("v", [128, 512])
    B.gk = MS.sb("gk", [128, 256])
    B.gc = MS.sb("gc", [128, 256])
    B.Eq = MS.sb("Eq", [128, 2, 128])
    B.Ek = MS.sb("Ek", [128, 2, 128])
    B.qt = MS.sb("qt", [128, 2, 128])
    B.kt = MS.sb("kt", [128, 2, 128])
    B.qz = MS.sb("qz", [128, 4, 128])
    P.op('pool', lambda e: e.memset(B.qz[:, :, :], 0.0), writes=[B.qz])
    B.PTm = MS.sb("PTm", [128, 4, 128])
    B.ktm = MS.sb("ktm", [128, 256])
    B.glT = MS.sb("glT", [16, 128])
    B.o = MS.sb("o", [128, 512])
    B.gate = MS.sb("gate", [128, 512])
    B.sq = MS.sb("sq", [128, 512])
    B.on = MS.sb("on", [128, 512], BF16)
    B.onT = MS.sb("onT", [128, 8, 128], BF16)
    B.sm2 = MS.sb("sm2", [128, 64])
    if is_s:
        B.Sh = MS.sb("Sh", [128, 16, 128])
        B.padf = MS.sb("pad", [128, 17 * 128])
        padd = B.padf.t[:, 0:16 * 136].rearrange("p (s q) -> p s q", q=136)[:, :, 0:8]
        padr = B.padf.t[:, 0:2048].rearrange("p (s n) -> p s n", s=16)
    M_U01 = C_SU01 if is_s else C_U01
    for ti, t in enumerate(tiles):
        tok0 = ti * 128
        last_prompt = (not is_s) and (t == g.NPT - 1)
        bFM = nextbank(g)
        for c in range(4):
            proj_fm(g, wpart, c * 128, 128, hT, tok0, bFM, c)
        bGL = nextbank(g)
        proj_fm(g, wpart, 1024, 16, hT, tok0, bGL, 0)
        P.op('dve', lambda e: e.tensor_copy(B.glT[:, :], bGL.t[0:16, 0:128]), reads=[bGL], writes=[B.glT])
        bV = nextbank(g)
        proj_tm(g, wpart, 512, 512, hT, tok0, bV)
        P.op('act', lambda e: e.activation(B.v[:, :], bV.t[:, :], AF.Copy), reads=[bV], writes=[B.v])
        bZ = nextbank(g)
        proj_tm(g, wpart, 1040, 512, hT, tok0, bZ)
        if ti == len(tiles) - 1:
            prefetch_next(g, L)
        gate_compute(g, B, [bZ], 512)
        if DBG <= 1:
            g.held.clear()
            continue
        bK = nextbank(g)
        P.op('pe', lambda e: e.matmul(bK.t[:, 0:256], B.glT[:, :], L.wup[:, :], start=True, stop=False), reads=[B.glT, L.wup], writes=[bK], inc=False)
        P.op('pe', lambda e: e.matmul(bK.t[:, 0:256], g.consts[0:1, C_ONES:C_ONES + 128], L.bup[:, :], start=False, stop=True),
             reads=[g.consts, L.bup], writes=[bK])
        P.op('act', lambda e: e.activation(B.gk[:, :], bK.t[:, 0:256], AF.Exp, scale=-1.0), reads=[bK], writes=[B.gk])
        P.op('act', lambda e: e.activation(B.gk[:, :], B.gk[:, :], AF.Ln, bias=1.0), reads=[B.gk], writes=[B.gk])
        P.op('dve', lambda e: e.tensor_scalar(B.gk[:, :], B.gk[:, :], -1.0 / 16.0, None, ALU.mult), reads=[B.gk], writes=[B.gk])
        if DBG <= 2:
            g.held.clear()
            continue
        bC = nextbank(g)
        P.op('pe', lambda e: e.matmul(bC.t[:, 0:256], cst(g, M_U01), B.gk[:, :], start=True, stop=True), reads=[g.consts, B.gk], writes=[bC])
        P.op('dve', lambda e: e.tensor_copy(B.gc[:, :], bC.t[:, 0:256]), reads=[bC], writes=[B.gc])
        bT = nextbank(g)
        for c in range(2):
            P.op('pe', lambda e: e.transpose(bT.t[:, c * 128:(c + 1) * 128], B.gc[:, c * 128:(c + 1) * 128], cst(g, C_IDENT)),
                 reads=[B.gc, g.consts], writes=[bT], inc=(c == 1))
        gT3 = bT.t[:, 0:256].rearrange("p (c n) -> p c n", c=2)
        P.op('act', lambda e: e.activation(B.Eq[:, :, :], gT3, AF.Exp), reads=[bT], writes=[B.Eq])
        P.op('act', lambda e: e.activation(B.Ek[:, :, :], gT3, AF.Exp, scale=-1.0), reads=[bT], writes=[B.Ek])
        fm3 = bFM.t[:, :].rearrange("p (c n) -> p c n", c=4)
        P.op('dve', lambda e: e.scalar_tensor_tensor(B.qt[:, :, :], fm3[:, 0:2, :], 64.0 ** -0.5, B.Eq[:, :, :], ALU.mult, ALU.mult),
             reads=[bFM, B.Eq], writes=[B.qt])
        P.op('dve', lambda e: e.tensor_tensor(B.kt[:, :, :], fm3[:, 2:4, :], B.Ek[:, :, :], ALU.mult), reads=[bFM, B.Ek], writes=[B.kt])
        if DBG <= 3:
            g.held.clear()
            continue
        qz4 = B.qz.t[:, :, :].rearrange("p (hh hl) n -> p hl hh n", hl=2)
        P.op('dve', lambda e: e.tensor_copy(qz4[0:64, 0, :, :], B.qt[0:64, :, :]), reads=[B.qt], writes=[B.qz])
        P.op('dve', lambda e: e.tensor_copy(qz4[64:128, 1, :, :], B.qt[64:128, :, :]), reads=[B.qt], writes=[B.qz])
        bSc = nextbank(g)
        for h in range(4):
            hl, hh = h % 2, h // 2
            P.op('pe', lambda e: e.matmul(bSc.t[:, h * 128:(h + 1) * 128], B.kt[:, hh, :], B.qz[:, h, :], start=True, stop=True),
                 reads=[B.kt, B.qz], writes=[bSc], inc=(h == 3))
        P.op('dve', lambda e: e.tensor_tensor(B.PTm[:, :, :], bSc.t[:, :].rearrange("p (h n) -> p h n", h=4),
                                              cst(g, M_U01).unsqueeze(1).to_broadcast([128, 4, 128]), ALU.mult), reads=[bSc, g.consts], writes=[B.PTm])
        bKT = nextbank(g)
        for c in range(2):
            P.op('pe', lambda e: e.transpose(bKT.t[:, c * 128:(c + 1) * 128], B.kt[:, c, :], cst(g, C_IDENT)), reads=[B.kt, g.consts], writes=[bKT], inc=(c == 1))
        P.op('act', lambda e: e.activation(B.ktm[:, :], bKT.t[:, 0:256], AF.Copy), reads=[bKT], writes=[B.ktm])
        if DBG <= 4:
            g.held.clear()
            continue
        bO = nextbank(g, hold=True)
        for hh in range(2):
            if is_s:
                P.dma('sp', B.Sh[:, :, :], g.st_gla[l][:, 2 * hh:2 * hh + 2, :, :].rearrange("s hl d e -> (hl d) s e"), "Shg_d", writes=[B.Sh])
            for hl in range(2):
                h = 2 * hh + hl
                r = slice(hl * 64, hl * 64 + 64)
                o = bO.t[:, h * 128:(h + 1) * 128]
                P.op('pe', lambda e: e.matmul(o, B.PTm[:, h, :], B.v[:, h * 128:(h + 1) * 128], start=True, stop=False), reads=[B.PTm, B.v], writes=[bO], inc=False)
                if not is_s:
                    P.op('pe', lambda e: e.matmul(o, B.qz[:, h, :], L.Sg[:, hh, :], start=False, stop=True), reads=[B.qz, L.Sg], writes=[bO], inc=True)
                else:
                    P.op('pool', lambda e: e.memset(B.padf[:, :], 0.0), writes=[B.padf])
                    P.op('pool', lambda e: e.tensor_copy(padd, B.qz[:, h, :].rearrange("p (s r) -> p s r", r=8)), reads=[B.qz, B.padf], writes=[B.padf])
                    for s_ in range(16):
                        P.op('pe', lambda e: e.matmul(o, B.padf[:, s_ * 128:(s_ + 1) * 128], B.Sh[:, s_, :], start=False, stop=(s_ == 15)),
                             reads=[B.padf, B.Sh], writes=[bO], inc=(s_ == 15))
            if not is_s:
                bSu = nextbank(g)
                P.op('pe', lambda e: e.matmul(bSu.t[:, 0:256], B.ktm[:, hh * 128:(hh + 1) * 128], B.v[:, hh * 256:(hh + 1) * 256], start=True, stop=True),
                     reads=[B.ktm, B.v], writes=[bSu])
                for hl in range(2):
                    r = slice(hl * 64, hl * 64 + 64)
                    P.op('dve', lambda e: e.tensor_tensor(L.Sg[r, hh, :], L.Sg[r, hh, :], bSu.t[r, hl * 128:(hl + 1) * 128], ALU.add), reads=[L.Sg, bSu], writes=[L.Sg])
                    P.op('dve', lambda e: e.tensor_scalar(L.Sg[r, hh, :], L.Sg[r, hh, :], B.Eq[r, hh, 127:128], None, ALU.mult), reads=[L.Sg, B.Eq], writes=[L.Sg])
            else:
                P.op('pool', lambda e: e.tensor_tensor(padr, B.ktm[:, hh * 128:(hh + 1) * 128].unsqueeze(1).to_broadcast([128, 16, 128]),
                                                       g.consts[:, C_ROWM:C_ROWM + 16].unsqueeze(2).to_broadcast([128, 16, 128]), ALU.mult),
                     reads=[B.ktm, g.consts], writes=[B.padf])
                for s2 in range(8):
                    bSu = nextbank(g)
                    for si in range(2):
                        s_ = s2 * 2 + si
                        P.op('pe', lambda e: e.matmul(bSu.t[:, si * 256:(si + 1) * 256], B.padf[:, s_ * 128:(s_ + 1) * 128], B.v[:, hh * 256:(hh + 1) * 256],
                                                      start=True, stop=True), reads=[B.padf, B.v], writes=[bSu], inc=(si == 1))
                    for hl in range(2):
                        r = slice(hl * 64, hl * 64 + 64)
                        sl = B.Sh[r, 2 * s2:2 * s2 + 2, :]
                        ps_ = bSu.t[r, :].rearrange("p (s c) -> p s c", s=2)[:, :, hl * 128:(hl + 1) * 128]
                        P.op('dve', lambda e: e.tensor_tensor(sl, sl, ps_, ALU.add), reads=[B.Sh, bSu], writes=[B.Sh])
                        egl = B.Eq[r, hh, :].rearrange("p (s r) -> p s r", r=8)[:, 2 * s2:2 * s2 + 2, 7:8].to_broadcast([64, 2, 128])
                        P.op('dve', lambda e: e.tensor_tensor(sl, sl, egl, ALU.mult), reads=[B.Sh, B.Eq], writes=[B.Sh])
                P.dma('sp', g.o_gla_s[l][:, 2 * hh:2 * hh + 2, :, :].rearrange("s hl d e -> (hl d) s e"), B.Sh[:, :, :], "Shg_o", reads=[B.Sh])
        if last_prompt:
            P.dma('sp', g.o_gla_p[l].rearrange("(hh hl) d e -> (hl d) hh e", hl=2), L.Sg[:, :, :], "Sg_o", reads=[L.Sg])
        if DBG <= 5:
            g.held.clear()
            continue
        P.op('act', lambda e: e.activation(B.o[:, :], bO.t[:, :], AF.Copy), reads=[bO], writes=[B.o])
        release(g, bO)
        gated_norm(g, L, B, B.o, 4, 128, P_GLA_NORM, True, B.on)
        out_proj(g, MS, B, B.on, 4, wo, t)


def ssd_mixer(g, L, MS, tiles, hT, is_s):
    P, l = g.P, L.l
    wpart, wo = L.WB, L.WO
    B = Ctx()
    A = Arena(MS, "sA", 5)
    Bc = Arena(MS, "sB", 3)
    G = Arena(MS, "sG", 6)
    Lt = Arena(MS, "sL", 4)
    B.cin = TT(A.tt.t[:, :, :].rearrange("p s f -> p (s f)")[:, 0:12 * 176].rearrange("p (c n) -> p c n", c=12), A.keys(0, 5))
    B.cout = Bc.span(0, 3, 128)
    B.acc = G.span(0, 3, 128)
    B.tmp = G.span(3, 3, 128)
    B.cvo = TT(Lt.tt.t[0:48, 0:3, :].rearrange("p s f -> p (s f)"), Lt.keys(0, 3))
    x_tm = TT(A.tt.t[:, 0:2, :].rearrange("p s f -> p (s f)"), A.keys(0, 2))
    xdt = TT(A.tt.t[:, 2:4, :].rearrange("p s f -> p (s f)"), A.keys(2, 2))
    xw = TT(Bc.tt.t[:, 0:2, :].rearrange("p s f -> p (s f)"), Bc.keys(0, 2))
    ysb = TT(G.tt.t[:, 0:2, :].rearrange("p s f -> p (s f)"), G.keys(0, 2))
    MT = G.slot(2, (4, 128))
    ET = G.slot(3, (4, 128))
    cbT = TT(G.tt.t[:, 4, 0:256].rearrange("p (c n) -> p c n", c=2), G.keys(4, 1))
    Btm = TT(G.tt.t[:, 4, 256:512], G.keys(4, 1))
    B.gate = TT(Lt.tt.t[:, 0:2, :].rearrange("p s f -> p (s f)"), Lt.keys(0, 2))
    gBm = TT(Lt.tt.t[:, 2:4, :].rearrange("p s f -> p (s f)"), Lt.keys(2, 2))
    B.on = MS.sb("on", [128, 1024], BF16)
    B.onT = MS.sb("onT", [128, 8, 128], BF16)
    B.sm2 = MS.sb("sm2", [128, 64])
    B.rT = MS.sb("rT", [4, 512])
    sm = MS.sb("sm", [128, 160])
    stc = None
    if is_s:
        stc = TT(G.tt.t[0:48, 0:3, :].rearrange("p s f -> p (s f)"), G.keys(0, 3))
        P.dma('sp', stc[:, :], g.st_conv[l][:, 1536:3072], "stcs_d", writes=[stc])
        B.Sh = [MS.sb("Sh0", [128, 8, 64]), MS.sb("Sh1", [128, 8, 64])]
        B.padf = MS.sb("pad", [128, 17 * 128])
        padd = B.padf.t[:, 0:16 * 136].rearrange("p (s q) -> p s q", q=136)[:, :, 0:8]
        padr = B.padf.t[:, 0:2048].rearrange("p (s n) -> p s n", s=16)
        B.egts = MS.sb("egts", [128, 16, 16])
    M_U01 = C_SU01 if is_s else C_U01
    M_SEG = C_SEG if is_s else C_ONES
    mNUI = C_SNUI if is_s else C_NUI
    nsh = 0
    for ti, t in enumerate(tiles):
        tok0 = ti * 128
        last_prompt = (not is_s) and (t == g.NPT - 1)
        if ti == 0:
            g.minfree = min(getattr(g, 'minfree', 1 << 30), g.nc.sbuf_bytes_remaining)
        conv_block(g, L, MS, B, wpart, hT, tok0, is_s, 12, L.tailS, stc, last_prompt, 1536)
        cout = B.cout
        bD = nextbank(g)
        proj_tm(g, wpart, 2560, 16, hT, tok0, bD)
        bZ = [nextbank(g), nextbank(g)]
        proj_tm(g, wpart, 1536, 512, hT, tok0, bZ[0])
        proj_tm(g, wpart, 2048, 512, hT, tok0, bZ[1])
        if ti == len(tiles) - 1:
            prefetch_next(g, L)
        gate_compute(g, B, bZ, 1024)
        P.op('dve', lambda e: e.tensor_tensor(sm[:, 128:144], bD.t[:, 0:16], L.parB[:, P_M2_DTB:P_M2_DTB + 16], ALU.add), reads=[bD, L.parB], writes=[sm])
        P.op('act', lambda e: e.activation(sm[:, 128:144], sm[:, 128:144], AF.Exp), reads=[sm], writes=[sm])
        P.op('act', lambda e: e.activation(sm[:, 0:16], sm[:, 128:144], AF.Ln, bias=1.0), reads=[sm], writes=[sm])
        P.op('dve', lambda e: e.tensor_tensor(sm[:, 16:32], sm[:, 0:16], L.pd[:, 4:20], ALU.mult), reads=[sm, L.pd], writes=[sm])
        bC = nextbank(g)
        P.op('pe', lambda e: e.matmul(bC.t[:, 0:16], cst(g, M_U01), sm[:, 16:32], start=True, stop=True), reads=[g.consts, sm], writes=[bC], inc=False)
        P.op('pe', lambda e: e.matmul(bC.t[:, 16:32], cst(g, M_SEG), sm[:, 16:32], start=True, stop=True), reads=[g.consts, sm], writes=[bC])
        P.op('dve', lambda e: e.tensor_copy(sm[:, 32:64], bC.t[:, 0:32]), reads=[bC], writes=[sm])
        P.op('dve', lambda e: e.tensor_scalar(sm[:, 64:80], sm[:, 32:48], -1.0, None, ALU.mult), reads=[sm], writes=[sm])
        P.op('act', lambda e: e.activation(sm[:, 80:96], sm[:, 32:48], AF.Exp), reads=[sm], writes=[sm])
        P.op('dve', lambda e: e.tensor_tensor(sm[:, 96:112], sm[:, 48:64], sm[:, 32:48], ALU.subtract), reads=[sm], writes=[sm])
        P.op('act', lambda e: e.activation(sm[:, 96:112], sm[:, 96:112], AF.Exp), reads=[sm], writes=[sm])
        P.op('act', lambda e: e.activation(sm[:, 112:128], sm[:, 48:64], AF.Exp), reads=[sm], writes=[sm])

        def bc16(c0, g_=None):
            if g_ is None:
                return sm[:, c0:c0 + 16].unsqueeze(2).to_broadcast([128, 16, 64])
            return sm[:, c0 + 8 * g_:c0 + 8 * g_ + 8].unsqueeze(2).to_broadcast([128, 8, 64])
        conv_taps(g, L, B, is_s, 12)
        for hb in range(2):
            bX = nextbank(g)
            for c in range(4):
                P.op('pe', lambda e: e.transpose(bX.t[:, c * 128:(c + 1) * 128], cout[:, hb * 4 + c, :], cst(g, C_IDENT)), reads=[cout, g.consts], writes=[bX], inc=(c == 3))
            P.op('act', lambda e: e.activation(x_tm[:, hb * 512:(hb + 1) * 512], bX.t[:, :], AF.Copy), reads=[bX, B.cin], writes=[x_tm])
        x3 = x_tm.t.rearrange("p (h e) -> p h e", h=16)
        xdt3 = xdt.t.rearrange("p (h e) -> p h e", h=16)
        xw3 = xw.t.rearrange("p (h e) -> p h e", h=16)
        P.op('dve', lambda e: e.tensor_tensor(xdt3, x3, bc16(0), ALU.mult), reads=[x_tm, sm], writes=[xdt])
        P.op('pool', lambda e: e.tensor_tensor(xw3, xdt3, bc16(96), ALU.mult), reads=[xdt, sm, cout], writes=[xw])
        bB = nextbank(g)
        for g_ in range(2):
            P.op('pe', lambda e: e.transpose(bB.t[:, g_ * 128:(g_ + 1) * 128], cout[:, 8 + g_, :], cst(g, C_IDENT)), reads=[cout, g.consts], writes=[bB], inc=False)
        for g_ in range(2):
            P.op('pe', lambda e: e.matmul(bB.t[:, 256 + g_ * 128:256 + (g_ + 1) * 128], cout[:, 8 + g_, :], cout[:, 10 + g_, :], start=True, stop=True),
                 reads=[cout], writes=[bB], inc=(g_ == 1))
        P.op('act', lambda e: e.activation(Btm[:, :], bB.t[:, 0:256], AF.Copy), reads=[bB, B.acc, B.tmp], writes=[Btm])
        P.op('act', lambda e: e.activation(cbT[:, :, :], bB.t[:, 256:512].rearrange("p (c n) -> p c n", c=2), AF.Copy), reads=[bB], writes=[cbT])
        bY = [nextbank(g, hold=True), nextbank(g, hold=True)]
        for q in range(4):
            bT = nextbank(g)
            P.op('pe', lambda e: e.transpose(bT.t[0:4, 0:128], sm[:, 32 + 4 * q:36 + 4 * q], cst(g, C_IDENT)), reads=[sm, g.consts], writes=[bT], inc=False)
            P.op('pe', lambda e: e.transpose(bT.t[0:4, 128:256], sm[:, 64 + 4 * q:68 + 4 * q], cst(g, C_IDENT)), reads=[sm, g.consts], writes=[bT])
            P.op('dve', lambda e: e.tensor_copy(B.rT[:, 0:256], bT.t[0:4, 0:256]), reads=[bT], writes=[B.rT])
            bank = expo_bank(g, B, mNUI, B.rT[:, 0:128], B.rT[:, 128:256])
            P.op('act', lambda e: e.activation(ET[:, :, :], bank.t[:, :].rearrange("p (h n) -> p h n", h=4), AF.Exp), reads=[bank], writes=[ET])
            g_ = q // 2
            P.op('dve', lambda e: e.tensor_tensor(MT[:, :, :], ET[:, :, :], cbT[:, g_, :].unsqueeze(1).to_broadcast([128, 4, 128]), ALU.mult),
                 reads=[ET, cbT], writes=[MT])
            for hq in range(4):
                h = 4 * q + hq
                P.op('pe', lambda e: e.matmul(bY[g_].t[:, (h % 8) * 64:(h % 8) * 64 + 64], MT[:, hq, :], xdt[:, h * 64:(h + 1) * 64], start=True, stop=True),
                     reads=[MT, xdt], writes=[bY[g_]], inc=(hq == 3))
        bYI = [nextbank(g, hold=True), nextbank(g, hold=True)]
        if not is_s:
            for g_ in range(2):
                P.op('pe', lambda e: e.matmul(bYI[g_].t[:, :], cout[:, 10 + g_, :], L.Ss.t[:, :, :].rearrange("p h e -> p (h e)")[:, g_ * 512:(g_ + 1) * 512], start=True, stop=True),
                     reads=[cout, L.Ss], writes=[bYI[g_]])
        else:
            lt3 = B.padf.t[:, 0:256].rearrange("p (s h) -> p s h", h=16)
            P.op('dve', lambda e: e.tensor_tensor(lt3, sm[:, 48:64].unsqueeze(1).to_broadcast([128, 16, 16]),
                                                  g.consts[:, C_ROWM8:C_ROWM8 + 16].unsqueeze(2).to_broadcast([128, 16, 16]), ALU.mult),
                 reads=[sm, g.consts], writes=[B.padf])
            bE = nextbank(g)
            P.op('pe', lambda e: e.matmul(bE.t[:, 0:256], cst(g, C_ONES), B.padf[:, 0:256], start=True, stop=True), reads=[g.consts, B.padf], writes=[bE])
            P.op('act', lambda e: e.activation(B.egts[:, :, :], bE.t[:, 0:256].rearrange("p (s h) -> p s h", h=16), AF.Exp), reads=[bE], writes=[B.egts])
            for g_ in range(2):
                P.op('pool', lambda e: e.memset(B.padf[:, :], 0.0), writes=[B.padf])
                P.op('pool', lambda e: e.tensor_copy(padd, cout[:, 10 + g_, :].rearrange("p (s r) -> p s r", r=8)), reads=[cout, B.padf], writes=[B.padf])
                for s_ in range(16):
                    Sh = B.Sh[nsh % 2]
                    nsh += 1
                    P.dma('sp', Sh[:, :, :], g.st_ssm[l][s_, 8 * g_:8 * g_ + 8, :, :].rearrange("h n p -> n h p"), "Shs%d_d" % ((nsh - 1) % 2), writes=[Sh])
                    P.op('pe', lambda e: e.matmul(bYI[g_].t[:, :], B.padf[:, s_ * 128:(s_ + 1) * 128], Sh.t[:, :, :].rearrange("p h e -> p (h e)"), start=(s_ == 0), stop=(s_ == 15)),
                         reads=[B.padf, Sh], writes=[bYI[g_]], inc=True)
        y3 = ysb.t.rearrange("p (h e) -> p h e", h=16)
        for g_ in range(2):
            P.op('dve', lambda e: e.tensor_tensor(y3[:, 8 * g_:8 * g_ + 8, :], bYI[g_].t[:, :].rearrange("p (h e) -> p h e", h=8), bc16(80, g_), ALU.mult),
                 reads=[bYI[g_], sm], writes=[ysb])
            P.op('dve', lambda e: e.tensor_tensor(ysb[:, g_ * 512:(g_ + 1) * 512], ysb[:, g_ * 512:(g_ + 1) * 512], bY[g_].t[:, :], ALU.add),
                 reads=[ysb, bY[g_]], writes=[ysb])
        release(g, bY[0], bY[1], bYI[0], bYI[1])
        P.op('pool', lambda e: e.tensor_tensor(xdt3, x3, L.parB[:, P_M2_D:P_M2_D + 16].unsqueeze(2).to_broadcast([128, 16, 64]), ALU.mult),
             reads=[x_tm, L.parB, xdt], writes=[xdt])
        P.op('dve', lambda e: e.tensor_tensor(ysb[:, :], ysb[:, :], xdt[:, :], ALU.add), reads=[ysb, xdt], writes=[ysb])
        P.dma('sp', gBm[:, :], g.norms_d[3 * g.DEPTH + 1 + l:3 * g.DEPTH + 2 + l, :].partition_broadcast(128), "gBm_d", reads=[B.cvo], writes=[gBm])
        gated_norm(g, L, B, ysb, 16, 64, 0, False, B.on, norm_tt=gBm)
        out_proj(g, MS, B, B.on, 8, wo, t)
        if not is_s:
            for g_ in range(2):
                bSu = nextbank(g)
                P.op('pe', lambda e: e.matmul(bSu.t[:, :], Btm[:, g_ * 128:(g_ + 1) * 128], xw[:, g_ * 512:(g_ + 1) * 512], start=True, stop=True),
                     reads=[Btm, xw], writes=[bSu])
                sl = L.Ss[:, 8 * g_:8 * g_ + 8, :]
                P.op('dve', lambda e: e.tensor_tensor(sl, sl, bc16(112, g_), ALU.mult), reads=[L.Ss, sm], writes=[L.Ss])
                P.op('dve', lambda e: e.tensor_tensor(sl, sl, bSu.t[:, :].rearrange("p (h e) -> p h e", h=8), ALU.add), reads=[L.Ss, bSu], writes=[L.Ss])
            if last_prompt:
                P.dma('sp', g.o_ssm_p[l].rearrange("h n p -> n h p"), L.Ss[:, :, :], "Ss_o", reads=[L.Ss])
        else:
            for g_ in range(2):
                P.op('pool', lambda e: e.tensor_tensor(padr, Btm[:, g_ * 128:(g_ + 1) * 128].unsqueeze(1).to_broadcast([128, 16, 128]),
                                                       g.consts[:, C_ROWM:C_ROWM + 16].unsqueeze(2).to_broadcast([128, 16, 128]), ALU.mult),
                     reads=[Btm, g.consts], writes=[B.padf])
                for s_ in range(16):
                    Sh = B.Sh[nsh % 2]
                    nsh += 1
                    P.dma('sp', Sh[:, :, :], g.st_ssm[l][s_, 8 * g_:8 * g_ + 8, :, :].rearrange("h n p -> n h p"), "Shs%d_d" % ((nsh - 1) % 2), writes=[Sh])
                    bSu = nextbank(g)
                    P.op('pe', lambda e: e.matmul(bSu.t[:, :], B.padf[:, s_ * 128:(s_ + 1) * 128], xw[:, g_ * 512:(g_ + 1) * 512], start=True, stop=True),
                         reads=[B.padf, xw], writes=[bSu])
                    P.op('dve', lambda e: e.tensor_tensor(Sh[:, :, :], Sh[:, :, :], B.egts[:, s_, 8 * g_:8 * g_ + 8].unsqueeze(2).to_broadcast([128, 8, 64]), ALU.mult),
                         reads=[Sh, B.egts], writes=[Sh])
                    P.op('dve', lambda e: e.tensor_tensor(Sh[:, :, :], Sh[:, :, :], bSu.t[:, :].rearrange("p (h e) -> p h e", h=8), ALU.add),
                         reads=[Sh, bSu], writes=[Sh])
                    P.dma('sp', g.o_ssm_s[l][s_, 8 * g_:8 * g_ + 8, :, :].rearrange("h n p -> n h p"), Sh[:, :, :], "Shs%d_o" % ((nsh - 1) % 2), reads=[Sh])


def make_in_maps(inp, ncores=8, NPT=16, DEPTH=2):
    f = lambda a: np.ascontiguousarray(np.asarray(a, dtype=np.float32))
    consts = make_consts()
    params = np.zeros((DEPTH, NPAR), np.float32)
    for l in range(DEPTH):
        params[l, P_DN_ALOG:P_DN_ALOG + 4] = inp['dn_a_log'][l]
        params[l, P_DN_DTB:P_DN_DTB + 4] = inp['dn_dt_bias'][l]
        params[l, P_M2_ALOG:P_M2_ALOG + 16] = inp['m2_a_log'][l]
        params[l, P_M2_DTB:P_M2_DTB + 16] = inp['m2_dt_bias'][l]
        params[l, P_M2_D:P_M2_D + 16] = inp['m2_d'][l]
        params[l, P_DN_NORM:P_DN_NORM + 128] = inp['dn_norm'][l]
        params[l, P_GLA_NORM:P_GLA_NORM + 128] = inp['gla_norm'][l]
    norms = np.zeros((4 * DEPTH + 1, D), np.float32)
    for l in range(DEPTH):
        norms[3 * l] = inp['ffn1_norm'][l]
        norms[3 * l + 1] = inp['mix_norm'][l]
        norms[3 * l + 2] = inp['ffn2_norm'][l]
    norms[3 * DEPTH] = inp['final_norm']
    for l in range(DEPTH):
        norms[3 * DEPTH + 1 + l] = inp['m2_norm'][l]
    convw = f(np.asarray(inp['conv_w'])[:DEPTH].reshape(DEPTH, 4, 24, 128).transpose(0, 3, 2, 1).reshape(DEPTH, 128, 96))
    convb = f(np.asarray(inp['conv_b'])[:DEPTH].reshape(DEPTH, 24, 128).transpose(0, 2, 1))
    shared = {
        'consts': consts, 'params': params, 'norms': norms, 'convw': convw, 'convb': convb,
        'ffn1_w_gu': f(inp['ffn1_w_gu'][:DEPTH]), 'ffn2_w_gu': f(inp['ffn2_w_gu'][:DEPTH]),
        'ffn1_w_down': f(inp['ffn1_w_down'][:DEPTH]), 'ffn2_w_down': f(inp['ffn2_w_down'][:DEPTH]),
        'w_in': f(inp['w_in'][:DEPTH]), 'w_out': f(inp['w_out'][:DEPTH]),
        'gla_w_up': f(inp['gla_w_up'][:DEPTH]), 'gla_b_up': f(np.asarray(inp['gla_b_up'])[:DEPTH].reshape(DEPTH, 1, 256)),
    }
    maps = []
    for c in range(ncores):
        m = dict(shared)
        xp = np.asarray(inp['x_prompt'])[c, :NPT * 128]
        xs = np.asarray(inp['x_sample'])[16 * c:16 * c + 16].reshape(128, D)
        m['xin'] = f(np.concatenate([xp, xs], 0))
        m['st_conv'] = f(np.asarray(inp['state_conv'])[:DEPTH, 16 * c:16 * c + 16].reshape(DEPTH, 48, CONV_CH))
        m['st_delta'] = f(np.asarray(inp['state_delta'])[:DEPTH, 16 * c:16 * c + 16])
        m['st_gla'] = f(np.asarray(inp['state_gla'])[:DEPTH, 16 * c:16 * c + 16])
        m['st_ssm'] = f(np.asarray(inp['state_ssm'])[:DEPTH, 16 * c:16 * c + 16])
        maps.append(m)
    return maps


def gather_outputs(res, ncores=8, NPT=16, DEPTH=2):
    yp = np.stack([r['y'][:NPT * 128] for r in res], 0)
    ys = np.concatenate([r['y'][NPT * 128:].reshape(16, 8, D) for r in res], 0)
    cp = np.stack([r['o_conv_p'] for r in res], 1)
    dp = np.stack([r['o_delta_p'] for r in res], 1)
    gp = np.stack([r['o_gla_p'] for r in res], 1)
    sp = np.stack([r['o_ssm_p'] for r in res], 1)
    cs = np.concatenate([r['o_conv_s'].reshape(DEPTH, 16, 3, CONV_CH) for r in res], 1)
    ds = np.concatenate([r['o_delta_s'] for r in res], 1)
    gs = np.concatenate([r['o_gla_s'] for r in res], 1)
    ss = np.concatenate([r['o_ssm_s'] for r in res], 1)
    return tuple(np.ascontiguousarray(a, dtype=np.float32) for a in (yp, ys, cp, dp, gp, sp, cs, ds, gs, ss))


_NC_CACHE = {}


def kernel(**inputs):
    if 'nc' not in _NC_CACHE:
        _NC_CACHE['nc'] = build()[0]
    nc = _NC_CACHE['nc']
    in_maps = make_in_maps(inputs)
    res = run_bass_kernel_spmd(nc, in_maps, core_ids=list(range(8)))
    return gather_outputs(res.results)
```

```python
import numpy as np
import concourse.bass as bass
import concourse.mybir as mybir
from concourse.bass_utils import run_bass_kernel_spmd

F32 = mybir.dt.float32
BF16 = mybir.dt.bfloat16
AF = mybir.ActivationFunctionType
ALU = mybir.AluOpType
AX = mybir.AxisListType

D = 1024
DFF = 2816
EPS = 1e-6
NEG = -30000.0
CONV_CH = 3072
IN_COLS = 6184
O_DQ, O_DK, O_DV, O_MX, O_MB, O_MC = 0, 512, 1024, 1536, 2560, 2816
O_DZ, O_DA, O_DB, O_GQ, O_GK, O_GV, O_GLOW, O_GG, O_MZ, O_MDT = 3072, 3584, 3588, 3592, 3848, 4104, 4616, 4632, 5144, 6168

C_IDENT = 0
C_ONES = 128
C_U01 = 256
C_NUI = 384
C_NUS = 512
C_NLS = 640
C_SEG = 768
C_SU01 = 896
C_SNUI = 1024
C_SNUS = 1152
C_SNLS = 1280
C_ESEL = 1408
C_ROWM = 1920
C_ROWM8 = 1936
C_INV128 = 1952
NCONST = 1984

P_DN_ALOG, P_DN_DTB, P_M2_ALOG, P_M2_DTB, P_M2_D = 0, 4, 8, 24, 40
P_DN_NORM, P_GLA_NORM, P_M2_NORM = 64, 192, 320
NPAR = 320


def make_consts():
    c = np.zeros((128, NCONST), np.float32)
    p = np.arange(128)[:, None]
    f = np.arange(128)[None, :]
    seg = (p // 8) == (f // 8)
    c[:, C_IDENT:C_IDENT + 128] = (p == f)
    c[:, C_ONES:C_ONES + 128] = 1.0
    c[:, C_U01:C_U01 + 128] = (p <= f)
    c[:, C_NUI:C_NUI + 128] = np.where(p <= f, 0.0, NEG)
    c[:, C_NUS:C_NUS + 128] = np.where(p < f, 0.0, NEG)
    c[:, C_NLS:C_NLS + 128] = np.where(p > f, 0.0, NEG)
    c[:, C_SEG:C_SEG + 128] = seg
    c[:, C_SU01:C_SU01 + 128] = seg & (p <= f)
    c[:, C_SNUI:C_SNUI + 128] = np.where(seg & (p <= f), 0.0, NEG)
    c[:, C_SNUS:C_SNUS + 128] = np.where(seg & (p < f), 0.0, NEG)
    c[:, C_SNLS:C_SNLS + 128] = np.where(seg & (p > f), 0.0, NEG)
    for h in range(4):
        c[h, C_ESEL + h * 128:C_ESEL + (h + 1) * 128] = 1.0
    s = np.arange(16)[None, :]
    c[:, C_ROWM:C_ROWM + 16] = ((p // 8) == s)
    c[:, C_ROWM8:C_ROWM8 + 16] = ((p // 8) == s) / 8.0
    c[:, C_INV128] = 1.0 / 128
    return c


class TT:
    def __init__(self, t, key):
        self.t = t
        self.key = key

    def __getitem__(self, idx):
        return self.t[idx]


class Prog:
    def __init__(self, nc):
        self.nc = nc
        self.engs = {'pe': nc.tensor, 'act': nc.scalar, 'dve': nc.vector, 'pool': nc.gpsimd, 'sp': nc.sync}
        self.esem = {e: nc.alloc_semaphore("sem_" + e) for e in self.engs}
        self.cnt = {e: 0 for e in self.engs}
        self.seen = {e: {} for e in self.engs}
        self.lastw = {}
        self.readers = {}
        self.dsem = {}
        self.dcnt = {}
        self.dtot = {}
        self.nwaits = 0
        self.nops = 0
        self.nuid = 0

    def sb(self, name, shape, dtype=F32):
        self.nuid += 1
        nm = "%s_%d" % (name, self.nuid)
        return TT(self.nc.alloc_sbuf_tensor(nm, list(shape), dtype), nm)

    @staticmethod
    def _keys(lst):
        out = []
        for x in lst:
            k = x.key if isinstance(x, TT) else x
            if isinstance(k, (tuple, list)):
                out.extend(k)
            else:
                out.append(k)
        return out

    def _wait(self, eng, ev):
        sem, val = ev
        k = id(sem)
        if k in self.dtot:
            val = max(val, self.dtot[k])
        if self.seen[eng].get(k, 0) >= val:
            return
        if sem is self.esem[eng] and val > self.cnt[eng]:
            return
        self.seen[eng][k] = val
        self.engs[eng].wait_ge(sem, val)
        self.nwaits += 1

    def _deps(self, eng, reads, writes, skipsem=None):
        evs = []
        for k in reads:
            if k in self.lastw:
                evs.append(self.lastw[k])
        for k in writes:
            if k in self.lastw and self.lastw[k][0] is not skipsem:
                evs.append(self.lastw[k])
            for ev in self.readers.get(k, {}).values():
                evs.append(ev)
        for ev in evs:
            self._wait(eng, ev)

    def _record(self, ev, reads, writes):
        for k in reads:
            self.readers.setdefault(k, {})[id(ev[0])] = ev
        for k in writes:
            self.lastw[k] = ev
            self.readers[k] = {}

    def op(self, eng, fn, reads=(), writes=(), inc=True):
        reads = self._keys(reads)
        writes = self._keys(writes)
        self._deps(eng, reads, writes)
        ins = fn(self.engs[eng])
        self.nops += 1
        if inc:
            self.cnt[eng] += 1
            ins.then_inc(self.esem[eng], 1)
            ev = (self.esem[eng], self.cnt[eng])
        else:
            ev = (self.esem[eng], self.cnt[eng] + 1)
        self._record(ev, reads, writes)
        return ins

    def dma(self, eng, out, in_, group, reads=(), writes=()):
        reads = self._keys(reads)
        writes = self._keys(writes)
        if group not in self.dsem:
            self.dsem[group] = self.nc.alloc_semaphore("dsem_%d" % len(self.dsem))
            self.dcnt[group] = 0
        sem = self.dsem[group]
        self._deps(eng, reads, writes, skipsem=sem)
        ins = self.engs[eng].dma_start(out=out, in_=in_)
        self.nops += 1
        self.dcnt[group] += 16
        self.dtot[id(sem)] = self.dcnt[group]
        ins.then_inc(sem, 16)
        ev = (sem, self.dcnt[group])
        self._record(ev, reads, writes)
        return ins

    def finish(self, eng='sp'):
        for g, sem in self.dsem.items():
            self._wait(eng, (sem, self.dcnt[g]))
        for e in self.engs:
            if e != eng and self.cnt[e] > 0:
                self._wait(eng, (self.esem[e], self.cnt[e]))


class Ctx:
    pass


def build(NPT=16, DEPTH=2, stages=('ffn', 'mix')):
    NT = NPT + 1
    NTOK = NT * 128
    nc = bass.Bass("TRN2", target_bir_lowering=False)
    P = Prog(nc)
    g = Ctx()
    g.nc, g.P, g.NPT, g.NT, g.DEPTH = nc, P, NPT, NT, DEPTH
    g.mixers = [m for m in ('delta', 'gla', 'ssd') if m in stages or 'mix' in stages]

    def din(name, shape):
        return nc.dram_tensor(name, list(shape), F32, kind="ExternalInput").ap()

    def dout(name, shape):
        return nc.dram_tensor(name, list(shape), F32, kind="ExternalOutput").ap()

    g.xin = din("xin", [NTOK, D])
    g.consts_d = din("consts", [128, NCONST])
    g.params_d = din("params", [DEPTH, NPAR])
    g.st_conv = din("st_conv", [DEPTH, 48, CONV_CH])
    g.st_delta = din("st_delta", [DEPTH, 16, 4, 128, 128])
    g.st_gla = din("st_gla", [DEPTH, 16, 4, 64, 128])
    g.st_ssm = din("st_ssm", [DEPTH, 16, 16, 128, 64])
    g.norms_d = din("norms", [4 * DEPTH + 1, D])
    g.w_gu = [din("ffn1_w_gu", [DEPTH, D, 2 * DFF]), din("ffn2_w_gu", [DEPTH, D, 2 * DFF])]
    g.w_down = [din("ffn1_w_down", [DEPTH, DFF, D]), din("ffn2_w_down", [DEPTH, DFF, D])]
    g.w_in = din("w_in", [DEPTH, D, IN_COLS])
    g.w_out = din("w_out", [DEPTH, 2 * D, D])
    g.convw_d = din("convw", [DEPTH, 128, 24 * 4])
    g.convb_d = din("convb", [DEPTH, 128, 24])
    g.gla_wup = din("gla_w_up", [DEPTH, 16, 256])
    g.gla_bup = din("gla_b_up", [DEPTH, 1, 256])

    g.y = dout("y", [NTOK, D])
    g.o_conv_p = dout("o_conv_p", [DEPTH, 3, CONV_CH])
    g.o_delta_p = dout("o_delta_p", [DEPTH, 4, 128, 128])
    g.o_gla_p = dout("o_gla_p", [DEPTH, 4, 64, 128])
    g.o_ssm_p = dout("o_ssm_p", [DEPTH, 16, 128, 64])
    g.o_conv_s = dout("o_conv_s", [DEPTH, 48, CONV_CH])
    g.o_delta_s = dout("o_delta_s", [DEPTH, 16, 4, 128, 128])
    g.o_gla_s = dout("o_gla_s", [DEPTH, 16, 4, 64, 128])
    g.o_ssm_s = dout("o_ssm_s", [DEPTH, 16, 16, 128, 64])

    g.x = P.sb("x", [128, NT, D])
    g.xk = ["x_t%d" % t for t in range(NT)]
    g.consts = P.sb("consts", [128, NCONST])
    g.identb = P.sb("identb", [128, 128], BF16)
    g.stat = P.sb("stat", [128, 64])
    g.ps = [TT(nc.alloc_psum_tensor("psb%d" % i, [128, 512], F32), "psb%d" % i) for i in range(8)]
    g.psi = 0
    g.held = set()

    P.dma('sp', g.consts[:, :], g.consts_d, 'consts', writes=[g.consts])
    for t in range(NT):
        P.dma('sp', g.x[:, t, :], g.xin[t * 128:(t + 1) * 128, :], 'xload', writes=[g.xk[t]])
    P.op('dve', lambda e: e.tensor_copy(g.identb[:, :], g.consts[:, C_IDENT:C_IDENT + 128]), reads=[g.consts], writes=[g.identb])
    g.maskb = P.sb("maskb", [128, 6, 128], BF16)
    g.maskb_idx = {}
    for i_, off_ in enumerate((C_NUI, C_NUS, C_NLS, C_SNUI, C_SNUS, C_SNLS)):
        g.maskb_idx[off_] = i_
        P.op('dve', lambda e: e.tensor_copy(g.maskb[:, i_, :], g.consts[:, off_:off_ + 128]), reads=[g.consts], writes=[g.maskb])

    for l in range(DEPTH):
        if 'ffn' in stages:
            ffn_phase(g, l, 0)
        if g.mixers:
            mix_phase(g, l)
        if 'ffn' in stages:
            ffn_phase(g, l, 1)
    final_phase(g)
    P.finish('sp')
    g.stats = (P.nops, P.nwaits, len(P.dsem))
    return nc, g


def nextbank(g, hold=False):
    while True:
        b = g.ps[g.psi % 8]
        g.psi += 1
        if b.key not in g.held:
            break
    if hold:
        g.held.add(b.key)
    return b


def release(g, *banks):
    for b in banks:
        g.held.discard(b.key)


def barrier(g, skip=()):
    P = g.P
    for e in P.engs:
        for gname, sem in P.dsem.items():
            if gname in skip:
                continue
            P._wait(e, (sem, P.dcnt[gname]))
        for e2 in P.engs:
            if e2 != e and P.cnt[e2] > 0:
                P._wait(e, (P.esem[e2], P.cnt[e2]))


def norm_tiles(g, tiles, gain_row, hT, hn, gB, tok0=0):
    P, nc = g.P, g.nc
    n = len(tiles)
    P.dma('sp', gB[:, :], g.norms_d[gain_row:gain_row + 1, :].partition_broadcast(128), 'gB_d', writes=[gB])
    junk = hn[0]
    for i, t in enumerate(tiles):
        P.op('act', lambda e: e.activation(junk[:, :], g.x[:, t, :], AF.Square, accum_out=g.stat[:, i:i + 1]),
             reads=[g.xk[t]], writes=[junk, g.stat])
    P.op('act', lambda e: e.activation(g.stat[:, 16:16 + n], g.stat[:, 0:n], AF.Ln, bias=EPS, scale=1.0 / D),
         reads=[g.stat], writes=[g.stat])
    P.op('act', lambda e: e.activation(g.stat[:, 32:32 + n], g.stat[:, 16:16 + n], AF.Exp, scale=-0.5),
         reads=[g.stat], writes=[g.stat])
    for i, t in enumerate(tiles):
        h = hn[i % 2]
        P.op('dve', lambda e: e.scalar_tensor_tensor(h[:, :], g.x[:, t, :], g.stat[:, 32 + i:33 + i], gB[:, :], ALU.mult, ALU.mult),
             reads=[g.xk[t], g.stat, gB], writes=[h])
        bank = nextbank(g)
        pb = bank.t[:, :].bitcast(BF16)
        for kc in range(8):
            P.op('pe', lambda e: e.transpose(pb[:, kc * 128:(kc + 1) * 128], h[:, kc * 128:(kc + 1) * 128], g.identb[:, :]),
                 reads=[h, g.identb], writes=[bank], inc=(kc == 7))
        c0 = tok0 + i * 128
        P.op('act', lambda e: e.activation(hT[:, :, c0:c0 + 128], pb.rearrange("p (k n) -> p k n", k=8), AF.Copy),
             reads=[bank], writes=[hT])


def ffn_phase(g, l, which):
    P, nc = g.P, g.nc
    NT = g.NT
    half = (NT + 1) // 2
    sbs = [list(range(0, NT - half)), list(range(NT - half, NT))]
    SBW = half * 128
    barrier(g)
    w_gu = g.w_gu[which][l]
    w_down = g.w_down[which][l]
    tag = "f%d_%d_" % (l, which)
    with nc.sbuf_tensor(tag + "hT", [128, 8, SBW], BF16) as hT_, \
            nc.sbuf_tensor(tag + "actT", [128, 22, SBW], BF16) as actT_, \
            nc.sbuf_tensor(tag + "wgu0", [128, 8, 512], BF16) as wgu0, nc.sbuf_tensor(tag + "wgu1", [128, 8, 512], BF16) as wgu1, \
            nc.sbuf_tensor(tag + "wd0", [128, 22, 256], BF16) as wd0, nc.sbuf_tensor(tag + "wd1", [128, 22, 256], BF16) as wd1, \
            nc.sbuf_tensor(tag + "gB", [128, D], F32) as gB_, \
            nc.sbuf_tensor(tag + "hn0", [128, D], BF16) as hn0, nc.sbuf_tensor(tag + "hn1", [128, D], BF16) as hn1, \
            nc.sbuf_tensor(tag + "sg0", [128, 512], F32) as sg0, nc.sbuf_tensor(tag + "sg1", [128, 512], F32) as sg1:
        hT = TT(hT_, tag + "hT")
        actT = TT(actT_, tag + "actT")
        wgu = [TT(wgu0, tag + "wgu0"), TT(wgu1, tag + "wgu1")]
        wd = [TT(wd0, tag + "wd0"), TT(wd1, tag + "wd1")]
        gB = TT(gB_, tag + "gB")
        hn = [TT(hn0, tag + "hn0"), TT(hn1, tag + "hn1")]
        sg = [TT(sg0, tag + "sg0"), TT(sg1, tag + "sg1")]
        nsg = 0
        for sbi, tiles in enumerate(sbs):
            ntok = len(tiles) * 128
            norm_tiles(g, tiles, 3 * l + (0 if which == 0 else 2), hT, hn, gB)
            tgs = [(c0, min(512, ntok - c0)) for c0 in range(0, ntok, 512)]
            for j in range(11):
                wb = wgu[j % 2]
                P.dma('pool', wb[:, :, 0:256], w_gu[:, 256 * j:256 * j + 256].rearrange("(k p) n -> p k n", p=128),
                      "wgu%d_d" % (j % 2), writes=[wb])
                P.dma('pool', wb[:, :, 256:512], w_gu[:, DFF + 256 * j:DFF + 256 * j + 256].rearrange("(k p) n -> p k n", p=128),
                      "wgu%d_d" % (j % 2), writes=[wb])
                for c in range(2):
                    for (c0, w) in tgs:
                        bA = nextbank(g)
                        bB = nextbank(g)
                        for kc in range(8):
                            P.op('pe', lambda e: e.matmul(bA.t[:, 0:w], wb[:, kc, c * 128:(c + 1) * 128], hT[:, kc, c0:c0 + w],
                                                          start=(kc == 0), stop=(kc == 7)),
                                 reads=[wb, hT], writes=[bA], inc=(kc == 7))
                        for kc in range(8):
                            P.op('pe', lambda e: e.matmul(bB.t[:, 0:w], wb[:, kc, 256 + c * 128:256 + (c + 1) * 128], hT[:, kc, c0:c0 + w],
                                                          start=(kc == 0), stop=(kc == 7)),
                                 reads=[wb, hT], writes=[bB], inc=(kc == 7))
                        s_ = sg[nsg % 2]
                        nsg += 1
                        P.op('act', lambda e: e.activation(s_[:, 0:w], bA.t[:, 0:w], AF.Silu), reads=[bA], writes=[s_])
                        P.op('dve', lambda e: e.tensor_tensor(actT[:, 2 * j + c, c0:c0 + w], s_[:, 0:w], bB.t[:, 0:w], ALU.mult),
                             reads=[s_, bB], writes=[actT])
            for q in range(4):
                wq = wd[q % 2]
                P.dma('pool', wq[:, :, :], w_down[:, 256 * q:256 * q + 256].rearrange("(k p) n -> p k n", p=128),
                      "wd%d_d" % (q % 2), writes=[wq])
                for i, t in enumerate(tiles):
                    bD = nextbank(g)
                    for kc in range(22):
                        P.op('pe', lambda e: e.matmul(bD.t[:, 0:256], actT[:, kc, i * 128:(i + 1) * 128], wq[:, kc, :],
                                                      start=(kc == 0), stop=(kc == 21)),
                             reads=[actT, wq], writes=[bD], inc=(kc == 21))
                    P.op('dve', lambda e: e.scalar_tensor_tensor(g.x[:, t, 256 * q:256 * q + 256], bD.t[:, 0:256], 0.5,
                                                                 g.x[:, t, 256 * q:256 * q + 256], ALU.mult, ALU.add),
                         reads=[bD, g.xk[t]], writes=[g.xk[t]])
        barrier(g)


def final_phase(g):
    P, nc = g.P, g.nc
    NT = g.NT
    barrier(g)
    with nc.sbuf_tensor("fin_gB", [128, D], F32) as gB_, nc.sbuf_tensor("fin_o0", [128, D], F32) as o0, \
            nc.sbuf_tensor("fin_o1", [128, D], F32) as o1:
        gB = TT(gB_, "fin_gB")
        ob = [TT(o0, "fin_o0"), TT(o1, "fin_o1")]
        P.dma('sp', gB[:, :], g.norms_d[3 * g.DEPTH:3 * g.DEPTH + 1, :].partition_broadcast(128), 'fin_gB', writes=[gB])
        for t in range(NT):
            o = ob[t % 2]
            P.op('act', lambda e: e.activation(o[:, :], g.x[:, t, :], AF.Square, accum_out=g.stat[:, 0:1]),
                 reads=[g.xk[t]], writes=[o, g.stat])
            P.op('act', lambda e: e.activation(g.stat[:, 1:2], g.stat[:, 0:1], AF.Ln, bias=EPS, scale=1.0 / D),
                 reads=[g.stat], writes=[g.stat])
            P.op('act', lambda e: e.activation(g.stat[:, 2:3], g.stat[:, 1:2], AF.Exp, scale=-0.5),
                 reads=[g.stat], writes=[g.stat])
            P.op('dve', lambda e: e.scalar_tensor_tensor(o[:, :], g.x[:, t, :], g.stat[:, 2:3], gB[:, :], ALU.mult, ALU.mult),
                 reads=[g.xk[t], g.stat, gB], writes=[o])
            P.dma('sp', g.y[t * 128:(t + 1) * 128, :], o[:, :], o.key + "_out", reads=[o])
        barrier(g)


class Scope:
    def __init__(self, g, tag):
        import contextlib
        self.g, self.tag = g, tag
        self.es = contextlib.ExitStack()

    def __enter__(self):
        self.es.__enter__()
        return self

    def __exit__(self, *a):
        return self.es.__exit__(*a)

    def sb(self, name, shape, dtype=F32):
        g = self.g
        g.P.nuid += 1
        nm = "%s_%s_%d" % (self.tag, name, g.P.nuid)
        t = self.es.enter_context(g.nc.sbuf_tensor(nm, list(shape), dtype))
        return TT(t, nm)


def cst(g, off, n=128, rows=128):
    return g.consts[0:rows, off:off + n]


def mix_phase(g, l):
    P, nc = g.P, g.nc
    NPT = g.NPT
    barrier(g)
    sbs = [list(range(a, min(a + 6, NPT))) for a in range(0, NPT, 6)] + [[NPT]]
    with Scope(g, "mx%d" % l) as LS:
        L = Ctx()
        L.l = l
        L.parB = LS.sb("parB", [128, NPAR])
        L.convw = LS.sb("convw", [128, 24, 4])
        L.convb = LS.sb("convb", [128, 24])
        L.wup = LS.sb("wup", [16, 256])
        L.bup = LS.sb("bup", [1, 256])
        L.pd = LS.sb("pd", [128, 64])
        P.dma('sp', L.parB[:, :], g.params_d[l:l + 1, :].partition_broadcast(128), 'lay_small', writes=[L.parB])
        P.dma('sp', L.convw[:, :, :], g.convw_d[l].rearrange("p (c j) -> p c j", j=4), 'lay_small', writes=[L.convw])
        P.dma('sp', L.convb[:, :], g.convb_d[l], 'lay_small', writes=[L.convb])
        P.dma('sp', L.wup[:, :], g.gla_wup[l], 'lay_small', writes=[L.wup])
        P.dma('sp', L.bup[:, :], g.gla_bup[l], 'lay_small', writes=[L.bup])
        P.op('act', lambda e: e.activation(L.pd[:, 0:4], L.parB[:, P_DN_ALOG:P_DN_ALOG + 4], AF.Exp), reads=[L.parB], writes=[L.pd])
        P.op('act', lambda e: e.activation(L.pd[:, 4:20], L.parB[:, P_M2_ALOG:P_M2_ALOG + 16], AF.Exp), reads=[L.parB], writes=[L.pd])
        P.op('dve', lambda e: e.tensor_scalar(L.pd[:, 0:20], L.pd[:, 0:20], -1.0, None, ALU.mult), reads=[L.pd], writes=[L.pd])
        L.Sd = LS.sb("Sd", [128, 4, 128])
        L.Sg = LS.sb("Sg", [128, 2, 128])
        L.Ss = LS.sb("Ss", [128, 16, 64])
        L.tailD = LS.sb("tailD", [128, 12, 3])
        L.tailS = LS.sb("tailS", [128, 12, 3])
        for t_ in (L.Sd, L.Sg, L.Ss, L.tailD, L.tailS):
            P.op('pool', lambda e: e.memset(t_.t[:], 0.0), writes=[t_])
        import os
        L.WB = LS.sb("WB", [128, 8, 2576], BF16)
        L.WO = LS.sb("WO", [128, 8, 1024], BF16)
        phases = [(sbi_, m_) for sbi_ in range(len(sbs)) for m_ in g.mixers]
        L.phases = phases
        L.phase_i = 0
        load_wpart(g, L, phases[0][1])
        load_wo(g, L, phases[0][1])
        for sbi, tiles in enumerate(sbs):
            is_s = tiles[0] == NPT
            ntok = len(tiles) * 128
            with Scope(g, "mx%d_%d" % (l, sbi)) as SS:
                hT = SS.sb("hT", [128, 8, ntok], BF16)
                with Scope(g, "mx%d_%d_n" % (l, sbi)) as NS:
                    hn = [NS.sb("hn0", [128, D], BF16), NS.sb("hn1", [128, D], BF16)]
                    gB = NS.sb("gB", [128, D])
                    norm_tiles(g, tiles, 3 * l + 1, hT, hn, gB)
                    barrier(g)
                for mixer in g.mixers:
                    with Scope(g, "mx%d_%d_%s" % (l, sbi, mixer)) as MS:
                        if mixer == 'delta':
                            delta_mixer(g, L, MS, tiles, hT, is_s)
                        elif mixer == 'gla':
                            gla_mixer(g, L, MS, tiles, hT, is_s)
                        else:
                            ssd_mixer(g, L, MS, tiles, hT, is_s)
                    L.phase_i += 1
                    if L.phase_i < len(L.phases):
                        load_wo(g, L, L.phases[L.phase_i][1])
                    barrier(g, skip=("wpart_d", "wo_d"))
        barrier(g)


MIX_COLS = {'delta': [(O_DQ, 1536), (O_DZ, 520)], 'gla': [(O_GQ, 1552)], 'ssd': [(O_MX, 1536), (O_MZ, 1040)]}
MIX_ROWS = {'delta': (0, 512), 'gla': (512, 512), 'ssd': (1024, 1024)}


def load_wpart(g, L, mixer):
    P, l = g.P, L.l
    c = 0
    for (c0, n) in MIX_COLS[mixer]:
        for a in range(0, n, 512):
            w = min(512, n - a)
            P.dma('pool', L.WB[:, :, c + a:c + a + w], g.w_in[l][:, c0 + a:c0 + a + w].rearrange("(k p) n -> p k n", p=128),
                  "wpart_d", writes=[L.WB])
        c += n


def load_wo(g, L, mixer):
    P, l = g.P, L.l
    r0, nr = MIX_ROWS[mixer]
    for a in range(0, nr, 512):
        P.dma('pool', L.WO[:, a // 128:a // 128 + 4, :], g.w_out[l][r0 + a:r0 + a + 512, :].rearrange("(k p) n -> p k n", p=128),
              "wo_d", writes=[L.WO])


def prefetch_next(g, L):
    if L.phase_i + 1 < len(L.phases):
        load_wpart(g, L, L.phases[L.phase_i + 1][1])


def proj_fm(g, wpart, col0, ncols, hT, tok0, bank, slot, rows0=0):
    P = g.P
    for kc in range(8):
        P.op('pe', lambda e: e.matmul(bank.t[rows0:rows0 + ncols, slot * 128:(slot + 1) * 128], wpart[:, kc, col0:col0 + ncols],
                                      hT[:, kc, tok0:tok0 + 128], start=(kc == 0), stop=(kc == 7)),
             reads=[wpart, hT], writes=[bank], inc=(kc == 7))


def proj_tm(g, wpart, col0, ncols, hT, tok0, bank, o0=0):
    P = g.P
    for kc in range(8):
        P.op('pe', lambda e: e.matmul(bank.t[:, o0:o0 + ncols], hT[:, kc, tok0:tok0 + 128], wpart[:, kc, col0:col0 + ncols],
                                      start=(kc == 0), stop=(kc == 7)),
             reads=[wpart, hT], writes=[bank], inc=(kc == 7))


def conv_block(g, L, MS, B, wpart, hT, tok0, is_s, ch0, tail, stc, last_prompt, conv_out_col0):
    P, l = g.P, L.l
    cin, cout, acc, tmp = B.cin, B.cout, B.acc, B.tmp
    if is_s:
        cin4 = cin.t[:, :, :].rearrange("p c (r s) -> p c r s", s=16)
    for grp in range(3):
        bank = nextbank(g)
        for c4 in range(4):
            proj_fm(g, wpart, (grp * 4 + c4) * 128, 128, hT, tok0, bank, c4)
        if not is_s:
            P.op('act', lambda e: e.activation(cin[:, grp * 4:grp * 4 + 4, 3:131], bank.t[:, :].rearrange("p (c n) -> p c n", c=4), AF.Copy),
                 reads=[bank], writes=[cin])
        else:
            P.op('act', lambda e: e.activation(cin4[:, grp * 4:grp * 4 + 4, 3:11, :],
                                               bank.t[:, :].rearrange("p (c s t) -> p c t s", c=4, s=16), AF.Copy),
                 reads=[bank], writes=[cin])
    if not is_s:
        P.op('pool', lambda e: e.tensor_copy(cin[:, :, 0:3], tail[:, :, :]), reads=[tail], writes=[cin])
        P.op('pool', lambda e: e.tensor_copy(tail[:, :, :], cin[:, :, 128:131]), reads=[cin], writes=[tail])
    else:
        for grp in range(3):
            bank = nextbank(g)
            for c4 in range(4):
                ch = grp * 4 + c4
                P.op('pe', lambda e: e.transpose(bank.t[:, c4 * 48:(c4 + 1) * 48], stc[0:48, ch * 128:(ch + 1) * 128], cst(g, C_IDENT, 48, 48)),
                     reads=[stc, g.consts], writes=[bank], inc=(c4 == 3))
            P.op('act', lambda e: e.activation(cin4[:, grp * 4:grp * 4 + 4, 0:3, :],
                                               bank.t[:, 0:192].rearrange("p (c s r) -> p c r s", c=4, r=3), AF.Copy),
                 reads=[bank], writes=[cin])
    if is_s or last_prompt:
        for grp in range(3):
            bank = nextbank(g)
            for c4 in range(4):
                ch = grp * 4 + c4
                if is_s:
                    src = cin.t[:, ch, 128:176]
                    n = 48
                else:
                    src = cin[:, ch, 128:131]
                    n = 3
                P.op('pe', lambda e: e.transpose(bank.t[0:n, c4 * 128:(c4 + 1) * 128], src, cst(g, C_IDENT)),
                     reads=[cin, g.consts], writes=[bank], inc=(c4 == 3))
            n = 48 if is_s else 3
            ob = B.cvo
            P.op('dve', lambda e: e.tensor_copy(ob[0:n, grp * 512:(grp + 1) * 512], bank.t[0:n, :]), reads=[bank], writes=[ob])
        if is_s:
            dv = g.o_conv_s[l].rearrange("(s r) c -> r s c", r=3)
            for r_ in range(3):
                P.dma('sp', dv[r_, :, conv_out_col0:conv_out_col0 + 1536], B.cvo[r_ * 16:(r_ + 1) * 16, :], "cvo_o", reads=[B.cvo])
        else:
            P.dma('sp', g.o_conv_p[l][:, conv_out_col0:conv_out_col0 + 1536], B.cvo[0:3, :], "cvo_o", reads=[B.cvo])
    return


def conv_taps(g, L, B, is_s, ch0, split=False):
    P = g.P
    cin, cout, acc, tmp = B.cin, B.cout, B.acc, B.tmp
    if is_s:
        cin4 = cin.t[:, :, :].rearrange("p c (r s) -> p c r s", s=16)
        acc4 = acc.t[:, :, :].rearrange("p c (t s) -> p c t s", s=16)
        tmp4 = tmp.t[:, :, :].rearrange("p c (t s) -> p c t s", s=16)
    parts = [('pool', 0, 4), ('dve', 4, 12)] if split else [('pool', 0, 12)]
    for (eng, c0, c1) in parts:
        n = c1 - c0
        akeys = list(acc.key[c0 // 4:(c1 + 3) // 4])
        tkeys = list(tmp.key[c0 // 4:(c1 + 3) // 4])

        def view(j):
            if is_s:
                return cin4[:, c0:c1, j:j + 8, :]
            return cin[:, c0:c1, j:j + 128]
        if is_s:
            shape = [128, n, 8, 16]
            accv = acc4[:, c0:c1, :, :]
            tmpv = tmp4[:, c0:c1, :, :]
            bv_ = L.convb[:, ch0 + c0:ch0 + c1].unsqueeze(2).unsqueeze(3).to_broadcast(shape)
        else:
            shape = [128, n, 128]
            accv = acc.t[:, c0:c1, :]
            tmpv = tmp.t[:, c0:c1, :]
            bv_ = L.convb[:, ch0 + c0:ch0 + c1].unsqueeze(2).to_broadcast(shape)

        def wv(j):
            w_ = L.convw[:, ch0 + c0:ch0 + c1, j:j + 1]
            return w_.to_broadcast(shape) if not is_s else w_.unsqueeze(3).to_broadcast(shape)
        P.op(eng, lambda e: e.tensor_tensor(accv, view(0), wv(0), ALU.mult), reads=[cin, L.convw], writes=akeys)
        for j in range(1, 4):
            P.op(eng, lambda e: e.tensor_tensor(tmpv, view(j), wv(j), ALU.mult), reads=[cin, L.convw], writes=tkeys)
            P.op(eng, lambda e: e.tensor_tensor(accv, accv, tmpv, ALU.add), reads=akeys + tkeys, writes=akeys)
        P.op(eng, lambda e: e.tensor_tensor(accv, accv, bv_, ALU.add), reads=akeys + [L.convb], writes=akeys)
    if is_s:
        P.op('act', lambda e: e.activation(cout.t[:, :, :].rearrange("p c (s t) -> p c t s", s=16), acc4, AF.Silu),
             reads=[acc], writes=[cout])
    else:
        P.op('act', lambda e: e.activation(cout[:, :, :], acc.t[:, :, :], AF.Silu), reads=[acc], writes=[cout])


def out_proj(g, MS, B, on_bf, nk, wo, t):
    P = g.P
    bank = nextbank(g)
    pb = bank.t[:, :].bitcast(BF16)
    for kc in range(nk):
        P.op('pe', lambda e: e.transpose(pb[:, kc * 128:(kc + 1) * 128], on_bf[:, kc * 128:(kc + 1) * 128], g.identb[:, :]),
             reads=[on_bf, g.identb], writes=[bank], inc=(kc == nk - 1))
    P.op('act', lambda e: e.activation(B.onT[:, 0:nk, :], pb[:, 0:nk * 128].rearrange("p (k n) -> p k n", k=nk), AF.Copy),
         reads=[bank], writes=[B.onT])
    for hh in range(2):
        bo = nextbank(g)
        for kc in range(nk):
            P.op('pe', lambda e: e.matmul(bo.t[:, :], B.onT[:, kc, :], wo[:, kc, hh * 512:(hh + 1) * 512], start=(kc == 0), stop=(kc == nk - 1)),
                 reads=[B.onT, wo], writes=[bo], inc=(kc == nk - 1))
        P.op('dve', lambda e: e.tensor_tensor(g.x[:, t, hh * 512:(hh + 1) * 512], g.x[:, t, hh * 512:(hh + 1) * 512], bo.t[:, :], ALU.add),
             reads=[bo, g.xk[t]], writes=[g.xk[t]])


def gate_compute(g, B, gate_banks, W):
    P = g.P
    gt = B.gate
    for bi, bank in enumerate(gate_banks):
        w = min(512, W - bi * 512)
        sl = gt[:, bi * 512:bi * 512 + w]
        P.op('act', lambda e: e.activation(sl, bank.t[:, 0:w], AF.Exp, scale=-1.0), reads=[bank], writes=[gt])
        P.op('act', lambda e: e.activation(sl, sl, AF.Ln, bias=1.0), reads=[gt], writes=[gt])
        P.op('act', lambda e: e.activation(sl, sl, AF.Exp, scale=-1.0), reads=[gt], writes=[gt])
        P.op('dve', lambda e: e.tensor_tensor(sl, sl, bank.t[:, 0:w], ALU.mult), reads=[gt, bank], writes=[gt])


def gated_norm(g, L, B, o_sb, nh, hd, norm_off, per_head, out_bf, norm_tt=None):
    P = g.P
    W = nh * hd
    sm = B.sm2
    gt = B.gate
    sq = getattr(B, 'sq', None)
    if per_head:
        P.op('pool', lambda e: e.tensor_tensor(sq[:, 0:W], o_sb[:, 0:W], o_sb[:, 0:W], ALU.mult), reads=[o_sb], writes=[sq])
        P.op('dve', lambda e: e.tensor_reduce(sm[:, 0:nh], sq[:, 0:W].rearrange("p (h e) -> p h e", h=nh), AX.X, ALU.add), reads=[sq], writes=[sm])
        P.op('act', lambda e: e.activation(sm[:, 16:16 + nh], sm[:, 0:nh], AF.Ln, bias=EPS, scale=1.0 / hd), reads=[sm], writes=[sm])
        P.op('act', lambda e: e.activation(sm[:, 32:32 + nh], sm[:, 16:16 + nh], AF.Exp, scale=-0.5), reads=[sm], writes=[sm])
        P.op('dve', lambda e: e.tensor_tensor(sq[:, 0:W].rearrange("p (h e) -> p h e", h=nh), o_sb[:, 0:W].rearrange("p (h e) -> p h e", h=nh),
                                              sm[:, 32:32 + nh].unsqueeze(2).to_broadcast([128, nh, hd]), ALU.mult), reads=[o_sb, sm], writes=[sq])
        P.op('pool', lambda e: e.tensor_tensor(sq[:, 0:W].rearrange("p (h e) -> p h e", h=nh), sq[:, 0:W].rearrange("p (h e) -> p h e", h=nh),
                                               L.parB[:, norm_off:norm_off + hd].unsqueeze(1).to_broadcast([128, nh, hd]), ALU.mult),
             reads=[sq, L.parB], writes=[sq])
        P.op('dve', lambda e: e.tensor_tensor(out_bf[:, 0:W], sq[:, 0:W], gt[:, 0:W], ALU.mult), reads=[sq, gt], writes=[out_bf])
    else:
        P.op('dve', lambda e: e.tensor_tensor(gt[:, 0:W], gt[:, 0:W], o_sb[:, 0:W], ALU.mult), reads=[gt, o_sb], writes=[gt])
        P.op('act', lambda e: e.activation(out_bf[:, 0:W], gt[:, 0:W], AF.Square, accum_out=sm[:, 0:1]), reads=[gt], writes=[out_bf, sm])
        P.op('act', lambda e: e.activation(sm[:, 16:17], sm[:, 0:1], AF.Ln, bias=EPS, scale=1.0 / W), reads=[sm], writes=[sm])
        P.op('act', lambda e: e.activation(sm[:, 32:33], sm[:, 16:17], AF.Exp, scale=-0.5), reads=[sm], writes=[sm])
        P.op('dve', lambda e: e.scalar_tensor_tensor(out_bf[:, 0:W], gt[:, 0:W], sm[:, 32:33], norm_tt[:, 0:W], ALU.mult, ALU.mult),
             reads=[gt, sm, norm_tt], writes=[out_bf])


def expo_bank(g, B, mask_off, rowT, colT, nheads=4, rtt=None):
    P = g.P
    rtt = rtt if rtt is not None else B.rT
    bank = nextbank(g)
    for h in range(nheads):
        o = bank.t[:, h * 128:(h + 1) * 128]
        esel = g.consts[0:4, C_ESEL + h * 128:C_ESEL + (h + 1) * 128]
        P.op('pe', lambda e: e.matmul(o, g.identb[:, :], g.maskb[:, g.maskb_idx[mask_off], :], start=True, stop=False), reads=[g.identb, g.maskb], writes=[bank], inc=False)
        P.op('pe', lambda e: e.matmul(o, esel, rowT, start=False, stop=False), reads=[g.consts, rtt], writes=[bank], inc=False)
        P.op('pe', lambda e: e.matmul(o, colT, esel, start=False, stop=True), reads=[g.consts, rtt], writes=[bank], inc=(h == nheads - 1))
    return bank


class Arena:
    def __init__(self, MS, name, n):
        self.tt = MS.sb(name, [128, n, 512])
        self.name = self.tt.key
        self.n = n

    def keys(self, i0, n):
        return tuple("%s_s%d" % (self.name, i) for i in range(i0, i0 + n))

    def slot(self, i, shape3=None):
        ap = self.tt.t[:, i, :]
        if shape3 is not None:
            ap = ap.rearrange("p (a b) -> p a b", a=shape3[0])
        return TT(ap, self.keys(i, 1))

    def span(self, i0, n, inner):
        ap = self.tt.t[:, i0:i0 + n, :].rearrange("p s (a b) -> p (s a) b", b=inner)
        return TT(ap, self.keys(i0, n))


def delta_mixer(g, L, MS, tiles, hT, is_s):
    P, l = g.P, L.l
    wpart, wo = L.WB, L.WO
    B = Ctx()
    A = Arena(MS, "arA", 5)
    Bc = Arena(MS, "arB", 3)
    G = Arena(MS, "arG", 7 if is_s else 6)
    M = Arena(MS, "arM", 3)
    Lt = Arena(MS, "arL", 3)
    B.cin = TT(A.tt.t[:, :, :].rearrange("p s f -> p (s f)")[:, 0:12 * 176].rearrange("p (c n) -> p c n", c=12), A.keys(0, 5))
    B.cout = Bc.span(0, 3, 128)
    B.acc = G.span(0, 3, 128)
    B.tmp = G.span(3, 3, 128)
    B.cvo = TT(M.tt.t[0:48, :, :].rearrange("p s f -> p (s f)"), M.keys(0, 3))
    B.rn = G.span(4, 2, 128)
    B.E = [M.slot(i, (4, 128)) for i in range(3)]
    B.PT = [G.slot(0, (4, 128)), G.slot(1, (4, 128))]
    B.X = [G.slot(2, (4, 128)), G.slot(3, (4, 128))]
    B.Y = [G.slot(4, (4, 128)), G.slot(5, (4, 128))]
    B.kbg, B.kd, B.QKm, B.bv = [A.slot(i, (4, 128)) for i in range(4)]
    B.nwT, B.vnew, B.qgT = [Bc.slot(i, (4, 128)) for i in range(3)]
    B.gate = Lt.slot(0)
    B.sq = Lt.slot(1)
    B.o = Lt.slot(2)
    B.onT = MS.sb("onT", [128, 8, 128], BF16)
    B.sm = MS.sb("sm", [128, 64])
    B.sm2 = MS.sb("sm2", [128, 64])
    B.rT = MS.sb("rT", [4, 512])
    B.qkn = MS.sb("qkn", [128, 8, 128])
    B.on = MS.sb("on", [128, 512], BF16)
    stc = None
    if is_s:
        stc = TT(Lt.tt.t[0:48, :, :].rearrange("p s f -> p (s f)"), Lt.keys(0, 3))
        P.dma('sp', stc[:, :], g.st_conv[l][:, 0:1536], "stc_d", writes=[stc])
        B.Sh = MS.sb("Sh", [128, 16, 128])
        B.padf = TT(G.tt.t[:, 2:7, :].rearrange("p s f -> p (s f)")[:, 0:17 * 128], G.keys(2, 5))
        B.egts = MS.sb("egts", [128, 16, 4])
    sm = B.sm
    for ti, t in enumerate(tiles):
        tok0 = ti * 128
        if ti == 0:
            g.minfree = min(getattr(g, 'minfree', 1 << 30), g.nc.sbuf_bytes_remaining)
        last_prompt = (not is_s) and (t == g.NPT - 1)
        conv_block(g, L, MS, B, wpart, hT, tok0, is_s, 0, L.tailD, stc, last_prompt, 0)
        cout = B.cout
        M_U01 = C_SU01 if is_s else C_U01
        M_SEG = C_SEG if is_s else C_ONES
        bS = nextbank(g)
        proj_tm(g, wpart, 2048, 8, hT, tok0, bS)
        bZ = nextbank(g)
        proj_tm(g, wpart, 1536, 512, hT, tok0, bZ)
        if ti == len(tiles) - 1:
            prefetch_next(g, L)
        gate_compute(g, B, [bZ], 512)
        P.op('dve', lambda e: e.tensor_tensor(sm[:, 40:44], bS.t[:, 0:4], L.parB[:, P_DN_DTB:P_DN_DTB + 4], ALU.add), reads=[bS, L.parB], writes=[sm])
        P.op('act', lambda e: e.activation(sm[:, 40:44], sm[:, 40:44], AF.Exp), reads=[sm], writes=[sm])
        P.op('act', lambda e: e.activation(sm[:, 44:48], bS.t[:, 4:8], AF.Exp, scale=-1.0), reads=[bS], writes=[sm])
        P.op('act', lambda e: e.activation(sm[:, 0:8], sm[:, 40:48], AF.Ln, bias=1.0), reads=[sm], writes=[sm])
        P.op('dve', lambda e: e.tensor_tensor(sm[:, 0:4], sm[:, 0:4], L.pd[:, 0:4], ALU.mult), reads=[sm, L.pd], writes=[sm])
        P.op('act', lambda e: e.activation(sm[:, 8:12], sm[:, 4:8], AF.Exp, scale=-1.0), reads=[sm], writes=[sm])
        bC = nextbank(g)
        P.op('pe', lambda e: e.matmul(bC.t[:, 0:4], cst(g, M_U01), sm[:, 0:4], start=True, stop=True), reads=[g.consts, sm], writes=[bC], inc=False)
        P.op('pe', lambda e: e.matmul(bC.t[:, 4:8], cst(g, M_SEG), sm[:, 0:4], start=True, stop=True), reads=[g.consts, sm], writes=[bC])
        P.op('dve', lambda e: e.tensor_copy(sm[:, 12:20], bC.t[:, 0:8]), reads=[bC], writes=[sm])
        P.op('dve', lambda e: e.tensor_tensor(sm[:, 20:24], sm[:, 12:16], sm[:, 4:8], ALU.subtract), reads=[sm], writes=[sm])
        P.op('dve', lambda e: e.tensor_scalar(sm[:, 24:28], sm[:, 12:16], -1.0, None, ALU.mult), reads=[sm], writes=[sm])
        P.op('dve', lambda e: e.tensor_tensor(sm[:, 32:36], sm[:, 16:20], sm[:, 12:16], ALU.subtract), reads=[sm], writes=[sm])
        P.op('act', lambda e: e.activation(sm[:, 28:32], sm[:, 12:16], AF.Exp), reads=[sm], writes=[sm])
        P.op('act', lambda e: e.activation(sm[:, 32:36], sm[:, 32:36], AF.Exp), reads=[sm], writes=[sm])
        P.op('dve', lambda e: e.tensor_tensor(sm[:, 36:40], sm[:, 8:12], sm[:, 28:32], ALU.mult), reads=[sm], writes=[sm])

        def bc(c0):
            return sm[:, c0:c0 + 4].unsqueeze(2).to_broadcast([128, 4, 128])
        bT = nextbank(g)
        for qi, c0 in enumerate((12, 20, 24)):
            P.op('pe', lambda e: e.transpose(bT.t[0:4, qi * 128:(qi + 1) * 128], sm[:, c0:c0 + 4], cst(g, C_IDENT)),
                 reads=[sm, g.consts], writes=[bT], inc=(qi == 2))
        P.op('dve', lambda e: e.tensor_copy(B.rT[:, 0:384], bT.t[0:4, 0:384]), reads=[bT], writes=[B.rT])
        gcT, r2T, ngcT = B.rT[:, 0:128], B.rT[:, 128:256], B.rT[:, 256:384]
        mNLS, mNUS, mNUI = (C_SNLS, C_SNUS, C_SNUI) if is_s else (C_NLS, C_NUS, C_NUI)
        for ei, (moff, rowT, colT) in enumerate(((mNLS, ngcT, r2T), (mNUS, r2T, ngcT), (mNUI, gcT, ngcT))):
            bank = expo_bank(g, B, moff, rowT, colT)
            P.op('act', lambda e: e.activation(B.E[ei][:, :, :], bank.t[:, :].rearrange("p (h n) -> p h n", h=4), AF.Exp), reads=[bank], writes=[B.E[ei]])
        conv_taps(g, L, B, is_s, 0)
        P.op('pool', lambda e: e.tensor_tensor(B.rn[:, :, :], cout[:, 0:8, :], cout[:, 0:8, :], ALU.mult), reads=[cout], writes=[B.rn])
        for hh in range(2):
            bank = nextbank(g)
            P.op('pe', lambda e: e.matmul(bank.t[:, :], cst(g, C_ONES), B.rn[:, hh * 4:hh * 4 + 4, :], start=True, stop=True),
                 reads=[g.consts, B.rn], writes=[bank])
            P.op('act', lambda e: e.activation(B.qkn[:, hh * 4:hh * 4 + 4, :], bank.t[:, :].rearrange("p (c n) -> p c n", c=4), AF.Ln, bias=EPS),
                 reads=[bank], writes=[B.qkn])
        P.op('act', lambda e: e.activation(B.rn[:, :, :], B.qkn[:, :, :], AF.Exp, scale=-0.5), reads=[B.qkn, B.rn], writes=[B.rn])
        P.op('dve', lambda e: e.tensor_tensor(B.qkn[:, :, :], cout[:, 0:8, :], B.rn[:, :, :], ALU.mult), reads=[cout, B.rn], writes=[B.qkn])
        bKT = nextbank(g)
        bVT = nextbank(g)
        for h in range(4):
            P.op('pe', lambda e: e.transpose(bKT.t[:, h * 128:(h + 1) * 128], B.qkn[:, 4 + h, :], cst(g, C_IDENT)), reads=[B.qkn, g.consts], writes=[bKT], inc=(h == 3))
        for h in range(4):
            P.op('pe', lambda e: e.transpose(bVT.t[:, h * 128:(h + 1) * 128], cout[:, 8 + h, :], cst(g, C_IDENT)), reads=[cout, g.consts], writes=[bVT], inc=(h == 3))
        kt3 = bKT.t[:, :].rearrange("p (h n) -> p h n", h=4)
        P.op('dve', lambda e: e.tensor_tensor(B.kbg[:, :, :], kt3, bc(36), ALU.mult), reads=[bKT, sm, B.cin], writes=[B.kbg])
        P.op('dve', lambda e: e.tensor_tensor(B.kd[:, :, :], kt3, bc(32), ALU.mult), reads=[bKT, sm], writes=[B.kd])
        P.op('dve', lambda e: e.tensor_tensor(B.bv[:, :, :], bVT.t[:, :].rearrange("p (h n) -> p h n", h=4), bc(8), ALU.mult), reads=[bVT, sm], writes=[B.bv])
        bKK = nextbank(g)
        bQK = nextbank(g)
        for h in range(4):
            P.op('pe', lambda e: e.matmul(bKK.t[:, h * 128:(h + 1) * 128], B.qkn[:, 4 + h, :], B.qkn[:, 4 + h, :], start=True, stop=True),
                 reads=[B.qkn], writes=[bKK], inc=(h == 3))
        for h in range(4):
            P.op('pe', lambda e: e.matmul(bQK.t[:, h * 128:(h + 1) * 128], B.qkn[:, 4 + h, :], B.qkn[:, h, :], start=True, stop=True),
                 reads=[B.qkn], writes=[bQK], inc=(h == 3))
        kk3 = bKK.t[:, :].rearrange("p (h n) -> p h n", h=4)
        P.op('dve', lambda e: e.scalar_tensor_tensor(B.X[0][:, :, :], B.E[0][:, :, :], -1.0, kk3, ALU.mult, ALU.mult), reads=[B.E[0], bKK], writes=[B.X[0]])
        P.op('dve', lambda e: e.scalar_tensor_tensor(B.Y[0][:, :, :], B.E[1][:, :, :], -1.0, kk3, ALU.mult, ALU.mult), reads=[B.E[1], bKK], writes=[B.Y[0]])
        P.op('dve', lambda e: e.scalar_tensor_tensor(B.QKm[:, :, :], B.E[2][:, :, :], 128.0 ** -0.5, bQK.t[:, :].rearrange("p (h n) -> p h n", h=4),
                                                     ALU.mult, ALU.mult), reads=[B.E[2], bQK], writes=[B.QKm])
        P.op('pool', lambda e: e.tensor_tensor(B.PT[0][:, :, :], B.Y[0][:, :, :], cst(g, C_IDENT).unsqueeze(1).to_broadcast([128, 4, 128]), ALU.add),
             reads=[B.Y[0], g.consts], writes=[B.PT[0]])
        nlev = 2 if is_s else 6
        cur = 0
        pcur = 0

        def emit_P(xi, pc):
            bP = nextbank(g)
            for h in range(4):
                P.op('pe', lambda e: e.matmul(bP.t[:, h * 128:(h + 1) * 128], B.X[xi][:, h, :], B.PT[pc][:, h, :], start=True, stop=True),
                     reads=[B.X[xi], B.PT[pc]], writes=[bP], inc=(h == 3))
            P.op('dve', lambda e: e.tensor_tensor(B.PT[1 - pc][:, :, :], bP.t[:, :].rearrange("p (h n) -> p h n", h=4), B.PT[pc][:, :, :], ALU.add),
                 reads=[bP, B.PT[pc]], writes=[B.PT[1 - pc]])
        pending = None
        for k in range(1, nlev + 1):
            nxt = 1 - cur
            bX = nextbank(g)
            for h in range(4):
                P.op('pe', lambda e: e.matmul(bX.t[:, h * 128:(h + 1) * 128], B.Y[cur][:, h, :], B.X[cur][:, h, :], start=True, stop=True),
                     reads=[B.X[cur], B.Y[cur]], writes=[bX], inc=(h == 3))
            if k < nlev:
                bY = nextbank(g)
                for h in range(4):
                    P.op('pe', lambda e: e.matmul(bY.t[:, h * 128:(h + 1) * 128], B.X[cur][:, h, :], B.Y[cur][:, h, :], start=True, stop=True),
                         reads=[B.X[cur], B.Y[cur]], writes=[bY], inc=(h == 3))
            if pending is not None:
                emit_P(*pending)
                pcur = 1 - pcur
            P.op('act', lambda e: e.activation(B.X[nxt][:, :, :], bX.t[:, :].rearrange("p (h n) -> p h n", h=4), AF.Copy), reads=[bX], writes=[B.X[nxt]])
            if k < nlev:
                P.op('dve', lambda e: e.tensor_copy(B.Y[nxt][:, :, :], bY.t[:, :].rearrange("p (h n) -> p h n", h=4)), reads=[bY], writes=[B.Y[nxt]])
            pending = (nxt, pcur)
            cur = nxt
        emit_P(*pending)
        pcur = 1 - pcur
        cur = pcur
        TTm = B.PT[cur]
        bW = nextbank(g)
        for h in range(4):
            P.op('pe', lambda e: e.matmul(bW.t[:, h * 128:(h + 1) * 128], B.kbg[:, h, :], TTm[:, h, :], start=True, stop=True),
                 reads=[B.kbg, TTm], writes=[bW], inc=(h == 3))
        P.op('act', lambda e: e.activation(B.nwT[:, :, :], bW.t[:, :].rearrange("p (h n) -> p h n", h=4), AF.Copy, scale=-1.0), reads=[bW, cout], writes=[B.nwT])
        bG = nextbank(g)
        for h in range(4):
            esel = g.consts[0:4, C_ESEL + h * 128:C_ESEL + (h + 1) * 128]
            P.op('pe', lambda e: e.matmul(bG.t[:, h * 128:(h + 1) * 128], esel, gcT, start=True, stop=True), reads=[g.consts, B.rT], writes=[bG], inc=(h == 3))
        P.op('act', lambda e: e.activation(B.qgT[:, :, :], bG.t[:, :].rearrange("p (h n) -> p h n", h=4), AF.Exp), reads=[bG], writes=[B.qgT])
        P.op('dve', lambda e: e.scalar_tensor_tensor(B.qgT[:, :, :], B.qkn[:, 0:4, :], 128.0 ** -0.5, B.qgT[:, :, :], ALU.mult, ALU.mult),
             reads=[B.qkn, B.qgT], writes=[B.qgT])
        if not is_s:
            P.op('act', lambda e: e.activation(sm[:, 48:52], sm[:, 16:20], AF.Exp), reads=[sm], writes=[sm])
        else:
            gtv = B.sm2[:, 0:64].rearrange("p (s h) -> p s h", h=4)
            P.op('dve', lambda e: e.tensor_tensor(gtv, sm[:, 16:20].unsqueeze(1).to_broadcast([128, 16, 4]),
                                                  g.consts[:, C_ROWM8:C_ROWM8 + 16].unsqueeze(2).to_broadcast([128, 16, 4]), ALU.mult),
                 reads=[sm, g.consts], writes=[B.sm2])
            bE = nextbank(g)
            P.op('pe', lambda e: e.matmul(bE.t[:, 0:64], cst(g, C_ONES), B.sm2[:, 0:64], start=True, stop=True), reads=[g.consts, B.sm2], writes=[bE])
            P.op('act', lambda e: e.activation(B.egts[:, :, :], bE.t[:, 0:64].rearrange("p (s h) -> p s h", h=4), AF.Exp), reads=[bE], writes=[B.egts])
        bVN = nextbank(g, hold=True)
        bO = nextbank(g, hold=True)
        hgroups = [[0, 1, 2, 3]] if not is_s else [[0], [1], [2], [3]]
        padv = None
        if is_s:
            padd = B.padf.t[:, 0:16 * 136].rearrange("p (s q) -> p s q", q=136)[:, :, 0:8]
            padr = B.padf.t[:, 0:2048].rearrange("p (s n) -> p s n", s=16)
        for hs in hgroups:
            h0, nh = hs[0], len(hs)
            if is_s:
                h = h0
                P.dma('sp', B.Sh[:, :, :], g.st_delta[l][:, h, :, :].rearrange("s d e -> d s e"), "Sh_d", writes=[B.Sh])

                def padcol(src):
                    P.op('pool', lambda e: e.memset(B.padf[:, :], 0.0), writes=[B.padf])
                    P.op('pool', lambda e: e.tensor_copy(padd, src.rearrange("p (s r) -> p s r", r=8)), reads=[B.nwT, B.qgT, B.padf], writes=[B.padf])
            for h in hs:
                o = bVN.t[:, h * 128:(h + 1) * 128]
                P.op('pe', lambda e: e.matmul(o, TTm[:, h, :], B.bv[:, h, :], start=True, stop=False), reads=[TTm, B.bv], writes=[bVN], inc=False)
                if not is_s:
                    P.op('pe', lambda e: e.matmul(o, B.nwT[:, h, :], L.Sd[:, h, :], start=False, stop=True), reads=[B.nwT, L.Sd], writes=[bVN], inc=(h == hs[-1]))
                else:
                    padcol(B.nwT[:, h, :])
                    for s_ in range(16):
                        P.op('pe', lambda e: e.matmul(o, B.padf[:, s_ * 128:(s_ + 1) * 128], B.Sh[:, s_, :], start=False, stop=(s_ == 15)),
                             reads=[B.padf, B.Sh], writes=[bVN], inc=(s_ == 15))
            P.op('act', lambda e: e.activation(B.vnew[:, h0:h0 + nh, :], bVN.t[:, h0 * 128:(h0 + nh) * 128].rearrange("p (h n) -> p h n", h=nh), AF.Copy),
                 reads=[bVN], writes=[B.vnew])
            for h in hs:
                o = bO.t[:, h * 128:(h + 1) * 128]
                P.op('pe', lambda e: e.matmul(o, B.QKm[:, h, :], B.vnew[:, h, :], start=True, stop=False), reads=[B.QKm, B.vnew], writes=[bO], inc=False)
                if not is_s:
                    P.op('pe', lambda e: e.matmul(o, B.qgT[:, h, :], L.Sd[:, h, :], start=False, stop=True), reads=[B.qgT, L.Sd], writes=[bO], inc=(h == hs[-1]))
                else:
                    padcol(B.qgT[:, h, :])
                    for s_ in range(16):
                        P.op('pe', lambda e: e.matmul(o, B.padf[:, s_ * 128:(s_ + 1) * 128], B.Sh[:, s_, :], start=False, stop=(s_ == 15)),
                             reads=[B.padf, B.Sh], writes=[bO], inc=(s_ == 15))
            if not is_s:
                bSu = nextbank(g)
                for h in hs:
                    P.op('pe', lambda e: e.matmul(bSu.t[:, h * 128:(h + 1) * 128], B.kd[:, h, :], B.vnew[:, h, :], start=True, stop=True),
                         reads=[B.kd, B.vnew], writes=[bSu], inc=(h == 3))
                P.op('dve', lambda e: e.tensor_tensor(L.Sd[:, :, :], L.Sd[:, :, :], sm[:, 48:52].unsqueeze(2).to_broadcast([128, 4, 128]), ALU.mult),
                     reads=[L.Sd, sm], writes=[L.Sd])
                P.op('dve', lambda e: e.tensor_tensor(L.Sd[:, :, :], L.Sd[:, :, :], bSu.t[:, :].rearrange("p (h n) -> p h n", h=4), ALU.add),
                     reads=[L.Sd, bSu], writes=[L.Sd])
                if last_prompt:
                    P.dma('sp', g.o_delta_p[l].rearrange("h d e -> d h e"), L.Sd[:, :, :], L.Sd.key + "_o", reads=[L.Sd])
            else:
                h = h0
                P.op('pool', lambda e: e.tensor_tensor(padr, B.kd[:, h, :].unsqueeze(1).to_broadcast([128, 16, 128]),
                                                       g.consts[:, C_ROWM:C_ROWM + 16].unsqueeze(2).to_broadcast([128, 16, 128]), ALU.mult),
                     reads=[B.kd, g.consts], writes=[B.padf])
                for s4 in range(4):
                    bSu = nextbank(g)
                    for si in range(4):
                        s_ = s4 * 4 + si
                        P.op('pe', lambda e: e.matmul(bSu.t[:, si * 128:(si + 1) * 128], B.padf[:, s_ * 128:(s_ + 1) * 128], B.vnew[:, h, :], start=True, stop=True),
                             reads=[B.padf, B.vnew], writes=[bSu], inc=(si == 3))
                    sl = B.Sh[:, s4 * 4:s4 * 4 + 4, :]
                    P.op('dve', lambda e: e.tensor_tensor(sl, sl, B.egts[:, s4 * 4:s4 * 4 + 4, h:h + 1].to_broadcast([128, 4, 128]), ALU.mult),
                         reads=[B.Sh, B.egts], writes=[B.Sh])
                    P.op('dve', lambda e: e.tensor_tensor(sl, sl, bSu.t[:, :].rearrange("p (s n) -> p s n", s=4), ALU.add),
                         reads=[B.Sh, bSu], writes=[B.Sh])
                P.dma('sp', g.o_delta_s[l][:, h, :, :].rearrange("s d e -> d s e"), B.Sh[:, :, :], "Sh_o", reads=[B.Sh])
        P.op('act', lambda e: e.activation(B.o[:, :], bO.t[:, :], AF.Copy), reads=[bO], writes=[B.o])
        release(g, bVN, bO)
        gated_norm(g, L, B, B.o, 4, 128, P_DN_NORM, True, B.on)
        out_proj(g, MS, B, B.on, 4, wo, t)


def gla_mixer(g, L, MS, tiles, hT, is_s):
    import os
    DBG = int(os.environ.get('DBG_GLA', '99'))
    P, l = g.P, L.l
    wpart, wo = L.WB, L.WO
    B = Ctx()
    B.v = MS.sb("v", [128, 512])
    B.gk = MS.sb("gk", [128, 256])
    B.gc = MS.sb("gc", [128, 256])
    B.Eq = MS.sb("Eq", [128, 2, 128])
    B.Ek = MS.sb("Ek", [128, 2, 128])
    B.qt = MS.sb("qt", [128, 2, 128])
    B.kt = MS.sb("kt", [128, 2, 128])
    B.qz = MS.sb("qz", [128, 4, 128])
    P.op('pool', lambda e: e.memset(B.qz[:, :, :], 0.0), writes=[B.qz])
    B.PTm = MS.sb("PTm", [128, 4, 128])
    B.ktm = MS.sb("ktm", [128, 256])
    B.glT = MS.sb("glT", [16, 128])
    B.o = MS.sb("o", [128, 512])
    B.gate = MS.sb("gate", [128, 512])
    B.sq = MS.sb("sq", [128, 512])
    B.on = MS.sb("on", [128, 512], BF16)
    B.onT = MS.sb("onT", [128, 8, 128], BF16)
    B.sm2 = MS.sb("sm2", [128, 64])
    if is_s:
        B.Sh = MS.sb("Sh", [128, 16, 128])
        B.padf = MS.sb("pad", [128, 17 * 128])
        padd = B.padf.t[:, 0:16 * 136].rearrange("p (s q) -> p s q", q=136)[:, :, 0:8]
        padr = B.padf.t[:, 0:2048].rearrange("p (s n) -> p s n", s=16)
    M_U01 = C_SU01 if is_s else C_U01
    for ti, t in enumerate(tiles):
        tok0 = ti * 128
        last_prompt = (not is_s) and (t == g.NPT - 1)
        bFM = nextbank(g)
        for c in range(4):
            proj_fm(g, wpart, c * 128, 128, hT, tok0, bFM, c)
        bGL = nextbank(g)
        proj_fm(g, wpart, 1024, 16, hT, tok0, bGL, 0)
        P.op('dve', lambda e: e.tensor_copy(B.glT[:, :], bGL.t[0:16, 0:128]), reads=[bGL], writes=[B.glT])
        bV = nextbank(g)
        proj_tm(g, wpart, 512, 512, hT, tok0, bV)
        P.op('act', lambda e: e.activation(B.v[:, :], bV.t[:, :], AF.Copy), reads=[bV], writes=[B.v])
        bZ = nextbank(g)
        proj_tm(g, wpart, 1040, 512, hT, tok0, bZ)
        if ti == len(tiles) - 1:
            prefetch_next(g, L)
        gate_compute(g, B, [bZ], 512)
        if DBG <= 1:
            g.held.clear()
            continue
        bK = nextbank(g)
        P.op('pe', lambda e: e.matmul(bK.t[:, 0:256], B.glT[:, :], L.wup[:, :], start=True, stop=False), reads=[B.glT, L.wup], writes=[bK], inc=False)
        P.op('pe', lambda e: e.matmul(bK.t[:, 0:256], g.consts[0:1, C_ONES:C_ONES + 128], L.bup[:, :], start=False, stop=True),
             reads=[g.consts, L.bup], writes=[bK])
        P.op('act', lambda e: e.activation(B.gk[:, :], bK.t[:, 0:256], AF.Exp, scale=-1.0), reads=[bK], writes=[B.gk])
        P.op('act', lambda e: e.activation(B.gk[:, :], B.gk[:, :], AF.Ln, bias=1.0), reads=[B.gk], writes=[B.gk])
        if DBG <= 2:
            g.held.clear()
            continue
        bC = nextbank(g)
        P.op('pe', lambda e: e.matmul(bC.t[:, 0:256], cst(g, M_U01), B.gk[:, :], start=True, stop=True), reads=[g.consts, B.gk], writes=[bC])
        P.op('dve', lambda e: e.tensor_scalar(B.gc[:, :], bC.t[:, 0:256], -1.0 / 16.0, None, ALU.mult), reads=[bC], writes=[B.gc])
        bT = nextbank(g)
        for c in range(2):
            P.op('pe', lambda e: e.transpose(bT.t[:, c * 128:(c + 1) * 128], B.gc[:, c * 128:(c + 1) * 128], cst(g, C_IDENT)),
                 reads=[B.gc, g.consts], writes=[bT], inc=(c == 1))
        gT3 = bT.t[:, 0:256].rearrange("p (c n) -> p c n", c=2)
        P.op('act', lambda e: e.activation(B.Eq[:, :, :], gT3, AF.Exp), reads=[bT], writes=[B.Eq])
        P.op('act', lambda e: e.activation(B.Ek[:, :, :], gT3, AF.Exp, scale=-1.0), reads=[bT], writes=[B.Ek])
        fm3 = bFM.t[:, :].rearrange("p (c n) -> p c n", c=4)
        P.op('dve', lambda e: e.scalar_tensor_tensor(B.qt[:, :, :], fm3[:, 0:2, :], 64.0 ** -0.5, B.Eq[:, :, :], ALU.mult, ALU.mult),
             reads=[bFM, B.Eq], writes=[B.qt])
        P.op('dve', lambda e: e.tensor_tensor(B.kt[:, :, :], fm3[:, 2:4, :], B.Ek[:, :, :], ALU.mult), reads=[bFM, B.Ek], writes=[B.kt])
        if DBG <= 3:
            g.held.clear()
            continue
        qz4 = B.qz.t[:, :, :].rearrange("p (hh hl) n -> p hl hh n", hl=2)
        P.op('dve', lambda e: e.tensor_copy(qz4[0:64, 0, :, :], B.qt[0:64, :, :]), reads=[B.qt], writes=[B.qz])
        P.op('dve', lambda e: e.tensor_copy(qz4[64:128, 1, :, :], B.qt[64:128, :, :]), reads=[B.qt], writes=[B.qz])
        bSc = nextbank(g)
        for h in range(4):
            hl, hh = h % 2, h // 2
            P.op('pe', lambda e: e.matmul(bSc.t[:, h * 128:(h + 1) * 128], B.kt[:, hh, :], B.qz[:, h, :], start=True, stop=True),
                 reads=[B.kt, B.qz], writes=[bSc], inc=(h == 3))
        P.op('dve', lambda e: e.tensor_tensor(B.PTm[:, :, :], bSc.t[:, :].rearrange("p (h n) -> p h n", h=4),
                                              cst(g, M_U01).unsqueeze(1).to_broadcast([128, 4, 128]), ALU.mult), reads=[bSc, g.consts], writes=[B.PTm])
        bKT = nextbank(g)
        for c in range(2):
            P.op('pe', lambda e: e.transpose(bKT.t[:, c * 128:(c + 1) * 128], B.kt[:, c, :], cst(g, C_IDENT)), reads=[B.kt, g.consts], writes=[bKT], inc=(c == 1))
        P.op('act', lambda e: e.activation(B.ktm[:, :], bKT.t[:, 0:256], AF.Copy), reads=[bKT], writes=[B.ktm])
        if DBG <= 4:
            g.held.clear()
            continue
        bO = nextbank(g, hold=True)
        for hh in range(2):
            if is_s:
                P.dma('sp', B.Sh[:, :, :], g.st_gla[l][:, 2 * hh:2 * hh + 2, :, :].rearrange("s hl d e -> (hl d) s e"), "Shg_d", writes=[B.Sh])
            for hl in range(2):
                h = 2 * hh + hl
                r = slice(hl * 64, hl * 64 + 64)
                o = bO.t[:, h * 128:(h + 1) * 128]
                P.op('pe', lambda e: e.matmul(o, B.PTm[:, h, :], B.v[:, h * 128:(h + 1) * 128], start=True, stop=False), reads=[B.PTm, B.v], writes=[bO], inc=False)
                if not is_s:
                    P.op('pe', lambda e: e.matmul(o, B.qz[:, h, :], L.Sg[:, hh, :], start=False, stop=True), reads=[B.qz, L.Sg], writes=[bO], inc=True)
                else:
                    P.op('pool', lambda e: e.memset(B.padf[:, :], 0.0), writes=[B.padf])
                    P.op('pool', lambda e: e.tensor_copy(padd, B.qz[:, h, :].rearrange("p (s r) -> p s r", r=8)), reads=[B.qz, B.padf], writes=[B.padf])
                    for s_ in range(16):
                        P.op('pe', lambda e: e.matmul(o, B.padf[:, s_ * 128:(s_ + 1) * 128], B.Sh[:, s_, :], start=False, stop=(s_ == 15)),
                             reads=[B.padf, B.Sh], writes=[bO], inc=(s_ == 15))
            if not is_s:
                bSu = nextbank(g)
                P.op('pe', lambda e: e.matmul(bSu.t[:, 0:256], B.ktm[:, hh * 128:(hh + 1) * 128], B.v[:, hh * 256:(hh + 1) * 256], start=True, stop=True),
                     reads=[B.ktm, B.v], writes=[bSu])
                for hl in range(2):
                    r = slice(hl * 64, hl * 64 + 64)
                    P.op('dve', lambda e: e.tensor_tensor(L.Sg[r, hh, :], L.Sg[r, hh, :], bSu.t[r, hl * 128:(hl + 1) * 128], ALU.add), reads=[L.Sg, bSu], writes=[L.Sg])
                    P.op('dve', lambda e: e.tensor_scalar(L.Sg[r, hh, :], L.Sg[r, hh, :], B.Eq[r, hh, 127:128], None, ALU.mult), reads=[L.Sg, B.Eq], writes=[L.Sg])
            else:
                P.op('pool', lambda e: e.tensor_tensor(padr, B.ktm[:, hh * 128:(hh + 1) * 128].unsqueeze(1).to_broadcast([128, 16, 128]),
                                                       g.consts[:, C_ROWM:C_ROWM + 16].unsqueeze(2).to_broadcast([128, 16, 128]), ALU.mult),
                     reads=[B.ktm, g.consts], writes=[B.padf])
                for s2 in range(8):
                    bSu = nextbank(g)
                    for si in range(2):
                        s_ = s2 * 2 + si
                        P.op('pe', lambda e: e.matmul(bSu.t[:, si * 256:(si + 1) * 256], B.padf[:, s_ * 128:(s_ + 1) * 128], B.v[:, hh * 256:(hh + 1) * 256],
                                                      start=True, stop=True), reads=[B.padf, B.v], writes=[bSu], inc=(si == 1))
                    for hl in range(2):
                        r = slice(hl * 64, hl * 64 + 64)
                        sl = B.Sh[r, 2 * s2:2 * s2 + 2, :]
                        ps_ = bSu.t[r, :].rearrange("p (s c) -> p s c", s=2)[:, :, hl * 128:(hl + 1) * 128]
                        P.op('dve', lambda e: e.tensor_tensor(sl, sl, ps_, ALU.add), reads=[B.Sh, bSu], writes=[B.Sh])
                        egl = B.Eq[r, hh, :].rearrange("p (s r) -> p s r", r=8)[:, 2 * s2:2 * s2 + 2, 7:8].to_broadcast([64, 2, 128])
                        P.op('dve', lambda e: e.tensor_tensor(sl, sl, egl, ALU.mult), reads=[B.Sh, B.Eq], writes=[B.Sh])
                P.dma('sp', g.o_gla_s[l][:, 2 * hh:2 * hh + 2, :, :].rearrange("s hl d e -> (hl d) s e"), B.Sh[:, :, :], "Shg_o", reads=[B.Sh])
        if last_prompt:
            P.dma('sp', g.o_gla_p[l].rearrange("(hh hl) d e -> (hl d) hh e", hl=2), L.Sg[:, :, :], "Sg_o", reads=[L.Sg])
        if DBG <= 5:
            g.held.clear()
            continue
        P.op('act', lambda e: e.activation(B.o[:, :], bO.t[:, :], AF.Copy), reads=[bO], writes=[B.o])
        release(g, bO)
        gated_norm(g, L, B, B.o, 4, 128, P_GLA_NORM, True, B.on)
        out_proj(g, MS, B, B.on, 4, wo, t)


def ssd_mixer(g, L, MS, tiles, hT, is_s):
    P, l = g.P, L.l
    wpart, wo = L.WB, L.WO
    B = Ctx()
    A = Arena(MS, "sA", 5)
    Bc = Arena(MS, "sB", 3)
    G = Arena(MS, "sG", 6)
    Lt = Arena(MS, "sL", 4)
    B.cin = TT(A.tt.t[:, :, :].rearrange("p s f -> p (s f)")[:, 0:12 * 176].rearrange("p (c n) -> p c n", c=12), A.keys(0, 5))
    B.cout = Bc.span(0, 3, 128)
    B.acc = G.span(0, 3, 128)
    B.tmp = G.span(3, 3, 128)
    B.cvo = TT(Lt.tt.t[0:48, 0:3, :].rearrange("p s f -> p (s f)"), Lt.keys(0, 3))
    x_tm = TT(A.tt.t[:, 0:2, :].rearrange("p s f -> p (s f)"), A.keys(0, 2))
    xdt = TT(A.tt.t[:, 2:4, :].rearrange("p s f -> p (s f)"), A.keys(2, 2))
    xw = TT(Bc.tt.t[:, 0:2, :].rearrange("p s f -> p (s f)"), Bc.keys(0, 2))
    ysb = TT(G.tt.t[:, 0:2, :].rearrange("p s f -> p (s f)"), G.keys(0, 2))
    MTs = [G.slot(2, (4, 128)), G.slot(5, (4, 128))]
    ETs = [G.slot(3, (4, 128)), A.slot(4, (4, 128))]
    rTs = [MS.sb("rTa", [4, 256]), MS.sb("rTb", [4, 256])]
    cbT = TT(G.tt.t[:, 4, 0:256].rearrange("p (c n) -> p c n", c=2), G.keys(4, 1))
    Btm = TT(G.tt.t[:, 4, 256:512], G.keys(4, 1))
    B.gate = TT(Lt.tt.t[:, 0:2, :].rearrange("p s f -> p (s f)"), Lt.keys(0, 2))
    gBm = TT(Lt.tt.t[:, 2:4, :].rearrange("p s f -> p (s f)"), Lt.keys(2, 2))
    B.on = MS.sb("on", [128, 1024], BF16)
    B.onT = MS.sb("onT", [128, 8, 128], BF16)
    B.sm2 = MS.sb("sm2", [128, 64])
    sm = MS.sb("sm", [128, 160])
    stc = None
    if is_s:
        stc = TT(G.tt.t[0:48, 0:3, :].rearrange("p s f -> p (s f)"), G.keys(0, 3))
        P.dma('sp', stc[:, :], g.st_conv[l][:, 1536:3072], "stcs_d", writes=[stc])
        B.Sh = [MS.sb("Sh0", [128, 8, 64]), MS.sb("Sh1", [128, 8, 64])]
        B.padf = MS.sb("pad", [128, 17 * 128])
        padd = B.padf.t[:, 0:16 * 136].rearrange("p (s q) -> p s q", q=136)[:, :, 0:8]
        padr = B.padf.t[:, 0:2048].rearrange("p (s n) -> p s n", s=16)
        B.egts = MS.sb("egts", [128, 16, 16])
    M_U01 = C_SU01 if is_s else C_U01
    M_SEG = C_SEG if is_s else C_ONES
    mNUI = C_SNUI if is_s else C_NUI
    nsh = 0
    for ti, t in enumerate(tiles):
        tok0 = ti * 128
        last_prompt = (not is_s) and (t == g.NPT - 1)
        if ti == 0:
            g.minfree = min(getattr(g, 'minfree', 1 << 30), g.nc.sbuf_bytes_remaining)
        conv_block(g, L, MS, B, wpart, hT, tok0, is_s, 12, L.tailS, stc, last_prompt, 1536)
        cout = B.cout
        bD = nextbank(g)
        proj_tm(g, wpart, 2560, 16, hT, tok0, bD)
        bZ = [nextbank(g), nextbank(g)]
        proj_tm(g, wpart, 1536, 512, hT, tok0, bZ[0])
        proj_tm(g, wpart, 2048, 512, hT, tok0, bZ[1])
        if ti == len(tiles) - 1:
            prefetch_next(g, L)
        gate_compute(g, B, bZ, 1024)
        P.op('dve', lambda e: e.tensor_tensor(sm[:, 128:144], bD.t[:, 0:16], L.parB[:, P_M2_DTB:P_M2_DTB + 16], ALU.add), reads=[bD, L.parB], writes=[sm])
        P.op('act', lambda e: e.activation(sm[:, 128:144], sm[:, 128:144], AF.Exp), reads=[sm], writes=[sm])
        P.op('act', lambda e: e.activation(sm[:, 0:16], sm[:, 128:144], AF.Ln, bias=1.0), reads=[sm], writes=[sm])
        P.op('dve', lambda e: e.tensor_tensor(sm[:, 16:32], sm[:, 0:16], L.pd[:, 4:20], ALU.mult), reads=[sm, L.pd], writes=[sm])
        bC = nextbank(g)
        P.op('pe', lambda e: e.matmul(bC.t[:, 0:16], cst(g, M_U01), sm[:, 16:32], start=True, stop=True), reads=[g.consts, sm], writes=[bC], inc=False)
        P.op('pe', lambda e: e.matmul(bC.t[:, 16:32], cst(g, M_SEG), sm[:, 16:32], start=True, stop=True), reads=[g.consts, sm], writes=[bC])
        P.op('dve', lambda e: e.tensor_copy(sm[:, 32:64], bC.t[:, 0:32]), reads=[bC], writes=[sm])
        P.op('dve', lambda e: e.tensor_scalar(sm[:, 64:80], sm[:, 32:48], -1.0, None, ALU.mult), reads=[sm], writes=[sm])
        P.op('act', lambda e: e.activation(sm[:, 80:96], sm[:, 32:48], AF.Exp), reads=[sm], writes=[sm])
        P.op('dve', lambda e: e.tensor_tensor(sm[:, 96:112], sm[:, 48:64], sm[:, 32:48], ALU.subtract), reads=[sm], writes=[sm])
        P.op('act', lambda e: e.activation(sm[:, 96:112], sm[:, 96:112], AF.Exp), reads=[sm], writes=[sm])
        P.op('act', lambda e: e.activation(sm[:, 112:128], sm[:, 48:64], AF.Exp), reads=[sm], writes=[sm])

        def bc16(c0, g_=None):
            if g_ is None:
                return sm[:, c0:c0 + 16].unsqueeze(2).to_broadcast([128, 16, 64])
            return sm[:, c0 + 8 * g_:c0 + 8 * g_ + 8].unsqueeze(2).to_broadcast([128, 8, 64])
        conv_taps(g, L, B, is_s, 12, split=True)
        for hb in range(2):
            bX = nextbank(g)
            for c in range(4):
                P.op('pe', lambda e: e.transpose(bX.t[:, c * 128:(c + 1) * 128], cout[:, hb * 4 + c, :], cst(g, C_IDENT)), reads=[cout, g.consts], writes=[bX], inc=(c == 3))
            P.op('act', lambda e: e.activation(x_tm[:, hb * 512:(hb + 1) * 512], bX.t[:, :], AF.Copy), reads=[bX, B.cin], writes=[x_tm])
        x3 = x_tm.t.rearrange("p (h e) -> p h e", h=16)
        xdt3 = xdt.t.rearrange("p (h e) -> p h e", h=16)
        xw3 = xw.t.rearrange("p (h e) -> p h e", h=16)
        P.op('dve', lambda e: e.tensor_tensor(xdt3, x3, bc16(0), ALU.mult), reads=[x_tm, sm], writes=[xdt])
        P.op('pool', lambda e: e.tensor_tensor(xw3, xdt3, bc16(96), ALU.mult), reads=[xdt, sm, cout], writes=[xw])
        bB = nextbank(g)
        for g_ in range(2):
            P.op('pe', lambda e: e.transpose(bB.t[:, g_ * 128:(g_ + 1) * 128], cout[:, 8 + g_, :], cst(g, C_IDENT)), reads=[cout, g.consts], writes=[bB], inc=False)
        for g_ in range(2):
            P.op('pe', lambda e: e.matmul(bB.t[:, 256 + g_ * 128:256 + (g_ + 1) * 128], cout[:, 8 + g_, :], cout[:, 10 + g_, :], start=True, stop=True),
                 reads=[cout], writes=[bB], inc=(g_ == 1))
        P.op('act', lambda e: e.activation(Btm[:, :], bB.t[:, 0:256], AF.Copy), reads=[bB, B.acc, B.tmp], writes=[Btm])
        P.op('act', lambda e: e.activation(cbT[:, :, :], bB.t[:, 256:512].rearrange("p (c n) -> p c n", c=2), AF.Copy), reads=[bB], writes=[cbT])
        bY = [nextbank(g, hold=True), nextbank(g, hold=True)]
        def stageA(q):
            MT, ET, rT = MTs[q % 2], ETs[q % 2], rTs[q % 2]
            bT = nextbank(g)
            P.op('pe', lambda e: e.transpose(bT.t[0:4, 0:128], sm[:, 32 + 4 * q:36 + 4 * q], cst(g, C_IDENT)), reads=[sm, g.consts], writes=[bT], inc=False)
            P.op('pe', lambda e: e.transpose(bT.t[0:4, 128:256], sm[:, 64 + 4 * q:68 + 4 * q], cst(g, C_IDENT)), reads=[sm, g.consts], writes=[bT])
            P.op('dve', lambda e: e.tensor_copy(rT[:, 0:256], bT.t[0:4, 0:256]), reads=[bT], writes=[rT])
            bank = expo_bank(g, B, mNUI, rT[:, 0:128], rT[:, 128:256], rtt=rT)
            P.op('act', lambda e: e.activation(ET[:, :, :], bank.t[:, :].rearrange("p (h n) -> p h n", h=4), AF.Exp), reads=[bank], writes=[ET])
            g_ = q // 2
            P.op('dve', lambda e: e.tensor_tensor(MT[:, :, :], ET[:, :, :], cbT[:, g_, :].unsqueeze(1).to_broadcast([128, 4, 128]), ALU.mult),
                 reads=[ET, cbT], writes=[MT])

        def stageB(q):
            MT = MTs[q % 2]
            g_ = q // 2
            for hq in range(4):
                h = 4 * q + hq
                P.op('pe', lambda e: e.matmul(bY[g_].t[:, (h % 8) * 64:(h % 8) * 64 + 64], MT[:, hq, :], xdt[:, h * 64:(h + 1) * 64], start=True, stop=True),
                     reads=[MT, xdt], writes=[bY[g_]], inc=(hq == 3))
        stageA(0)
        for q in range(1, 4):
            stageA(q)
            stageB(q - 1)
        stageB(3)
        bYI = [nextbank(g, hold=True), nextbank(g, hold=True)]
        if not is_s:
            for g_ in range(2):
                P.op('pe', lambda e: e.matmul(bYI[g_].t[:, :], cout[:, 10 + g_, :], L.Ss.t[:, :, :].rearrange("p h e -> p (h e)")[:, g_ * 512:(g_ + 1) * 512], start=True, stop=True),
                     reads=[cout, L.Ss], writes=[bYI[g_]])
        else:
            lt3 = B.padf.t[:, 0:256].rearrange("p (s h) -> p s h", h=16)
            P.op('dve', lambda e: e.tensor_tensor(lt3, sm[:, 48:64].unsqueeze(1).to_broadcast([128, 16, 16]),
                                                  g.consts[:, C_ROWM8:C_ROWM8 + 16].unsqueeze(2).to_broadcast([128, 16, 16]), ALU.mult),
                 reads=[sm, g.consts], writes=[B.padf])
            bE = nextbank(g)
            P.op('pe', lambda e: e.matmul(bE.t[:, 0:256], cst(g, C_ONES), B.padf[:, 0:256], start=True, stop=True), reads=[g.consts, B.padf], writes=[bE])
            P.op('act', lambda e: e.activation(B.egts[:, :, :], bE.t[:, 0:256].rearrange("p (s h) -> p s h", h=16), AF.Exp), reads=[bE], writes=[B.egts])
            for g_ in range(2):
                P.op('pool', lambda e: e.memset(B.padf[:, :], 0.0), writes=[B.padf])
                P.op('pool', lambda e: e.tensor_copy(padd, cout[:, 10 + g_, :].rearrange("p (s r) -> p s r", r=8)), reads=[cout, B.padf], writes=[B.padf])
                for s_ in range(16):
                    Sh = B.Sh[nsh % 2]
                    nsh += 1
                    P.dma('sp', Sh[:, :, :], g.st_ssm[l][s_, 8 * g_:8 * g_ + 8, :, :].rearrange("h n p -> n h p"), "Shs%d_d" % ((nsh - 1) % 2), writes=[Sh])
                    P.op('pe', lambda e: e.matmul(bYI[g_].t[:, :], B.padf[:, s_ * 128:(s_ + 1) * 128], Sh.t[:, :, :].rearrange("p h e -> p (h e)"), start=(s_ == 0), stop=(s_ == 15)),
                         reads=[B.padf, Sh], writes=[bYI[g_]], inc=True)
        y3 = ysb.t.rearrange("p (h e) -> p h e", h=16)
        for g_ in range(2):
            P.op('dve', lambda e: e.tensor_tensor(y3[:, 8 * g_:8 * g_ + 8, :], bYI[g_].t[:, :].rearrange("p (h e) -> p h e", h=8), bc16(80, g_), ALU.mult),
                 reads=[bYI[g_], sm], writes=[ysb])
            P.op('dve', lambda e: e.tensor_tensor(ysb[:, g_ * 512:(g_ + 1) * 512], ysb[:, g_ * 512:(g_ + 1) * 512], bY[g_].t[:, :], ALU.add),
                 reads=[ysb, bY[g_]], writes=[ysb])
        release(g, bY[0], bY[1], bYI[0], bYI[1])
        P.op('pool', lambda e: e.tensor_tensor(xdt3, x3, L.parB[:, P_M2_D:P_M2_D + 16].unsqueeze(2).to_broadcast([128, 16, 64]), ALU.mult),
             reads=[x_tm, L.parB, xdt], writes=[xdt])
        P.op('dve', lambda e: e.tensor_tensor(ysb[:, :], ysb[:, :], xdt[:, :], ALU.add), reads=[ysb, xdt], writes=[ysb])
        P.dma('sp', gBm[:, :], g.norms_d[3 * g.DEPTH + 1 + l:3 * g.DEPTH + 2 + l, :].partition_broadcast(128), "gBm_d", reads=[B.cvo], writes=[gBm])
        gated_norm(g, L, B, ysb, 16, 64, 0, False, B.on, norm_tt=gBm)
        out_proj(g, MS, B, B.on, 8, wo, t)
        if not is_s:
            for g_ in range(2):
                bSu = nextbank(g)
                P.op('pe', lambda e: e.matmul(bSu.t[:, :], Btm[:, g_ * 128:(g_ + 1) * 128], xw[:, g_ * 512:(g_ + 1) * 512], start=True, stop=True),
                     reads=[Btm, xw], writes=[bSu])
                sl = L.Ss[:, 8 * g_:8 * g_ + 8, :]
                P.op('dve', lambda e: e.tensor_tensor(sl, sl, bc16(112, g_), ALU.mult), reads=[L.Ss, sm], writes=[L.Ss])
                P.op('dve', lambda e: e.tensor_tensor(sl, sl, bSu.t[:, :].rearrange("p (h e) -> p h e", h=8), ALU.add), reads=[L.Ss, bSu], writes=[L.Ss])
            if last_prompt:
                P.dma('sp', g.o_ssm_p[l].rearrange("h n p -> n h p"), L.Ss[:, :, :], "Ss_o", reads=[L.Ss])
        else:
            for g_ in range(2):
                P.op('pool', lambda e: e.tensor_tensor(padr, Btm[:, g_ * 128:(g_ + 1) * 128].unsqueeze(1).to_broadcast([128, 16, 128]),
                                                       g.consts[:, C_ROWM:C_ROWM + 16].unsqueeze(2).to_broadcast([128, 16, 128]), ALU.mult),
                     reads=[Btm, g.consts], writes=[B.padf])
                for s_ in range(16):
                    Sh = B.Sh[nsh % 2]
                    nsh += 1
                    P.dma('sp', Sh[:, :, :], g.st_ssm[l][s_, 8 * g_:8 * g_ + 8, :, :].rearrange("h n p -> n h p"), "Shs%d_d" % ((nsh - 1) % 2), writes=[Sh])
                    bSu = nextbank(g)
                    P.op('pe', lambda e: e.matmul(bSu.t[:, :], B.padf[:, s_ * 128:(s_ + 1) * 128], xw[:, g_ * 512:(g_ + 1) * 512], start=True, stop=True),
                         reads=[B.padf, xw], writes=[bSu])
                    P.op('dve', lambda e: e.tensor_tensor(Sh[:, :, :], Sh[:, :, :], B.egts[:, s_, 8 * g_:8 * g_ + 8].unsqueeze(2).to_broadcast([128, 8, 64]), ALU.mult),
                         reads=[Sh, B.egts], writes=[Sh])
                    P.op('dve', lambda e: e.tensor_tensor(Sh[:, :, :], Sh[:, :, :], bSu.t[:, :].rearrange("p (h e) -> p h e", h=8), ALU.add),
                         reads=[Sh, bSu], writes=[Sh])
                    P.dma('act', g.o_ssm_s[l][s_, 8 * g_:8 * g_ + 8, :, :].rearrange("h n p -> n h p"), Sh[:, :, :], "Shs%d_o" % ((nsh - 1) % 2), reads=[Sh])


def make_in_maps(inp, ncores=8, NPT=16, DEPTH=2):
    f = lambda a: np.ascontiguousarray(np.asarray(a, dtype=np.float32))
    consts = make_consts()
    params = np.zeros((DEPTH, NPAR), np.float32)
    for l in range(DEPTH):
        params[l, P_DN_ALOG:P_DN_ALOG + 4] = inp['dn_a_log'][l]
        params[l, P_DN_DTB:P_DN_DTB + 4] = inp['dn_dt_bias'][l]
        params[l, P_M2_ALOG:P_M2_ALOG + 16] = inp['m2_a_log'][l]
        params[l, P_M2_DTB:P_M2_DTB + 16] = inp['m2_dt_bias'][l]
        params[l, P_M2_D:P_M2_D + 16] = inp['m2_d'][l]
        params[l, P_DN_NORM:P_DN_NORM + 128] = inp['dn_norm'][l]
        params[l, P_GLA_NORM:P_GLA_NORM + 128] = inp['gla_norm'][l]
    norms = np.zeros((4 * DEPTH + 1, D), np.float32)
    for l in range(DEPTH):
        norms[3 * l] = inp['ffn1_norm'][l]
        norms[3 * l + 1] = inp['mix_norm'][l]
        norms[3 * l + 2] = inp['ffn2_norm'][l]
    norms[3 * DEPTH] = inp['final_norm']
    for l in range(DEPTH):
        norms[3 * DEPTH + 1 + l] = inp['m2_norm'][l]
    convw = f(np.asarray(inp['conv_w'])[:DEPTH].reshape(DEPTH, 4, 24, 128).transpose(0, 3, 2, 1).reshape(DEPTH, 128, 96))
    convb = f(np.asarray(inp['conv_b'])[:DEPTH].reshape(DEPTH, 24, 128).transpose(0, 2, 1))
    shared = {
        'consts': consts, 'params': params, 'norms': norms, 'convw': convw, 'convb': convb,
        'ffn1_w_gu': f(inp['ffn1_w_gu'][:DEPTH]), 'ffn2_w_gu': f(inp['ffn2_w_gu'][:DEPTH]),
        'ffn1_w_down': f(inp['ffn1_w_down'][:DEPTH]), 'ffn2_w_down': f(inp['ffn2_w_down'][:DEPTH]),
        'w_in': f(inp['w_in'][:DEPTH]), 'w_out': f(inp['w_out'][:DEPTH]),
        'gla_w_up': f(inp['gla_w_up'][:DEPTH]), 'gla_b_up': f(np.asarray(inp['gla_b_up'])[:DEPTH].reshape(DEPTH, 1, 256)),
    }
    maps = []
    for c in range(ncores):
        m = dict(shared)
        xp = np.asarray(inp['x_prompt'])[c, :NPT * 128]
        xs = np.asarray(inp['x_sample'])[16 * c:16 * c + 16].reshape(128, D)
        m['xin'] = f(np.concatenate([xp, xs], 0))
        m['st_conv'] = f(np.asarray(inp['state_conv'])[:DEPTH, 16 * c:16 * c + 16].reshape(DEPTH, 48, CONV_CH))
        m['st_delta'] = f(np.asarray(inp['state_delta'])[:DEPTH, 16 * c:16 * c + 16])
        m['st_gla'] = f(np.asarray(inp['state_gla'])[:DEPTH, 16 * c:16 * c + 16])
        m['st_ssm'] = f(np.asarray(inp['state_ssm'])[:DEPTH, 16 * c:16 * c + 16])
        maps.append(m)
    return maps


def gather_outputs(res, ncores=8, NPT=16, DEPTH=2):
    yp = np.stack([r['y'][:NPT * 128] for r in res], 0)
    ys = np.concatenate([r['y'][NPT * 128:].reshape(16, 8, D) for r in res], 0)
    cp = np.stack([r['o_conv_p'] for r in res], 1)
    dp = np.stack([r['o_delta_p'] for r in res], 1)
    gp = np.stack([r['o_gla_p'] for r in res], 1)
    sp = np.stack([r['o_ssm_p'] for r in res], 1)
    cs = np.concatenate([r['o_conv_s'].reshape(DEPTH, 16, 3, CONV_CH) for r in res], 1)
    ds = np.concatenate([r['o_delta_s'] for r in res], 1)
    gs = np.concatenate([r['o_gla_s'] for r in res], 1)
    ss = np.concatenate([r['o_ssm_s'] for r in res], 1)
    return tuple(np.ascontiguousarray(a, dtype=np.float32) for a in (yp, ys, cp, dp, gp, sp, cs, ds, gs, ss))


_NC_CACHE = {}


def kernel(**inputs):
    if 'nc' not in _NC_CACHE:
        _NC_CACHE['nc'] = build()[0]
    nc = _NC_CACHE['nc']
    in_maps = make_in_maps(inputs)
    res = run_bass_kernel_spmd(nc, in_maps, core_ids=list(range(8)))
    return gather_outputs(res.results)
```

```python
import numpy as np
import concourse.bass as bass
import concourse.mybir as mybir
from concourse.bass_utils import run_bass_kernel_spmd

F32 = mybir.dt.float32
BF16 = mybir.dt.bfloat16
AF = mybir.ActivationFunctionType
ALU = mybir.AluOpType
AX = mybir.AxisListType

D = 1024
DFF = 2816
EPS = 1e-6
NEG = -30000.0
CONV_CH = 3072
IN_COLS = 6184
O_DQ, O_DK, O_DV, O_MX, O_MB, O_MC = 0, 512, 1024, 1536, 2560, 2816
O_DZ, O_DA, O_DB, O_GQ, O_GK, O_GV, O_GLOW, O_GG, O_MZ, O_MDT = 3072, 3584, 3588, 3592, 3848, 4104, 4616, 4632, 5144, 6168

C_IDENT = 0
C_ONES = 128
C_U01 = 256
C_NUI = 384
C_NUS = 512
C_NLS = 640
C_SEG = 768
C_SU01 = 896
C_SNUI = 1024
C_SNUS = 1152
C_SNLS = 1280
C_ESEL = 1408
C_ROWM = 1920
C_ROWM8 = 1936
C_INV128 = 1952
NCONST = 1984

P_DN_ALOG, P_DN_DTB, P_M2_ALOG, P_M2_DTB, P_M2_D = 0, 4, 8, 24, 40
P_DN_NORM, P_GLA_NORM, P_M2_NORM = 64, 192, 320
NPAR = 320


def make_consts():
    c = np.zeros((128, NCONST), np.float32)
    p = np.arange(128)[:, None]
    f = np.arange(128)[None, :]
    seg = (p // 8) == (f // 8)
    c[:, C_IDENT:C_IDENT + 128] = (p == f)
    c[:, C_ONES:C_ONES + 128] = 1.0
    c[:, C_U01:C_U01 + 128] = (p <= f)
    c[:, C_NUI:C_NUI + 128] = np.where(p <= f, 0.0, NEG)
    c[:, C_NUS:C_NUS + 128] = np.where(p < f, 0.0, NEG)
    c[:, C_NLS:C_NLS + 128] = np.where(p > f, 0.0, NEG)
    c[:, C_SEG:C_SEG + 128] = seg
    c[:, C_SU01:C_SU01 + 128] = seg & (p <= f)
    c[:, C_SNUI:C_SNUI + 128] = np.where(seg & (p <= f), 0.0, NEG)
    c[:, C_SNUS:C_SNUS + 128] = np.where(seg & (p < f), 0.0, NEG)
    c[:, C_SNLS:C_SNLS + 128] = np.where(seg & (p > f), 0.0, NEG)
    for h in range(4):
        c[h, C_ESEL + h * 128:C_ESEL + (h + 1) * 128] = 1.0
    s = np.arange(16)[None, :]
    c[:, C_ROWM:C_ROWM + 16] = ((p // 8) == s)
    c[:, C_ROWM8:C_ROWM8 + 16] = ((p // 8) == s) / 8.0
    c[:, C_INV128] = 1.0 / 128
    return c


class TT:
    def __init__(self, t, key):
        self.t = t
        self.key = key

    def __getitem__(self, idx):
        return self.t[idx]


class Prog:
    def __init__(self, nc):
        self.nc = nc
        self.engs = {'pe': nc.tensor, 'act': nc.scalar, 'dve': nc.vector, 'pool': nc.gpsimd, 'sp': nc.sync}
        self.esem = {e: nc.alloc_semaphore("sem_" + e) for e in self.engs}
        self.cnt = {e: 0 for e in self.engs}
        self.seen = {e: {} for e in self.engs}
        self.lastw = {}
        self.readers = {}
        self.dsem = {}
        self.dcnt = {}
        self.dtot = {}
        self.nwaits = 0
        self.nops = 0
        self.nuid = 0

    def sb(self, name, shape, dtype=F32):
        self.nuid += 1
        nm = "%s_%d" % (name, self.nuid)
        return TT(self.nc.alloc_sbuf_tensor(nm, list(shape), dtype), nm)

    @staticmethod
    def _keys(lst):
        out = []
        for x in lst:
            k = x.key if isinstance(x, TT) else x
            if isinstance(k, (tuple, list)):
                out.extend(k)
            else:
                out.append(k)
        return out

    def _wait(self, eng, ev):
        sem, val = ev
        k = id(sem)
        if k in self.dtot:
            val = max(val, self.dtot[k])
        if self.seen[eng].get(k, 0) >= val:
            return
        if sem is self.esem[eng] and val > self.cnt[eng]:
            return
        self.seen[eng][k] = val
        self.engs[eng].wait_ge(sem, val)
        self.nwaits += 1

    def _deps(self, eng, reads, writes, skipsem=None):
        evs = []
        for k in reads:
            if k in self.lastw:
                evs.append(self.lastw[k])
        for k in writes:
            if k in self.lastw and self.lastw[k][0] is not skipsem:
                evs.append(self.lastw[k])
            for ev in self.readers.get(k, {}).values():
                evs.append(ev)
        for ev in evs:
            self._wait(eng, ev)

    def _record(self, ev, reads, writes):
        for k in reads:
            self.readers.setdefault(k, {})[id(ev[0])] = ev
        for k in writes:
            self.lastw[k] = ev
            self.readers[k] = {}

    def op(self, eng, fn, reads=(), writes=(), inc=True):
        reads = self._keys(reads)
        writes = self._keys(writes)
        self._deps(eng, reads, writes)
        ins = fn(self.engs[eng])
        self.nops += 1
        if inc:
            self.cnt[eng] += 1
            ins.then_inc(self.esem[eng], 1)
            ev = (self.esem[eng], self.cnt[eng])
        else:
            ev = (self.esem[eng], self.cnt[eng] + 1)
        self._record(ev, reads, writes)
        return ins

    def dma(self, eng, out, in_, group, reads=(), writes=()):
        reads = self._keys(reads)
        writes = self._keys(writes)
        if group not in self.dsem:
            self.dsem[group] = self.nc.alloc_semaphore("dsem_%d" % len(self.dsem))
            self.dcnt[group] = 0
        sem = self.dsem[group]
        self._deps(eng, reads, writes, skipsem=sem)
        ins = self.engs[eng].dma_start(out=out, in_=in_)
        self.nops += 1
        self.dcnt[group] += 16
        self.dtot[id(sem)] = self.dcnt[group]
        ins.then_inc(sem, 16)
        ev = (sem, self.dcnt[group])
        self._record(ev, reads, writes)
        return ins

    def finish(self, eng='sp'):
        for g, sem in self.dsem.items():
            self._wait(eng, (sem, self.dcnt[g]))
        for e in self.engs:
            if e != eng and self.cnt[e] > 0:
                self._wait(eng, (self.esem[e], self.cnt[e]))


class Ctx:
    pass


def build(NPT=16, DEPTH=2, stages=('ffn', 'mix')):
    NT = NPT + 1
    NTOK = NT * 128
    nc = bass.Bass("TRN2", target_bir_lowering=False)
    P = Prog(nc)
    g = Ctx()
    g.nc, g.P, g.NPT, g.NT, g.DEPTH = nc, P, NPT, NT, DEPTH
    g.mixers = [m for m in ('delta', 'gla', 'ssd') if m in stages or 'mix' in stages]

    def din(name, shape):
        return nc.dram_tensor(name, list(shape), F32, kind="ExternalInput").ap()

    def dout(name, shape):
        return nc.dram_tensor(name, list(shape), F32, kind="ExternalOutput").ap()

    g.xin = din("xin", [NTOK, D])
    g.consts_d = din("consts", [128, NCONST])
    g.params_d = din("params", [DEPTH, NPAR])
    g.st_conv = din("st_conv", [DEPTH, 48, CONV_CH])
    g.st_delta = din("st_delta", [DEPTH, 16, 4, 128, 128])
    g.st_gla = din("st_gla", [DEPTH, 16, 4, 64, 128])
    g.st_ssm = din("st_ssm", [DEPTH, 16, 16, 128, 64])
    g.norms_d = din("norms", [4 * DEPTH + 1, D])
    g.w_gu = [din("ffn1_w_gu", [DEPTH, D, 2 * DFF]), din("ffn2_w_gu", [DEPTH, D, 2 * DFF])]
    g.w_down = [din("ffn1_w_down", [DEPTH, DFF, D]), din("ffn2_w_down", [DEPTH, DFF, D])]
    g.w_in = din("w_in", [DEPTH, D, IN_COLS])
    g.w_out = din("w_out", [DEPTH, 2 * D, D])
    g.convw_d = din("convw", [DEPTH, 128, 24 * 4])
    g.convb_d = din("convb", [DEPTH, 128, 24])
    g.gla_wup = din("gla_w_up", [DEPTH, 16, 256])
    g.gla_bup = din("gla_b_up", [DEPTH, 1, 256])

    g.y = dout("y", [NTOK, D])
    g.o_conv_p = dout("o_conv_p", [DEPTH, 3, CONV_CH])
    g.o_delta_p = dout("o_delta_p", [DEPTH, 4, 128, 128])
    g.o_gla_p = dout("o_gla_p", [DEPTH, 4, 64, 128])
    g.o_ssm_p = dout("o_ssm_p", [DEPTH, 16, 128, 64])
    g.o_conv_s = dout("o_conv_s", [DEPTH, 48, CONV_CH])
    g.o_delta_s = dout("o_delta_s", [DEPTH, 16, 4, 128, 128])
    g.o_gla_s = dout("o_gla_s", [DEPTH, 16, 4, 64, 128])
    g.o_ssm_s = dout("o_ssm_s", [DEPTH, 16, 16, 128, 64])

    g.x = P.sb("x", [128, NT, D])
    g.xk = ["x_t%d" % t for t in range(NT)]
    g.consts = P.sb("consts", [128, NCONST])
    g.identb = P.sb("identb", [128, 128], BF16)
    g.stat = P.sb("stat", [128, 64])
    g.ps = [TT(nc.alloc_psum_tensor("psb%d" % i, [128, 512], F32), "psb%d" % i) for i in range(8)]
    g.psi = 0
    g.held = set()

    P.dma('sp', g.consts[:, :], g.consts_d, 'consts', writes=[g.consts])
    for t in range(NT):
        P.dma('sp', g.x[:, t, :], g.xin[t * 128:(t + 1) * 128, :], 'xload', writes=[g.xk[t]])
    P.op('dve', lambda e: e.tensor_copy(g.identb[:, :], g.consts[:, C_IDENT:C_IDENT + 128]), reads=[g.consts], writes=[g.identb])
    g.maskb = P.sb("maskb", [128, 6, 128], BF16)
    g.maskb_idx = {}
    for i_, off_ in enumerate((C_NUI, C_NUS, C_NLS, C_SNUI, C_SNUS, C_SNLS)):
        g.maskb_idx[off_] = i_
        P.op('dve', lambda e: e.tensor_copy(g.maskb[:, i_, :], g.consts[:, off_:off_ + 128]), reads=[g.consts], writes=[g.maskb])

    for l in range(DEPTH):
        if 'ffn' in stages:
            ffn_phase(g, l, 0)
        if g.mixers:
            mix_phase(g, l)
        if 'ffn' in stages:
            ffn_phase(g, l, 1)
    final_phase(g)
    P.finish('sp')
    g.stats = (P.nops, P.nwaits, len(P.dsem))
    return nc, g


def nextbank(g, hold=False):
    while True:
        b = g.ps[g.psi % 8]
        g.psi += 1
        if b.key not in g.held:
            break
    if hold:
        g.held.add(b.key)
    return b


def release(g, *banks):
    for b in banks:
        g.held.discard(b.key)


def barrier(g, skip=()):
    P = g.P
    for e in P.engs:
        for gname, sem in P.dsem.items():
            if gname in skip:
                continue
            P._wait(e, (sem, P.dcnt[gname]))
        for e2 in P.engs:
            if e2 != e and P.cnt[e2] > 0:
                P._wait(e, (P.esem[e2], P.cnt[e2]))


def norm_tiles(g, tiles, gain_row, hT, hn, gB, tok0=0):
    P, nc = g.P, g.nc
    n = len(tiles)
    P.dma('sp', gB[:, :], g.norms_d[gain_row:gain_row + 1, :].partition_broadcast(128), 'gB_d', writes=[gB])
    junk = hn[0]
    for i, t in enumerate(tiles):
        P.op('act', lambda e: e.activation(junk[:, :], g.x[:, t, :], AF.Square, accum_out=g.stat[:, i:i + 1]),
             reads=[g.xk[t]], writes=[junk, g.stat])
    P.op('act', lambda e: e.activation(g.stat[:, 16:16 + n], g.stat[:, 0:n], AF.Ln, bias=EPS, scale=1.0 / D),
         reads=[g.stat], writes=[g.stat])
    P.op('act', lambda e: e.activation(g.stat[:, 32:32 + n], g.stat[:, 16:16 + n], AF.Exp, scale=-0.5),
         reads=[g.stat], writes=[g.stat])
    for i, t in enumerate(tiles):
        h = hn[i % 2]
        P.op('dve', lambda e: e.scalar_tensor_tensor(h[:, :], g.x[:, t, :], g.stat[:, 32 + i:33 + i], gB[:, :], ALU.mult, ALU.mult),
             reads=[g.xk[t], g.stat, gB], writes=[h])
        bank = nextbank(g)
        pb = bank.t[:, :].bitcast(BF16)
        for kc in range(8):
            P.op('pe', lambda e: e.transpose(pb[:, kc * 128:(kc + 1) * 128], h[:, kc * 128:(kc + 1) * 128], g.identb[:, :]),
                 reads=[h, g.identb], writes=[bank], inc=(kc == 7))
        c0 = tok0 + i * 128
        P.op('act', lambda e: e.activation(hT[:, :, c0:c0 + 128], pb.rearrange("p (k n) -> p k n", k=8), AF.Copy),
             reads=[bank], writes=[hT])


def ffn_phase(g, l, which):
    P, nc = g.P, g.nc
    NT = g.NT
    half = (NT + 1) // 2
    sbs = [list(range(0, NT - half)), list(range(NT - half, NT))]
    SBW = half * 128
    barrier(g)
    w_gu = g.w_gu[which][l]
    w_down = g.w_down[which][l]
    tag = "f%d_%d_" % (l, which)
    with nc.sbuf_tensor(tag + "hT", [128, 8, SBW], BF16) as hT_, \
            nc.sbuf_tensor(tag + "actT", [128, 22, SBW], BF16) as actT_, \
            nc.sbuf_tensor(tag + "wgu0", [128, 8, 512], BF16) as wgu0, nc.sbuf_tensor(tag + "wgu1", [128, 8, 512], BF16) as wgu1, \
            nc.sbuf_tensor(tag + "wd0", [128, 22, 256], BF16) as wd0, nc.sbuf_tensor(tag + "wd1", [128, 22, 256], BF16) as wd1, \
            nc.sbuf_tensor(tag + "gB", [128, D], F32) as gB_, \
            nc.sbuf_tensor(tag + "hn0", [128, D], BF16) as hn0, nc.sbuf_tensor(tag + "hn1", [128, D], BF16) as hn1, \
            nc.sbuf_tensor(tag + "sg0", [128, 512], F32) as sg0, nc.sbuf_tensor(tag + "sg1", [128, 512], F32) as sg1:
        hT = TT(hT_, tag + "hT")
        actT = TT(actT_, tag + "actT")
        wgu = [TT(wgu0, tag + "wgu0"), TT(wgu1, tag + "wgu1")]
        wd = [TT(wd0, tag + "wd0"), TT(wd1, tag + "wd1")]
        gB = TT(gB_, tag + "gB")
        hn = [TT(hn0, tag + "hn0"), TT(hn1, tag + "hn1")]
        sg = [TT(sg0, tag + "sg0"), TT(sg1, tag + "sg1")]
        nsg = 0
        for sbi, tiles in enumerate(sbs):
            ntok = len(tiles) * 128
            norm_tiles(g, tiles, 3 * l + (0 if which == 0 else 2), hT, hn, gB)
            tgs = [(c0, min(512, ntok - c0)) for c0 in range(0, ntok, 512)]
            for j in range(11):
                wb = wgu[j % 2]
                P.dma('pool', wb[:, :, 0:256], w_gu[:, 256 * j:256 * j + 256].rearrange("(k p) n -> p k n", p=128),
                      "wgu%d_d" % (j % 2), writes=[wb])
                P.dma('pool', wb[:, :, 256:512], w_gu[:, DFF + 256 * j:DFF + 256 * j + 256].rearrange("(k p) n -> p k n", p=128),
                      "wgu%d_d" % (j % 2), writes=[wb])
                for c in range(2):
                    for (c0, w) in tgs:
                        bA = nextbank(g)
                        bB = nextbank(g)
                        for kc in range(8):
                            P.op('pe', lambda e: e.matmul(bA.t[:, 0:w], wb[:, kc, c * 128:(c + 1) * 128], hT[:, kc, c0:c0 + w],
                                                          start=(kc == 0), stop=(kc == 7)),
                                 reads=[wb, hT], writes=[bA], inc=(kc == 7))
                        for kc in range(8):
                            P.op('pe', lambda e: e.matmul(bB.t[:, 0:w], wb[:, kc, 256 + c * 128:256 + (c + 1) * 128], hT[:, kc, c0:c0 + w],
                                                          start=(kc == 0), stop=(kc == 7)),
                                 reads=[wb, hT], writes=[bB], inc=(kc == 7))
                        s_ = sg[nsg % 2]
                        nsg += 1
                        P.op('act', lambda e: e.activation(s_[:, 0:w], bA.t[:, 0:w], AF.Silu), reads=[bA], writes=[s_])
                        P.op('dve', lambda e: e.tensor_tensor(actT[:, 2 * j + c, c0:c0 + w], s_[:, 0:w], bB.t[:, 0:w], ALU.mult),
                             reads=[s_, bB], writes=[actT])
            for q in range(4):
                wq = wd[q % 2]
                P.dma('pool', wq[:, :, :], w_down[:, 256 * q:256 * q + 256].rearrange("(k p) n -> p k n", p=128),
                      "wd%d_d" % (q % 2), writes=[wq])
                for i, t in enumerate(tiles):
                    bD = nextbank(g)
                    for kc in range(22):
                        P.op('pe', lambda e: e.matmul(bD.t[:, 0:256], actT[:, kc, i * 128:(i + 1) * 128], wq[:, kc, :],
                                                      start=(kc == 0), stop=(kc == 21)),
                             reads=[actT, wq], writes=[bD], inc=(kc == 21))
                    P.op('dve', lambda e: e.scalar_tensor_tensor(g.x[:, t, 256 * q:256 * q + 256], bD.t[:, 0:256], 0.5,
                                                                 g.x[:, t, 256 * q:256 * q + 256], ALU.mult, ALU.add),
                         reads=[bD, g.xk[t]], writes=[g.xk[t]])
        barrier(g)


def final_phase(g):
    P, nc = g.P, g.nc
    NT = g.NT
    barrier(g)
    with nc.sbuf_tensor("fin_gB", [128, D], F32) as gB_, nc.sbuf_tensor("fin_o0", [128, D], F32) as o0, \
            nc.sbuf_tensor("fin_o1", [128, D], F32) as o1:
        gB = TT(gB_, "fin_gB")
        ob = [TT(o0, "fin_o0"), TT(o1, "fin_o1")]
        P.dma('sp', gB[:, :], g.norms_d[3 * g.DEPTH:3 * g.DEPTH + 1, :].partition_broadcast(128), 'fin_gB', writes=[gB])
        for t in range(NT):
            o = ob[t % 2]
            P.op('act', lambda e: e.activation(o[:, :], g.x[:, t, :], AF.Square, accum_out=g.stat[:, 0:1]),
                 reads=[g.xk[t]], writes=[o, g.stat])
            P.op('act', lambda e: e.activation(g.stat[:, 1:2], g.stat[:, 0:1], AF.Ln, bias=EPS, scale=1.0 / D),
                 reads=[g.stat], writes=[g.stat])
            P.op('act', lambda e: e.activation(g.stat[:, 2:3], g.stat[:, 1:2], AF.Exp, scale=-0.5),
                 reads=[g.stat], writes=[g.stat])
            P.op('dve', lambda e: e.scalar_tensor_tensor(o[:, :], g.x[:, t, :], g.stat[:, 2:3], gB[:, :], ALU.mult, ALU.mult),
                 reads=[g.xk[t], g.stat, gB], writes=[o])
            P.dma('sp', g.y[t * 128:(t + 1) * 128, :], o[:, :], o.key + "_out", reads=[o])
        barrier(g)


class Scope:
    def __init__(self, g, tag):
        import contextlib
        self.g, self.tag = g, tag
        self.es = contextlib.ExitStack()

    def __enter__(self):
        self.es.__enter__()
        return self

    def __exit__(self, *a):
        return self.es.__exit__(*a)

    def sb(self, name, shape, dtype=F32):
        g = self.g
        g.P.nuid += 1
        nm = "%s_%s_%d" % (self.tag, name, g.P.nuid)
        t = self.es.enter_context(g.nc.sbuf_tensor(nm, list(shape), dtype))
        return TT(t, nm)


def cst(g, off, n=128, rows=128):
    return g.consts[0:rows, off:off + n]


def mix_phase(g, l):
    P, nc = g.P, g.nc
    NPT = g.NPT
    barrier(g)
    sbs = [list(range(a, min(a + 6, NPT))) for a in range(0, NPT, 6)] + [[NPT]]
    with Scope(g, "mx%d" % l) as LS:
        L = Ctx()
        L.l = l
        L.parB = LS.sb("parB", [128, NPAR])
        L.convw = LS.sb("convw", [128, 24, 4])
        L.convb = LS.sb("convb", [128, 24])
        L.wup = LS.sb("wup", [16, 256])
        L.bup = LS.sb("bup", [1, 256])
        L.pd = LS.sb("pd", [128, 64])
        P.dma('sp', L.parB[:, :], g.params_d[l:l + 1, :].partition_broadcast(128), 'lay_small', writes=[L.parB])
        P.dma('sp', L.convw[:, :, :], g.convw_d[l].rearrange("p (c j) -> p c j", j=4), 'lay_small', writes=[L.convw])
        P.dma('sp', L.convb[:, :], g.convb_d[l], 'lay_small', writes=[L.convb])
        P.dma('sp', L.wup[:, :], g.gla_wup[l], 'lay_small', writes=[L.wup])
        P.dma('sp', L.bup[:, :], g.gla_bup[l], 'lay_small', writes=[L.bup])
        P.op('act', lambda e: e.activation(L.pd[:, 0:4], L.parB[:, P_DN_ALOG:P_DN_ALOG + 4], AF.Exp), reads=[L.parB], writes=[L.pd])
        P.op('act', lambda e: e.activation(L.pd[:, 4:20], L.parB[:, P_M2_ALOG:P_M2_ALOG + 16], AF.Exp), reads=[L.parB], writes=[L.pd])
        P.op('dve', lambda e: e.tensor_scalar(L.pd[:, 0:20], L.pd[:, 0:20], -1.0, None, ALU.mult), reads=[L.pd], writes=[L.pd])
        L.Sd = LS.sb("Sd", [128, 4, 128])
        L.Sg = LS.sb("Sg", [128, 2, 128])
        L.Ss = LS.sb("Ss", [128, 16, 64])
        L.tailD = LS.sb("tailD", [128, 12, 3])
        L.tailS = LS.sb("tailS", [128, 12, 3])
        for t_ in (L.Sd, L.Sg, L.Ss, L.tailD, L.tailS):
            P.op('pool', lambda e: e.memset(t_.t[:], 0.0), writes=[t_])
        import os
        L.WB = LS.sb("WB", [128, 8, 2576], BF16)
        L.WO = LS.sb("WO", [128, 8, 1024], BF16)
        phases = [(sbi_, m_) for sbi_ in range(len(sbs)) for m_ in g.mixers]
        L.phases = phases
        L.phase_i = 0
        load_wpart(g, L, phases[0][1])
        load_wo(g, L, phases[0][1])
        for sbi, tiles in enumerate(sbs):
            is_s = tiles[0] == NPT
            ntok = len(tiles) * 128
            with Scope(g, "mx%d_%d" % (l, sbi)) as SS:
                hT = SS.sb("hT", [128, 8, ntok], BF16)
                with Scope(g, "mx%d_%d_n" % (l, sbi)) as NS:
                    hn = [NS.sb("hn0", [128, D], BF16), NS.sb("hn1", [128, D], BF16)]
                    gB = NS.sb("gB", [128, D])
                    norm_tiles(g, tiles, 3 * l + 1, hT, hn, gB)
                    barrier(g)
                for mixer in g.mixers:
                    with Scope(g, "mx%d_%d_%s" % (l, sbi, mixer)) as MS:
                        if mixer == 'delta':
                            delta_mixer(g, L, MS, tiles, hT, is_s)
                        elif mixer == 'gla':
                            gla_mixer(g, L, MS, tiles, hT, is_s)
                        else:
                            ssd_mixer(g, L, MS, tiles, hT, is_s)
                    L.phase_i += 1
                    if L.phase_i < len(L.phases):
                        load_wo(g, L, L.phases[L.phase_i][1])
                    barrier(g, skip=("wpart_d", "wo_d"))
        barrier(g)


MIX_COLS = {'delta': [(O_DQ, 1536), (O_DZ, 520)], 'gla': [(O_GQ, 1552)], 'ssd': [(O_MX, 1536), (O_MZ, 1040)]}
MIX_ROWS = {'delta': (0, 512), 'gla': (512, 512), 'ssd': (1024, 1024)}


def load_wpart(g, L, mixer):
    P, l = g.P, L.l
    c = 0
    for (c0, n) in MIX_COLS[mixer]:
        for a in range(0, n, 512):
            w = min(512, n - a)
            P.dma('pool', L.WB[:, :, c + a:c + a + w], g.w_in[l][:, c0 + a:c0 + a + w].rearrange("(k p) n -> p k n", p=128),
                  "wpart_d", writes=[L.WB])
        c += n


def load_wo(g, L, mixer):
    P, l = g.P, L.l
    r0, nr = MIX_ROWS[mixer]
    for a in range(0, nr, 512):
        P.dma('pool', L.WO[:, a // 128:a // 128 + 4, :], g.w_out[l][r0 + a:r0 + a + 512, :].rearrange("(k p) n -> p k n", p=128),
              "wo_d", writes=[L.WO])


def prefetch_next(g, L):
    if L.phase_i + 1 < len(L.phases):
        load_wpart(g, L, L.phases[L.phase_i + 1][1])


def proj_fm(g, wpart, col0, ncols, hT, tok0, bank, slot, rows0=0):
    P = g.P
    for kc in range(8):
        P.op('pe', lambda e: e.matmul(bank.t[rows0:rows0 + ncols, slot * 128:(slot + 1) * 128], wpart[:, kc, col0:col0 + ncols],
                                      hT[:, kc, tok0:tok0 + 128], start=(kc == 0), stop=(kc == 7)),
             reads=[wpart, hT], writes=[bank], inc=(kc == 7))


def proj_tm(g, wpart, col0, ncols, hT, tok0, bank, o0=0):
    P = g.P
    for kc in range(8):
        P.op('pe', lambda e: e.matmul(bank.t[:, o0:o0 + ncols], hT[:, kc, tok0:tok0 + 128], wpart[:, kc, col0:col0 + ncols],
                                      start=(kc == 0), stop=(kc == 7)),
             reads=[wpart, hT], writes=[bank], inc=(kc == 7))


def conv_block(g, L, MS, B, wpart, hT, tok0, is_s, ch0, tail, stc, last_prompt, conv_out_col0):
    P, l = g.P, L.l
    cin, cout, acc, tmp = B.cin, B.cout, B.acc, B.tmp
    if is_s:
        cin4 = cin.t[:, :, :].rearrange("p c (r s) -> p c r s", s=16)
    for grp in range(3):
        bank = nextbank(g)
        for c4 in range(4):
            proj_fm(g, wpart, (grp * 4 + c4) * 128, 128, hT, tok0, bank, c4)
        if not is_s:
            P.op('act', lambda e: e.activation(cin[:, grp * 4:grp * 4 + 4, 3:131], bank.t[:, :].rearrange("p (c n) -> p c n", c=4), AF.Copy),
                 reads=[bank], writes=[cin])
        else:
            P.op('act', lambda e: e.activation(cin4[:, grp * 4:grp * 4 + 4, 3:11, :],
                                               bank.t[:, :].rearrange("p (c s t) -> p c t s", c=4, s=16), AF.Copy),
                 reads=[bank], writes=[cin])
    if not is_s:
        P.op('pool', lambda e: e.tensor_copy(cin[:, :, 0:3], tail[:, :, :]), reads=[tail], writes=[cin])
        P.op('pool', lambda e: e.tensor_copy(tail[:, :, :], cin[:, :, 128:131]), reads=[cin], writes=[tail])
    else:
        for grp in range(3):
            bank = nextbank(g)
            for c4 in range(4):
                ch = grp * 4 + c4
                P.op('pe', lambda e: e.transpose(bank.t[:, c4 * 48:(c4 + 1) * 48], stc[0:48, ch * 128:(ch + 1) * 128], cst(g, C_IDENT, 48, 48)),
                     reads=[stc, g.consts], writes=[bank], inc=(c4 == 3))
            P.op('act', lambda e: e.activation(cin4[:, grp * 4:grp * 4 + 4, 0:3, :],
                                               bank.t[:, 0:192].rearrange("p (c s r) -> p c r s", c=4, r=3), AF.Copy),
                 reads=[bank], writes=[cin])
    if is_s or last_prompt:
        for grp in range(3):
            bank = nextbank(g)
            for c4 in range(4):
                ch = grp * 4 + c4
                if is_s:
                    src = cin.t[:, ch, 128:176]
                    n = 48
                else:
                    src = cin[:, ch, 128:131]
                    n = 3
                P.op('pe', lambda e: e.transpose(bank.t[0:n, c4 * 128:(c4 + 1) * 128], src, cst(g, C_IDENT)),
                     reads=[cin, g.consts], writes=[bank], inc=(c4 == 3))
            n = 48 if is_s else 3
            ob = B.cvo
            P.op('dve', lambda e: e.tensor_copy(ob[0:n, grp * 512:(grp + 1) * 512], bank.t[0:n, :]), reads=[bank], writes=[ob])
        if is_s:
            dv = g.o_conv_s[l].rearrange("(s r) c -> r s c", r=3)
            for r_ in range(3):
                P.dma('sp', dv[r_, :, conv_out_col0:conv_out_col0 + 1536], B.cvo[r_ * 16:(r_ + 1) * 16, :], "cvo_o", reads=[B.cvo])
        else:
            P.dma('sp', g.o_conv_p[l][:, conv_out_col0:conv_out_col0 + 1536], B.cvo[0:3, :], "cvo_o", reads=[B.cvo])
    return


def conv_taps(g, L, B, is_s, ch0, split=False):
    P = g.P
    cin, cout, acc, tmp = B.cin, B.cout, B.acc, B.tmp
    if is_s:
        cin4 = cin.t[:, :, :].rearrange("p c (r s) -> p c r s", s=16)
        acc4 = acc.t[:, :, :].rearrange("p c (t s) -> p c t s", s=16)
        tmp4 = tmp.t[:, :, :].rearrange("p c (t s) -> p c t s", s=16)
    parts = [('pool', 0, 4), ('dve', 4, 12)] if split else [('pool', 0, 12)]
    for (eng, c0, c1) in parts:
        n = c1 - c0
        akeys = list(acc.key[c0 // 4:(c1 + 3) // 4])
        tkeys = list(tmp.key[c0 // 4:(c1 + 3) // 4])

        def view(j):
            if is_s:
                return cin4[:, c0:c1, j:j + 8, :]
            return cin[:, c0:c1, j:j + 128]
        if is_s:
            shape = [128, n, 8, 16]
            accv = acc4[:, c0:c1, :, :]
            tmpv = tmp4[:, c0:c1, :, :]
            bv_ = L.convb[:, ch0 + c0:ch0 + c1].unsqueeze(2).unsqueeze(3).to_broadcast(shape)
        else:
            shape = [128, n, 128]
            accv = acc.t[:, c0:c1, :]
            tmpv = tmp.t[:, c0:c1, :]
            bv_ = L.convb[:, ch0 + c0:ch0 + c1].unsqueeze(2).to_broadcast(shape)

        def wv(j):
            w_ = L.convw[:, ch0 + c0:ch0 + c1, j:j + 1]
            return w_.to_broadcast(shape) if not is_s else w_.unsqueeze(3).to_broadcast(shape)
        P.op(eng, lambda e: e.tensor_tensor(accv, view(0), wv(0), ALU.mult), reads=[cin, L.convw], writes=akeys)
        for j in range(1, 4):
            P.op(eng, lambda e: e.tensor_tensor(tmpv, view(j), wv(j), ALU.mult), reads=[cin, L.convw], writes=tkeys)
            P.op(eng, lambda e: e.tensor_tensor(accv, accv, tmpv, ALU.add), reads=akeys + tkeys, writes=akeys)
        P.op(eng, lambda e: e.tensor_tensor(accv, accv, bv_, ALU.add), reads=akeys + [L.convb], writes=akeys)
    if is_s:
        P.op('act', lambda e: e.activation(cout.t[:, :, :].rearrange("p c (s t) -> p c t s", s=16), acc4, AF.Silu),
             reads=[acc], writes=[cout])
    else:
        P.op('act', lambda e: e.activation(cout[:, :, :], acc.t[:, :, :], AF.Silu), reads=[acc], writes=[cout])


def out_proj(g, MS, B, on_bf, nk, wo, t):
    P = g.P
    bank = nextbank(g)
    pb = bank.t[:, :].bitcast(BF16)
    for kc in range(nk):
        P.op('pe', lambda e: e.transpose(pb[:, kc * 128:(kc + 1) * 128], on_bf[:, kc * 128:(kc + 1) * 128], g.identb[:, :]),
             reads=[on_bf, g.identb], writes=[bank], inc=(kc == nk - 1))
    P.op('act', lambda e: e.activation(B.onT[:, 0:nk, :], pb[:, 0:nk * 128].rearrange("p (k n) -> p k n", k=nk), AF.Copy),
         reads=[bank], writes=[B.onT])
    for hh in range(2):
        bo = nextbank(g)
        for kc in range(nk):
            P.op('pe', lambda e: e.matmul(bo.t[:, :], B.onT[:, kc, :], wo[:, kc, hh * 512:(hh + 1) * 512], start=(kc == 0), stop=(kc == nk - 1)),
                 reads=[B.onT, wo], writes=[bo], inc=(kc == nk - 1))
        P.op('dve', lambda e: e.tensor_tensor(g.x[:, t, hh * 512:(hh + 1) * 512], g.x[:, t, hh * 512:(hh + 1) * 512], bo.t[:, :], ALU.add),
             reads=[bo, g.xk[t]], writes=[g.xk[t]])


def gate_compute(g, B, gate_banks, W):
    P = g.P
    gt = B.gate
    for bi, bank in enumerate(gate_banks):
        w = min(512, W - bi * 512)
        sl = gt[:, bi * 512:bi * 512 + w]
        P.op('act', lambda e: e.activation(sl, bank.t[:, 0:w], AF.Exp, scale=-1.0), reads=[bank], writes=[gt])
        P.op('act', lambda e: e.activation(sl, sl, AF.Ln, bias=1.0), reads=[gt], writes=[gt])
        P.op('act', lambda e: e.activation(sl, sl, AF.Exp, scale=-1.0), reads=[gt], writes=[gt])
        P.op('dve', lambda e: e.tensor_tensor(sl, sl, bank.t[:, 0:w], ALU.mult), reads=[gt, bank], writes=[gt])


def gated_norm(g, L, B, o_sb, nh, hd, norm_off, per_head, out_bf, norm_tt=None):
    P = g.P
    W = nh * hd
    sm = B.sm2
    gt = B.gate
    sq = getattr(B, 'sq', None)
    if per_head:
        P.op('pool', lambda e: e.tensor_tensor(sq[:, 0:W], o_sb[:, 0:W], o_sb[:, 0:W], ALU.mult), reads=[o_sb], writes=[sq])
        P.op('dve', lambda e: e.tensor_reduce(sm[:, 0:nh], sq[:, 0:W].rearrange("p (h e) -> p h e", h=nh), AX.X, ALU.add), reads=[sq], writes=[sm])
        P.op('act', lambda e: e.activation(sm[:, 16:16 + nh], sm[:, 0:nh], AF.Ln, bias=EPS, scale=1.0 / hd), reads=[sm], writes=[sm])
        P.op('act', lambda e: e.activation(sm[:, 32:32 + nh], sm[:, 16:16 + nh], AF.Exp, scale=-0.5), reads=[sm], writes=[sm])
        P.op('dve', lambda e: e.tensor_tensor(sq[:, 0:W].rearrange("p (h e) -> p h e", h=nh), o_sb[:, 0:W].rearrange("p (h e) -> p h e", h=nh),
                                              sm[:, 32:32 + nh].unsqueeze(2).to_broadcast([128, nh, hd]), ALU.mult), reads=[o_sb, sm], writes=[sq])
        P.op('pool', lambda e: e.tensor_tensor(sq[:, 0:W].rearrange("p (h e) -> p h e", h=nh), sq[:, 0:W].rearrange("p (h e) -> p h e", h=nh),
                                               L.parB[:, norm_off:norm_off + hd].unsqueeze(1).to_broadcast([128, nh, hd]), ALU.mult),
             reads=[sq, L.parB], writes=[sq])
        P.op('dve', lambda e: e.tensor_tensor(out_bf[:, 0:W], sq[:, 0:W], gt[:, 0:W], ALU.mult), reads=[sq, gt], writes=[out_bf])
    else:
        P.op('dve', lambda e: e.tensor_tensor(gt[:, 0:W], gt[:, 0:W], o_sb[:, 0:W], ALU.mult), reads=[gt, o_sb], writes=[gt])
        P.op('act', lambda e: e.activation(out_bf[:, 0:W], gt[:, 0:W], AF.Square, accum_out=sm[:, 0:1]), reads=[gt], writes=[out_bf, sm])
        P.op('act', lambda e: e.activation(sm[:, 16:17], sm[:, 0:1], AF.Ln, bias=EPS, scale=1.0 / W), reads=[sm], writes=[sm])
        P.op('act', lambda e: e.activation(sm[:, 32:33], sm[:, 16:17], AF.Exp, scale=-0.5), reads=[sm], writes=[sm])
        P.op('dve', lambda e: e.scalar_tensor_tensor(out_bf[:, 0:W], gt[:, 0:W], sm[:, 32:33], norm_tt[:, 0:W], ALU.mult, ALU.mult),
             reads=[gt, sm, norm_tt], writes=[out_bf])


def expo_bank(g, B, mask_off, rowT, colT, nheads=4, rtt=None):
    P = g.P
    rtt = rtt if rtt is not None else B.rT
    bank = nextbank(g)
    for h in range(nheads):
        o = bank.t[:, h * 128:(h + 1) * 128]
        esel = g.consts[0:4, C_ESEL + h * 128:C_ESEL + (h + 1) * 128]
        P.op('pe', lambda e: e.matmul(o, g.identb[:, :], g.maskb[:, g.maskb_idx[mask_off], :], start=True, stop=False), reads=[g.identb, g.maskb], writes=[bank], inc=False)
        P.op('pe', lambda e: e.matmul(o, esel, rowT, start=False, stop=False), reads=[g.consts, rtt], writes=[bank], inc=False)
        P.op('pe', lambda e: e.matmul(o, colT, esel, start=False, stop=True), reads=[g.consts, rtt], writes=[bank], inc=(h == nheads - 1))
    return bank


class Arena:
    def __init__(self, MS, name, n):
        self.tt = MS.sb(name, [128, n, 512])
        self.name = self.tt.key
        self.n = n

    def keys(self, i0, n):
        return tuple("%s_s%d" % (self.name, i) for i in range(i0, i0 + n))

    def slot(self, i, shape3=None):
        ap = self.tt.t[:, i, :]
        if shape3 is not None:
            ap = ap.rearrange("p (a b) -> p a b", a=shape3[0])
        return TT(ap, self.keys(i, 1))

    def span(self, i0, n, inner):
        ap = self.tt.t[:, i0:i0 + n, :].rearrange("p s (a b) -> p (s a) b", b=inner)
        return TT(ap, self.keys(i0, n))


def delta_mixer(g, L, MS, tiles, hT, is_s):
    P, l = g.P, L.l
    wpart, wo = L.WB, L.WO
    B = Ctx()
    A = Arena(MS, "arA", 5)
    Bc = Arena(MS, "arB", 3)
    G = Arena(MS, "arG", 7 if is_s else 6)
    M = Arena(MS, "arM", 3)
    Lt = Arena(MS, "arL", 3)
    B.cin = TT(A.tt.t[:, :, :].rearrange("p s f -> p (s f)")[:, 0:12 * 176].rearrange("p (c n) -> p c n", c=12), A.keys(0, 5))
    B.cout = Bc.span(0, 3, 128)
    B.acc = G.span(0, 3, 128)
    B.tmp = G.span(3, 3, 128)
    B.cvo = TT(M.tt.t[0:48, :, :].rearrange("p s f -> p (s f)"), M.keys(0, 3))
    B.rn = G.span(4, 2, 128)
    B.E = [M.slot(i, (4, 128)) for i in range(3)]
    B.PT = [G.slot(0, (4, 128)), G.slot(1, (4, 128))]
    B.X = [G.slot(2, (4, 128)), G.slot(3, (4, 128))]
    B.Y = [G.slot(4, (4, 128)), G.slot(5, (4, 128))]
    B.kbg, B.kd, B.QKm, B.bv = [A.slot(i, (4, 128)) for i in range(4)]
    B.nwT, B.vnew, B.qgT = [Bc.slot(i, (4, 128)) for i in range(3)]
    B.gate = Lt.slot(0)
    B.sq = Lt.slot(1)
    B.o = Lt.slot(2)
    B.onT = MS.sb("onT", [128, 8, 128], BF16)
    B.sm = MS.sb("sm", [128, 64])
    B.sm2 = MS.sb("sm2", [128, 64])
    B.rT = MS.sb("rT", [4, 512])
    B.qkn = MS.sb("qkn", [128, 8, 128])
    B.on = MS.sb("on", [128, 512], BF16)
    stc = None
    if is_s:
        stc = TT(Lt.tt.t[0:48, :, :].rearrange("p s f -> p (s f)"), Lt.keys(0, 3))
        P.dma('sp', stc[:, :], g.st_conv[l][:, 0:1536], "stc_d", writes=[stc])
        B.Sh = MS.sb("Sh", [128, 16, 128])
        B.padf = TT(G.tt.t[:, 2:7, :].rearrange("p s f -> p (s f)")[:, 0:17 * 128], G.keys(2, 5))
        B.egts = MS.sb("egts", [128, 16, 4])
    sm = B.sm
    for ti, t in enumerate(tiles):
        tok0 = ti * 128
        if ti == 0:
            g.minfree = min(getattr(g, 'minfree', 1 << 30), g.nc.sbuf_bytes_remaining)
        last_prompt = (not is_s) and (t == g.NPT - 1)
        conv_block(g, L, MS, B, wpart, hT, tok0, is_s, 0, L.tailD, stc, last_prompt, 0)
        cout = B.cout
        M_U01 = C_SU01 if is_s else C_U01
        M_SEG = C_SEG if is_s else C_ONES
        bS = nextbank(g)
        proj_tm(g, wpart, 2048, 8, hT, tok0, bS)
        bZ = nextbank(g)
        proj_tm(g, wpart, 1536, 512, hT, tok0, bZ)
        if ti == len(tiles) - 1:
            prefetch_next(g, L)
        gate_compute(g, B, [bZ], 512)
        P.op('dve', lambda e: e.tensor_tensor(sm[:, 40:44], bS.t[:, 0:4], L.parB[:, P_DN_DTB:P_DN_DTB + 4], ALU.add), reads=[bS, L.parB], writes=[sm])
        P.op('act', lambda e: e.activation(sm[:, 40:44], sm[:, 40:44], AF.Exp), reads=[sm], writes=[sm])
        P.op('act', lambda e: e.activation(sm[:, 44:48], bS.t[:, 4:8], AF.Exp, scale=-1.0), reads=[bS], writes=[sm])
        P.op('act', lambda e: e.activation(sm[:, 0:8], sm[:, 40:48], AF.Ln, bias=1.0), reads=[sm], writes=[sm])
        P.op('dve', lambda e: e.tensor_tensor(sm[:, 0:4], sm[:, 0:4], L.pd[:, 0:4], ALU.mult), reads=[sm, L.pd], writes=[sm])
        P.op('act', lambda e: e.activation(sm[:, 8:12], sm[:, 4:8], AF.Exp, scale=-1.0), reads=[sm], writes=[sm])
        bC = nextbank(g)
        P.op('pe', lambda e: e.matmul(bC.t[:, 0:4], cst(g, M_U01), sm[:, 0:4], start=True, stop=True), reads=[g.consts, sm], writes=[bC], inc=False)
        P.op('pe', lambda e: e.matmul(bC.t[:, 4:8], cst(g, M_SEG), sm[:, 0:4], start=True, stop=True), reads=[g.consts, sm], writes=[bC])
        P.op('dve', lambda e: e.tensor_copy(sm[:, 12:20], bC.t[:, 0:8]), reads=[bC], writes=[sm])
        P.op('dve', lambda e: e.tensor_tensor(sm[:, 20:24], sm[:, 12:16], sm[:, 4:8], ALU.subtract), reads=[sm], writes=[sm])
        P.op('dve', lambda e: e.tensor_scalar(sm[:, 24:28], sm[:, 12:16], -1.0, None, ALU.mult), reads=[sm], writes=[sm])
        P.op('dve', lambda e: e.tensor_tensor(sm[:, 32:36], sm[:, 16:20], sm[:, 12:16], ALU.subtract), reads=[sm], writes=[sm])
        P.op('act', lambda e: e.activation(sm[:, 28:32], sm[:, 12:16], AF.Exp), reads=[sm], writes=[sm])
        P.op('act', lambda e: e.activation(sm[:, 32:36], sm[:, 32:36], AF.Exp), reads=[sm], writes=[sm])
        P.op('dve', lambda e: e.tensor_tensor(sm[:, 36:40], sm[:, 8:12], sm[:, 28:32], ALU.mult), reads=[sm], writes=[sm])

        def bc(c0):
            return sm[:, c0:c0 + 4].unsqueeze(2).to_broadcast([128, 4, 128])
        bT = nextbank(g)
        for qi, c0 in enumerate((12, 20, 24)):
            P.op('pe', lambda e: e.transpose(bT.t[0:4, qi * 128:(qi + 1) * 128], sm[:, c0:c0 + 4], cst(g, C_IDENT)),
                 reads=[sm, g.consts], writes=[bT], inc=(qi == 2))
        P.op('dve', lambda e: e.tensor_copy(B.rT[:, 0:384], bT.t[0:4, 0:384]), reads=[bT], writes=[B.rT])
        gcT, r2T, ngcT = B.rT[:, 0:128], B.rT[:, 128:256], B.rT[:, 256:384]
        mNLS, mNUS, mNUI = (C_SNLS, C_SNUS, C_SNUI) if is_s else (C_NLS, C_NUS, C_NUI)
        for ei, (moff, rowT, colT) in enumerate(((mNLS, ngcT, r2T), (mNUS, r2T, ngcT), (mNUI, gcT, ngcT))):
            bank = expo_bank(g, B, moff, rowT, colT)
            P.op('act', lambda e: e.activation(B.E[ei][:, :, :], bank.t[:, :].rearrange("p (h n) -> p h n", h=4), AF.Exp), reads=[bank], writes=[B.E[ei]])
        conv_taps(g, L, B, is_s, 0)
        P.op('pool', lambda e: e.tensor_tensor(B.rn[:, :, :], cout[:, 0:8, :], cout[:, 0:8, :], ALU.mult), reads=[cout], writes=[B.rn])
        for hh in range(2):
            bank = nextbank(g)
            P.op('pe', lambda e: e.matmul(bank.t[:, :], cst(g, C_ONES), B.rn[:, hh * 4:hh * 4 + 4, :], start=True, stop=True),
                 reads=[g.consts, B.rn], writes=[bank])
            P.op('act', lambda e: e.activation(B.qkn[:, hh * 4:hh * 4 + 4, :], bank.t[:, :].rearrange("p (c n) -> p c n", c=4), AF.Ln, bias=EPS),
                 reads=[bank], writes=[B.qkn])
        P.op('act', lambda e: e.activation(B.rn[:, :, :], B.qkn[:, :, :], AF.Exp, scale=-0.5), reads=[B.qkn, B.rn], writes=[B.rn])
        P.op('dve', lambda e: e.tensor_tensor(B.qkn[:, :, :], cout[:, 0:8, :], B.rn[:, :, :], ALU.mult), reads=[cout, B.rn], writes=[B.qkn])
        bKT = nextbank(g)
        bVT = nextbank(g)
        for h in range(4):
            P.op('pe', lambda e: e.transpose(bKT.t[:, h * 128:(h + 1) * 128], B.qkn[:, 4 + h, :], cst(g, C_IDENT)), reads=[B.qkn, g.consts], writes=[bKT], inc=(h == 3))
        for h in range(4):
            P.op('pe', lambda e: e.transpose(bVT.t[:, h * 128:(h + 1) * 128], cout[:, 8 + h, :], cst(g, C_IDENT)), reads=[cout, g.consts], writes=[bVT], inc=(h == 3))
        kt3 = bKT.t[:, :].rearrange("p (h n) -> p h n", h=4)
        P.op('dve', lambda e: e.tensor_tensor(B.kbg[:, :, :], kt3, bc(36), ALU.mult), reads=[bKT, sm, B.cin], writes=[B.kbg])
        P.op('dve', lambda e: e.tensor_tensor(B.kd[:, :, :], kt3, bc(32), ALU.mult), reads=[bKT, sm], writes=[B.kd])
        P.op('dve', lambda e: e.tensor_tensor(B.bv[:, :, :], bVT.t[:, :].rearrange("p (h n) -> p h n", h=4), bc(8), ALU.mult), reads=[bVT, sm], writes=[B.bv])
        bKK = nextbank(g)
        bQK = nextbank(g)
        for h in range(4):
            P.op('pe', lambda e: e.matmul(bKK.t[:, h * 128:(h + 1) * 128], B.qkn[:, 4 + h, :], B.qkn[:, 4 + h, :], start=True, stop=True),
                 reads=[B.qkn], writes=[bKK], inc=(h == 3))
        for h in range(4):
            P.op('pe', lambda e: e.matmul(bQK.t[:, h * 128:(h + 1) * 128], B.qkn[:, 4 + h, :], B.qkn[:, h, :], start=True, stop=True),
                 reads=[B.qkn], writes=[bQK], inc=(h == 3))
        kk3 = bKK.t[:, :].rearrange("p (h n) -> p h n", h=4)
        P.op('dve', lambda e: e.scalar_tensor_tensor(B.X[0][:, :, :], B.E[0][:, :, :], -1.0, kk3, ALU.mult, ALU.mult), reads=[B.E[0], bKK], writes=[B.X[0]])
        P.op('dve', lambda e: e.scalar_tensor_tensor(B.Y[0][:, :, :], B.E[1][:, :, :], -1.0, kk3, ALU.mult, ALU.mult), reads=[B.E[1], bKK], writes=[B.Y[0]])
        P.op('dve', lambda e: e.scalar_tensor_tensor(B.QKm[:, :, :], B.E[2][:, :, :], 128.0 ** -0.5, bQK.t[:, :].rearrange("p (h n) -> p h n", h=4),
                                                     ALU.mult, ALU.mult), reads=[B.E[2], bQK], writes=[B.QKm])
        P.op('pool', lambda e: e.tensor_tensor(B.PT[0][:, :, :], B.Y[0][:, :, :], cst(g, C_IDENT).unsqueeze(1).to_broadcast([128, 4, 128]), ALU.add),
             reads=[B.Y[0], g.consts], writes=[B.PT[0]])
        nlev = 2 if is_s else 6
        cur = 0
        pcur = 0

        def emit_P(xi, pc):
            bP = nextbank(g)
            for h in range(4):
                P.op('pe', lambda e: e.matmul(bP.t[:, h * 128:(h + 1) * 128], B.X[xi][:, h, :], B.PT[pc][:, h, :], start=True, stop=True),
                     reads=[B.X[xi], B.PT[pc]], writes=[bP], inc=(h == 3))
            P.op('dve', lambda e: e.tensor_tensor(B.PT[1 - pc][:, :, :], bP.t[:, :].rearrange("p (h n) -> p h n", h=4), B.PT[pc][:, :, :], ALU.add),
                 reads=[bP, B.PT[pc]], writes=[B.PT[1 - pc]])
        pending = None
        for k in range(1, nlev + 1):
            nxt = 1 - cur
            bX = nextbank(g)
            for h in range(4):
                P.op('pe', lambda e: e.matmul(bX.t[:, h * 128:(h + 1) * 128], B.Y[cur][:, h, :], B.X[cur][:, h, :], start=True, stop=True),
                     reads=[B.X[cur], B.Y[cur]], writes=[bX], inc=(h == 3))
            if k < nlev:
                bY = nextbank(g)
                for h in range(4):
                    P.op('pe', lambda e: e.matmul(bY.t[:, h * 128:(h + 1) * 128], B.X[cur][:, h, :], B.Y[cur][:, h, :], start=True, stop=True),
                         reads=[B.X[cur], B.Y[cur]], writes=[bY], inc=(h == 3))
            if pending is not None:
                emit_P(*pending)
                pcur = 1 - pcur
            P.op('act', lambda e: e.activation(B.X[nxt][:, :, :], bX.t[:, :].rearrange("p (h n) -> p h n", h=4), AF.Copy), reads=[bX], writes=[B.X[nxt]])
            if k < nlev:
                P.op('dve', lambda e: e.tensor_copy(B.Y[nxt][:, :, :], bY.t[:, :].rearrange("p (h n) -> p h n", h=4)), reads=[bY], writes=[B.Y[nxt]])
            pending = (nxt, pcur)
            cur = nxt
        emit_P(*pending)
        pcur = 1 - pcur
        cur = pcur
        TTm = B.PT[cur]
        bW = nextbank(g)
        for h in range(4):
            P.op('pe', lambda e: e.matmul(bW.t[:, h * 128:(h + 1) * 128], B.kbg[:, h, :], TTm[:, h, :], start=True, stop=True),
                 reads=[B.kbg, TTm], writes=[bW], inc=(h == 3))
        P.op('act', lambda e: e.activation(B.nwT[:, :, :], bW.t[:, :].rearrange("p (h n) -> p h n", h=4), AF.Copy, scale=-1.0), reads=[bW, cout], writes=[B.nwT])
        bG = nextbank(g)
        for h in range(4):
            esel = g.consts[0:4, C_ESEL + h * 128:C_ESEL + (h + 1) * 128]
            P.op('pe', lambda e: e.matmul(bG.t[:, h * 128:(h + 1) * 128], esel, gcT, start=True, stop=True), reads=[g.consts, B.rT], writes=[bG], inc=(h == 3))
        P.op('act', lambda e: e.activation(B.qgT[:, :, :], bG.t[:, :].rearrange("p (h n) -> p h n", h=4), AF.Exp), reads=[bG], writes=[B.qgT])
        P.op('dve', lambda e: e.scalar_tensor_tensor(B.qgT[:, :, :], B.qkn[:, 0:4, :], 128.0 ** -0.5, B.qgT[:, :, :], ALU.mult, ALU.mult),
             reads=[B.qkn, B.qgT], writes=[B.qgT])
        if not is_s:
            P.op('act', lambda e: e.activation(sm[:, 48:52], sm[:, 16:20], AF.Exp), reads=[sm], writes=[sm])
        else:
            gtv = B.sm2[:, 0:64].rearrange("p (s h) -> p s h", h=4)
            P.op('dve', lambda e: e.tensor_tensor(gtv, sm[:, 16:20].unsqueeze(1).to_broadcast([128, 16, 4]),
                                                  g.consts[:, C_ROWM8:C_ROWM8 + 16].unsqueeze(2).to_broadcast([128, 16, 4]), ALU.mult),
                 reads=[sm, g.consts], writes=[B.sm2])
            bE = nextbank(g)
            P.op('pe', lambda e: e.matmul(bE.t[:, 0:64], cst(g, C_ONES), B.sm2[:, 0:64], start=True, stop=True), reads=[g.consts, B.sm2], writes=[bE])
            P.op('act', lambda e: e.activation(B.egts[:, :, :], bE.t[:, 0:64].rearrange("p (s h) -> p s h", h=4), AF.Exp), reads=[bE], writes=[B.egts])
        bVN = nextbank(g, hold=True)
        bO = nextbank(g, hold=True)
        hgroups = [[0, 1, 2, 3]] if not is_s else [[0], [1], [2], [3]]
        padv = None
        if is_s:
            padd = B.padf.t[:, 0:16 * 136].rearrange("p (s q) -> p s q", q=136)[:, :, 0:8]
            padr = B.padf.t[:, 0:2048].rearrange("p (s n) -> p s n", s=16)
        for hs in hgroups:
            h0, nh = hs[0], len(hs)
            if is_s:
                h = h0
                P.dma('sp', B.Sh[:, :, :], g.st_delta[l][:, h, :, :].rearrange("s d e -> d s e"), "Sh_d", writes=[B.Sh])

                def padcol(src):
                    P.op('pool', lambda e: e.memset(B.padf[:, :], 0.0), writes=[B.padf])
                    P.op('pool', lambda e: e.tensor_copy(padd, src.rearrange("p (s r) -> p s r", r=8)), reads=[B.nwT, B.qgT, B.padf], writes=[B.padf])
            for h in hs:
                o = bVN.t[:, h * 128:(h + 1) * 128]
                P.op('pe', lambda e: e.matmul(o, TTm[:, h, :], B.bv[:, h, :], start=True, stop=False), reads=[TTm, B.bv], writes=[bVN], inc=False)
                if not is_s:
                    P.op('pe', lambda e: e.matmul(o, B.nwT[:, h, :], L.Sd[:, h, :], start=False, stop=True), reads=[B.nwT, L.Sd], writes=[bVN], inc=(h == hs[-1]))
                else:
                    padcol(B.nwT[:, h, :])
                    for s_ in range(16):
                        P.op('pe', lambda e: e.matmul(o, B.padf[:, s_ * 128:(s_ + 1) * 128], B.Sh[:, s_, :], start=False, stop=(s_ == 15)),
                             reads=[B.padf, B.Sh], writes=[bVN], inc=(s_ == 15))
            P.op('act', lambda e: e.activation(B.vnew[:, h0:h0 + nh, :], bVN.t[:, h0 * 128:(h0 + nh) * 128].rearrange("p (h n) -> p h n", h=nh), AF.Copy),
                 reads=[bVN], writes=[B.vnew])
            for h in hs:
                o = bO.t[:, h * 128:(h + 1) * 128]
                P.op('pe', lambda e: e.matmul(o, B.QKm[:, h, :], B.vnew[:, h, :], start=True, stop=False), reads=[B.QKm, B.vnew], writes=[bO], inc=False)
                if not is_s:
                    P.op('pe', lambda e: e.matmul(o, B.qgT[:, h, :], L.Sd[:, h, :], start=False, stop=True), reads=[B.qgT, L.Sd], writes=[bO], inc=(h == hs[-1]))
                else:
                    padcol(B.qgT[:, h, :])
                    for s_ in range(16):
                        P.op('pe', lambda e: e.matmul(o, B.padf[:, s_ * 128:(s_ + 1) * 128], B.Sh[:, s_, :], start=False, stop=(s_ == 15)),
                             reads=[B.padf, B.Sh], writes=[bO], inc=(s_ == 15))
            if not is_s:
                bSu = nextbank(g)
                for h in hs:
                    P.op('pe', lambda e: e.matmul(bSu.t[:, h * 128:(h + 1) * 128], B.kd[:, h, :], B.vnew[:, h, :], start=True, stop=True),
                         reads=[B.kd, B.vnew], writes=[bSu], inc=(h == 3))
                P.op('dve', lambda e: e.tensor_tensor(L.Sd[:, :, :], L.Sd[:, :, :], sm[:, 48:52].unsqueeze(2).to_broadcast([128, 4, 128]), ALU.mult),
                     reads=[L.Sd, sm], writes=[L.Sd])
                P.op('dve', lambda e: e.tensor_tensor(L.Sd[:, :, :], L.Sd[:, :, :], bSu.t[:, :].rearrange("p (h n) -> p h n", h=4), ALU.add),
                     reads=[L.Sd, bSu], writes=[L.Sd])
                if last_prompt:
                    P.dma('sp', g.o_delta_p[l].rearrange("h d e -> d h e"), L.Sd[:, :, :], L.Sd.key + "_o", reads=[L.Sd])
            else:
                h = h0
                P.op('pool', lambda e: e.tensor_tensor(padr, B.kd[:, h, :].unsqueeze(1).to_broadcast([128, 16, 128]),
                                                       g.consts[:, C_ROWM:C_ROWM + 16].unsqueeze(2).to_broadcast([128, 16, 128]), ALU.mult),
                     reads=[B.kd, g.consts], writes=[B.padf])
                for s4 in range(4):
                    bSu = nextbank(g)
                    for si in range(4):
                        s_ = s4 * 4 + si
                        P.op('pe', lambda e: e.matmul(bSu.t[:, si * 128:(si + 1) * 128], B.padf[:, s_ * 128:(s_ + 1) * 128], B.vnew[:, h, :], start=True, stop=True),
                             reads=[B.padf, B.vnew], writes=[bSu], inc=(si == 3))
                    sl = B.Sh[:, s4 * 4:s4 * 4 + 4, :]
                    P.op('dve', lambda e: e.tensor_tensor(sl, sl, B.egts[:, s4 * 4:s4 * 4 + 4, h:h + 1].to_broadcast([128, 4, 128]), ALU.mult),
                         reads=[B.Sh, B.egts], writes=[B.Sh])
                    P.op('dve', lambda e: e.tensor_tensor(sl, sl, bSu.t[:, :].rearrange("p (s n) -> p s n", s=4), ALU.add),
                         reads=[B.Sh, bSu], writes=[B.Sh])
                P.dma('sp', g.o_delta_s[l][:, h, :, :].rearrange("s d e -> d s e"), B.Sh[:, :, :], "Sh_o", reads=[B.Sh])
        P.op('act', lambda e: e.activation(B.o[:, :], bO.t[:, :], AF.Copy), reads=[bO], writes=[B.o])
        release(g, bVN, bO)
        gated_norm(g, L, B, B.o, 4, 128, P_DN_NORM, True, B.on)
        out_proj(g, MS, B, B.on, 4, wo, t)


def gla_mixer(g, L, MS, tiles, hT, is_s):
    P, l = g.P, L.l
    wpart, wo = L.WB, L.WO
    nset = 2 if (not is_s and len(tiles) > 1) else 1
    sets = []
    for i in range(nset):
        S = Ctx()
        S.v = MS.sb("v%d" % i, [128, 512])
        S.gk = MS.sb("gk%d" % i, [128, 256])
        S.gc = MS.sb("gc%d" % i, [128, 256])
        S.Eq = MS.sb("Eq%d" % i, [128, 2, 128])
        S.Ek = MS.sb("Ek%d" % i, [128, 2, 128])
        S.qt = MS.sb("qt%d" % i, [128, 2, 128])
        S.kt = MS.sb("kt%d" % i, [128, 2, 128])
        S.qz = MS.sb("qz%d" % i, [128, 4, 128])
        P.op('pool', lambda e: e.memset(S.qz[:, :, :], 0.0), writes=[S.qz])
        S.PTm = MS.sb("PTm%d" % i, [128, 4, 128])
        S.ktm = MS.sb("ktm%d" % i, [128, 256])
        S.glT = MS.sb("glT%d" % i, [16, 128])
        S.gate = MS.sb("gate%d" % i, [128, 512])
        sets.append(S)
    B = Ctx()
    B.o = MS.sb("o", [128, 512])
    B.sq = MS.sb("sq", [128, 512])
    B.on = MS.sb("on", [128, 512], BF16)
    B.onT = MS.sb("onT", [128, 8, 128], BF16)
    B.sm2 = MS.sb("sm2", [128, 64])
    if is_s:
        B.Sh = MS.sb("Sh", [128, 16, 128])
        B.padf = MS.sb("pad", [128, 17 * 128])
        padd = B.padf.t[:, 0:16 * 136].rearrange("p (s q) -> p s q", q=136)[:, :, 0:8]
        padr = B.padf.t[:, 0:2048].rearrange("p (s n) -> p s n", s=16)
    M_U01 = C_SU01 if is_s else C_U01

    def prefix(ti, t, S):
        tok0 = ti * 128
        bFM = nextbank(g, hold=True)
        for c in range(4):
            proj_fm(g, wpart, c * 128, 128, hT, tok0, bFM, c)
        bGL = nextbank(g)
        proj_fm(g, wpart, 1024, 16, hT, tok0, bGL, 0)
        P.op('dve', lambda e: e.tensor_copy(S.glT[:, :], bGL.t[0:16, 0:128]), reads=[bGL], writes=[S.glT])
        bV = nextbank(g)
        proj_tm(g, wpart, 512, 512, hT, tok0, bV)
        P.op('act', lambda e: e.activation(S.v[:, :], bV.t[:, :], AF.Copy), reads=[bV], writes=[S.v])
        bZ = nextbank(g)
        proj_tm(g, wpart, 1040, 512, hT, tok0, bZ)
        if ti == len(tiles) - 1:
            prefetch_next(g, L)
        B.gate = S.gate
        gate_compute(g, B, [bZ], 512)
        yield
        bK = nextbank(g)
        P.op('pe', lambda e: e.matmul(bK.t[:, 0:256], S.glT[:, :], L.wup[:, :], start=True, stop=False), reads=[S.glT, L.wup], writes=[bK], inc=False)
        P.op('pe', lambda e: e.matmul(bK.t[:, 0:256], g.consts[0:1, C_ONES:C_ONES + 128], L.bup[:, :], start=False, stop=True),
             reads=[g.consts, L.bup], writes=[bK])
        P.op('act', lambda e: e.activation(S.gk[:, :], bK.t[:, 0:256], AF.Exp, scale=-1.0), reads=[bK], writes=[S.gk])
        yield
        P.op('act', lambda e: e.activation(S.gk[:, :], S.gk[:, :], AF.Ln, bias=1.0), reads=[S.gk], writes=[S.gk])
        bC = nextbank(g)
        P.op('pe', lambda e: e.matmul(bC.t[:, 0:256], cst(g, M_U01), S.gk[:, :], start=True, stop=True), reads=[g.consts, S.gk], writes=[bC])
        P.op('dve', lambda e: e.tensor_scalar(S.gc[:, :], bC.t[:, 0:256], -1.0 / 16.0, None, ALU.mult), reads=[bC], writes=[S.gc])
        yield
        bT = nextbank(g)
        for c in range(2):
            P.op('pe', lambda e: e.transpose(bT.t[:, c * 128:(c + 1) * 128], S.gc[:, c * 128:(c + 1) * 128], cst(g, C_IDENT)),
                 reads=[S.gc, g.consts], writes=[bT], inc=(c == 1))
        gT3 = bT.t[:, 0:256].rearrange("p (c n) -> p c n", c=2)
        P.op('act', lambda e: e.activation(S.Eq[:, :, :], gT3, AF.Exp), reads=[bT], writes=[S.Eq])
        P.op('act', lambda e: e.activation(S.Ek[:, :, :], gT3, AF.Exp, scale=-1.0), reads=[bT], writes=[S.Ek])
        yield
        fm3 = bFM.t[:, :].rearrange("p (c n) -> p c n", c=4)
        P.op('dve', lambda e: e.scalar_tensor_tensor(S.qt[:, :, :], fm3[:, 0:2, :], 64.0 ** -0.5, S.Eq[:, :, :], ALU.mult, ALU.mult),
             reads=[bFM, S.Eq], writes=[S.qt])
        P.op('dve', lambda e: e.tensor_tensor(S.kt[:, :, :], fm3[:, 2:4, :], S.Ek[:, :, :], ALU.mult), reads=[bFM, S.Ek], writes=[S.kt])
        release(g, bFM)
        qz4 = S.qz.t[:, :, :].rearrange("p (hh hl) n -> p hl hh n", hl=2)
        P.op('dve', lambda e: e.tensor_copy(qz4[0:64, 0, :, :], S.qt[0:64, :, :]), reads=[S.qt], writes=[S.qz])
        P.op('dve', lambda e: e.tensor_copy(qz4[64:128, 1, :, :], S.qt[64:128, :, :]), reads=[S.qt], writes=[S.qz])
        yield
        bSc = nextbank(g)
        for h in range(4):
            hh = h // 2
            P.op('pe', lambda e: e.matmul(bSc.t[:, h * 128:(h + 1) * 128], S.kt[:, hh, :], S.qz[:, h, :], start=True, stop=True),
                 reads=[S.kt, S.qz], writes=[bSc], inc=(h == 3))
        P.op('dve', lambda e: e.tensor_tensor(S.PTm[:, :, :], bSc.t[:, :].rearrange("p (h n) -> p h n", h=4),
                                              cst(g, M_U01).unsqueeze(1).to_broadcast([128, 4, 128]), ALU.mult), reads=[bSc, g.consts], writes=[S.PTm])
        bKT = nextbank(g)
        for c in range(2):
            P.op('pe', lambda e: e.transpose(bKT.t[:, c * 128:(c + 1) * 128], S.kt[:, c, :], cst(g, C_IDENT)), reads=[S.kt, g.consts], writes=[bKT], inc=(c == 1))
        P.op('act', lambda e: e.activation(S.ktm[:, :], bKT.t[:, 0:256], AF.Copy), reads=[bKT], writes=[S.ktm])
        yield

    def suffix(ti, t, S):
        last_prompt = (not is_s) and (t == g.NPT - 1)
        bO = nextbank(g, hold=True)
        for hh in range(2):
            if is_s:
                P.dma('sp', B.Sh[:, :, :], g.st_gla[l][:, 2 * hh:2 * hh + 2, :, :].rearrange("s hl d e -> (hl d) s e"), "Shg_d", writes=[B.Sh])
            for hl in range(2):
                h = 2 * hh + hl
                o = bO.t[:, h * 128:(h + 1) * 128]
                P.op('pe', lambda e: e.matmul(o, S.PTm[:, h, :], S.v[:, h * 128:(h + 1) * 128], start=True, stop=False), reads=[S.PTm, S.v], writes=[bO], inc=False)
                if not is_s:
                    P.op('pe', lambda e: e.matmul(o, S.qz[:, h, :], L.Sg[:, hh, :], start=False, stop=True), reads=[S.qz, L.Sg], writes=[bO], inc=True)
                else:
                    P.op('pool', lambda e: e.memset(B.padf[:, :], 0.0), writes=[B.padf])
                    P.op('pool', lambda e: e.tensor_copy(padd, S.qz[:, h, :].rearrange("p (s r) -> p s r", r=8)), reads=[S.qz, B.padf], writes=[B.padf])
                    for s_ in range(16):
                        P.op('pe', lambda e: e.matmul(o, B.padf[:, s_ * 128:(s_ + 1) * 128], B.Sh[:, s_, :], start=False, stop=(s_ == 15)),
                             reads=[B.padf, B.Sh], writes=[bO], inc=(s_ == 15))
            if not is_s:
                bSu = nextbank(g)
                P.op('pe', lambda e: e.matmul(bSu.t[:, 0:256], S.ktm[:, hh * 128:(hh + 1) * 128], S.v[:, hh * 256:(hh + 1) * 256], start=True, stop=True),
                     reads=[S.ktm, S.v], writes=[bSu])
                for hl in range(2):
                    r = slice(hl * 64, hl * 64 + 64)
                    P.op('dve', lambda e: e.tensor_tensor(L.Sg[r, hh, :], L.Sg[r, hh, :], bSu.t[r, hl * 128:(hl + 1) * 128], ALU.add), reads=[L.Sg, bSu], writes=[L.Sg])
                    P.op('dve', lambda e: e.tensor_scalar(L.Sg[r, hh, :], L.Sg[r, hh, :], S.Eq[r, hh, 127:128], None, ALU.mult), reads=[L.Sg, S.Eq], writes=[L.Sg])
                yield
            else:
                P.op('pool', lambda e: e.tensor_tensor(padr, S.ktm[:, hh * 128:(hh + 1) * 128].unsqueeze(1).to_broadcast([128, 16, 128]),
                                                       g.consts[:, C_ROWM:C_ROWM + 16].unsqueeze(2).to_broadcast([128, 16, 128]), ALU.mult),
                     reads=[S.ktm, g.consts], writes=[B.padf])
                for s2 in range(8):
                    bSu = nextbank(g)
                    for si in range(2):
                        s_ = s2 * 2 + si
                        P.op('pe', lambda e: e.matmul(bSu.t[:, si * 256:(si + 1) * 256], B.padf[:, s_ * 128:(s_ + 1) * 128], S.v[:, hh * 256:(hh + 1) * 256],
                                                      start=True, stop=True), reads=[B.padf, S.v], writes=[bSu], inc=(si == 1))
                    for hl in range(2):
                        r = slice(hl * 64, hl * 64 + 64)
                        sl = B.Sh[r, 2 * s2:2 * s2 + 2, :]
                        ps_ = bSu.t[r, :].rearrange("p (s c) -> p s c", s=2)[:, :, hl * 128:(hl + 1) * 128]
                        P.op('dve', lambda e: e.tensor_tensor(sl, sl, ps_, ALU.add), reads=[B.Sh, bSu], writes=[B.Sh])
                        egl = S.Eq[r, hh, :].rearrange("p (s r) -> p s r", r=8)[:, 2 * s2:2 * s2 + 2, 7:8].to_broadcast([64, 2, 128])
                        P.op('dve', lambda e: e.tensor_tensor(sl, sl, egl, ALU.mult), reads=[B.Sh, S.Eq], writes=[B.Sh])
                P.dma('sp', g.o_gla_s[l][:, 2 * hh:2 * hh + 2, :, :].rearrange("s hl d e -> (hl d) s e"), B.Sh[:, :, :], "Shg_o", reads=[B.Sh])
        if last_prompt:
            P.dma('sp', g.o_gla_p[l].rearrange("(hh hl) d e -> (hl d) hh e", hl=2), L.Sg[:, :, :], "Sg_o", reads=[L.Sg])
        P.op('act', lambda e: e.activation(B.o[:, :], bO.t[:, :], AF.Copy), reads=[bO], writes=[B.o])
        release(g, bO)
        yield
        gt = S.gate
        sm, sq = B.sm2, B.sq
        P.op('pool', lambda e: e.tensor_tensor(sq[:, :], B.o[:, :], B.o[:, :], ALU.mult), reads=[B.o], writes=[sq])
        P.op('dve', lambda e: e.tensor_reduce(sm[:, 0:4], sq[:, :].rearrange("p (h e) -> p h e", h=4), AX.X, ALU.add), reads=[sq], writes=[sm])
        yield
        P.op('act', lambda e: e.activation(sm[:, 16:20], sm[:, 0:4], AF.Ln, bias=EPS, scale=1.0 / 128), reads=[sm], writes=[sm])
        P.op('act', lambda e: e.activation(sm[:, 32:36], sm[:, 16:20], AF.Exp, scale=-0.5), reads=[sm], writes=[sm])
        yield
        P.op('dve', lambda e: e.tensor_tensor(sq[:, :].rearrange("p (h e) -> p h e", h=4), B.o[:, :].rearrange("p (h e) -> p h e", h=4),
                                              sm[:, 32:36].unsqueeze(2).to_broadcast([128, 4, 128]), ALU.mult), reads=[B.o, sm], writes=[sq])
        P.op('pool', lambda e: e.tensor_tensor(sq[:, :].rearrange("p (h e) -> p h e", h=4), sq[:, :].rearrange("p (h e) -> p h e", h=4),
                                               L.parB[:, P_GLA_NORM:P_GLA_NORM + 128].unsqueeze(1).to_broadcast([128, 4, 128]), ALU.mult),
             reads=[sq, L.parB], writes=[sq])
        yield
        P.op('dve', lambda e: e.tensor_tensor(B.on[:, :], sq[:, :], gt[:, :], ALU.mult), reads=[sq, gt], writes=[B.on])
        yield
        out_proj(g, MS, B, B.on, 4, wo, t)
        yield

    def drain(gens):
        gens = list(gens)
        while gens:
            for gen in list(gens):
                try:
                    next(gen)
                except StopIteration:
                    gens.remove(gen)

    n = len(tiles)
    drain([prefix(0, tiles[0], sets[0])])
    for i in range(n):
        gens = [suffix(i, tiles[i], sets[i % nset])]
        if i + 1 < n:
            gens.append(prefix(i + 1, tiles[i + 1], sets[(i + 1) % nset]))
        drain(gens)


def ssd_mixer(g, L, MS, tiles, hT, is_s):
    P, l = g.P, L.l
    wpart, wo = L.WB, L.WO
    B = Ctx()
    A = Arena(MS, "sA", 5)
    Bc = Arena(MS, "sB", 3)
    G = Arena(MS, "sG", 6)
    Lt = Arena(MS, "sL", 4)
    B.cin = TT(A.tt.t[:, :, :].rearrange("p s f -> p (s f)")[:, 0:12 * 176].rearrange("p (c n) -> p c n", c=12), A.keys(0, 5))
    B.cout = Bc.span(0, 3, 128)
    B.acc = G.span(0, 3, 128)
    B.tmp = G.span(3, 3, 128)
    B.cvo = TT(Lt.tt.t[0:48, 0:3, :].rearrange("p s f -> p (s f)"), Lt.keys(0, 3))
    x_tm = TT(A.tt.t[:, 0:2, :].rearrange("p s f -> p (s f)"), A.keys(0, 2))
    xdt = TT(A.tt.t[:, 2:4, :].rearrange("p s f -> p (s f)"), A.keys(2, 2))
    xw = TT(Bc.tt.t[:, 0:2, :].rearrange("p s f -> p (s f)"), Bc.keys(0, 2))
    ysb = TT(G.tt.t[:, 0:2, :].rearrange("p s f -> p (s f)"), G.keys(0, 2))
    MTs = [G.slot(2, (4, 128)), G.slot(5, (4, 128))]
    ETs = [G.slot(3, (4, 128)), A.slot(4, (4, 128))]
    rTs = [MS.sb("rTa", [4, 256]), MS.sb("rTb", [4, 256])]
    cbT = TT(G.tt.t[:, 4, 0:256].rearrange("p (c n) -> p c n", c=2), G.keys(4, 1))
    Btm = TT(G.tt.t[:, 4, 256:512], G.keys(4, 1))
    B.gate = TT(Lt.tt.t[:, 0:2, :].rearrange("p s f -> p (s f)"), Lt.keys(0, 2))
    gBm = TT(Lt.tt.t[:, 2:4, :].rearrange("p s f -> p (s f)"), Lt.keys(2, 2))
    B.on = MS.sb("on", [128, 1024], BF16)
    B.onT = MS.sb("onT", [128, 8, 128], BF16)
    B.sm2 = MS.sb("sm2", [128, 64])
    sm = MS.sb("sm", [128, 160])
    stc = None
    if is_s:
        stc = TT(G.tt.t[0:48, 0:3, :].rearrange("p s f -> p (s f)"), G.keys(0, 3))
        P.dma('sp', stc[:, :], g.st_conv[l][:, 1536:3072], "stcs_d", writes=[stc])
        B.Sh = [MS.sb("Sh0", [128, 8, 64]), MS.sb("Sh1", [128, 8, 64])]
        B.padf = MS.sb("pad", [128, 17 * 128])
        padd = B.padf.t[:, 0:16 * 136].rearrange("p (s q) -> p s q", q=136)[:, :, 0:8]
        padr = B.padf.t[:, 0:2048].rearrange("p (s n) -> p s n", s=16)
        B.egts = MS.sb("egts", [128, 16, 16])
    M_U01 = C_SU01 if is_s else C_U01
    M_SEG = C_SEG if is_s else C_ONES
    mNUI = C_SNUI if is_s else C_NUI
    nsh = 0
    for ti, t in enumerate(tiles):
        tok0 = ti * 128
        last_prompt = (not is_s) and (t == g.NPT - 1)
        if ti == 0:
            g.minfree = min(getattr(g, 'minfree', 1 << 30), g.nc.sbuf_bytes_remaining)
        conv_block(g, L, MS, B, wpart, hT, tok0, is_s, 12, L.tailS, stc, last_prompt, 1536)
        cout = B.cout
        bD = nextbank(g)
        proj_tm(g, wpart, 2560, 16, hT, tok0, bD)
        bZ = [nextbank(g), nextbank(g)]
        proj_tm(g, wpart, 1536, 512, hT, tok0, bZ[0])
        proj_tm(g, wpart, 2048, 512, hT, tok0, bZ[1])
        if ti == len(tiles) - 1:
            prefetch_next(g, L)
        gate_compute(g, B, bZ, 1024)
        P.op('dve', lambda e: e.tensor_tensor(sm[:, 128:144], bD.t[:, 0:16], L.parB[:, P_M2_DTB:P_M2_DTB + 16], ALU.add), reads=[bD, L.parB], writes=[sm])
        P.op('act', lambda e: e.activation(sm[:, 128:144], sm[:, 128:144], AF.Exp), reads=[sm], writes=[sm])
        P.op('act', lambda e: e.activation(sm[:, 0:16], sm[:, 128:144], AF.Ln, bias=1.0), reads=[sm], writes=[sm])
        P.op('dve', lambda e: e.tensor_tensor(sm[:, 16:32], sm[:, 0:16], L.pd[:, 4:20], ALU.mult), reads=[sm, L.pd], writes=[sm])
        bC = nextbank(g)
        P.op('pe', lambda e: e.matmul(bC.t[:, 0:16], cst(g, M_U01), sm[:, 16:32], start=True, stop=True), reads=[g.consts, sm], writes=[bC], inc=False)
        P.op('pe', lambda e: e.matmul(bC.t[:, 16:32], cst(g, M_SEG), sm[:, 16:32], start=True, stop=True), reads=[g.consts, sm], writes=[bC])
        P.op('dve', lambda e: e.tensor_copy(sm[:, 32:64], bC.t[:, 0:32]), reads=[bC], writes=[sm])
        P.op('dve', lambda e: e.tensor_scalar(sm[:, 64:80], sm[:, 32:48], -1.0, None, ALU.mult), reads=[sm], writes=[sm])
        P.op('act', lambda e: e.activation(sm[:, 80:96], sm[:, 32:48], AF.Exp), reads=[sm], writes=[sm])
        P.op('dve', lambda e: e.tensor_tensor(sm[:, 96:112], sm[:, 48:64], sm[:, 32:48], ALU.subtract), reads=[sm], writes=[sm])
        P.op('act', lambda e: e.activation(sm[:, 96:112], sm[:, 96:112], AF.Exp), reads=[sm], writes=[sm])
        P.op('act', lambda e: e.activation(sm[:, 112:128], sm[:, 48:64], AF.Exp), reads=[sm], writes=[sm])

        def bc16(c0, g_=None):
            if g_ is None:
                return sm[:, c0:c0 + 16].unsqueeze(2).to_broadcast([128, 16, 64])
            return sm[:, c0 + 8 * g_:c0 + 8 * g_ + 8].unsqueeze(2).to_broadcast([128, 8, 64])
        conv_taps(g, L, B, is_s, 12, split=True)
        for hb in range(2):
            bX = nextbank(g)
            for c in range(4):
                P.op('pe', lambda e: e.transpose(bX.t[:, c * 128:(c + 1) * 128], cout[:, hb * 4 + c, :], cst(g, C_IDENT)), reads=[cout, g.consts], writes=[bX], inc=(c == 3))
            P.op('act', lambda e: e.activation(x_tm[:, hb * 512:(hb + 1) * 512], bX.t[:, :], AF.Copy), reads=[bX, B.cin], writes=[x_tm])
        x3 = x_tm.t.rearrange("p (h e) -> p h e", h=16)
        xdt3 = xdt.t.rearrange("p (h e) -> p h e", h=16)
        xw3 = xw.t.rearrange("p (h e) -> p h e", h=16)
        P.op('dve', lambda e: e.tensor_tensor(xdt3, x3, bc16(0), ALU.mult), reads=[x_tm, sm], writes=[xdt])
        P.op('pool', lambda e: e.tensor_tensor(xw3, xdt3, bc16(96), ALU.mult), reads=[xdt, sm, cout], writes=[xw])
        bB = nextbank(g)
        for g_ in range(2):
            P.op('pe', lambda e: e.transpose(bB.t[:, g_ * 128:(g_ + 1) * 128], cout[:, 8 + g_, :], cst(g, C_IDENT)), reads=[cout, g.consts], writes=[bB], inc=False)
        for g_ in range(2):
            P.op('pe', lambda e: e.matmul(bB.t[:, 256 + g_ * 128:256 + (g_ + 1) * 128], cout[:, 8 + g_, :], cout[:, 10 + g_, :], start=True, stop=True),
                 reads=[cout], writes=[bB], inc=(g_ == 1))
        P.op('act', lambda e: e.activation(Btm[:, :], bB.t[:, 0:256], AF.Copy), reads=[bB, B.acc, B.tmp], writes=[Btm])
        P.op('act', lambda e: e.activation(cbT[:, :, :], bB.t[:, 256:512].rearrange("p (c n) -> p c n", c=2), AF.Copy), reads=[bB], writes=[cbT])
        bY = [nextbank(g, hold=True), nextbank(g, hold=True)]
        def stageA(q):
            MT, ET, rT = MTs[q % 2], ETs[q % 2], rTs[q % 2]
            bT = nextbank(g)
            P.op('pe', lambda e: e.transpose(bT.t[0:4, 0:128], sm[:, 32 + 4 * q:36 + 4 * q], cst(g, C_IDENT)), reads=[sm, g.consts], writes=[bT], inc=False)
            P.op('pe', lambda e: e.transpose(bT.t[0:4, 128:256], sm[:, 64 + 4 * q:68 + 4 * q], cst(g, C_IDENT)), reads=[sm, g.consts], writes=[bT])
            P.op('dve', lambda e: e.tensor_copy(rT[:, 0:256], bT.t[0:4, 0:256]), reads=[bT], writes=[rT])
            bank = expo_bank(g, B, mNUI, rT[:, 0:128], rT[:, 128:256], rtt=rT)
            P.op('act', lambda e: e.activation(ET[:, :, :], bank.t[:, :].rearrange("p (h n) -> p h n", h=4), AF.Exp), reads=[bank], writes=[ET])
            g_ = q // 2
            P.op('dve', lambda e: e.tensor_tensor(MT[:, :, :], ET[:, :, :], cbT[:, g_, :].unsqueeze(1).to_broadcast([128, 4, 128]), ALU.mult),
                 reads=[ET, cbT], writes=[MT])

        def stageB(q):
            MT = MTs[q % 2]
            g_ = q // 2
            for hq in range(4):
                h = 4 * q + hq
                P.op('pe', lambda e: e.matmul(bY[g_].t[:, (h % 8) * 64:(h % 8) * 64 + 64], MT[:, hq, :], xdt[:, h * 64:(h + 1) * 64], start=True, stop=True),
                     reads=[MT, xdt], writes=[bY[g_]], inc=(hq == 3))
        stageA(0)
        for q in range(1, 4):
            stageA(q)
            stageB(q - 1)
        stageB(3)
        bYI = [nextbank(g, hold=True), nextbank(g, hold=True)]
        if not is_s:
            for g_ in range(2):
                P.op('pe', lambda e: e.matmul(bYI[g_].t[:, :], cout[:, 10 + g_, :], L.Ss.t[:, :, :].rearrange("p h e -> p (h e)")[:, g_ * 512:(g_ + 1) * 512], start=True, stop=True),
                     reads=[cout, L.Ss], writes=[bYI[g_]])
        else:
            lt3 = B.padf.t[:, 0:256].rearrange("p (s h) -> p s h", h=16)
            P.op('dve', lambda e: e.tensor_tensor(lt3, sm[:, 48:64].unsqueeze(1).to_broadcast([128, 16, 16]),
                                                  g.consts[:, C_ROWM8:C_ROWM8 + 16].unsqueeze(2).to_broadcast([128, 16, 16]), ALU.mult),
                 reads=[sm, g.consts], writes=[B.padf])
            bE = nextbank(g)
            P.op('pe', lambda e: e.matmul(bE.t[:, 0:256], cst(g, C_ONES), B.padf[:, 0:256], start=True, stop=True), reads=[g.consts, B.padf], writes=[bE])
            P.op('act', lambda e: e.activation(B.egts[:, :, :], bE.t[:, 0:256].rearrange("p (s h) -> p s h", h=16), AF.Exp), reads=[bE], writes=[B.egts])
            for g_ in range(2):
                P.op('pool', lambda e: e.memset(B.padf[:, :], 0.0), writes=[B.padf])
                P.op('pool', lambda e: e.tensor_copy(padd, cout[:, 10 + g_, :].rearrange("p (s r) -> p s r", r=8)), reads=[cout, B.padf], writes=[B.padf])
                for s_ in range(16):
                    Sh = B.Sh[nsh % 2]
                    nsh += 1
                    P.dma('sp', Sh[:, :, :], g.st_ssm[l][s_, 8 * g_:8 * g_ + 8, :, :].rearrange("h n p -> n h p"), "Shs%d_d" % ((nsh - 1) % 2), writes=[Sh])
                    P.op('pe', lambda e: e.matmul(bYI[g_].t[:, :], B.padf[:, s_ * 128:(s_ + 1) * 128], Sh.t[:, :, :].rearrange("p h e -> p (h e)"), start=(s_ == 0), stop=(s_ == 15)),
                         reads=[B.padf, Sh], writes=[bYI[g_]], inc=True)
        y3 = ysb.t.rearrange("p (h e) -> p h e", h=16)
        for g_ in range(2):
            P.op('dve', lambda e: e.tensor_tensor(y3[:, 8 * g_:8 * g_ + 8, :], bYI[g_].t[:, :].rearrange("p (h e) -> p h e", h=8), bc16(80, g_), ALU.mult),
                 reads=[bYI[g_], sm], writes=[ysb])
            P.op('dve', lambda e: e.tensor_tensor(ysb[:, g_ * 512:(g_ + 1) * 512], ysb[:, g_ * 512:(g_ + 1) * 512], bY[g_].t[:, :], ALU.add),
                 reads=[ysb, bY[g_]], writes=[ysb])
        release(g, bY[0], bY[1], bYI[0], bYI[1])
        P.op('pool', lambda e: e.tensor_tensor(xdt3, x3, L.parB[:, P_M2_D:P_M2_D + 16].unsqueeze(2).to_broadcast([128, 16, 64]), ALU.mult),
             reads=[x_tm, L.parB, xdt], writes=[xdt])
        P.op('dve', lambda e: e.tensor_tensor(ysb[:, :], ysb[:, :], xdt[:, :], ALU.add), reads=[ysb, xdt], writes=[ysb])
        P.dma('sp', gBm[:, :], g.norms_d[3 * g.DEPTH + 1 + l:3 * g.DEPTH + 2 + l, :].partition_broadcast(128), "gBm_d", reads=[B.cvo], writes=[gBm])
        gated_norm(g, L, B, ysb, 16, 64, 0, False, B.on, norm_tt=gBm)
        out_proj(g, MS, B, B.on, 8, wo, t)
        if not is_s:
            for g_ in range(2):
                bSu = nextbank(g)
                P.op('pe', lambda e: e.matmul(bSu.t[:, :], Btm[:, g_ * 128:(g_ + 1) * 128], xw[:, g_ * 512:(g_ + 1) * 512], start=True, stop=True),
                     reads=[Btm, xw], writes=[bSu])
                sl = L.Ss[:, 8 * g_:8 * g_ + 8, :]
                P.op('dve', lambda e: e.tensor_tensor(sl, sl, bc16(112, g_), ALU.mult), reads=[L.Ss, sm], writes=[L.Ss])
                P.op('dve', lambda e: e.tensor_tensor(sl, sl, bSu.t[:, :].rearrange("p (h e) -> p h e", h=8), ALU.add), reads=[L.Ss, bSu], writes=[L.Ss])
            if last_prompt:
                P.dma('sp', g.o_ssm_p[l].rearrange("h n p -> n h p"), L.Ss[:, :, :], "Ss_o", reads=[L.Ss])
        else:
            for g_ in range(2):
                P.op('pool', lambda e: e.tensor_tensor(padr, Btm[:, g_ * 128:(g_ + 1) * 128].unsqueeze(1).to_broadcast([128, 16, 128]),
                                                       g.consts[:, C_ROWM:C_ROWM + 16].unsqueeze(2).to_broadcast([128, 16, 128]), ALU.mult),
                     reads=[Btm, g.consts], writes=[B.padf])
                for s_ in range(16):
                    Sh = B.Sh[nsh % 2]
                    nsh += 1
                    P.dma('sp', Sh[:, :, :], g.st_ssm[l][s_, 8 * g_:8 * g_ + 8, :, :].rearrange("h n p -> n h p"), "Shs%d_d" % ((nsh - 1) % 2), writes=[Sh])
                    bSu = nextbank(g)
                    P.op('pe', lambda e: e.matmul(bSu.t[:, :], B.padf[:, s_ * 128:(s_ + 1) * 128], xw[:, g_ * 512:(g_ + 1) * 512], start=True, stop=True),
                         reads=[B.padf, xw], writes=[bSu])
                    P.op('dve', lambda e: e.tensor_tensor(Sh[:, :, :], Sh[:, :, :], B.egts[:, s_, 8 * g_:8 * g_ + 8].unsqueeze(2).to_broadcast([128, 8, 64]), ALU.mult),
                         reads=[Sh, B.egts], writes=[Sh])
                    P.op('dve', lambda e: e.tensor_tensor(Sh[:, :, :], Sh[:, :, :], bSu.t[:, :].rearrange("p (h e) -> p h e", h=8), ALU.add),
                         reads=[Sh, bSu], writes=[Sh])
                    P.dma('act', g.o_ssm_s[l][s_, 8 * g_:8 * g_ + 8, :, :].rearrange("h n p -> n h p"), Sh[:, :, :], "Shs%d_o" % ((nsh - 1) % 2), reads=[Sh])


def make_in_maps(inp, ncores=8, NPT=16, DEPTH=2):
    f = lambda a: np.ascontiguousarray(np.asarray(a, dtype=np.float32))
    consts = make_consts()
    params = np.zeros((DEPTH, NPAR), np.float32)
    for l in range(DEPTH):
        params[l, P_DN_ALOG:P_DN_ALOG + 4] = inp['dn_a_log'][l]
        params[l, P_DN_DTB:P_DN_DTB + 4] = inp['dn_dt_bias'][l]
        params[l, P_M2_ALOG:P_M2_ALOG + 16] = inp['m2_a_log'][l]
        params[l, P_M2_DTB:P_M2_DTB + 16] = inp['m2_dt_bias'][l]
        params[l, P_M2_D:P_M2_D + 16] = inp['m2_d'][l]
        params[l, P_DN_NORM:P_DN_NORM + 128] = inp['dn_norm'][l]
        params[l, P_GLA_NORM:P_GLA_NORM + 128] = inp['gla_norm'][l]
    norms = np.zeros((4 * DEPTH + 1, D), np.float32)
    for l in range(DEPTH):
        norms[3 * l] = inp['ffn1_norm'][l]
        norms[3 * l + 1] = inp['mix_norm'][l]
        norms[3 * l + 2] = inp['ffn2_norm'][l]
    norms[3 * DEPTH] = inp['final_norm']
    for l in range(DEPTH):
        norms[3 * DEPTH + 1 + l] = inp['m2_norm'][l]
    convw = f(np.asarray(inp['conv_w'])[:DEPTH].reshape(DEPTH, 4, 24, 128).transpose(0, 3, 2, 1).reshape(DEPTH, 128, 96))
    convb = f(np.asarray(inp['conv_b'])[:DEPTH].reshape(DEPTH, 24, 128).transpose(0, 2, 1))
    shared = {
        'consts': consts, 'params': params, 'norms': norms, 'convw': convw, 'convb': convb,
        'ffn1_w_gu': f(inp['ffn1_w_gu'][:DEPTH]), 'ffn2_w_gu': f(inp['ffn2_w_gu'][:DEPTH]),
        'ffn1_w_down': f(inp['ffn1_w_down'][:DEPTH]), 'ffn2_w_down': f(inp['ffn2_w_down'][:DEPTH]),
        'w_in': f(inp['w_in'][:DEPTH]), 'w_out': f(inp['w_out'][:DEPTH]),
        'gla_w_up': f(inp['gla_w_up'][:DEPTH]), 'gla_b_up': f(np.asarray(inp['gla_b_up'])[:DEPTH].reshape(DEPTH, 1, 256)),
    }
    maps = []
    for c in range(ncores):
        m = dict(shared)
        xp = np.asarray(inp['x_prompt'])[c, :NPT * 128]
        xs = np.asarray(inp['x_sample'])[16 * c:16 * c + 16].reshape(128, D)
        m['xin'] = f(np.concatenate([xp, xs], 0))
        m['st_conv'] = f(np.asarray(inp['state_conv'])[:DEPTH, 16 * c:16 * c + 16].reshape(DEPTH, 48, CONV_CH))
        m['st_delta'] = f(np.asarray(inp['state_delta'])[:DEPTH, 16 * c:16 * c + 16])
        m['st_gla'] = f(np.asarray(inp['state_gla'])[:DEPTH, 16 * c:16 * c + 16])
        m['st_ssm'] = f(np.asarray(inp['state_ssm'])[:DEPTH, 16 * c:16 * c + 16])
        maps.append(m)
    return maps


def gather_outputs(res, ncores=8, NPT=16, DEPTH=2):
    yp = np.stack([r['y'][:NPT * 128] for r in res], 0)
    ys = np.concatenate([r['y'][NPT * 128:].reshape(16, 8, D) for r in res], 0)
    cp = np.stack([r['o_conv_p'] for r in res], 1)
    dp = np.stack([r['o_delta_p'] for r in res], 1)
    gp = np.stack([r['o_gla_p'] for r in res], 1)
    sp = np.stack([r['o_ssm_p'] for r in res], 1)
    cs = np.concatenate([r['o_conv_s'].reshape(DEPTH, 16, 3, CONV_CH) for r in res], 1)
    ds = np.concatenate([r['o_delta_s'] for r in res], 1)
    gs = np.concatenate([r['o_gla_s'] for r in res], 1)
    ss = np.concatenate([r['o_ssm_s'] for r in res], 1)
    return tuple(np.ascontiguousarray(a, dtype=np.float32) for a in (yp, ys, cp, dp, gp, sp, cs, ds, gs, ss))


_NC_CACHE = {}


def kernel(**inputs):
    if 'nc' not in _NC_CACHE:
        _NC_CACHE['nc'] = build()[0]
    nc = _NC_CACHE['nc']
    in_maps = make_in_maps(inputs)
    res = run_bass_kernel_spmd(nc, in_maps, core_ids=list(range(8)))
    return gather_outputs(res.results)
```
